# Optimizing a Trainium2 kernel written in Bass

```python
import math
import jax, jax.numpy as jnp
from jax import lax
import numpy as np

D_MODEL = 1024
BATCH = 8
SEQ = 4096
DEPTH = 2

CHUNK = 64
PLE_DIM = 256
LRU_WIDTH = 1280
LRU_HEADS = 10
LRU_HEAD_DIM = LRU_WIDTH // LRU_HEADS
LRU_CONV = 4
LRU_C = 8.0
S5_WIDTH = D_MODEL
S5_GROUP = 16
S5_GROUPS = S5_WIDTH // S5_GROUP
S5_STATE = 64
D_FF = 3 * D_MODEL
FFN_CONV = 3
IN_COLS = 2 * LRU_WIDTH + S5_WIDTH + 2 * D_MODEL
EPS = 1e-6

kernel_name = "griffin_s5_convffn_hybrid"


def rmsnorm(x, g):
    xf = x.astype(jnp.float32)
    y = xf * lax.rsqrt(jnp.mean(xf * xf, axis=-1, keepdims=True) + EPS)
    return (y * g.astype(jnp.float32)).astype(x.dtype)


def causal_dwconv(x, w, b):
    k = w.shape[0]
    s = x.shape[1]
    xp = jnp.pad(x, ((0, 0), (k - 1, 0), (0, 0)))
    y = b
    for j in range(k):
        y = y + xp[:, j:j + s] * w[j]
    return y


def rglru_branch(xa, gate_br, conv_w, conv_b, gx_w, gx_b, ga_w, ga_b, lam):
    xc = causal_dwconv(xa, conv_w, conv_b)
    bsz, s, _ = xc.shape
    xh = xc.reshape(bsz, s, LRU_HEADS, LRU_HEAD_DIM)
    gx = jax.nn.sigmoid(jnp.einsum('bshi,hij->bshj', xh, gx_w).reshape(bsz, s, LRU_WIDTH) + gx_b)
    ga = jax.nn.sigmoid(jnp.einsum('bshi,hij->bshj', xh, ga_w).reshape(bsz, s, LRU_WIDTH) + ga_b)
    log_a = (LRU_C * ga.astype(jnp.float32)) * jax.nn.log_sigmoid(lam.astype(jnp.float32))
    a = jnp.exp(log_a)
    mult = jnp.sqrt(jnp.maximum(-jnp.expm1(2.0 * log_a), 1e-12))
    u = mult * (gx * xc).astype(jnp.float32)

    def combine(e1, e2):
        return (e1[0] * e2[0], e2[0] * e1[1] + e2[1])

    _, h = lax.associative_scan(combine, (a, u), axis=1)
    return h.astype(xa.dtype) * jax.nn.gelu(gate_br)


def s5_ssm(u, a_re, a_im, log_dt, b_re, b_im, c_re, c_im, d):
    f32 = jnp.float32
    a_re = a_re.astype(f32); a_im = a_im.astype(f32)
    b_re = b_re.astype(f32); b_im = b_im.astype(f32)
    c_re = c_re.astype(f32); c_im = c_im.astype(f32)
    dt = jnp.exp(log_dt.astype(f32))[:, None]
    mag = jnp.exp(a_re * dt)
    lb_re = mag * jnp.cos(a_im * dt)
    lb_im = mag * jnp.sin(a_im * dt)
    nr = lb_re - 1.0
    ni = lb_im
    den = a_re * a_re + a_im * a_im
    coef_re = (nr * a_re + ni * a_im) / den
    coef_im = (ni * a_re - nr * a_im) / den
    bb_re = coef_re[..., None] * b_re - coef_im[..., None] * b_im
    bb_im = coef_re[..., None] * b_im + coef_im[..., None] * b_re

    bsz, s, _ = u.shape
    nc = s // CHUNK
    uc = u.astype(f32).reshape(bsz, nc, CHUNK, S5_GROUPS, S5_GROUP).transpose(1, 2, 0, 3, 4)
    a_el_re = jnp.broadcast_to(lb_re, (CHUNK, 1, S5_GROUPS, S5_STATE))
    a_el_im = jnp.broadcast_to(lb_im, (CHUNK, 1, S5_GROUPS, S5_STATE))

    def combine(e1, e2):
        a1r, a1i, b1r, b1i = e1
        a2r, a2i, b2r, b2i = e2
        return (a2r * a1r - a2i * a1i,
                a2r * a1i + a2i * a1r,
                a2r * b1r - a2i * b1i + b2r,
                a2r * b1i + a2i * b1r + b2i)

    def step(carry, u_blk):
        c_r, c_i = carry
        bu_re = jnp.einsum('lbgc,gpc->lbgp', u_blk, bb_re)
        bu_im = jnp.einsum('lbgc,gpc->lbgp', u_blk, bb_im)
        ar, ai, sr, si = lax.associative_scan(combine, (a_el_re, a_el_im, bu_re, bu_im), axis=0)
        x_re = sr + ar * c_r[None] - ai * c_i[None]
        x_im = si + ar * c_i[None] + ai * c_r[None]
        y = jnp.einsum('lbgp,gcp->lbgc', x_re, c_re) - jnp.einsum('lbgp,gcp->lbgc', x_im, c_im)
        return (x_re[-1], x_im[-1]), y

    init = (jnp.zeros((bsz, S5_GROUPS, S5_STATE), f32), jnp.zeros((bsz, S5_GROUPS, S5_STATE), f32))
    _, y = lax.scan(step, init, uc)
    y = y.transpose(2, 0, 1, 3, 4).reshape(bsz, s, S5_WIDTH)
    return (y + d.astype(f32) * u.astype(f32)).astype(u.dtype)


def setup_inputs(seed: int = 0) -> dict:
    key = jax.random.key(seed)
    ks = jax.random.split(key, 40)
    nrm = lambda k, shape, scale: jax.random.normal(k, shape, jnp.float32) * scale
    x = jax.random.normal(ks[0], (BATCH, SEQ, D_MODEL), jnp.float32)
    p = jax.random.normal(ks[1], (DEPTH, BATCH, SEQ, PLE_DIM), jnp.float32)
    g_mix = 1.0 + nrm(ks[2], (DEPTH, D_MODEL), 0.02)
    w_in = nrm(ks[3], (DEPTH, D_MODEL, IN_COLS), D_MODEL ** -0.5)
    conv_a_w = nrm(ks[4], (DEPTH, LRU_CONV, LRU_WIDTH), LRU_CONV ** -0.5)
    conv_a_b = nrm(ks[5], (DEPTH, LRU_WIDTH), 0.01)
    gate_x_w = nrm(ks[6], (DEPTH, LRU_HEADS, LRU_HEAD_DIM, LRU_HEAD_DIM), LRU_HEAD_DIM ** -0.5)
    gate_x_b = nrm(ks[7], (DEPTH, LRU_WIDTH), 0.01)
    gate_a_w = nrm(ks[8], (DEPTH, LRU_HEADS, LRU_HEAD_DIM, LRU_HEAD_DIM), LRU_HEAD_DIM ** -0.5)
    gate_a_b = nrm(ks[9], (DEPTH, LRU_WIDTH), 0.01)
    a_target = jax.random.uniform(ks[10], (DEPTH, LRU_WIDTH), jnp.float32, 0.9, 0.999)
    a0 = a_target ** (1.0 / LRU_C)
    lru_lambda = jnp.log(a0) - jnp.log1p(-a0)
    w_a_out = nrm(ks[11], (DEPTH, LRU_WIDTH, D_MODEL), LRU_WIDTH ** -0.5)
    s5_a_re = -0.5 + nrm(ks[12], (DEPTH, S5_GROUPS, S5_STATE), 0.01)
    s5_a_im = jnp.pi * jnp.arange(S5_STATE, dtype=jnp.float32) + nrm(ks[13], (DEPTH, S5_GROUPS, S5_STATE), 0.01)
    s5_log_dt = jax.random.uniform(ks[14], (DEPTH, S5_GROUPS), jnp.float32, math.log(1e-3), math.log(1e-1))
    s5_b_re = nrm(ks[15], (DEPTH, S5_GROUPS, S5_STATE, S5_GROUP), (2 * S5_GROUP) ** -0.5)
    s5_b_im = nrm(ks[16], (DEPTH, S5_GROUPS, S5_STATE, S5_GROUP), (2 * S5_GROUP) ** -0.5)
    s5_c_re = nrm(ks[17], (DEPTH, S5_GROUPS, S5_GROUP, S5_STATE), (2 * S5_STATE) ** -0.5)
    s5_c_im = nrm(ks[18], (DEPTH, S5_GROUPS, S5_GROUP, S5_STATE), (2 * S5_STATE) ** -0.5)
    s5_d = nrm(ks[19], (DEPTH, S5_WIDTH), 1.0)
    w_glu = nrm(ks[20], (DEPTH, S5_WIDTH, 2 * D_MODEL), S5_WIDTH ** -0.5)
    b_glu = nrm(ks[21], (DEPTH, 2 * D_MODEL), 0.01)
    w_o = nrm(ks[22], (DEPTH, D_MODEL, D_MODEL), D_MODEL ** -0.5)
    g_ffn = 1.0 + nrm(ks[23], (DEPTH, D_MODEL), 0.02)
    w_up = nrm(ks[24], (DEPTH, D_MODEL, 2 * D_FF), D_MODEL ** -0.5)
    conv_f_w = nrm(ks[25], (DEPTH, FFN_CONV, 2 * D_FF), FFN_CONV ** -0.5)
    conv_f_b = nrm(ks[26], (DEPTH, 2 * D_FF), 0.01)
    w_down = nrm(ks[27], (DEPTH, D_FF, D_MODEL), D_FF ** -0.5)
    g_ple = 1.0 + nrm(ks[28], (DEPTH, D_MODEL), 0.02)
    w_ple_gate = nrm(ks[29], (DEPTH, D_MODEL, D_MODEL), D_MODEL ** -0.5)
    w_ple_proj = nrm(ks[30], (DEPTH, PLE_DIM, D_MODEL), PLE_DIM ** -0.5)
    g_final = 1.0 + nrm(ks[31], (D_MODEL,), 0.02)
    return {"x": x, "p": p, "g_mix": g_mix, "w_in": w_in, "conv_a_w": conv_a_w, "conv_a_b": conv_a_b,
            "gate_x_w": gate_x_w, "gate_x_b": gate_x_b, "gate_a_w": gate_a_w, "gate_a_b": gate_a_b,
            "lru_lambda": lru_lambda, "w_a_out": w_a_out, "s5_a_re": s5_a_re, "s5_a_im": s5_a_im,
            "s5_log_dt": s5_log_dt, "s5_b_re": s5_b_re, "s5_b_im": s5_b_im, "s5_c_re": s5_c_re,
            "s5_c_im": s5_c_im, "s5_d": s5_d, "w_glu": w_glu, "b_glu": b_glu, "w_o": w_o,
            "g_ffn": g_ffn, "w_up": w_up, "conv_f_w": conv_f_w, "conv_f_b": conv_f_b, "w_down": w_down,
            "g_ple": g_ple, "w_ple_gate": w_ple_gate, "w_ple_proj": w_ple_proj, "g_final": g_final}


def reference(x, p, g_mix, w_in, conv_a_w, conv_a_b, gate_x_w, gate_x_b, gate_a_w, gate_a_b,
              lru_lambda, w_a_out, s5_a_re, s5_a_im, s5_log_dt, s5_b_re, s5_b_im, s5_c_re,
              s5_c_im, s5_d, w_glu, b_glu, w_o, g_ffn, w_up, conv_f_w, conv_f_b, w_down,
              g_ple, w_ple_gate, w_ple_proj, g_final):
    o1 = LRU_WIDTH
    o2 = 2 * LRU_WIDTH
    o3 = o2 + S5_WIDTH
    o4 = o3 + D_MODEL
    for i in range(DEPTH):
        h = rmsnorm(x, g_mix[i])
        z = h @ w_in[i]
        xa, gate_br, ub = z[..., :o1], z[..., o1:o2], z[..., o2:o3]
        m_a, m_b = z[..., o3:o4], z[..., o4:]
        ya = rglru_branch(xa, gate_br, conv_a_w[i], conv_a_b[i], gate_x_w[i], gate_x_b[i],
                          gate_a_w[i], gate_a_b[i], lru_lambda[i]) @ w_a_out[i]
        ys = jax.nn.gelu(s5_ssm(ub, s5_a_re[i], s5_a_im[i], s5_log_dt[i], s5_b_re[i], s5_b_im[i],
                                s5_c_re[i], s5_c_im[i], s5_d[i]))
        glu = ys @ w_glu[i] + b_glu[i]
        yb = glu[..., :D_MODEL] * jax.nn.sigmoid(glu[..., D_MODEL:])
        merged = jax.nn.sigmoid(m_a) * ya + jax.nn.sigmoid(m_b) * yb
        x = x + merged @ w_o[i]
        h = rmsnorm(x, g_ffn[i])
        up = causal_dwconv(h @ w_up[i], conv_f_w[i], conv_f_b[i])
        x = x + (jax.nn.gelu(up[..., :D_FF]) * up[..., D_FF:]) @ w_down[i]
        gate = jax.nn.sigmoid(rmsnorm(x, g_ple[i]) @ w_ple_gate[i])
        x = x + gate * (p[i] @ w_ple_proj[i])
    return rmsnorm(x, g_final)
```

```python
import math
from contextlib import ExitStack

import numpy as np
import concourse.bass as bass
import concourse.mybir as mybir
from concourse.bass_utils import run_bass_kernel_spmd

F32 = mybir.dt.float32
BF16 = mybir.dt.bfloat16
I32 = mybir.dt.int32
AF = mybir.ActivationFunctionType
ALU = mybir.AluOpType

SAME_ENGINE_SYNC = True
NSLOT = 4
TB = 512
NT = 4
SEQ = 4096
EPS = 1e-6
PI = math.pi


class Prog:
    def __init__(self, nc, es):
        self.nc = nc
        self.es = es
        self.eng = {"pe": nc.tensor, "act": nc.scalar, "dve": nc.vector,
                    "pool": nc.gpsimd, "sp": nc.sync}
        self.ops = []
        self.state = {}
        self.alias = {}

    @staticmethod
    def _split(key):
        if "/" in key:
            r, s = key.split("/", 1)
            return r, s
        return key, None

    def _st(self, root):
        return self.state.setdefault(root, {"w": None, "r": [], "subs": {}})

    def _deps_read(self, key, me):
        root, sub = self._split(key)
        st = self._st(root)
        deps = []
        if st["w"] is not None:
            deps.append(st["w"])
        if sub is None:
            for s in st["subs"].values():
                if s["w"] is not None:
                    deps.append(s["w"])
            st["r"].append(me)
        else:
            s = st["subs"].setdefault(sub, {"w": None, "r": []})
            if s["w"] is not None:
                deps.append(s["w"])
            s["r"].append(me)
        return deps

    def _deps_write_whole(self, root, me):
        st = self._st(root)
        deps = []
        if st["w"] is not None:
            deps.append(st["w"])
        deps.extend(st["r"])
        for s in st["subs"].values():
            if s["w"] is not None:
                deps.append(s["w"])
            deps.extend(s["r"])
        st["subs"] = {}
        st["r"] = []
        st["w"] = me
        return deps

    def _deps_write(self, key, me):
        root, sub = self._split(key)
        deps = []
        for other in self.alias.get(root, ()):
            deps.extend(self._deps_write_whole(other, me))
        if sub is None:
            deps.extend(self._deps_write_whole(root, me))
            return deps
        st = self._st(root)
        if st["w"] is not None:
            deps.append(st["w"])
        deps.extend(st["r"])
        s = st["subs"].setdefault(sub, {"w": None, "r": []})
        if s["w"] is not None:
            deps.append(s["w"])
        deps.extend(s["r"])
        s["w"] = me
        s["r"] = []
        return deps

    def op(self, eng, fn, r=(), w=(), dma=None):
        idx = len(self.ops)
        deps = set()
        for k in r:
            deps.update(self._deps_read(k, idx))
        for k in w:
            deps.update(self._deps_write(k, idx))
        deps.discard(idx)
        self.ops.append({"eng": eng, "fn": fn, "deps": deps, "dma": dma, "sig": False})
        return idx

    def _skip(self, t, o):
        return (t["dma"] is None and o["dma"] is None and t["eng"] == o["eng"]
                and (o["eng"] == "pe" or not SAME_ENGINE_SYNC))

    def emit(self):
        nc = self.nc
        ops = self.ops
        for o in ops:
            for d in o["deps"]:
                t = ops[d]
                if t["dma"] is None and not self._skip(t, o):
                    t["sig"] = True
        cnt = {}
        sems = {}

        def getsem(name):
            if name not in sems:
                sems[name] = self.es.enter_context(nc.semaphore("s_" + name))
            return sems[name]

        for o in ops:
            if o["dma"] is not None:
                k = "d_" + o["dma"]
                cnt[k] = cnt.get(k, 0) + 16
                o["tok"] = (k, cnt[k])
            elif o["sig"]:
                k = "e_" + o["eng"]
                cnt[k] = cnt.get(k, 0) + 1
                o["tok"] = (k, cnt[k])
            else:
                o["tok"] = None
        seen = {e: {} for e in self.eng}
        nwait = 0
        for o in ops:
            e = o["eng"]
            engine = self.eng[e]
            need = {}
            for d in o["deps"]:
                t = ops[d]
                if t["tok"] is None or self._skip(t, o):
                    continue
                k, v = t["tok"]
                if v > need.get(k, 0):
                    need[k] = v
            for k, v in need.items():
                if seen[e].get(k, 0) >= v:
                    continue
                engine.wait_ge(getsem(k), v)
                seen[e][k] = v
                nwait += 1
            ins = o["fn"](engine)
            if o["tok"] is not None:
                k, v = o["tok"]
                ins.then_inc(getsem(k), 16 if o["dma"] is not None else 1)
        sp = self.eng["sp"]
        for k, v in cnt.items():
            if k.startswith("d_"):
                sp.wait_ge(getsem(k), v)
        return {"n_ops": len(ops), "n_wait": nwait, "n_sems": len(sems)}


SPC = 576


def build(NBLK=8, NLAY=2, dbg=(), final=True):
    nc = bass.Bass("TRN2", target_bir_lowering=False)

    def din(name, shape, dt=F32):
        return nc.dram_tensor(name, shape, dt, kind="ExternalInput").ap()

    x_d = din("x", [SEQ, 1024])
    p_d = din("p", [2, SEQ, 256])
    w_in_d = din("w_in", [2, 1024, 5632])
    gxw_d = din("gate_x_w", [2, 10, 128, 128])
    gaw_d = din("gate_a_w", [2, 10, 128, 128])
    w_ao_d = din("w_a_out", [2, 1280, 1024])
    w_glu_d = din("w_glu", [2, 1024, 2048])
    w_o_d = din("w_o", [2, 1024, 1024])
    w_up_d = din("w_up", [2, 1024, 6144])
    w_dn_d = din("w_down", [2, 3072, 1024])
    w_pg_d = din("w_ple_gate", [2, 1024, 1024])
    w_pp_d = din("w_ple_proj", [2, 256, 1024])
    sp_d = din("spk", [2, 128, SPC])
    s5f_d = din("s5f", [2, 4, 128, 1024])
    cst_d = din("cst", [128, 128 + 8 * 240 + 1 + 1024])
    y_d = nc.dram_tensor("y", [SEQ, 1024], F32, kind="ExternalOutput").ap()
    s5m_d = nc.dram_tensor("s5m", [2, 4, 128, 8192], BF16, kind="Internal").ap()
    dbg_out = {}

    es = ExitStack()
    with es:
        P = Prog(nc, es)

        def sb(name, shape, dt=F32):
            return es.enter_context(nc.sbuf_tensor(name, shape, dt))

        X = sb("X", [128, NT, 1024])
        HS = sb("HS", [128, 2, 1024], BF16)
        HT = sb("HT", [128, 8, TB], BF16)
        SQ = sb("SQ", [128, 1024], BF16)
        ST = sb("ST", [128, 16])
        WS = sb("WS", [128, NSLOT, 4096], BF16)
        IDB = sb("IDB", [128, 128], BF16)
        IDF = sb("IDF", [128, 128])
        ZS = sb("ZS", [128, 8, 240], BF16)
        SGN = sb("SGN", [128, 1])
        GF = sb("GF", [128, 1024])
        SPm = sb("SPm", [128, 2, SPC])
        C8 = sb("C8", [128, 2, 20])
        S5C = sb("S5C", [128, 2, 3, 64])
        LRUH = sb("LRUH", [128, 2, 10])
        XAH = sb("XAH", [128, 2, 10, 3])
        S5CAR = sb("S5CAR", [128, 2, 2, 64])
        FT = sb("FT", [128, 2, 48, 2])
        PTOK = sb("PTOK", [128, NT, 256])
        PB = sb("PB", [128, NT, 256], BF16)
        PTT = sb("PTT", [128, 2, TB], BF16)
        T1 = sb("T1", [128, 2, 64])
        T2 = sb("T2", [128, 2, 64])
        ARENA = 92160
        AR = sb("AR", [128, ARENA // 4])

        PM = es.enter_context(nc.psum_tensor("PM", [128, 4, 512], F32))
        PX = es.enter_context(nc.psum_tensor("PX", [128, 2, 512], F32))
        PTR = es.enter_context(nc.psum_tensor("PTR", [128, 2, 1024], BF16))

        arena_ranges = {}

        def av(name, off, nbytes, dt=F32, **re):
            assert off % 4 == 0 and nbytes % 4 == 0 and off + nbytes <= ARENA, (name, off, nbytes)
            arena_ranges[name] = (off, off + nbytes)
            v = AR[:, off // 4:(off + nbytes) // 4]
            if dt is not F32:
                v = v.bitcast(dt)
            if re:
                pat = re.pop("pat")
                v = v.rearrange(pat, **re)
            return v

        OA = av("OA", 0, 10240, BF16, pat="p (k n) -> p k n", n=TB)
        XA = av("XA", 10240, 20600, F32, pat="p (k n) -> p k n", n=515)
        GBf = av("GB", 30840, 10240, BF16, pat="p (k n) -> p k n", n=TB)
        LT = av("LT", 41080, 22528)
        UB = av("UB", 10240, 8192, BF16, pat="p (k n) -> p k n", n=TB)
        U = av("U", 18432, 8192, BF16, pat="p (g c) -> p g c", c=64)
        S2 = av("S2", 26624, 33280, F32, pat="p (h g c) -> p h g c", h=2, g=64)
        XB = av("XB", 59904, 8192, BF16, pat="p (g c) -> p g c", c=64)
        YSB = av("YSB", 68096, 8192, BF16, pat="p (g c) -> p g c", c=64)
        YS = av("YS", 76288, 8192, BF16, pat="p (k n) -> p k n", n=TB)
        YB = av("YB", 26624, 16384, F32, pat="p (k n) -> p k n", n=TB)
        SMA = av("SMA", 43008, 8192, BF16, pat="p (k n) -> p k n", n=TB)
        MGB = av("MGB", 51200, 8192, BF16, pat="p (k n) -> p k n", n=TB)
        TMP = av("TMP", 84480, 4096, F32, pat="p (k n) -> p k n", n=TB)
        ACTB = av("ACTB", 0, 24576, BF16, pat="p (k n) -> p k n", n=TB)
        UPF = av("UPF", 24576, 8224, F32, pat="p (k n) -> p k n", n=514)
        CV = av("CV", 32800, 8192, F32, pat="p (k n) -> p k n", n=TB)
        EX = av("EX", 0, 15360, F32, pat="p (g n) -> p g n", n=240)
        F1 = av("F1", 15360, 1024, F32, pat="p (g n) -> p g n", n=16)
        F2 = av("F2", 16384, 1024, F32, pat="p (g n) -> p g n", n=16)
        G1 = av("G1", 17408, 1024, F32, pat="p (g n) -> p g n", n=16)
        G2 = av("G2", 18432, 1024, F32, pat="p (g n) -> p g n", n=16)
        BBv = av("BB", 19456, 1024, F32, pat="p (g n) -> p g n", n=16)
        BBS = av("BBS", 20480, 1024, F32, pat="p (g n) -> p g n", n=16)
        Q1 = av("Q1", 21504, 1024, F32, pat="p (g n) -> p g n", n=16)
        Q2 = av("Q2", 22528, 1024, F32, pat="p (g n) -> p g n", n=16)
        CST = av("CST", 23552, 1024, F32, pat="p (g n) -> p g n", n=16)
        WAB = av("WAB", 24576, 4096, BF16, pat="p (g n) -> p g n", n=128)
        WASB = av("WASB", 28672, 4096, BF16, pat="p (g n) -> p g n", n=128)
        WCB = av("WCB", 32768, 4096, BF16, pat="p (g n) -> p g n", n=128)
        TBB = av("TBB", 36864, 4096, BF16, pat="p (g n) -> p g n", n=128)
        SC = av("SC", 40960, 64 * 4 * 40, F32, pat="p (k n) -> p k n", n=64)
        names = list(arena_ranges)
        for a in names:
            for b in names:
                if a != b:
                    (a0, a1), (b0, b1) = arena_ranges[a], arena_ranges[b]
                    if a0 < b1 and b0 < a1:
                        P.alias.setdefault(a, []).append(b)

        def mm(out, lhsT, rhs, start, stop, r, w):
            P.op("pe", lambda e: e.matmul(out, lhsT=lhsT, rhs=rhs, start=start, stop=stop), r=r, w=w)

        def tr(out, in_, ident, r, w):
            P.op("pe", lambda e: e.transpose(out, in_, ident), r=r, w=w)

        def act(out, in_, func, r, w, bias=None, scale=None, accum=None):
            kw = {}
            if bias is not None:
                kw["bias"] = bias
            if scale is not None:
                kw["scale"] = scale
            if accum is not None:
                kw["accum_out"] = accum
            P.op("act", lambda e: e.activation(out=out, in_=in_, func=func, **kw), r=r, w=w)

        def ts(eng, out, in0, s1, s2, op0, op1, r, w):
            if s2 is None:
                P.op(eng, lambda e: e.tensor_scalar(out=out, in0=in0, scalar1=s1, scalar2=None, op0=op0), r=r, w=w)
            else:
                P.op(eng, lambda e: e.tensor_scalar(out=out, in0=in0, scalar1=s1, scalar2=s2, op0=op0, op1=op1), r=r, w=w)

        def tt(eng, out, in0, in1, op, r, w):
            P.op(eng, lambda e: e.tensor_tensor(out=out, in0=in0, in1=in1, op=op), r=r, w=w)

        def stt(eng, out, in0, scalar, in1, op0, op1, r, w):
            P.op(eng, lambda e: e.scalar_tensor_tensor(out=out, in0=in0, scalar=scalar, in1=in1, op0=op0, op1=op1), r=r, w=w)

        def cp(eng, out, in_, r, w):
            if eng == "act":
                P.op("act", lambda e: e.copy(out=out, in_=in_), r=r, w=w)
            else:
                P.op(eng, lambda e: e.tensor_copy(out=out, in_=in_), r=r, w=w)

        def mset(eng, out, val, w):
            P.op(eng, lambda e: e.memset(out, val), w=w)

        def dma(eng, out, in_, key, r, w):
            P.op(eng, lambda e: e.dma_start(out=out, in_=in_), r=r, w=w, dma=key)

        def tap(name, ap, shape, r):
            if name in dbg:
                d = nc.dram_tensor("dbg_" + name, shape, ap.dtype, kind="ExternalOutput").ap()
                dbg_out[name] = d
                dma("sp", d, ap, "dbg_" + name, r, ["dbgdram_" + name])

        wctr = [0]

        def wload(src, nk, ncols, cast=True):
            s = wctr[0] % NSLOT
            wctr[0] += 1
            v = WS[:, s, 0:nk * ncols].rearrange("p (k n) -> p k n", n=ncols)
            dma("pool" if cast else "sp", v, src, "w%d" % s, [], ["ws/%d" % s])
            return v, "ws/%d" % s

        pmc = [0]

        def pm():
            b = pmc[0] % 4
            pmc[0] += 1
            return PM[:, b, :], "pm/%d" % b

        pxc = [0]

        def px():
            b = pxc[0] % 2
            pxc[0] += 1
            return PX[:, b, :], "px/%d" % b

        ptc = [0]

        def ptr():
            b = ptc[0] % 2
            ptc[0] += 1
            return PTR[:, b, :], "ptr/%d" % b

        dma("sp", IDF[:], cst_d[:, 0:128], "c0", [], ["IDF"])
        dma("pool", IDB[:], cst_d[:, 0:128], "c1", [], ["IDB"])
        dma("pool", ZS[:].rearrange("p a b -> p (a b)"), cst_d[:, 128:128 + 1920], "c2", [], ["ZS"])
        dma("sp", SGN[:], cst_d[:, 2048:2049], "c3", [], ["SGN"])
        dma("sp", GF[:], cst_d[:, 2049:2049 + 1024], "c4", [], ["GF"])
        mset("dve", ST[:, 8:12], -0.5, ["ST/nh"])
        mset("dve", LRUH[:], 0.0, ["LRUH"])
        mset("dve", XAH[:], 0.0, ["XAH"])
        mset("dve", S5CAR[:], 0.0, ["S5CAR"])
        mset("dve", FT[:], 0.0, ["FT"])

        for i in range(NLAY):
            dma("sp", SPm[:, i, :], sp_d[i], "spm%d" % i, [], ["SPm/%d" % i])
            spk = "SPm/%d" % i
            act(C8[:, i, 0:10], SPm[:, i, 102:112], AF.Exp, [spk], ["C8/%d" % i], scale=-1.0)
            act(C8[:, i, 0:10], C8[:, i, 0:10], AF.Ln, ["C8/%d" % i], ["C8/%d" % i], bias=1.0)
            ts("dve", C8[:, i, 10:20], C8[:, i, 0:10], -16.0, None, ALU.mult, None, ["C8/%d" % i], ["C8/%d" % i])
            ts("dve", C8[:, i, 0:10], C8[:, i, 0:10], -8.0, None, ALU.mult, None, ["C8/%d" % i], ["C8/%d" % i])

            a_re = SPm[:, i, 320:384]
            a_im = SPm[:, i, 384:448]
            ldt = SPm[:, i, 448:512]
            DM = SPm[:, i, 512:576]
            k = "SC"
            DT, ADT, TH, MAG, QQ, RR, MSK, SN, CS = (SC[:, j, :] for j in range(9))
            LBR, LBI, NR, DEN, CR, CI, W2, TA, TBt = (SC[:, 9 + j, :] for j in range(9))
            PRn = SC[:, 18:27, :]
            PIn = SC[:, 27:36, :]
            QI = SC[:, 36, :].bitcast(I32)
            act(DT, ldt, AF.Exp, [spk], [k])
            tt("dve", ADT, a_re, DT, ALU.mult, [spk, k], [k])
            tt("dve", TH, a_im, DT, ALU.mult, [spk, k], [k])
            act(MAG, ADT, AF.Exp, [k], [k])

            def sincos(dst, shift):
                ts("dve", RR, TH, shift, None, ALU.add, None, [k], [k])
                ts("dve", QQ, RR, 1.0 / (2 * PI), None, ALU.mult, None, [k], [k])
                cp("dve", QI, QQ, [k], [k])
                cp("dve", QQ, QI, [k], [k])
                stt("dve", RR, QQ, -2 * PI, RR, ALU.mult, ALU.add, [k], [k])
                ts("dve", MSK, RR, PI, -2 * PI, ALU.is_gt, ALU.mult, [k], [k])
                tt("dve", RR, RR, MSK, ALU.add, [k], [k])
                ts("dve", MSK, RR, -PI, 2 * PI, ALU.is_lt, ALU.mult, [k], [k])
                tt("dve", RR, RR, MSK, ALU.add, [k], [k])
                ts("dve", RR, RR, PI, -PI, ALU.min, ALU.max, [k], [k])
                act(dst, RR, AF.Sin, [k], [k])

            sincos(SN, 0.0)
            sincos(CS, PI / 2)
            tt("dve", LBR, MAG, CS, ALU.mult, [k], [k])
            tt("dve", LBI, MAG, SN, ALU.mult, [k], [k])
            mset("dve", PRn[:, 0, :], 1.0, [k])
            mset("dve", PIn[:, 0, :], 0.0, [k])
            for q in range(8):
                tt("dve", TA, PRn[:, q, :], LBR, ALU.mult, [k], [k])
                tt("dve", TBt, PIn[:, q, :], LBI, ALU.mult, [k], [k])
                tt("dve", PRn[:, q + 1, :], TA, TBt, ALU.subtract, [k], [k])
                tt("dve", TA, PRn[:, q, :], LBI, ALU.mult, [k], [k])
                tt("dve", TBt, PIn[:, q, :], LBR, ALU.mult, [k], [k])
                tt("dve", PIn[:, q + 1, :], TA, TBt, ALU.add, [k], [k])
            ts("dve", NR, LBR, -1.0, None, ALU.add, None, [k], [k])
            tt("dve", TA, a_re, a_re, ALU.mult, [spk, k], [k])
            tt("dve", TBt, a_im, a_im, ALU.mult, [spk, k], [k])
            tt("dve", DEN, TA, TBt, ALU.add, [k], [k])
            P.op("dve", lambda e, DEN=DEN: e.reciprocal(out=DEN, in_=DEN), r=[k], w=[k])
            tt("dve", TA, NR, a_re, ALU.mult, [spk, k], [k])
            tt("dve", TBt, LBI, a_im, ALU.mult, [spk, k], [k])
            tt("dve", TA, TA, TBt, ALU.add, [k], [k])
            tt("dve", CR, TA, DEN, ALU.mult, [k], [k])
            tt("dve", TA, LBI, a_re, ALU.mult, [spk, k], [k])
            tt("dve", TBt, NR, a_im, ALU.mult, [spk, k], [k])
            tt("dve", TA, TA, TBt, ALU.subtract, [k], [k])
            tt("dve", CI, TA, DEN, ALU.mult, [k], [k])
            ts("dve", W2, CI, SGN[:, 0:1], None, ALU.mult, None, [k, "SGN"], [k])
            cp("dve", S5C[:, i, 0, :], PRn[:, 8, :], [k], ["S5C/%d" % i])
            ts("dve", S5C[:, i, 1, :], PIn[:, 8, :], SGN[:, 0:1], None, ALU.mult, None, [k, "SGN"], ["S5C/%d" % i])
            ts("dve", S5C[:, i, 2, :], PIn[:, 8, :], SGN[:, 0:1], -1.0, ALU.mult, ALU.mult, [k, "SGN"], ["S5C/%d" % i])

            def bc(v, g0):
                return v[:, g0:g0 + 16].unsqueeze(2).to_broadcast([128, 16, 16])

            for gb in range(4):
                g0 = gb * 16
                for q, (tile_, nm) in enumerate(((F1, "F1"), (F2, "F2"), (G1, "G1"), (G2, "G2"))):
                    dma("sp", tile_, s5f_d[i, q, :, g0 * 16:(g0 + 16) * 16].rearrange("p (g n) -> p g n", n=16),
                        "s5f%d" % q, [], [nm])
                tt("dve", Q1, F1, bc(CR, g0), ALU.mult, ["F1", k], ["Q1"])
                tt("dve", Q2, F2, bc(W2, g0), ALU.mult, ["F2", k], ["Q2"])
                tt("dve", BBv, Q1, Q2, ALU.add, ["Q1", "Q2"], ["BB"])
                tt("dve", Q1, F2, bc(CR, g0), ALU.mult, ["F2", k], ["Q1"])
                tt("dve", Q2, F1, bc(W2, g0), ALU.mult, ["F1", k], ["Q2"])
                tt("dve", BBS, Q1, Q2, ALU.subtract, ["Q1", "Q2"], ["BBS"])
                mset("dve", EX[:, :, 128:240], 0.0, ["EX"])
                for j in range(8):
                    kk = 7 - j
                    tt("dve", Q1, BBv, bc(PRn[:, kk, :], g0), ALU.mult, ["BB", k], ["Q1"])
                    tt("dve", Q2, BBS, bc(PIn[:, kk, :], g0), ALU.mult, ["BBS", k], ["Q2"])
                    stt("dve", EX[:, :, j * 16:(j + 1) * 16], Q2, SGN[:, 0:1], Q1, ALU.mult, ALU.add,
                        ["Q1", "Q2", "SGN"], ["EX"])
                ts("dve", CST, G1, SGN[:, 0:1], -1.0, ALU.mult, ALU.mult, ["G1", "SGN"], ["CST"])
                for gq in range(4):
                    bank, bk = pm()
                    for gi in range(4):
                        gl = gq * 4 + gi
                        tr(bank[:, gi * 128:(gi + 1) * 128], EX[:, gl, 0:128], IDF[:], ["EX", "IDF"], [bk])
                    bv = bank.rearrange("p (g n) -> p g n", n=128)
                    cp("act", WAB[:, gq * 4:(gq + 1) * 4, :], bv, [bk], ["WAB"])
                    cp("dve", WASB[:, gq * 4:(gq + 1) * 4, 0:64], bv[:, :, 64:128], [bk], ["WASB"])
                    cp("dve", WASB[:, gq * 4:(gq + 1) * 4, 64:128], bv[:, :, 0:64], [bk], ["WASB"])
                    bank, bk = pm()
                    for gi in range(4):
                        gl = gq * 4 + gi
                        for l in range(8):
                            mm(bank[:, gi * 128 + l * 16: gi * 128 + (l + 1) * 16],
                               EX[:, gl, (7 - l) * 16:(7 - l) * 16 + 128], CST[:, gl, :], True, True,
                               ["EX", "CST"], [bk])
                        stt("dve", TBB[:, gl, :], IDF[:], DM[:, g0 + gl:g0 + gl + 1], bank[:, gi * 128:(gi + 1) * 128],
                            ALU.mult, ALU.add, [bk, "IDF", spk], ["TBB"])
                for l in range(8):
                    tt("dve", Q1, G1, bc(PRn[:, l + 1, :], g0), ALU.mult, ["G1", k], ["Q1"])
                    tt("dve", Q2, G2, bc(PIn[:, l + 1, :], g0), ALU.mult, ["G2", k], ["Q2"])
                    ts("dve", Q1, Q1, SGN[:, 0:1], -1.0, ALU.mult, ALU.mult, ["Q1", "SGN"], ["Q1"])
                    tt("dve", WCB[:, :, l * 16:(l + 1) * 16], Q1, Q2, ALU.subtract, ["Q1", "Q2"], ["WCB"])
                for q, (tile_, nm) in enumerate(((WAB, "WAB"), (WASB, "WASB"), (WCB, "WCB"), (TBB, "TBB"))):
                    dma("sp", s5m_d[i, q, :, g0 * 128:(g0 + 16) * 128].rearrange("p (g n) -> p g n", n=128),
                        tile_, "s5mo%d" % q, [nm], ["s5m%d_%d" % (i, q)])

        def kview(ap):
            return ap.rearrange("(k p) n -> p k n", p=128)

        w_in_v = [kview(w_in_d[i]) for i in range(2)]
        w_ao_v = [kview(w_ao_d[i]) for i in range(2)]
        w_glu_v = [kview(w_glu_d[i]) for i in range(2)]
        w_o_v = [kview(w_o_d[i]) for i in range(2)]
        w_up_v = [kview(w_up_d[i]) for i in range(2)]
        w_dn_v = [kview(w_dn_d[i]) for i in range(2)]
        w_pg_v = [kview(w_pg_d[i]) for i in range(2)]
        w_pp_v = [kview(w_pp_d[i]) for i in range(2)]
        gxw_v = [gxw_d[i].rearrange("h i j -> i h j") for i in range(2)]
        gaw_v = [gaw_d[i].rearrange("h i j -> i h j") for i in range(2)]
        x_v = x_d.rearrange("(b t p) d -> b p t d", t=NT, p=128)
        y_v = y_d.rearrange("(b t p) d -> b p t d", t=NT, p=128)
        p_v = [p_d[i].rearrange("(b t p) d -> b p t d", t=NT, p=128) for i in range(2)]

        def norm(gcols, gkey):
            for t in range(NT):
                act(SQ[:], X[:, t, :], AF.Square, ["X/%d" % t], ["SQ", "ST/ss"], accum=ST[:, t:t + 1])
            ts("dve", ST[:, 4:8], ST[:, 0:4], 1.0 / 1024, EPS, ALU.mult, ALU.add, ["ST/ss"], ["ST/rs"])
            tt("pool", ST[:, 4:8], ST[:, 4:8], ST[:, 8:12], ALU.pow, ["ST/rs", "ST/nh"], ["ST/rs"])
            for t in range(NT):
                s = t % 2
                act(HS[:, s, :], X[:, t, :], AF.Copy, ["X/%d" % t, "ST/rs"], ["HS/%d" % s], scale=ST[:, 4 + t:5 + t])
                bank, bk = ptr()
                for kt in range(8):
                    tr(bank[:, kt * 128:(kt + 1) * 128], HS[:, s, kt * 128:(kt + 1) * 128], IDB[:],
                       ["HS/%d" % s, "IDB"], [bk])
                tt("dve", HT[:, :, t * 128:(t + 1) * 128], bank.rearrange("p (k n) -> p k n", n=128),
                   gcols.unsqueeze(2).to_broadcast([128, 8, 128]), ALU.mult, [bk, gkey], ["HT/%d" % t])

        HTall = ["HT/%d" % t for t in range(NT)]

        def proj_fm(ws, wk, j, nk, rhs_of, rkeys):
            bank, bk = pm()
            for kt in range(nk):
                mm(bank, ws[:, kt, j * 128:(j + 1) * 128], rhs_of(kt), kt == 0, kt == nk - 1, [wk] + rkeys, [bk])
            return bank, bk

        def layer(i, blk):
            spk = "SPm/%d" % i
            first = blk == 0
            dma("sp", PTOK[:], p_v[i][blk], "ptok", [], ["PTOK"])
            norm(SPm[:, i, 0:8], spk)
            tap("ht1_%d_%d" % (i, blk), HT[:], [128, 8, TB], HTall)

            for c in range(5):
                ws, wk = wload(w_in_v[i][:, :, c * 512:(c + 1) * 512], 8, 512)
                for j in range(4):
                    t = 4 * c + j
                    bank, bk = proj_fm(ws, wk, j, 8, lambda kt: HT[:, kt, :], HTall)
                    if t < 10:
                        cp("act", XA[:, t, 3:515], bank, [bk], ["XA/%d" % t])
                    else:
                        act(GBf[:, t - 10, :], bank, AF.Gelu_apprx_tanh, [bk], ["GB/%d" % (t - 10)])
            wgx, wgxk = wload(gxw_v[i], 10, 128)
            wga, wgak = wload(gaw_v[i], 10, 128)
            for t in range(10):
                q = t % 2
                base = q * 2816
                xc = LT[:, base:base + 512]
                gx = LT[:, base + 512:base + 1024]
                aa = LT[:, base + 1024:base + 1536]
                m2 = LT[:, base + 1536:base + 2048]
                hh = LT[:, base + 2048:base + 2560]
                xcb = LT[:, base + 2560:base + 2816].bitcast(BF16)
                lk = "LT/%d" % q
                xk = "XA/%d" % t
                cp("dve", XA[:, t, 0:3], XAH[:, i, t, :], ["XAH/%d_%d" % (i, t)], [xk])
                cw = lambda j: SPm[:, i, 32 + t * 4 + j:33 + t * 4 + j]
                ts("dve", xc, XA[:, t, 0:512], cw(0), SPm[:, i, 72 + t:73 + t], ALU.mult, ALU.add, [xk, spk], [lk])
                for j in range(1, 4):
                    stt("dve", xc, XA[:, t, j:j + 512], cw(j), xc, ALU.mult, ALU.add, [xk, spk, lk], [lk])
                cp("dve", XAH[:, i, t, :], XA[:, t, 512:515], [xk], ["XAH/%d_%d" % (i, t)])
                cp("act", xcb, xc, [lk], [lk])
                bx, bxk = px()
                mm(bx, wgx[:, t, :], xcb, True, True, [wgxk, lk], [bxk])
                ba, bak = px()
                mm(ba, wga[:, t, :], xcb, True, True, [wgak, lk], [bak])
                act(gx, bx, AF.Sigmoid, [bxk, spk], [lk], bias=SPm[:, i, 82 + t:83 + t])
                act(hh, ba, AF.Sigmoid, [bak, spk], [lk], bias=SPm[:, i, 92 + t:93 + t])
                act(aa, hh, AF.Exp, [lk, "C8/%d" % i], [lk], scale=C8[:, i, t:t + 1])
                act(m2, hh, AF.Exp, [lk, "C8/%d" % i], [lk], scale=C8[:, i, 10 + t:11 + t])
                act(m2, m2, AF.Sqrt, [lk], [lk], bias=1.0, scale=-1.0)
                tt("dve", gx, gx, xc, ALU.mult, [lk], [lk])
                stt("dve", m2, m2, 1e-6, gx, ALU.max, ALU.mult, [lk], [lk])
                hk = "LRUH/%d_%d" % (i, t)
                P.op("dve", lambda e, hh=hh, aa=aa, m2=m2, t=t: e.tensor_tensor_scan(
                    out=hh, data0=aa, data1=m2, initial=LRUH[:, i, t:t + 1], op0=ALU.mult, op1=ALU.add),
                    r=[lk, hk], w=[lk])
                cp("dve", LRUH[:, i, t:t + 1], hh[:, 511:512], [lk], [hk])
                tt("dve", OA[:, t, :], hh, GBf[:, t, :], ALU.mult, [lk, "GB/%d" % t], ["OA/%d" % t])
            tap("oa_%d_%d" % (i, blk), OA[:], [128, 10, TB], ["OA"])

            for c in range(5, 7):
                ws, wk = wload(w_in_v[i][:, :, c * 512:(c + 1) * 512], 8, 512)
                for j in range(4):
                    t = 4 * c + j - 20
                    bank, bk = proj_fm(ws, wk, j, 8, lambda kt: HT[:, kt, :], HTall)
                    cp("act", UB[:, t, :], bank, [bk], ["UB/%d" % t])
            for j in range(8):
                bank, bk = pm()
                bv = bank.rearrange("p (g c) -> p g c", c=64)
                for gl in range(8):
                    for l in range(8):
                        mm(bv[:, gl, :], ZS[:, gl, 112 - 16 * l:240 - 16 * l], UB[:, j, l::8], l == 0, l == 7,
                           ["ZS", "UB/%d" % j], [bk])
                cp("act" if j % 2 else "dve", U[:, j * 8:(j + 1) * 8, :], bv, [bk], ["U/%d" % j])
            for h in range(2):
                for half in range(2):
                    ws, wk = wload(s5m_d[i, h, :, half * 4096:(half + 1) * 4096].rearrange("p (g n) -> p g n", n=128),
                                   32, 128, cast=False)
                    P.ops[-1]["deps"].update(P._deps_read("s5m%d_%d" % (i, h), len(P.ops) - 1))
                    for j4 in range(4):
                        j = half * 4 + j4
                        bank, bk = pm()
                        bv = bank.rearrange("p (g c) -> p g c", c=64)
                        for gl in range(8):
                            mm(bv[:, gl, :], ws[:, j4 * 8 + gl, :], U[:, j * 8 + gl, :], True, True,
                               [wk, "U/%d" % j], [bk])
                        cp("act" if h else "dve", S2[:, h, j * 8:(j + 1) * 8, 1:65], bv, [bk], ["S2/%d" % h])
            cp("dve", S2[:, :, :, 0], S5CAR[:, i, :, :], ["S5CAR/%d" % i], ["S2"])
            ArB = S5C[:, i, 0, :].unsqueeze(1).to_broadcast([128, 2, 64])
            sck = "S5C/%d" % i
            for c in range(64):
                tt("dve", T1[:], S2[:, :, :, c], ArB, ALU.mult, ["S2", sck], ["T1"])
                tt("dve", T2[:, 0, :], S2[:, 1, :, c], S5C[:, i, 1, :], ALU.mult, ["S2", sck], ["T2"])
                tt("dve", T2[:, 1, :], S2[:, 0, :, c], S5C[:, i, 2, :], ALU.mult, ["S2", sck], ["T2"])
                tt("dve", T1[:], T1[:], T2[:], ALU.add, ["T1", "T2"], ["T1"])
                tt("dve", S2[:, :, :, c + 1], S2[:, :, :, c + 1], T1[:], ALU.add, ["S2", "T1"], ["S2"])
            cp("dve", S5CAR[:, i, :, :], S2[:, :, :, 64], ["S2"], ["S5CAR/%d" % i])
            cp("act", XB[:], S2[:, 0, :, 0:64], ["S2"], ["XB"])
            tap("xb_%d_%d" % (i, blk), XB[:], [128, 64, 64], ["XB"])
            for half in range(2):
                wt_, wtk = wload(s5m_d[i, 3, :, half * 4096:(half + 1) * 4096].rearrange("p (g n) -> p g n", n=128),
                                 32, 128, cast=False)
                P.ops[-1]["deps"].update(P._deps_read("s5m%d_3" % i, len(P.ops) - 1))
                wc_, wck = wload(s5m_d[i, 2, :, half * 4096:(half + 1) * 4096].rearrange("p (g n) -> p g n", n=128),
                                 32, 128, cast=False)
                P.ops[-1]["deps"].update(P._deps_read("s5m%d_2" % i, len(P.ops) - 1))
                for j4 in range(4):
                    j = half * 4 + j4
                    bank, bk = pm()
                    bv = bank.rearrange("p (g c) -> p g c", c=64)
                    for gl in range(8):
                        g = j * 8 + gl
                        mm(bv[:, gl, :], wt_[:, j4 * 8 + gl, :], U[:, g, :], True, False, [wtk, "U/%d" % j], [bk])
                        mm(bv[:, gl, :], wc_[:, j4 * 8 + gl, :], XB[:, g, :], False, True, [wck, "XB"], [bk])
                    cp("act" if j % 2 else "dve", YSB[:, j * 8:(j + 1) * 8, :], bv, [bk], ["YSB/%d" % j])
            for j in range(8):
                bank, bk = pm()
                for l in range(8):
                    for gl in range(8):
                        mm(bank[:, l::8], ZS[:, l, 112 - 16 * gl:240 - 16 * gl], YSB[:, j * 8 + gl, :],
                           gl == 0, gl == 7, ["ZS", "YSB/%d" % j], [bk])
                act(YS[:, j, :], bank, AF.Gelu_apprx_tanh, [bk], ["YS/%d" % j])
            tap("ys_%d_%d" % (i, blk), YS[:], [128, 8, TB], ["YS"])
            YSall = ["YS/%d" % j for j in range(8)]
            for c in (0, 2, 1, 3):
                ws, wk = wload(w_glu_v[i][:, :, c * 512:(c + 1) * 512], 8, 512)
                for j in range(4):
                    t = 4 * c + j
                    bank, bk = proj_fm(ws, wk, j, 8, lambda kt: YS[:, kt, :], YSall)
                    bcol = SPm[:, i, 112 + t:113 + t]
                    if t < 8:
                        act(YB[:, t, :], bank, AF.Identity, [bk, spk], ["YB/%d" % t], bias=bcol)
                    else:
                        s = t % 2
                        act(TMP[:, s, :], bank, AF.Sigmoid, [bk, spk], ["TMP/%d" % s], bias=bcol)
                        tt("dve", YB[:, t - 8, :], YB[:, t - 8, :], TMP[:, s, :], ALU.mult,
                           ["YB/%d" % (t - 8), "TMP/%d" % s], ["YB/%d" % (t - 8)])
            for c in range(7, 11):
                ws, wk = wload(w_in_v[i][:, :, c * 512:(c + 1) * 512], 8, 512)
                for j in range(4):
                    t = 4 * c + j
                    bank, bk = proj_fm(ws, wk, j, 8, lambda kt: HT[:, kt, :], HTall)
                    if t < 36:
                        act(SMA[:, t - 28, :], bank, AF.Sigmoid, [bk], ["SMA/%d" % (t - 28)])
                    else:
                        s = t % 2
                        act(TMP[:, s, :], bank, AF.Sigmoid, [bk], ["TMP/%d" % s])
                        tt("dve", YB[:, t - 36, :], YB[:, t - 36, :], TMP[:, s, :], ALU.mult,
                           ["YB/%d" % (t - 36), "TMP/%d" % s], ["YB/%d" % (t - 36)])
            OAall = ["OA/%d" % t for t in range(10)]
            for c in range(4):
                ws, wk = wload(w_ao_v[i][:, :, c * 256:(c + 1) * 256], 10, 256)
                for j in range(2):
                    n = 2 * c + j
                    bank, bk = proj_fm(ws, wk, j, 10, lambda kt: OA[:, kt, :], OAall)
                    s = n % 2
                    tt("dve", TMP[:, s, :], bank, SMA[:, n, :], ALU.mult, [bk, "SMA/%d" % n], ["TMP/%d" % s])
                    tt("dve", MGB[:, n, :], TMP[:, s, :], YB[:, n, :], ALU.add, ["TMP/%d" % s, "YB/%d" % n],
                       ["MGB/%d" % n])
            tap("mg_%d_%d" % (i, blk), MGB[:], [128, 8, TB], ["MGB"])
            MGall = ["MGB/%d" % n for n in range(8)]
            for h in range(2):
                ws, wk = wload(w_o_v[i][:, :, h * 512:(h + 1) * 512], 8, 512)
                for t in range(NT):
                    bank, bk = pm()
                    for kt in range(8):
                        mm(bank, MGB[:, kt, t * 128:(t + 1) * 128], ws[:, kt, :], kt == 0, kt == 7, [wk] + MGall, [bk])
                    tt("dve", X[:, t, h * 512:(h + 1) * 512], bank, X[:, t, h * 512:(h + 1) * 512], ALU.add,
                       [bk, "X/%d" % t], ["X/%d" % t])
            tap("x1_%d_%d" % (i, blk), X[:], [128, NT, 1024], ["X"])

            norm(SPm[:, i, 8:16], spk)
            for c6 in range(6):
                for half in range(2):
                    c = c6 + 6 * half
                    ws, wk = wload(w_up_v[i][:, :, c * 512:(c + 1) * 512], 8, 512)
                    for j in range(4):
                        t = 4 * c + j
                        o = t % 24
                        q = (2 * half + (j % 2))
                        bank, bk = proj_fm(ws, wk, j, 8, lambda kt: HT[:, kt, :], HTall)
                        uk = "UPF/%d" % q
                        ck = "CV/%d" % q
                        ftk = "FT/%d_%d" % (i, t)
                        cp("act", UPF[:, q, 2:514], bank, [bk], [uk])
                        eng = "dve"
                        cp(eng, UPF[:, q, 0:2], FT[:, i, t, :], [ftk], [uk])
                        fw_ = lambda jj: SPm[:, i, 128 + t * 3 + jj:129 + t * 3 + jj]
                        ts(eng, CV[:, q, :], UPF[:, q, 0:512], fw_(0), SPm[:, i, 272 + t:273 + t], ALU.mult, ALU.add,
                           [uk, spk], [ck])
                        for jj in range(1, 3):
                            stt(eng, CV[:, q, :], UPF[:, q, jj:jj + 512], fw_(jj), CV[:, q, :], ALU.mult, ALU.add,
                                [uk, spk, ck], [ck])
                        cp(eng, FT[:, i, t, :], UPF[:, q, 512:514], [uk], [ftk])
                        if half == 0:
                            act(ACTB[:, o, :], CV[:, q, :], AF.Gelu_apprx_tanh, [ck], ["ACTB/%d" % o])
                        else:
                            tt("dve", ACTB[:, o, :], ACTB[:, o, :], CV[:, q, :], ALU.mult, ["ACTB/%d" % o, ck],
                               ["ACTB/%d" % o])
            tap("actb_%d_%d" % (i, blk), ACTB[:], [128, 24, TB], ["ACTB"])
            ACall = ["ACTB/%d" % o for o in range(24)]
            for h in range(2):
                banks = [pm() for _ in range(NT)]
                for kc in range(3):
                    ws, wk = wload(w_dn_v[i][:, kc * 8:(kc + 1) * 8, h * 512:(h + 1) * 512], 8, 512)
                    for t in range(NT):
                        bank, bk = banks[t]
                        for k8 in range(8):
                            mm(bank, ACTB[:, kc * 8 + k8, t * 128:(t + 1) * 128], ws[:, k8, :],
                               kc == 0 and k8 == 0, kc == 2 and k8 == 7, [wk] + ACall, [bk])
                for t in range(NT):
                    bank, bk = banks[t]
                    tt("dve", X[:, t, h * 512:(h + 1) * 512], bank, X[:, t, h * 512:(h + 1) * 512], ALU.add,
                       [bk, "X/%d" % t], ["X/%d" % t])
            tap("x2_%d_%d" % (i, blk), X[:], [128, NT, 1024], ["X"])

            norm(SPm[:, i, 16:24], spk)
            cp("dve", PB[:], PTOK[:], ["PTOK"], ["PB"])
            bank, bk = ptr()
            bv = bank.rearrange("p (k n) -> p k n", n=TB)
            for t in range(NT):
                for kk in range(2):
                    tr(bv[:, kk, t * 128:(t + 1) * 128], PB[:, t, kk * 128:(kk + 1) * 128], IDB[:], ["PB", "IDB"], [bk])
            cp("act", PTT[:], bv, [bk], ["PTT"])
            for h in range(2):
                wg, wgk = wload(w_pg_v[i][:, :, h * 512:(h + 1) * 512], 8, 512)
                wp, wpk = wload(w_pp_v[i][:, :, h * 512:(h + 1) * 512], 2, 512)
                for t in range(NT):
                    bg, bgk = pm()
                    for kt in range(8):
                        mm(bg, HT[:, kt, t * 128:(t + 1) * 128], wg[:, kt, :], kt == 0, kt == 7, [wgk, "HT/%d" % t], [bgk])
                    bp, bpk = px()
                    for kk in range(2):
                        mm(bp, PTT[:, kk, t * 128:(t + 1) * 128], wp[:, kk, :], kk == 0, kk == 1, [wpk, "PTT"], [bpk])
                    s = t % 2
                    act(TMP[:, s, :], bg, AF.Sigmoid, [bgk], ["TMP/%d" % s])
                    tt("dve", TMP[:, s, :], TMP[:, s, :], bp, ALU.mult, ["TMP/%d" % s, bpk], ["TMP/%d" % s])
                    tt("dve", X[:, t, h * 512:(h + 1) * 512], TMP[:, s, :], X[:, t, h * 512:(h + 1) * 512], ALU.add,
                       ["TMP/%d" % s, "X/%d" % t], ["X/%d" % t])
            tap("x3_%d_%d" % (i, blk), X[:], [128, NT, 1024], ["X"])

        for blk in range(NBLK):
            dma("sp", X[:], x_v[blk], "xin", [], ["X"])
            for i in range(NLAY):
                layer(i, blk)
            if final:
                for t in range(NT):
                    act(SQ[:], X[:, t, :], AF.Square, ["X/%d" % t], ["SQ", "ST/ss"], accum=ST[:, t:t + 1])
                ts("dve", ST[:, 4:8], ST[:, 0:4], 1.0 / 1024, EPS, ALU.mult, ALU.add, ["ST/ss"], ["ST/rs"])
                tt("pool", ST[:, 4:8], ST[:, 4:8], ST[:, 8:12], ALU.pow, ["ST/rs", "ST/nh"], ["ST/rs"])
                for t in range(NT):
                    stt("dve", X[:, t, :], X[:, t, :], ST[:, 4 + t:5 + t], GF[:], ALU.mult, ALU.mult,
                        ["X/%d" % t, "ST/rs", "GF"], ["X/%d" % t])
            dma("sp", y_v[blk], X[:], "yout", ["X"], ["ydram/%d" % blk])

        with nc.allow_non_contiguous_dma(reason="small strided parameter loads"):
            info = P.emit()
    return nc, dbg_out, info


def _cols(v):
    return np.ascontiguousarray(v.reshape(-1, 128).T)


def host_layout(inp):
    f = np.float32
    spk = np.zeros((2, 128, SPC), f)
    s5f = np.zeros((2, 4, 128, 1024), f)
    for i in range(2):
        s = spk[i]
        s[:, 0:8] = _cols(inp["g_mix"][i])
        s[:, 8:16] = _cols(inp["g_ffn"][i])
        s[:, 16:24] = _cols(inp["g_ple"][i])
        s[:, 24:32] = _cols(inp["g_final"])
        caw = inp["conv_a_w"][i]
        for j in range(4):
            s[:, 32 + j:72:4] = _cols(caw[j])
        s[:, 72:82] = _cols(inp["conv_a_b"][i])
        s[:, 82:92] = _cols(inp["gate_x_b"][i])
        s[:, 92:102] = _cols(inp["gate_a_b"][i])
        s[:, 102:112] = _cols(inp["lru_lambda"][i])
        s[:, 112:128] = _cols(inp["b_glu"][i])
        cfw = inp["conv_f_w"][i]
        for j in range(3):
            s[:, 128 + j:272:3] = _cols(cfw[j])
        s[:, 272:320] = _cols(inp["conv_f_b"][i])
        are_t = inp["s5_a_re"][i].T
        aim_t = inp["s5_a_im"][i].T
        s[:, 320:384] = np.concatenate([are_t, are_t], 0)
        s[:, 384:448] = np.concatenate([aim_t, aim_t], 0)
        s[:, 448:512] = np.broadcast_to(inp["s5_log_dt"][i][None, :], (128, 64))
        dm = inp["s5_d"][i].reshape(64, 16).T
        s[:, 512:576] = np.tile(dm, (8, 1))
        bre = inp["s5_b_re"][i].transpose(1, 0, 2).reshape(64, 1024)
        bim = inp["s5_b_im"][i].transpose(1, 0, 2).reshape(64, 1024)
        cre = inp["s5_c_re"][i].transpose(2, 0, 1).reshape(64, 1024)
        cim = inp["s5_c_im"][i].transpose(2, 0, 1).reshape(64, 1024)
        s5f[i, 0] = np.concatenate([bre, bim], 0)
        s5f[i, 1] = np.concatenate([bim, bre], 0)
        s5f[i, 2] = np.concatenate([cre, cim], 0)
        s5f[i, 3] = np.concatenate([cim, cre], 0)
    cst = np.zeros((128, 128 + 1920 + 1 + 1024), f)
    cst[:, 0:128] = np.eye(128, dtype=f)
    z = np.zeros((128, 8, 240), f)
    for gl in range(8):
        for ch in range(16):
            z[gl * 16 + ch, gl, 112 + ch] = 1.0
    cst[:, 128:2048] = z.reshape(128, 1920)
    cst[0:64, 2048] = -1.0
    cst[64:128, 2048] = 1.0
    cst[:, 2049:] = np.broadcast_to(inp["g_final"][None, :], (128, 1024))
    shared = {"spk": spk, "s5f": s5f, "cst": cst}
    for k in ("w_in", "gate_x_w", "gate_a_w", "w_a_out", "w_glu", "w_o", "w_up", "w_down",
              "w_ple_gate", "w_ple_proj"):
        shared[k] = np.ascontiguousarray(inp[k], dtype=f)
    return shared


_CACHE = {}


def kernel(**inputs):
    inp = {k: np.asarray(v) for k, v in inputs.items()}
    shared = host_layout(inp)
    if "nc" not in _CACHE:
        _CACHE["nc"] = build()[0]
    nc = _CACHE["nc"]
    in_maps = []
    for b in range(8):
        m = dict(shared)
        m["x"] = np.ascontiguousarray(inp["x"][b], dtype=np.float32)
        m["p"] = np.ascontiguousarray(inp["p"][:, b], dtype=np.float32)
        in_maps.append(m)
    res = run_bass_kernel_spmd(nc, in_maps, core_ids=list(range(8)))
    return np.stack([np.asarray(r["y"], dtype=np.float32) for r in res.results], 0)
```

```python
import math
from contextlib import ExitStack

import numpy as np
import concourse.bass as bass
import concourse.mybir as mybir
from concourse.bass_utils import run_bass_kernel_spmd

F32 = mybir.dt.float32
BF16 = mybir.dt.bfloat16
I32 = mybir.dt.int32
AF = mybir.ActivationFunctionType
ALU = mybir.AluOpType

SAME_ENGINE_SYNC = ("act", "pool", "dve", "sp")
NSLOT = 4
TB = 512
NT = 4
SEQ = 4096
EPS = 1e-6
PI = math.pi


class Prog:
    def __init__(self, nc, es):
        self.nc = nc
        self.es = es
        self.eng = {"pe": nc.tensor, "act": nc.scalar, "dve": nc.vector,
                    "pool": nc.gpsimd, "sp": nc.sync}
        self.ops = []
        self.state = {}
        self.alias = {}
        self._defer = None

    @staticmethod
    def _split(key):
        if "/" in key:
            r, s = key.split("/", 1)
            return r, s
        return key, None

    def _st(self, root):
        return self.state.setdefault(root, {"w": None, "r": [], "subs": {}})

    def _deps_read(self, key, me):
        root, sub = self._split(key)
        st = self._st(root)
        deps = []
        if st["w"] is not None:
            deps.append(st["w"])
        if sub is None:
            for s in st["subs"].values():
                if s["w"] is not None:
                    deps.append(s["w"])
            st["r"].append(me)
        else:
            s = st["subs"].setdefault(sub, {"w": None, "r": []})
            if s["w"] is not None:
                deps.append(s["w"])
            s["r"].append(me)
        return deps

    def _deps_write_whole(self, root, me):
        st = self._st(root)
        deps = []
        if st["w"] is not None:
            deps.append(st["w"])
        deps.extend(st["r"])
        for s in st["subs"].values():
            if s["w"] is not None:
                deps.append(s["w"])
            deps.extend(s["r"])
        st["subs"] = {}
        st["r"] = []
        st["w"] = me
        return deps

    def _deps_write(self, key, me):
        root, sub = self._split(key)
        deps = []
        for other in self.alias.get(root, ()):
            deps.extend(self._deps_write_whole(other, me))
        if sub is None:
            deps.extend(self._deps_write_whole(root, me))
            return deps
        st = self._st(root)
        if st["w"] is not None:
            deps.append(st["w"])
        deps.extend(st["r"])
        s = st["subs"].setdefault(sub, {"w": None, "r": []})
        if s["w"] is not None:
            deps.append(s["w"])
        deps.extend(s["r"])
        s["w"] = me
        s["r"] = []
        return deps

    def defer(self):
        self._defer = []
        return self._defer

    def end_defer(self):
        self._defer = None

    def interleave(self, lists):
        assert self._defer is None
        pos = [0] * len(lists)
        tot = sum(len(l) for l in lists)
        for _ in range(tot):
            best, bf = None, None
            for i, l in enumerate(lists):
                if pos[i] < len(l):
                    f = (pos[i] + 0.5) / len(l)
                    if bf is None or f < bf:
                        best, bf = i, f
            a = lists[best][pos[best]]
            pos[best] += 1
            self.op(*a[0], **a[1])

    def op(self, eng, fn, r=(), w=(), dma=None):
        if self._defer is not None:
            self._defer.append(((eng, fn), {"r": list(r), "w": list(w), "dma": dma}))
            return -1
        idx = len(self.ops)
        deps = set()
        for k in r:
            deps.update(self._deps_read(k, idx))
        for k in w:
            deps.update(self._deps_write(k, idx))
        deps.discard(idx)
        self.ops.append({"eng": eng, "fn": fn, "deps": deps, "dma": dma, "sig": False})
        return idx

    def _skip(self, t, o):
        return (t["dma"] is None and o["dma"] is None and t["eng"] == o["eng"]
                and o["eng"] not in SAME_ENGINE_SYNC)

    def emit(self):
        nc = self.nc
        ops = self.ops
        for o in ops:
            for d in o["deps"]:
                t = ops[d]
                if t["dma"] is None and not self._skip(t, o):
                    t["sig"] = True
        cnt = {}
        sems = {}

        def getsem(name):
            if name not in sems:
                sems[name] = self.es.enter_context(nc.semaphore("s_" + name))
            return sems[name]

        for o in ops:
            if o["dma"] is not None:
                k = "d_" + o["dma"]
                cnt[k] = cnt.get(k, 0) + 16
                o["tok"] = (k, cnt[k])
            elif o["sig"]:
                k = "e_" + o["eng"]
                cnt[k] = cnt.get(k, 0) + 1
                o["tok"] = (k, cnt[k])
            else:
                o["tok"] = None
        seen = {e: {} for e in self.eng}
        nwait = 0
        for o in ops:
            e = o["eng"]
            engine = self.eng[e]
            need = {}
            for d in o["deps"]:
                t = ops[d]
                if t["tok"] is None or self._skip(t, o):
                    continue
                k, v = t["tok"]
                if v > need.get(k, 0):
                    need[k] = v
            for k, v in need.items():
                if seen[e].get(k, 0) >= v:
                    continue
                engine.wait_ge(getsem(k), v)
                seen[e][k] = v
                nwait += 1
            ins = o["fn"](engine)
            if o["tok"] is not None:
                k, v = o["tok"]
                ins.then_inc(getsem(k), 16 if o["dma"] is not None else 1)
        sp = self.eng["sp"]
        for k, v in cnt.items():
            if k.startswith("d_"):
                sp.wait_ge(getsem(k), v)
        return {"n_ops": len(ops), "n_wait": nwait, "n_sems": len(sems)}


SPC = 576


def build(NBLK=8, NLAY=2, dbg=(), final=True):
    nc = bass.Bass("TRN2", target_bir_lowering=False)

    def din(name, shape, dt=F32):
        return nc.dram_tensor(name, shape, dt, kind="ExternalInput").ap()

    x_d = din("x", [SEQ, 1024])
    p_d = din("p", [2, SEQ, 256])
    w_in_d = din("w_in", [2, 1024, 5632])
    gxw_d = din("gate_x_w", [2, 10, 128, 128])
    gaw_d = din("gate_a_w", [2, 10, 128, 128])
    w_ao_d = din("w_a_out", [2, 1280, 1024])
    w_glu_d = din("w_glu", [2, 1024, 2048])
    w_o_d = din("w_o", [2, 1024, 1024])
    w_up_d = din("w_up", [2, 1024, 6144])
    w_dn_d = din("w_down", [2, 3072, 1024])
    w_pg_d = din("w_ple_gate", [2, 1024, 1024])
    w_pp_d = din("w_ple_proj", [2, 256, 1024])
    sp_d = din("spk", [2, 128, SPC])
    s5f_d = din("s5f", [2, 4, 128, 1024])
    cst_d = din("cst", [128, 128 + 8 * 240 + 1 + 1024])
    y_d = nc.dram_tensor("y", [SEQ, 1024], F32, kind="ExternalOutput").ap()
    s5m_d = nc.dram_tensor("s5m", [2, 4, 128, 8192], BF16, kind="Internal").ap()
    dbg_out = {}

    es = ExitStack()
    with es:
        P = Prog(nc, es)

        def sb(name, shape, dt=F32):
            return es.enter_context(nc.sbuf_tensor(name, shape, dt))

        X = sb("X", [128, NT, 1024])
        HT = sb("HT", [128, 8, TB], BF16)
        ST = sb("ST", [128, 16])
        WS = sb("WS", [128, NSLOT, 4096], BF16)
        IDB = sb("IDB", [128, 128], BF16)
        IDF = sb("IDF", [128, 128])
        ZS = sb("ZS", [128, 8, 240], BF16)
        SGN = sb("SGN", [128, 1])
        GF = sb("GF", [128, 1024])
        SPm = sb("SPm", [128, 2, SPC])
        C8 = sb("C8", [128, 2, 20])
        S5C = sb("S5C", [128, 2, 3, 64])
        LRUH = sb("LRUH", [128, 2, 10])
        XAH = sb("XAH", [128, 2, 10, 3])
        S5CAR = sb("S5CAR", [128, 2, 2, 64])
        FT = sb("FT", [128, 2, 48, 2])
        T1 = sb("T1", [128, 2, 64])
        T2 = sb("T2", [128, 2, 64])
        ARENA = 134776
        AR = sb("AR", [128, ARENA // 4])

        PM = es.enter_context(nc.psum_tensor("PM", [128, 4, 512], F32))
        PX = es.enter_context(nc.psum_tensor("PX", [128, 2, 512], F32))
        PTR = es.enter_context(nc.psum_tensor("PTR", [128, 2, 1024], BF16))

        arena_ranges = {}

        def av(name, off, nbytes, dt=F32, **re):
            assert off % 4 == 0 and nbytes % 4 == 0 and off + nbytes <= ARENA, (name, off, nbytes)
            arena_ranges[name] = (off, off + nbytes)
            v = AR[:, off // 4:(off + nbytes) // 4]
            if dt is not F32:
                v = v.bitcast(dt)
            if re:
                pat = re.pop("pat")
                v = v.rearrange(pat, **re)
            return v

        OA = av("OA", 0, 10240, BF16, pat="p (k n) -> p k n", n=TB)
        XA = av("XA", 10240, 20600, F32, pat="p (k n) -> p k n", n=515)
        GBf = av("GB", 30840, 10240, BF16, pat="p (k n) -> p k n", n=TB)
        LT = av("LT", 41080, 22528)
        YSB = av("YSB", 10240, 8192, BF16, pat="p (g c) -> p g c", c=64)
        YS = av("YS", 18432, 8192, BF16, pat="p (k n) -> p k n", n=TB)
        UB = av("UB", 63608, 8192, BF16, pat="p (k n) -> p k n", n=TB)
        XB = av("XB", 63608, 8192, BF16, pat="p (g c) -> p g c", c=64)
        U = av("U", 71800, 8192, BF16, pat="p (g c) -> p g c", c=64)
        S2 = av("S2", 79992, 33280, F32, pat="p (h g c) -> p h g c", h=2, g=64)
        SMA = av("SMA", 113272, 8192, BF16, pat="p (k n) -> p k n", n=TB)
        SMB = av("SMB", 121464, 8192, BF16, pat="p (k n) -> p k n", n=TB)
        TMP = av("TMP", 41080, 4096, F32, pat="p (k n) -> p k n", n=TB)
        HS = av("HS", 45176, 4096, BF16, pat="p (k n) -> p k n", n=1024)
        SQ = av("SQ", 49272, 2048, BF16)
        WG = av("WG", 129656, 5120, BF16, pat="p (w k n) -> p w k n", w=2, k=10)
        YB = av("YB", 79992, 16384, F32, pat="p (k n) -> p k n", n=TB)
        MGB = av("MGB", 96376, 8192, BF16, pat="p (k n) -> p k n", n=TB)
        PTOK = av("PTOK", 104568, 4096, F32, pat="p (t n) -> p t n", n=256)
        PB = av("PB", 108664, 2048, BF16, pat="p (t n) -> p t n", n=256)
        PTT = av("PTT", 110712, 2048, BF16, pat="p (k n) -> p k n", n=TB)
        ACTB = av("ACTB", 0, 24576, BF16, pat="p (k n) -> p k n", n=TB)
        UPF = av("UPF", 24576, 8224, F32, pat="p (k n) -> p k n", n=514)
        CV = av("CV", 32800, 8192, F32, pat="p (k n) -> p k n", n=TB)
        PTMP = av("PTMP", 40992, 4096, F32, pat="p (k n) -> p k n", n=TB)
        EX = av("EX", 0, 15360, F32, pat="p (g n) -> p g n", n=240)
        F1 = av("F1", 15360, 1024, F32, pat="p (g n) -> p g n", n=16)
        F2 = av("F2", 16384, 1024, F32, pat="p (g n) -> p g n", n=16)
        G1 = av("G1", 17408, 1024, F32, pat="p (g n) -> p g n", n=16)
        G2 = av("G2", 18432, 1024, F32, pat="p (g n) -> p g n", n=16)
        BBv = av("BB", 19456, 1024, F32, pat="p (g n) -> p g n", n=16)
        BBS = av("BBS", 20480, 1024, F32, pat="p (g n) -> p g n", n=16)
        Q1 = av("Q1", 21504, 1024, F32, pat="p (g n) -> p g n", n=16)
        Q2 = av("Q2", 22528, 1024, F32, pat="p (g n) -> p g n", n=16)
        CST = av("CST", 23552, 1024, F32, pat="p (g n) -> p g n", n=16)
        WAB = av("WAB", 24576, 4096, BF16, pat="p (g n) -> p g n", n=128)
        WASB = av("WASB", 28672, 4096, BF16, pat="p (g n) -> p g n", n=128)
        WCB = av("WCB", 32768, 4096, BF16, pat="p (g n) -> p g n", n=128)
        TBB = av("TBB", 36864, 4096, BF16, pat="p (g n) -> p g n", n=128)
        SC = av("SC", 40960, 64 * 4 * 40, F32, pat="p (k n) -> p k n", n=64)
        names = list(arena_ranges)
        for a in names:
            for b in names:
                if a != b:
                    (a0, a1), (b0, b1) = arena_ranges[a], arena_ranges[b]
                    if a0 < b1 and b0 < a1:
                        P.alias.setdefault(a, []).append(b)

        def mm(out, lhsT, rhs, start, stop, r, w):
            P.op("pe", lambda e: e.matmul(out, lhsT=lhsT, rhs=rhs, start=start, stop=stop), r=r, w=w)

        def tr(out, in_, ident, r, w):
            P.op("pe", lambda e: e.transpose(out, in_, ident), r=r, w=w)

        def act(out, in_, func, r, w, bias=None, scale=None, accum=None):
            kw = {}
            if bias is not None:
                kw["bias"] = bias
            if scale is not None:
                kw["scale"] = scale
            if accum is not None:
                kw["accum_out"] = accum
            P.op("act", lambda e: e.activation(out=out, in_=in_, func=func, **kw), r=r, w=w)

        def ts(eng, out, in0, s1, s2, op0, op1, r, w):
            if s2 is None:
                P.op(eng, lambda e: e.tensor_scalar(out=out, in0=in0, scalar1=s1, scalar2=None, op0=op0), r=r, w=w)
            else:
                P.op(eng, lambda e: e.tensor_scalar(out=out, in0=in0, scalar1=s1, scalar2=s2, op0=op0, op1=op1), r=r, w=w)

        def tt(eng, out, in0, in1, op, r, w):
            P.op(eng, lambda e: e.tensor_tensor(out=out, in0=in0, in1=in1, op=op), r=r, w=w)

        def stt(eng, out, in0, scalar, in1, op0, op1, r, w):
            P.op(eng, lambda e: e.scalar_tensor_tensor(out=out, in0=in0, scalar=scalar, in1=in1, op0=op0, op1=op1), r=r, w=w)

        def cp(eng, out, in_, r, w):
            if eng == "act":
                P.op("act", lambda e: e.copy(out=out, in_=in_), r=r, w=w)
            else:
                P.op(eng, lambda e: e.tensor_copy(out=out, in_=in_), r=r, w=w)

        def mset(eng, out, val, w):
            P.op(eng, lambda e: e.memset(out, val), w=w)

        def dma(eng, out, in_, key, r, w):
            P.op(eng, lambda e: e.dma_start(out=out, in_=in_), r=r, w=w, dma=key)

        def tap(name, ap, shape, r):
            if name in dbg:
                d = nc.dram_tensor("dbg_" + name, shape, ap.dtype, kind="ExternalOutput").ap()
                dbg_out[name] = d
                dma("sp", d, ap, "dbg_" + name, r, ["dbgdram_" + name])

        wctr = [0]

        def wload(src, nk, ncols, cast=True, rk=()):
            s = wctr[0] % NSLOT
            wctr[0] += 1
            v = WS[:, s, 0:nk * ncols].rearrange("p (k n) -> p k n", n=ncols)
            dma("pool" if cast else "sp", v, src, "w%d" % s, list(rk), ["ws/%d" % s])
            return v, "ws/%d" % s

        pmc = [0]

        def pm():
            b = pmc[0] % 4
            pmc[0] += 1
            return PM[:, b, :], "pm/%d" % b

        pxc = [0]

        def px():
            b = pxc[0] % 2
            pxc[0] += 1
            return PX[:, b, :], "px/%d" % b

        ptc = [0]

        def ptr():
            b = ptc[0] % 2
            ptc[0] += 1
            return PTR[:, b, :], "ptr/%d" % b

        dma("sp", IDF[:], cst_d[:, 0:128], "c0", [], ["IDF"])
        dma("pool", IDB[:], cst_d[:, 0:128], "c1", [], ["IDB"])
        dma("pool", ZS[:].rearrange("p a b -> p (a b)"), cst_d[:, 128:128 + 1920], "c2", [], ["ZS"])
        dma("sp", SGN[:], cst_d[:, 2048:2049], "c3", [], ["SGN"])
        dma("sp", GF[:], cst_d[:, 2049:2049 + 1024], "c4", [], ["GF"])
        mset("dve", ST[:, 8:12], -0.5, ["ST/nh"])
        mset("dve", LRUH[:], 0.0, ["LRUH"])
        mset("dve", XAH[:], 0.0, ["XAH"])
        mset("dve", S5CAR[:], 0.0, ["S5CAR"])
        mset("dve", FT[:], 0.0, ["FT"])

        for i in range(NLAY):
            dma("sp", SPm[:, i, :], sp_d[i], "spm%d" % i, [], ["SPm/%d" % i])
            spk = "SPm/%d" % i
            act(C8[:, i, 0:10], SPm[:, i, 102:112], AF.Exp, [spk], ["C8/%d" % i], scale=-1.0)
            act(C8[:, i, 0:10], C8[:, i, 0:10], AF.Ln, ["C8/%d" % i], ["C8/%d" % i], bias=1.0)
            ts("dve", C8[:, i, 10:20], C8[:, i, 0:10], -16.0, None, ALU.mult, None, ["C8/%d" % i], ["C8/%d" % i])
            ts("dve", C8[:, i, 0:10], C8[:, i, 0:10], -8.0, None, ALU.mult, None, ["C8/%d" % i], ["C8/%d" % i])

            a_re = SPm[:, i, 320:384]
            a_im = SPm[:, i, 384:448]
            ldt = SPm[:, i, 448:512]
            DM = SPm[:, i, 512:576]
            k = "SC"
            DT, ADT, TH, MAG, QQ, RR, MSK, SN, CS = (SC[:, j, :] for j in range(9))
            LBR, LBI, NR, DEN, CR, CI, W2, TA, TBt = (SC[:, 9 + j, :] for j in range(9))
            PRn = SC[:, 18:27, :]
            PIn = SC[:, 27:36, :]
            QI = SC[:, 36, :].bitcast(I32)
            act(DT, ldt, AF.Exp, [spk], [k])
            tt("dve", ADT, a_re, DT, ALU.mult, [spk, k], [k])
            tt("dve", TH, a_im, DT, ALU.mult, [spk, k], [k])
            act(MAG, ADT, AF.Exp, [k], [k])

            def sincos(dst, shift):
                ts("dve", RR, TH, shift, None, ALU.add, None, [k], [k])
                ts("dve", QQ, RR, 1.0 / (2 * PI), None, ALU.mult, None, [k], [k])
                cp("dve", QI, QQ, [k], [k])
                cp("dve", QQ, QI, [k], [k])
                stt("dve", RR, QQ, -2 * PI, RR, ALU.mult, ALU.add, [k], [k])
                ts("dve", MSK, RR, PI, -2 * PI, ALU.is_gt, ALU.mult, [k], [k])
                tt("dve", RR, RR, MSK, ALU.add, [k], [k])
                ts("dve", MSK, RR, -PI, 2 * PI, ALU.is_lt, ALU.mult, [k], [k])
                tt("dve", RR, RR, MSK, ALU.add, [k], [k])
                ts("dve", RR, RR, PI, -PI, ALU.min, ALU.max, [k], [k])
                act(dst, RR, AF.Sin, [k], [k])

            sincos(SN, 0.0)
            sincos(CS, PI / 2)
            tt("dve", LBR, MAG, CS, ALU.mult, [k], [k])
            tt("dve", LBI, MAG, SN, ALU.mult, [k], [k])
            mset("dve", PRn[:, 0, :], 1.0, [k])
            mset("dve", PIn[:, 0, :], 0.0, [k])
            for q in range(8):
                tt("dve", TA, PRn[:, q, :], LBR, ALU.mult, [k], [k])
                tt("dve", TBt, PIn[:, q, :], LBI, ALU.mult, [k], [k])
                tt("dve", PRn[:, q + 1, :], TA, TBt, ALU.subtract, [k], [k])
                tt("dve", TA, PRn[:, q, :], LBI, ALU.mult, [k], [k])
                tt("dve", TBt, PIn[:, q, :], LBR, ALU.mult, [k], [k])
                tt("dve", PIn[:, q + 1, :], TA, TBt, ALU.add, [k], [k])
            ts("dve", NR, LBR, -1.0, None, ALU.add, None, [k], [k])
            tt("dve", TA, a_re, a_re, ALU.mult, [spk, k], [k])
            tt("dve", TBt, a_im, a_im, ALU.mult, [spk, k], [k])
            tt("dve", DEN, TA, TBt, ALU.add, [k], [k])
            P.op("dve", lambda e, DEN=DEN: e.reciprocal(out=DEN, in_=DEN), r=[k], w=[k])
            tt("dve", TA, NR, a_re, ALU.mult, [spk, k], [k])
            tt("dve", TBt, LBI, a_im, ALU.mult, [spk, k], [k])
            tt("dve", TA, TA, TBt, ALU.add, [k], [k])
            tt("dve", CR, TA, DEN, ALU.mult, [k], [k])
            tt("dve", TA, LBI, a_re, ALU.mult, [spk, k], [k])
            tt("dve", TBt, NR, a_im, ALU.mult, [spk, k], [k])
            tt("dve", TA, TA, TBt, ALU.subtract, [k], [k])
            tt("dve", CI, TA, DEN, ALU.mult, [k], [k])
            ts("dve", W2, CI, SGN[:, 0:1], None, ALU.mult, None, [k, "SGN"], [k])
            cp("dve", S5C[:, i, 0, :], PRn[:, 8, :], [k], ["S5C/%d" % i])
            ts("dve", S5C[:, i, 1, :], PIn[:, 8, :], SGN[:, 0:1], None, ALU.mult, None, [k, "SGN"], ["S5C/%d" % i])
            ts("dve", S5C[:, i, 2, :], PIn[:, 8, :], SGN[:, 0:1], -1.0, ALU.mult, ALU.mult, [k, "SGN"], ["S5C/%d" % i])

            def bc(v, g0):
                return v[:, g0:g0 + 16].unsqueeze(2).to_broadcast([128, 16, 16])

            for gb in range(4):
                g0 = gb * 16
                for q, (tile_, nm) in enumerate(((F1, "F1"), (F2, "F2"), (G1, "G1"), (G2, "G2"))):
                    dma("sp", tile_, s5f_d[i, q, :, g0 * 16:(g0 + 16) * 16].rearrange("p (g n) -> p g n", n=16),
                        "s5f%d" % q, [], [nm])
                tt("dve", Q1, F1, bc(CR, g0), ALU.mult, ["F1", k], ["Q1"])
                tt("dve", Q2, F2, bc(W2, g0), ALU.mult, ["F2", k], ["Q2"])
                tt("dve", BBv, Q1, Q2, ALU.add, ["Q1", "Q2"], ["BB"])
                tt("dve", Q1, F2, bc(CR, g0), ALU.mult, ["F2", k], ["Q1"])
                tt("dve", Q2, F1, bc(W2, g0), ALU.mult, ["F1", k], ["Q2"])
                tt("dve", BBS, Q1, Q2, ALU.subtract, ["Q1", "Q2"], ["BBS"])
                mset("dve", EX[:, :, 128:240], 0.0, ["EX"])
                for j in range(8):
                    kk = 7 - j
                    tt("dve", Q1, BBv, bc(PRn[:, kk, :], g0), ALU.mult, ["BB", k], ["Q1"])
                    tt("dve", Q2, BBS, bc(PIn[:, kk, :], g0), ALU.mult, ["BBS", k], ["Q2"])
                    stt("dve", EX[:, :, j * 16:(j + 1) * 16], Q2, SGN[:, 0:1], Q1, ALU.mult, ALU.add,
                        ["Q1", "Q2", "SGN"], ["EX"])
                ts("dve", CST, G1, SGN[:, 0:1], -1.0, ALU.mult, ALU.mult, ["G1", "SGN"], ["CST"])
                for gq in range(4):
                    bank, bk = pm()
                    for gi in range(4):
                        gl = gq * 4 + gi
                        tr(bank[:, gi * 128:(gi + 1) * 128], EX[:, gl, 0:128], IDF[:], ["EX", "IDF"], [bk])
                    bv = bank.rearrange("p (g n) -> p g n", n=128)
                    cp("act", WAB[:, gq * 4:(gq + 1) * 4, :], bv, [bk], ["WAB"])
                    cp("dve", WASB[:, gq * 4:(gq + 1) * 4, 0:64], bv[:, :, 64:128], [bk], ["WASB"])
                    cp("dve", WASB[:, gq * 4:(gq + 1) * 4, 64:128], bv[:, :, 0:64], [bk], ["WASB"])
                    bank, bk = pm()
                    for gi in range(4):
                        gl = gq * 4 + gi
                        for l in range(8):
                            mm(bank[:, gi * 128 + l * 16: gi * 128 + (l + 1) * 16],
                               EX[:, gl, (7 - l) * 16:(7 - l) * 16 + 128], CST[:, gl, :], True, True,
                               ["EX", "CST"], [bk])
                        stt("dve", TBB[:, gl, :], IDF[:], DM[:, g0 + gl:g0 + gl + 1], bank[:, gi * 128:(gi + 1) * 128],
                            ALU.mult, ALU.add, [bk, "IDF", spk], ["TBB"])
                for l in range(8):
                    tt("dve", Q1, G1, bc(PRn[:, l + 1, :], g0), ALU.mult, ["G1", k], ["Q1"])
                    tt("dve", Q2, G2, bc(PIn[:, l + 1, :], g0), ALU.mult, ["G2", k], ["Q2"])
                    ts("dve", Q1, Q1, SGN[:, 0:1], -1.0, ALU.mult, ALU.mult, ["Q1", "SGN"], ["Q1"])
                    tt("dve", WCB[:, :, l * 16:(l + 1) * 16], Q1, Q2, ALU.subtract, ["Q1", "Q2"], ["WCB"])
                for q, (tile_, nm) in enumerate(((WAB, "WAB"), (WASB, "WASB"), (WCB, "WCB"), (TBB, "TBB"))):
                    dma("sp", s5m_d[i, q, :, g0 * 128:(g0 + 16) * 128].rearrange("p (g n) -> p g n", n=128),
                        tile_, "s5mo%d" % q, [nm], ["s5m%d_%d" % (i, q)])

        def kview(ap):
            return ap.rearrange("(k p) n -> p k n", p=128)

        w_in_v = [kview(w_in_d[i]) for i in range(2)]
        w_ao_v = [kview(w_ao_d[i]) for i in range(2)]
        w_glu_v = [kview(w_glu_d[i]) for i in range(2)]
        w_o_v = [kview(w_o_d[i]) for i in range(2)]
        w_up_v = [kview(w_up_d[i]) for i in range(2)]
        w_dn_v = [kview(w_dn_d[i]) for i in range(2)]
        w_pg_v = [kview(w_pg_d[i]) for i in range(2)]
        w_pp_v = [kview(w_pp_d[i]) for i in range(2)]
        gxw_v = [gxw_d[i].rearrange("h i j -> i h j") for i in range(2)]
        gaw_v = [gaw_d[i].rearrange("h i j -> i h j") for i in range(2)]
        x_v = x_d.rearrange("(b t p) d -> b p t d", t=NT, p=128)
        y_v = y_d.rearrange("(b t p) d -> b p t d", t=NT, p=128)
        p_v = [p_d[i].rearrange("(b t p) d -> b p t d", t=NT, p=128) for i in range(2)]

        def norm(gcols, gkey):
            for t in range(NT):
                act(SQ[:], X[:, t, :], AF.Square, ["X/%d" % t], ["SQ", "ST/ss"], accum=ST[:, t:t + 1])
            ts("dve", ST[:, 4:8], ST[:, 0:4], 1.0 / 1024, EPS, ALU.mult, ALU.add, ["ST/ss"], ["ST/rs"])
            tt("pool", ST[:, 4:8], ST[:, 4:8], ST[:, 8:12], ALU.pow, ["ST/rs", "ST/nh"], ["ST/rs"])
            for t in range(NT):
                s = t % 2
                act(HS[:, s, :], X[:, t, :], AF.Copy, ["X/%d" % t, "ST/rs"], ["HS/%d" % s], scale=ST[:, 4 + t:5 + t])
                bank, bk = ptr()
                for kt in range(8):
                    tr(bank[:, kt * 128:(kt + 1) * 128], HS[:, s, kt * 128:(kt + 1) * 128], IDB[:],
                       ["HS/%d" % s, "IDB"], [bk])
                tt("dve", HT[:, :, t * 128:(t + 1) * 128], bank.rearrange("p (k n) -> p k n", n=128),
                   gcols.unsqueeze(2).to_broadcast([128, 8, 128]), ALU.mult, [bk, gkey], ["HT/%d" % t])

        HTall = ["HT/%d" % t for t in range(NT)]

        def proj_fm(ws, wk, j, nk, rhs_of, rkeys):
            bank, bk = pm()
            for kt in range(nk):
                mm(bank, ws[:, kt, j * 128:(j + 1) * 128], rhs_of(kt), kt == 0, kt == nk - 1, [wk] + rkeys, [bk])
            return bank, bk

        def layer(i, blk):
            spk = "SPm/%d" % i
            norm(SPm[:, i, 0:8], spk)
            tap("ht1_%d_%d" % (i, blk), HT[:], [128, 8, TB], HTall)

            for c in range(7):
                ws, wk = wload(w_in_v[i][:, :, c * 512:(c + 1) * 512], 8, 512)
                for j in range(4):
                    t = 4 * c + j
                    bank, bk = proj_fm(ws, wk, j, 8, lambda kt: HT[:, kt, :], HTall)
                    if t < 10:
                        cp("act", XA[:, t, 3:515], bank, [bk], ["XA/%d" % t])
                    elif t < 20:
                        act(GBf[:, t - 10, :], bank, AF.Gelu_apprx_tanh, [bk], ["GB/%d" % (t - 10)])
                    else:
                        cp("act" if t % 2 else "dve", UB[:, t - 20, :], bank, [bk], ["UB/%d" % (t - 20)])

            def lru_tile(t, wgx, wgxk, wga, wgak):
                q = t % 2
                base = q * 2816
                xc = LT[:, base:base + 512]
                gx = LT[:, base + 512:base + 1024]
                aa = LT[:, base + 1024:base + 1536]
                m2 = LT[:, base + 1536:base + 2048]
                hh = LT[:, base + 2048:base + 2560]
                xcb = LT[:, base + 2560:base + 2816].bitcast(BF16)
                lk = "LT/%d" % q
                xk = "XA/%d" % t
                cp("dve", XA[:, t, 0:3], XAH[:, i, t, :], ["XAH/%d_%d" % (i, t)], [xk])
                cw = lambda j: SPm[:, i, 32 + t * 4 + j:33 + t * 4 + j]
                ts("dve", xc, XA[:, t, 0:512], cw(0), SPm[:, i, 72 + t:73 + t], ALU.mult, ALU.add, [xk, spk], [lk])
                for j in range(1, 4):
                    stt("dve", xc, XA[:, t, j:j + 512], cw(j), xc, ALU.mult, ALU.add, [xk, spk, lk], [lk])
                cp("dve", XAH[:, i, t, :], XA[:, t, 512:515], [xk], ["XAH/%d_%d" % (i, t)])
                cp("act", xcb, xc, [lk], [lk])
                bx, bxk = px()
                mm(bx, wgx[:, t, :], xcb, True, True, [wgxk, lk], [bxk])
                ba, bak = px()
                mm(ba, wga[:, t, :], xcb, True, True, [wgak, lk], [bak])
                act(gx, bx, AF.Sigmoid, [bxk, spk], [lk], bias=SPm[:, i, 82 + t:83 + t])
                act(hh, ba, AF.Sigmoid, [bak, spk], [lk], bias=SPm[:, i, 92 + t:93 + t])
                act(aa, hh, AF.Exp, [lk, "C8/%d" % i], [lk], scale=C8[:, i, t:t + 1])
                act(m2, hh, AF.Exp, [lk, "C8/%d" % i], [lk], scale=C8[:, i, 10 + t:11 + t])
                act(m2, m2, AF.Sqrt, [lk], [lk], bias=1.0, scale=-1.0)
                tt("dve", gx, gx, xc, ALU.mult, [lk], [lk])
                stt("dve", m2, m2, 1e-6, gx, ALU.max, ALU.mult, [lk], [lk])
                hk = "LRUH/%d_%d" % (i, t)
                P.op("dve", lambda e, hh=hh, aa=aa, m2=m2, t=t: e.tensor_tensor_scan(
                    out=hh, data0=aa, data1=m2, initial=LRUH[:, i, t:t + 1], op0=ALU.mult, op1=ALU.add),
                    r=[lk, hk], w=[lk])
                cp("dve", LRUH[:, i, t:t + 1], hh[:, 511:512], [lk], [hk])
                tt("dve", OA[:, t, :], hh, GBf[:, t, :], ALU.mult, [lk, "GB/%d" % t], ["OA/%d" % t])

            L_lru = P.defer()
            dma("pool", WG[:, 0], gxw_v[i], "wg0", [], ["WG/0"])
            dma("pool", WG[:, 1], gaw_v[i], "wg1", [], ["WG/1"])
            wgx, wgxk, wga, wgak = WG[:, 0], "WG/0", WG[:, 1], "WG/1"
            for t in range(10):
                lru_tile(t, wgx, wgxk, wga, wgak)
            P.end_defer()

            for j in range(8):
                bank, bk = pm()
                bv = bank.rearrange("p (g c) -> p g c", c=64)
                for gl in range(8):
                    for l in range(8):
                        mm(bv[:, gl, :], ZS[:, gl, 112 - 16 * l:240 - 16 * l], UB[:, j, l::8], l == 0, l == 7,
                           ["ZS", "UB/%d" % j], [bk])
                cp("act" if j % 2 else "dve", U[:, j * 8:(j + 1) * 8, :], bv, [bk], ["U/%d" % j])
            for h in range(2):
                for half in range(2):
                    ws, wk = wload(s5m_d[i, h, :, half * 4096:(half + 1) * 4096].rearrange("p (g n) -> p g n", n=128),
                                   32, 128, cast=False, rk=["s5m%d_%d" % (i, h)])
                    for j4 in range(4):
                        j = half * 4 + j4
                        bank, bk = pm()
                        bv = bank.rearrange("p (g c) -> p g c", c=64)
                        for gl in range(8):
                            mm(bv[:, gl, :], ws[:, j4 * 8 + gl, :], U[:, j * 8 + gl, :], True, True,
                               [wk, "U/%d" % j], [bk])
                        cp("act" if h else "dve", S2[:, h, j * 8:(j + 1) * 8, 1:65], bv, [bk], ["S2/%d" % h])
            L_rec = P.defer()
            cp("dve", S2[:, :, :, 0], S5CAR[:, i, :, :], ["S5CAR/%d" % i], ["S2"])
            ArB = S5C[:, i, 0, :].unsqueeze(1).to_broadcast([128, 2, 64])
            sck = "S5C/%d" % i
            for c in range(64):
                tt("dve", T1[:], S2[:, :, :, c], ArB, ALU.mult, ["S2", sck], ["T1"])
                tt("dve", T2[:, 0, :], S2[:, 1, :, c], S5C[:, i, 1, :], ALU.mult, ["S2", sck], ["T2"])
                tt("dve", T2[:, 1, :], S2[:, 0, :, c], S5C[:, i, 2, :], ALU.mult, ["S2", sck], ["T2"])
                tt("dve", T1[:], T1[:], T2[:], ALU.add, ["T1", "T2"], ["T1"])
                tt("dve", S2[:, :, :, c + 1], S2[:, :, :, c + 1], T1[:], ALU.add, ["S2", "T1"], ["S2"])
            cp("dve", S5CAR[:, i, :, :], S2[:, :, :, 64], ["S2"], ["S5CAR/%d" % i])
            cp("act", XB[:], S2[:, 0, :, 0:64], ["S2"], ["XB"])
            P.end_defer()
            L_p3 = P.defer()
            for c in range(7, 11):
                ws, wk = wload(w_in_v[i][:, :, c * 512:(c + 1) * 512], 8, 512)
                for j in range(4):
                    t = 4 * c + j
                    bank, bk = proj_fm(ws, wk, j, 8, lambda kt: HT[:, kt, :], HTall)
                    if t < 36:
                        act(SMA[:, t - 28, :], bank, AF.Sigmoid, [bk], ["SMA/%d" % (t - 28)])
                    else:
                        act(SMB[:, t - 36, :], bank, AF.Sigmoid, [bk], ["SMB/%d" % (t - 36)])
            P.end_defer()
            P.interleave([L_lru, L_rec, L_p3])
            tap("oa_%d_%d" % (i, blk), OA[:], [128, 10, TB], ["OA"])
            tap("xb_%d_%d" % (i, blk), XB[:], [128, 64, 64], ["XB"])
            dma("sp", PTOK[:], p_v[i][blk], "ptok", [], ["PTOK"])
            for half in range(2):
                wt_, wtk = wload(s5m_d[i, 3, :, half * 4096:(half + 1) * 4096].rearrange("p (g n) -> p g n", n=128),
                                 32, 128, cast=False, rk=["s5m%d_3" % i])
                wc_, wck = wload(s5m_d[i, 2, :, half * 4096:(half + 1) * 4096].rearrange("p (g n) -> p g n", n=128),
                                 32, 128, cast=False, rk=["s5m%d_2" % i])
                for j4 in range(4):
                    j = half * 4 + j4
                    bank, bk = pm()
                    bv = bank.rearrange("p (g c) -> p g c", c=64)
                    for gl in range(8):
                        g = j * 8 + gl
                        mm(bv[:, gl, :], wt_[:, j4 * 8 + gl, :], U[:, g, :], True, False, [wtk, "U/%d" % j], [bk])
                        mm(bv[:, gl, :], wc_[:, j4 * 8 + gl, :], XB[:, g, :], False, True, [wck, "XB"], [bk])
                    cp("act" if j % 2 else "dve", YSB[:, j * 8:(j + 1) * 8, :], bv, [bk], ["YSB/%d" % j])
            for j in range(8):
                bank, bk = pm()
                for l in range(8):
                    for gl in range(8):
                        mm(bank[:, l::8], ZS[:, l, 112 - 16 * gl:240 - 16 * gl], YSB[:, j * 8 + gl, :],
                           gl == 0, gl == 7, ["ZS", "YSB/%d" % j], [bk])
                act(YS[:, j, :], bank, AF.Gelu_apprx_tanh, [bk], ["YS/%d" % j])
            tap("ys_%d_%d" % (i, blk), YS[:], [128, 8, TB], ["YS"])
            YSall = ["YS/%d" % j for j in range(8)]
            for c in (0, 2, 1, 3):
                ws, wk = wload(w_glu_v[i][:, :, c * 512:(c + 1) * 512], 8, 512)
                for j in range(4):
                    t = 4 * c + j
                    bank, bk = proj_fm(ws, wk, j, 8, lambda kt: YS[:, kt, :], YSall)
                    bcol = SPm[:, i, 112 + t:113 + t]
                    if t < 8:
                        act(YB[:, t, :], bank, AF.Identity, [bk, spk], ["YB/%d" % t], bias=bcol)
                    else:
                        s = t % 2
                        n = t - 8
                        act(TMP[:, s, :], bank, AF.Sigmoid, [bk, spk], ["TMP/%d" % s], bias=bcol)
                        tt("dve", TMP[:, s, :], TMP[:, s, :], SMB[:, n, :], ALU.mult,
                           ["TMP/%d" % s, "SMB/%d" % n], ["TMP/%d" % s])
                        tt("dve", YB[:, n, :], YB[:, n, :], TMP[:, s, :], ALU.mult,
                           ["YB/%d" % n, "TMP/%d" % s], ["YB/%d" % n])
            OAall = ["OA/%d" % t for t in range(10)]
            for c in range(4):
                ws, wk = wload(w_ao_v[i][:, :, c * 256:(c + 1) * 256], 10, 256)
                for j in range(2):
                    n = 2 * c + j
                    bank, bk = proj_fm(ws, wk, j, 10, lambda kt: OA[:, kt, :], OAall)
                    s = n % 2
                    tt("dve", TMP[:, s, :], bank, SMA[:, n, :], ALU.mult, [bk, "SMA/%d" % n], ["TMP/%d" % s])
                    tt("dve", MGB[:, n, :], TMP[:, s, :], YB[:, n, :], ALU.add, ["TMP/%d" % s, "YB/%d" % n],
                       ["MGB/%d" % n])
            tap("mg_%d_%d" % (i, blk), MGB[:], [128, 8, TB], ["MGB"])
            MGall = ["MGB/%d" % n for n in range(8)]
            for h in range(2):
                ws, wk = wload(w_o_v[i][:, :, h * 512:(h + 1) * 512], 8, 512)
                for t in range(NT):
                    bank, bk = pm()
                    for kt in range(8):
                        mm(bank, MGB[:, kt, t * 128:(t + 1) * 128], ws[:, kt, :], kt == 0, kt == 7, [wk] + MGall, [bk])
                    tt("dve", X[:, t, h * 512:(h + 1) * 512], bank, X[:, t, h * 512:(h + 1) * 512], ALU.add,
                       [bk, "X/%d" % t], ["X/%d" % t])
            tap("x1_%d_%d" % (i, blk), X[:], [128, NT, 1024], ["X"])

            norm(SPm[:, i, 8:16], spk)

            def ffn_tile(ws, wk, c, j, half):
                t = 4 * c + j
                o = t % 24
                q = (2 * half + (j % 2))
                bank, bk = proj_fm(ws, wk, j, 8, lambda kt: HT[:, kt, :], HTall)
                uk = "UPF/%d" % q
                ck = "CV/%d" % q
                ftk = "FT/%d_%d" % (i, t)
                cp("act", UPF[:, q, 2:514], bank, [bk], [uk])
                fw_ = lambda jj: SPm[:, i, 128 + t * 3 + jj:129 + t * 3 + jj]
                cp("dve", UPF[:, q, 0:2], FT[:, i, t, :], [ftk], [uk])
                ts("dve", CV[:, q, :], UPF[:, q, 0:512], fw_(0), SPm[:, i, 272 + t:273 + t], ALU.mult, ALU.add,
                   [uk, spk], [ck])
                for jj in range(1, 3):
                    stt("dve", CV[:, q, :], UPF[:, q, jj:jj + 512], fw_(jj), CV[:, q, :], ALU.mult, ALU.add,
                        [uk, spk, ck], [ck])
                cp("dve", FT[:, i, t, :], UPF[:, q, 512:514], [uk], [ftk])
                if half == 0:
                    act(ACTB[:, o, :], CV[:, q, :], AF.Gelu_apprx_tanh, [ck], ["ACTB/%d" % o])
                else:
                    tt("dve", ACTB[:, o, :], ACTB[:, o, :], CV[:, q, :], ALU.mult, ["ACTB/%d" % o, ck],
                       ["ACTB/%d" % o])

            for c6 in range(6):
                for half in range(2):
                    c = c6 + 6 * half
                    ws, wk = wload(w_up_v[i][:, :, c * 512:(c + 1) * 512], 8, 512)
                    for jp in range(2):
                        lists = []
                        for j in (2 * jp, 2 * jp + 1):
                            lists.append(P.defer())
                            ffn_tile(ws, wk, c, j, half)
                            P.end_defer()
                        P.interleave(lists)
            tap("actb_%d_%d" % (i, blk), ACTB[:], [128, 24, TB], ["ACTB"])
            ACall = ["ACTB/%d" % o for o in range(24)]
            for h in range(2):
                banks = [pm() for _ in range(NT)]
                for kc in range(3):
                    ws, wk = wload(w_dn_v[i][:, kc * 8:(kc + 1) * 8, h * 512:(h + 1) * 512], 8, 512)
                    for t in range(NT):
                        bank, bk = banks[t]
                        for k8 in range(8):
                            mm(bank, ACTB[:, kc * 8 + k8, t * 128:(t + 1) * 128], ws[:, k8, :],
                               kc == 0 and k8 == 0, kc == 2 and k8 == 7, [wk] + ACall, [bk])
                for t in range(NT):
                    bank, bk = banks[t]
                    tt("dve", X[:, t, h * 512:(h + 1) * 512], bank, X[:, t, h * 512:(h + 1) * 512], ALU.add,
                       [bk, "X/%d" % t], ["X/%d" % t])
            tap("x2_%d_%d" % (i, blk), X[:], [128, NT, 1024], ["X"])

            norm(SPm[:, i, 16:24], spk)
            cp("dve", PB[:], PTOK[:], ["PTOK"], ["PB"])
            bank, bk = ptr()
            bv = bank.rearrange("p (k n) -> p k n", n=TB)
            for t in range(NT):
                for kk in range(2):
                    tr(bv[:, kk, t * 128:(t + 1) * 128], PB[:, t, kk * 128:(kk + 1) * 128], IDB[:], ["PB", "IDB"], [bk])
            cp("act", PTT[:], bv, [bk], ["PTT"])
            for h in range(2):
                wg, wgk = wload(w_pg_v[i][:, :, h * 512:(h + 1) * 512], 8, 512)
                wp, wpk = wload(w_pp_v[i][:, :, h * 512:(h + 1) * 512], 2, 512)
                for t in range(NT):
                    bg, bgk = pm()
                    for kt in range(8):
                        mm(bg, HT[:, kt, t * 128:(t + 1) * 128], wg[:, kt, :], kt == 0, kt == 7, [wgk, "HT/%d" % t], [bgk])
                    bp, bpk = px()
                    for kk in range(2):
                        mm(bp, PTT[:, kk, t * 128:(t + 1) * 128], wp[:, kk, :], kk == 0, kk == 1, [wpk, "PTT"], [bpk])
                    s = t % 2
                    act(TMP[:, s, :], bg, AF.Sigmoid, [bgk], ["TMP/%d" % s])
                    tt("dve", TMP[:, s, :], TMP[:, s, :], bp, ALU.mult, ["TMP/%d" % s, bpk], ["TMP/%d" % s])
                    tt("dve", X[:, t, h * 512:(h + 1) * 512], TMP[:, s, :], X[:, t, h * 512:(h + 1) * 512], ALU.add,
                       ["TMP/%d" % s, "X/%d" % t], ["X/%d" % t])
            tap("x3_%d_%d" % (i, blk), X[:], [128, NT, 1024], ["X"])

        for blk in range(NBLK):
            dma("sp", X[:], x_v[blk], "xin", [], ["X"])
            for i in range(NLAY):
                layer(i, blk)
            if final:
                for t in range(NT):
                    act(SQ[:], X[:, t, :], AF.Square, ["X/%d" % t], ["SQ", "ST/ss"], accum=ST[:, t:t + 1])
                ts("dve", ST[:, 4:8], ST[:, 0:4], 1.0 / 1024, EPS, ALU.mult, ALU.add, ["ST/ss"], ["ST/rs"])
                tt("pool", ST[:, 4:8], ST[:, 4:8], ST[:, 8:12], ALU.pow, ["ST/rs", "ST/nh"], ["ST/rs"])
                for t in range(NT):
                    stt("dve", X[:, t, :], X[:, t, :], ST[:, 4 + t:5 + t], GF[:], ALU.mult, ALU.mult,
                        ["X/%d" % t, "ST/rs", "GF"], ["X/%d" % t])
            dma("sp", y_v[blk], X[:], "yout", ["X"], ["ydram/%d" % blk])

        with nc.allow_non_contiguous_dma(reason="small strided parameter loads"):
            info = P.emit()
    return nc, dbg_out, info


def _cols(v):
    return np.ascontiguousarray(v.reshape(-1, 128).T)


def host_layout(inp):
    f = np.float32
    spk = np.zeros((2, 128, SPC), f)
    s5f = np.zeros((2, 4, 128, 1024), f)
    for i in range(2):
        s = spk[i]
        s[:, 0:8] = _cols(inp["g_mix"][i])
        s[:, 8:16] = _cols(inp["g_ffn"][i])
        s[:, 16:24] = _cols(inp["g_ple"][i])
        s[:, 24:32] = _cols(inp["g_final"])
        caw = inp["conv_a_w"][i]
        for j in range(4):
            s[:, 32 + j:72:4] = _cols(caw[j])
        s[:, 72:82] = _cols(inp["conv_a_b"][i])
        s[:, 82:92] = _cols(inp["gate_x_b"][i])
        s[:, 92:102] = _cols(inp["gate_a_b"][i])
        s[:, 102:112] = _cols(inp["lru_lambda"][i])
        s[:, 112:128] = _cols(inp["b_glu"][i])
        cfw = inp["conv_f_w"][i]
        for j in range(3):
            s[:, 128 + j:272:3] = _cols(cfw[j])
        s[:, 272:320] = _cols(inp["conv_f_b"][i])
        are_t = inp["s5_a_re"][i].T
        aim_t = inp["s5_a_im"][i].T
        s[:, 320:384] = np.concatenate([are_t, are_t], 0)
        s[:, 384:448] = np.concatenate([aim_t, aim_t], 0)
        s[:, 448:512] = np.broadcast_to(inp["s5_log_dt"][i][None, :], (128, 64))
        dm = inp["s5_d"][i].reshape(64, 16).T
        s[:, 512:576] = np.tile(dm, (8, 1))
        bre = inp["s5_b_re"][i].transpose(1, 0, 2).reshape(64, 1024)
        bim = inp["s5_b_im"][i].transpose(1, 0, 2).reshape(64, 1024)
        cre = inp["s5_c_re"][i].transpose(2, 0, 1).reshape(64, 1024)
        cim = inp["s5_c_im"][i].transpose(2, 0, 1).reshape(64, 1024)
        s5f[i, 0] = np.concatenate([bre, bim], 0)
        s5f[i, 1] = np.concatenate([bim, bre], 0)
        s5f[i, 2] = np.concatenate([cre, cim], 0)
        s5f[i, 3] = np.concatenate([cim, cre], 0)
    cst = np.zeros((128, 128 + 1920 + 1 + 1024), f)
    cst[:, 0:128] = np.eye(128, dtype=f)
    z = np.zeros((128, 8, 240), f)
    for gl in range(8):
        for ch in range(16):
            z[gl * 16 + ch, gl, 112 + ch] = 1.0
    cst[:, 128:2048] = z.reshape(128, 1920)
    cst[0:64, 2048] = -1.0
    cst[64:128, 2048] = 1.0
    cst[:, 2049:] = np.broadcast_to(inp["g_final"][None, :], (128, 1024))
    shared = {"spk": spk, "s5f": s5f, "cst": cst}
    for k in ("w_in", "gate_x_w", "gate_a_w", "w_a_out", "w_glu", "w_o", "w_up", "w_down",
              "w_ple_gate", "w_ple_proj"):
        shared[k] = np.ascontiguousarray(inp[k], dtype=f)
    return shared


_CACHE = {}


def kernel(**inputs):
    inp = {k: np.asarray(v) for k, v in inputs.items()}
    shared = host_layout(inp)
    if "nc" not in _CACHE:
        _CACHE["nc"] = build()[0]
    nc = _CACHE["nc"]
    in_maps = []
    for b in range(8):
        m = dict(shared)
        m["x"] = np.ascontiguousarray(inp["x"][b], dtype=np.float32)
        m["p"] = np.ascontiguousarray(inp["p"][:, b], dtype=np.float32)
        in_maps.append(m)
    res = run_bass_kernel_spmd(nc, in_maps, core_ids=list(range(8)))
    return np.stack([np.asarray(r["y"], dtype=np.float32) for r in res.results], 0)
```

```python
import math
from contextlib import ExitStack

import numpy as np
import concourse.bass as bass
import concourse.mybir as mybir
from concourse.bass_utils import run_bass_kernel_spmd

F32 = mybir.dt.float32
BF16 = mybir.dt.bfloat16
I32 = mybir.dt.int32
AF = mybir.ActivationFunctionType
ALU = mybir.AluOpType

SAME_ENGINE_SYNC = ("act", "pool", "dve", "sp")
NSLOT = 4
BIG_N = 128
TB = 512
NT = 4
SEQ = 4096
EPS = 1e-6
PI = math.pi


class Prog:
    def __init__(self, nc, es):
        self.nc = nc
        self.es = es
        self.eng = {"pe": nc.tensor, "act": nc.scalar, "dve": nc.vector,
                    "pool": nc.gpsimd, "sp": nc.sync}
        self.ops = []
        self.state = {}
        self.alias = {}
        self._defer = None

    @staticmethod
    def _split(key):
        if "/" in key:
            r, s = key.split("/", 1)
            return r, s
        return key, None

    def _st(self, root):
        return self.state.setdefault(root, {"w": None, "r": [], "subs": {}})

    def _deps_read(self, key, me):
        root, sub = self._split(key)
        st = self._st(root)
        deps = []
        if st["w"] is not None:
            deps.append(st["w"])
        if sub is None:
            for s in st["subs"].values():
                if s["w"] is not None:
                    deps.append(s["w"])
            st["r"].append(me)
        else:
            s = st["subs"].setdefault(sub, {"w": None, "r": []})
            if s["w"] is not None:
                deps.append(s["w"])
            s["r"].append(me)
        return deps

    def _deps_write_whole(self, root, me):
        st = self._st(root)
        deps = []
        if st["w"] is not None:
            deps.append(st["w"])
        deps.extend(st["r"])
        for s in st["subs"].values():
            if s["w"] is not None:
                deps.append(s["w"])
            deps.extend(s["r"])
        st["subs"] = {}
        st["r"] = []
        st["w"] = me
        return deps

    def _deps_write(self, key, me):
        root, sub = self._split(key)
        deps = []
        for other in self.alias.get(root, ()):
            deps.extend(self._deps_write_whole(other, me))
        if sub is None:
            deps.extend(self._deps_write_whole(root, me))
            return deps
        st = self._st(root)
        if st["w"] is not None:
            deps.append(st["w"])
        deps.extend(st["r"])
        s = st["subs"].setdefault(sub, {"w": None, "r": []})
        if s["w"] is not None:
            deps.append(s["w"])
        deps.extend(s["r"])
        s["w"] = me
        s["r"] = []
        return deps

    def defer(self):
        self._defer = []
        return self._defer

    def end_defer(self):
        self._defer = None

    def interleave(self, lists):
        assert self._defer is None
        pos = [0] * len(lists)
        tot = sum(len(l) for l in lists)
        for _ in range(tot):
            best, bf = None, None
            for i, l in enumerate(lists):
                if pos[i] < len(l):
                    f = (pos[i] + 0.5) / len(l)
                    if bf is None or f < bf:
                        best, bf = i, f
            a = lists[best][pos[best]]
            pos[best] += 1
            self.op(*a[0], **a[1])

    def op(self, eng, fn, r=(), w=(), dma=None, n=0, fs=False):
        if self._defer is not None:
            self._defer.append(((eng, fn), {"r": list(r), "w": list(w), "dma": dma, "n": n, "fs": fs}))
            return -1
        idx = len(self.ops)
        deps = set()
        for k in r:
            deps.update(self._deps_read(k, idx))
        for k in w:
            deps.update(self._deps_write(k, idx))
        deps.discard(idx)
        self.ops.append({"eng": eng, "fn": fn, "deps": deps, "dma": dma, "sig": False, "n": n, "fs": fs})
        return idx

    def _skip(self, t, o):
        if t["dma"] is not None or o["dma"] is not None or t["eng"] != o["eng"]:
            return False
        if o["eng"] not in SAME_ENGINE_SYNC:
            return True
        return t["n"] >= BIG_N and not o["fs"]

    def emit(self):
        nc = self.nc
        ops = self.ops
        for o in ops:
            for d in o["deps"]:
                t = ops[d]
                if t["dma"] is None and not self._skip(t, o):
                    t["sig"] = True
        cnt = {}
        sems = {}

        def getsem(name):
            if name not in sems:
                sems[name] = self.es.enter_context(nc.semaphore("s_" + name))
            return sems[name]

        for o in ops:
            if o["dma"] is not None:
                k = "d_" + o["dma"]
                cnt[k] = cnt.get(k, 0) + 16
                o["tok"] = (k, cnt[k])
            elif o["sig"]:
                k = "e_" + o["eng"]
                cnt[k] = cnt.get(k, 0) + 1
                o["tok"] = (k, cnt[k])
            else:
                o["tok"] = None
        seen = {e: {} for e in self.eng}
        nwait = 0
        for o in ops:
            e = o["eng"]
            engine = self.eng[e]
            need = {}
            for d in o["deps"]:
                t = ops[d]
                if t["tok"] is None or self._skip(t, o):
                    continue
                k, v = t["tok"]
                if v > need.get(k, 0):
                    need[k] = v
            for k, v in need.items():
                if seen[e].get(k, 0) >= v:
                    continue
                engine.wait_ge(getsem(k), v)
                seen[e][k] = v
                nwait += 1
            ins = o["fn"](engine)
            if o["tok"] is not None:
                k, v = o["tok"]
                ins.then_inc(getsem(k), 16 if o["dma"] is not None else 1)
        sp = self.eng["sp"]
        for k, v in cnt.items():
            if k.startswith("d_"):
                sp.wait_ge(getsem(k), v)
        return {"n_ops": len(ops), "n_wait": nwait, "n_sems": len(sems)}


SPC = 576


def build(NBLK=8, NLAY=2, dbg=(), final=True):
    nc = bass.Bass("TRN2", target_bir_lowering=False)

    def din(name, shape, dt=F32):
        return nc.dram_tensor(name, shape, dt, kind="ExternalInput").ap()

    x_d = din("x", [SEQ, 1024])
    p_d = din("p", [2, SEQ, 256])
    w_in_d = din("w_in", [2, 1024, 5632])
    gxw_d = din("gate_x_w", [2, 10, 128, 128])
    gaw_d = din("gate_a_w", [2, 10, 128, 128])
    w_ao_d = din("w_a_out", [2, 1280, 1024])
    w_glu_d = din("w_glu", [2, 1024, 2048])
    w_o_d = din("w_o", [2, 1024, 1024])
    w_up_d = din("w_up", [2, 1024, 6144])
    w_dn_d = din("w_down", [2, 3072, 1024])
    w_pg_d = din("w_ple_gate", [2, 1024, 1024])
    w_pp_d = din("w_ple_proj", [2, 256, 1024])
    sp_d = din("spk", [2, 128, SPC])
    s5f_d = din("s5f", [2, 4, 128, 1024])
    cst_d = din("cst", [128, 128 + 8 * 240 + 1 + 1024])
    y_d = nc.dram_tensor("y", [SEQ, 1024], F32, kind="ExternalOutput").ap()
    s5m_d = nc.dram_tensor("s5m", [2, 4, 128, 8192], BF16, kind="Internal").ap()
    dbg_out = {}

    es = ExitStack()
    with es:
        P = Prog(nc, es)

        def sb(name, shape, dt=F32):
            return es.enter_context(nc.sbuf_tensor(name, shape, dt))

        X = sb("X", [128, NT, 1024])
        HT = sb("HT", [128, 8, TB], BF16)
        ST = sb("ST", [128, 16])
        WS = sb("WS", [128, NSLOT, 4096], BF16)
        IDB = sb("IDB", [128, 128], BF16)
        IDF = sb("IDF", [128, 128])
        ZS = sb("ZS", [128, 8, 240], BF16)
        SGN = sb("SGN", [128, 1])
        GF = sb("GF", [128, 1024])
        SPm = sb("SPm", [128, 2, SPC])
        C8 = sb("C8", [128, 2, 20])
        S5C = sb("S5C", [128, 2, 3, 64])
        LRUH = sb("LRUH", [128, 2, 10])
        XAH = sb("XAH", [128, 2, 10, 3])
        S5CAR = sb("S5CAR", [128, 2, 2, 64])
        FT = sb("FT", [128, 2, 48, 2])
        T1 = sb("T1", [128, 2, 64])
        T2 = sb("T2", [128, 2, 64])
        ARENA = 134776
        AR = sb("AR", [128, ARENA // 4])

        PM = es.enter_context(nc.psum_tensor("PM", [128, 4, 512], F32))
        PX = es.enter_context(nc.psum_tensor("PX", [128, 2, 512], F32))
        PTR = es.enter_context(nc.psum_tensor("PTR", [128, 2, 1024], BF16))

        arena_ranges = {}

        def av(name, off, nbytes, dt=F32, **re):
            assert off % 4 == 0 and nbytes % 4 == 0 and off + nbytes <= ARENA, (name, off, nbytes)
            arena_ranges[name] = (off, off + nbytes)
            v = AR[:, off // 4:(off + nbytes) // 4]
            if dt is not F32:
                v = v.bitcast(dt)
            if re:
                pat = re.pop("pat")
                v = v.rearrange(pat, **re)
            return v

        OA = av("OA", 0, 10240, BF16, pat="p (k n) -> p k n", n=TB)
        XA = av("XA", 10240, 20600, F32, pat="p (k n) -> p k n", n=515)
        GBf = av("GB", 30840, 10240, BF16, pat="p (k n) -> p k n", n=TB)
        LT = av("LT", 41080, 22528)
        YSB = av("YSB", 10240, 8192, BF16, pat="p (g c) -> p g c", c=64)
        YS = av("YS", 18432, 8192, BF16, pat="p (k n) -> p k n", n=TB)
        UB = av("UB", 63608, 8192, BF16, pat="p (k n) -> p k n", n=TB)
        XB = av("XB", 63608, 8192, BF16, pat="p (g c) -> p g c", c=64)
        U = av("U", 71800, 8192, BF16, pat="p (g c) -> p g c", c=64)
        S2 = av("S2", 79992, 33280, F32, pat="p (h g c) -> p h g c", h=2, g=64)
        SMA = av("SMA", 113272, 8192, BF16, pat="p (k n) -> p k n", n=TB)
        SMB = av("SMB", 121464, 8192, BF16, pat="p (k n) -> p k n", n=TB)
        TMP = av("TMP", 41080, 4096, F32, pat="p (k n) -> p k n", n=TB)
        HS = av("HS", 45176, 4096, BF16, pat="p (k n) -> p k n", n=1024)
        SQ = av("SQ", 49272, 2048, BF16)
        WG = av("WG", 129656, 5120, BF16, pat="p (w k n) -> p w k n", w=2, k=10)
        YB = av("YB", 79992, 16384, F32, pat="p (k n) -> p k n", n=TB)
        MGB = av("MGB", 96376, 8192, BF16, pat="p (k n) -> p k n", n=TB)
        PTOK = av("PTOK", 104568, 4096, F32, pat="p (t n) -> p t n", n=256)
        PB = av("PB", 108664, 2048, BF16, pat="p (t n) -> p t n", n=256)
        PTT = av("PTT", 110712, 2048, BF16, pat="p (k n) -> p k n", n=TB)
        ACTB = av("ACTB", 0, 24576, BF16, pat="p (k n) -> p k n", n=TB)
        UPF = av("UPF", 24576, 8224, F32, pat="p (k n) -> p k n", n=514)
        CV = av("CV", 32800, 8192, F32, pat="p (k n) -> p k n", n=TB)
        PTMP = av("PTMP", 40992, 4096, F32, pat="p (k n) -> p k n", n=TB)
        EX = av("EX", 0, 15360, F32, pat="p (g n) -> p g n", n=240)
        F1 = av("F1", 15360, 1024, F32, pat="p (g n) -> p g n", n=16)
        F2 = av("F2", 16384, 1024, F32, pat="p (g n) -> p g n", n=16)
        G1 = av("G1", 17408, 1024, F32, pat="p (g n) -> p g n", n=16)
        G2 = av("G2", 18432, 1024, F32, pat="p (g n) -> p g n", n=16)
        BBv = av("BB", 19456, 1024, F32, pat="p (g n) -> p g n", n=16)
        BBS = av("BBS", 20480, 1024, F32, pat="p (g n) -> p g n", n=16)
        Q1 = av("Q1", 21504, 1024, F32, pat="p (g n) -> p g n", n=16)
        Q2 = av("Q2", 22528, 1024, F32, pat="p (g n) -> p g n", n=16)
        CST = av("CST", 23552, 1024, F32, pat="p (g n) -> p g n", n=16)
        WAB = av("WAB", 24576, 4096, BF16, pat="p (g n) -> p g n", n=128)
        WASB = av("WASB", 28672, 4096, BF16, pat="p (g n) -> p g n", n=128)
        WCB = av("WCB", 32768, 4096, BF16, pat="p (g n) -> p g n", n=128)
        TBB = av("TBB", 36864, 4096, BF16, pat="p (g n) -> p g n", n=128)
        SC = av("SC", 40960, 64 * 4 * 40, F32, pat="p (k n) -> p k n", n=64)
        names = list(arena_ranges)
        for a in names:
            for b in names:
                if a != b:
                    (a0, a1), (b0, b1) = arena_ranges[a], arena_ranges[b]
                    if a0 < b1 and b0 < a1:
                        P.alias.setdefault(a, []).append(b)

        def _fs(ap):
            k = 1
            for d in ap.shape[1:]:
                k *= d
            return k

        def mm(out, lhsT, rhs, start, stop, r, w):
            P.op("pe", lambda e: e.matmul(out, lhsT=lhsT, rhs=rhs, start=start, stop=stop), r=r, w=w)

        def tr(out, in_, ident, r, w):
            P.op("pe", lambda e: e.transpose(out, in_, ident), r=r, w=w)

        def act(out, in_, func, r, w, bias=None, scale=None, accum=None):
            kw = {}
            if bias is not None:
                kw["bias"] = bias
            if scale is not None:
                kw["scale"] = scale
            if accum is not None:
                kw["accum_out"] = accum
            P.op("act", lambda e: e.activation(out=out, in_=in_, func=func, **kw), r=r, w=w,
                 n=0 if accum is not None else _fs(out))

        def ts(eng, out, in0, s1, s2, op0, op1, r, w):
            if s2 is None:
                P.op(eng, lambda e: e.tensor_scalar(out=out, in0=in0, scalar1=s1, scalar2=None, op0=op0), r=r, w=w, n=_fs(out))
            else:
                P.op(eng, lambda e: e.tensor_scalar(out=out, in0=in0, scalar1=s1, scalar2=s2, op0=op0, op1=op1), r=r, w=w, n=_fs(out))

        def tt(eng, out, in0, in1, op, r, w):
            P.op(eng, lambda e: e.tensor_tensor(out=out, in0=in0, in1=in1, op=op), r=r, w=w, n=_fs(out))

        def stt(eng, out, in0, scalar, in1, op0, op1, r, w):
            P.op(eng, lambda e: e.scalar_tensor_tensor(out=out, in0=in0, scalar=scalar, in1=in1, op0=op0, op1=op1), r=r, w=w, n=_fs(out))

        def cp(eng, out, in_, r, w):
            if eng == "act":
                P.op("act", lambda e: e.copy(out=out, in_=in_), r=r, w=w, n=_fs(out))
            else:
                P.op(eng, lambda e: e.tensor_copy(out=out, in_=in_), r=r, w=w, n=_fs(out))

        def mset(eng, out, val, w):
            P.op(eng, lambda e: e.memset(out, val), w=w)

        def dma(eng, out, in_, key, r, w):
            P.op(eng, lambda e: e.dma_start(out=out, in_=in_), r=r, w=w, dma=key)

        def tap(name, ap, shape, r):
            if name in dbg:
                d = nc.dram_tensor("dbg_" + name, shape, ap.dtype, kind="ExternalOutput").ap()
                dbg_out[name] = d
                dma("sp", d, ap, "dbg_" + name, r, ["dbgdram_" + name])

        wctr = [0]

        def wload(src, nk, ncols, cast=True, rk=()):
            s = wctr[0] % NSLOT
            wctr[0] += 1
            v = WS[:, s, 0:nk * ncols].rearrange("p (k n) -> p k n", n=ncols)
            dma("pool" if cast else "sp", v, src, "w%d" % s, list(rk), ["ws/%d" % s])
            return v, "ws/%d" % s

        pmc = [0]

        def pm():
            b = pmc[0] % 4
            pmc[0] += 1
            return PM[:, b, :], "pm/%d" % b

        pxc = [0]

        def px():
            b = pxc[0] % 2
            pxc[0] += 1
            return PX[:, b, :], "px/%d" % b

        ptc = [0]

        def ptr():
            b = ptc[0] % 2
            ptc[0] += 1
            return PTR[:, b, :], "ptr/%d" % b

        dma("sp", IDF[:], cst_d[:, 0:128], "c0", [], ["IDF"])
        dma("pool", IDB[:], cst_d[:, 0:128], "c1", [], ["IDB"])
        dma("pool", ZS[:].rearrange("p a b -> p (a b)"), cst_d[:, 128:128 + 1920], "c2", [], ["ZS"])
        dma("sp", SGN[:], cst_d[:, 2048:2049], "c3", [], ["SGN"])
        dma("sp", GF[:], cst_d[:, 2049:2049 + 1024], "c4", [], ["GF"])
        mset("dve", ST[:, 8:12], -0.5, ["ST/nh"])
        mset("dve", LRUH[:], 0.0, ["LRUH"])
        mset("dve", XAH[:], 0.0, ["XAH"])
        mset("dve", S5CAR[:], 0.0, ["S5CAR"])
        mset("dve", FT[:], 0.0, ["FT"])

        for i in range(NLAY):
            dma("sp", SPm[:, i, :], sp_d[i], "spm%d" % i, [], ["SPm/%d" % i])
            spk = "SPm/%d" % i
            act(C8[:, i, 0:10], SPm[:, i, 102:112], AF.Exp, [spk], ["C8/%d" % i], scale=-1.0)
            act(C8[:, i, 0:10], C8[:, i, 0:10], AF.Ln, ["C8/%d" % i], ["C8/%d" % i], bias=1.0)
            ts("dve", C8[:, i, 10:20], C8[:, i, 0:10], -16.0, None, ALU.mult, None, ["C8/%d" % i], ["C8/%d" % i])
            ts("dve", C8[:, i, 0:10], C8[:, i, 0:10], -8.0, None, ALU.mult, None, ["C8/%d" % i], ["C8/%d" % i])

            a_re = SPm[:, i, 320:384]
            a_im = SPm[:, i, 384:448]
            ldt = SPm[:, i, 448:512]
            DM = SPm[:, i, 512:576]
            k = "SC"
            DT, ADT, TH, MAG, QQ, RR, MSK, SN, CS = (SC[:, j, :] for j in range(9))
            LBR, LBI, NR, DEN, CR, CI, W2, TA, TBt = (SC[:, 9 + j, :] for j in range(9))
            PRn = SC[:, 18:27, :]
            PIn = SC[:, 27:36, :]
            QI = SC[:, 36, :].bitcast(I32)
            act(DT, ldt, AF.Exp, [spk], [k])
            tt("dve", ADT, a_re, DT, ALU.mult, [spk, k], [k])
            tt("dve", TH, a_im, DT, ALU.mult, [spk, k], [k])
            act(MAG, ADT, AF.Exp, [k], [k])

            def sincos(dst, shift):
                ts("dve", RR, TH, shift, None, ALU.add, None, [k], [k])
                ts("dve", QQ, RR, 1.0 / (2 * PI), None, ALU.mult, None, [k], [k])
                cp("dve", QI, QQ, [k], [k])
                cp("dve", QQ, QI, [k], [k])
                stt("dve", RR, QQ, -2 * PI, RR, ALU.mult, ALU.add, [k], [k])
                ts("dve", MSK, RR, PI, -2 * PI, ALU.is_gt, ALU.mult, [k], [k])
                tt("dve", RR, RR, MSK, ALU.add, [k], [k])
                ts("dve", MSK, RR, -PI, 2 * PI, ALU.is_lt, ALU.mult, [k], [k])
                tt("dve", RR, RR, MSK, ALU.add, [k], [k])
                ts("dve", RR, RR, PI, -PI, ALU.min, ALU.max, [k], [k])
                act(dst, RR, AF.Sin, [k], [k])

            sincos(SN, 0.0)
            sincos(CS, PI / 2)
            tt("dve", LBR, MAG, CS, ALU.mult, [k], [k])
            tt("dve", LBI, MAG, SN, ALU.mult, [k], [k])
            mset("dve", PRn[:, 0, :], 1.0, [k])
            mset("dve", PIn[:, 0, :], 0.0, [k])
            for q in range(8):
                tt("dve", TA, PRn[:, q, :], LBR, ALU.mult, [k], [k])
                tt("dve", TBt, PIn[:, q, :], LBI, ALU.mult, [k], [k])
                tt("dve", PRn[:, q + 1, :], TA, TBt, ALU.subtract, [k], [k])
                tt("dve", TA, PRn[:, q, :], LBI, ALU.mult, [k], [k])
                tt("dve", TBt, PIn[:, q, :], LBR, ALU.mult, [k], [k])
                tt("dve", PIn[:, q + 1, :], TA, TBt, ALU.add, [k], [k])
            ts("dve", NR, LBR, -1.0, None, ALU.add, None, [k], [k])
            tt("dve", TA, a_re, a_re, ALU.mult, [spk, k], [k])
            tt("dve", TBt, a_im, a_im, ALU.mult, [spk, k], [k])
            tt("dve", DEN, TA, TBt, ALU.add, [k], [k])
            P.op("dve", lambda e, DEN=DEN: e.reciprocal(out=DEN, in_=DEN), r=[k], w=[k])
            tt("dve", TA, NR, a_re, ALU.mult, [spk, k], [k])
            tt("dve", TBt, LBI, a_im, ALU.mult, [spk, k], [k])
            tt("dve", TA, TA, TBt, ALU.add, [k], [k])
            tt("dve", CR, TA, DEN, ALU.mult, [k], [k])
            tt("dve", TA, LBI, a_re, ALU.mult, [spk, k], [k])
            tt("dve", TBt, NR, a_im, ALU.mult, [spk, k], [k])
            tt("dve", TA, TA, TBt, ALU.subtract, [k], [k])
            tt("dve", CI, TA, DEN, ALU.mult, [k], [k])
            ts("dve", W2, CI, SGN[:, 0:1], None, ALU.mult, None, [k, "SGN"], [k])
            cp("dve", S5C[:, i, 0, :], PRn[:, 8, :], [k], ["S5C/%d" % i])
            ts("dve", S5C[:, i, 1, :], PIn[:, 8, :], SGN[:, 0:1], None, ALU.mult, None, [k, "SGN"], ["S5C/%d" % i])
            ts("dve", S5C[:, i, 2, :], PIn[:, 8, :], SGN[:, 0:1], -1.0, ALU.mult, ALU.mult, [k, "SGN"], ["S5C/%d" % i])

            def bc(v, g0):
                return v[:, g0:g0 + 16].unsqueeze(2).to_broadcast([128, 16, 16])

            for gb in range(4):
                g0 = gb * 16
                for q, (tile_, nm) in enumerate(((F1, "F1"), (F2, "F2"), (G1, "G1"), (G2, "G2"))):
                    dma("sp", tile_, s5f_d[i, q, :, g0 * 16:(g0 + 16) * 16].rearrange("p (g n) -> p g n", n=16),
                        "s5f%d" % q, [], [nm])
                tt("dve", Q1, F1, bc(CR, g0), ALU.mult, ["F1", k], ["Q1"])
                tt("dve", Q2, F2, bc(W2, g0), ALU.mult, ["F2", k], ["Q2"])
                tt("dve", BBv, Q1, Q2, ALU.add, ["Q1", "Q2"], ["BB"])
                tt("dve", Q1, F2, bc(CR, g0), ALU.mult, ["F2", k], ["Q1"])
                tt("dve", Q2, F1, bc(W2, g0), ALU.mult, ["F1", k], ["Q2"])
                tt("dve", BBS, Q1, Q2, ALU.subtract, ["Q1", "Q2"], ["BBS"])
                mset("dve", EX[:, :, 128:240], 0.0, ["EX"])
                for j in range(8):
                    kk = 7 - j
                    tt("dve", Q1, BBv, bc(PRn[:, kk, :], g0), ALU.mult, ["BB", k], ["Q1"])
                    tt("dve", Q2, BBS, bc(PIn[:, kk, :], g0), ALU.mult, ["BBS", k], ["Q2"])
                    stt("dve", EX[:, :, j * 16:(j + 1) * 16], Q2, SGN[:, 0:1], Q1, ALU.mult, ALU.add,
                        ["Q1", "Q2", "SGN"], ["EX"])
                ts("dve", CST, G1, SGN[:, 0:1], -1.0, ALU.mult, ALU.mult, ["G1", "SGN"], ["CST"])
                for gq in range(4):
                    bank, bk = pm()
                    for gi in range(4):
                        gl = gq * 4 + gi
                        tr(bank[:, gi * 128:(gi + 1) * 128], EX[:, gl, 0:128], IDF[:], ["EX", "IDF"], [bk])
                    bv = bank.rearrange("p (g n) -> p g n", n=128)
                    cp("act", WAB[:, gq * 4:(gq + 1) * 4, :], bv, [bk], ["WAB"])
                    cp("dve", WASB[:, gq * 4:(gq + 1) * 4, 0:64], bv[:, :, 64:128], [bk], ["WASB"])
                    cp("dve", WASB[:, gq * 4:(gq + 1) * 4, 64:128], bv[:, :, 0:64], [bk], ["WASB"])
                    bank, bk = pm()
                    for gi in range(4):
                        gl = gq * 4 + gi
                        for l in range(8):
                            mm(bank[:, gi * 128 + l * 16: gi * 128 + (l + 1) * 16],
                               EX[:, gl, (7 - l) * 16:(7 - l) * 16 + 128], CST[:, gl, :], True, True,
                               ["EX", "CST"], [bk])
                        stt("dve", TBB[:, gl, :], IDF[:], DM[:, g0 + gl:g0 + gl + 1], bank[:, gi * 128:(gi + 1) * 128],
                            ALU.mult, ALU.add, [bk, "IDF", spk], ["TBB"])
                for l in range(8):
                    tt("dve", Q1, G1, bc(PRn[:, l + 1, :], g0), ALU.mult, ["G1", k], ["Q1"])
                    tt("dve", Q2, G2, bc(PIn[:, l + 1, :], g0), ALU.mult, ["G2", k], ["Q2"])
                    ts("dve", Q1, Q1, SGN[:, 0:1], -1.0, ALU.mult, ALU.mult, ["Q1", "SGN"], ["Q1"])
                    tt("dve", WCB[:, :, l * 16:(l + 1) * 16], Q1, Q2, ALU.subtract, ["Q1", "Q2"], ["WCB"])
                for q, (tile_, nm) in enumerate(((WAB, "WAB"), (WASB, "WASB"), (WCB, "WCB"), (TBB, "TBB"))):
                    dma("sp", s5m_d[i, q, :, g0 * 128:(g0 + 16) * 128].rearrange("p (g n) -> p g n", n=128),
                        tile_, "s5mo%d" % q, [nm], ["s5m%d_%d" % (i, q)])

        def kview(ap):
            return ap.rearrange("(k p) n -> p k n", p=128)

        w_in_v = [kview(w_in_d[i]) for i in range(2)]
        w_ao_v = [kview(w_ao_d[i]) for i in range(2)]
        w_glu_v = [kview(w_glu_d[i]) for i in range(2)]
        w_o_v = [kview(w_o_d[i]) for i in range(2)]
        w_up_v = [kview(w_up_d[i]) for i in range(2)]
        w_dn_v = [kview(w_dn_d[i]) for i in range(2)]
        w_pg_v = [kview(w_pg_d[i]) for i in range(2)]
        w_pp_v = [kview(w_pp_d[i]) for i in range(2)]
        gxw_v = [gxw_d[i].rearrange("h i j -> i h j") for i in range(2)]
        gaw_v = [gaw_d[i].rearrange("h i j -> i h j") for i in range(2)]
        x_v = x_d.rearrange("(b t p) d -> b p t d", t=NT, p=128)
        y_v = y_d.rearrange("(b t p) d -> b p t d", t=NT, p=128)
        p_v = [p_d[i].rearrange("(b t p) d -> b p t d", t=NT, p=128) for i in range(2)]

        def norm(gcols, gkey):
            for t in range(NT):
                act(SQ[:], X[:, t, :], AF.Square, ["X/%d" % t], ["SQ", "ST/ss"], accum=ST[:, t:t + 1])
            ts("dve", ST[:, 4:8], ST[:, 0:4], 1.0 / 1024, EPS, ALU.mult, ALU.add, ["ST/ss"], ["ST/rs"])
            tt("pool", ST[:, 4:8], ST[:, 4:8], ST[:, 8:12], ALU.pow, ["ST/rs", "ST/nh"], ["ST/rs"])
            for t in range(NT):
                s = t % 2
                act(HS[:, s, :], X[:, t, :], AF.Copy, ["X/%d" % t, "ST/rs"], ["HS/%d" % s], scale=ST[:, 4 + t:5 + t])
                bank, bk = ptr()
                for kt in range(8):
                    tr(bank[:, kt * 128:(kt + 1) * 128], HS[:, s, kt * 128:(kt + 1) * 128], IDB[:],
                       ["HS/%d" % s, "IDB"], [bk])
                tt("dve", HT[:, :, t * 128:(t + 1) * 128], bank.rearrange("p (k n) -> p k n", n=128),
                   gcols.unsqueeze(2).to_broadcast([128, 8, 128]), ALU.mult, [bk, gkey], ["HT/%d" % t])

        HTall = ["HT/%d" % t for t in range(NT)]

        def proj_fm(ws, wk, j, nk, rhs_of, rkeys):
            bank, bk = pm()
            for kt in range(nk):
                mm(bank, ws[:, kt, j * 128:(j + 1) * 128], rhs_of(kt), kt == 0, kt == nk - 1, [wk] + rkeys, [bk])
            return bank, bk

        def layer(i, blk):
            spk = "SPm/%d" % i
            norm(SPm[:, i, 0:8], spk)
            tap("ht1_%d_%d" % (i, blk), HT[:], [128, 8, TB], HTall)

            def win_chunk(c):
                ws, wk = wload(w_in_v[i][:, :, c * 512:(c + 1) * 512], 8, 512)
                for j in range(4):
                    t = 4 * c + j
                    bank, bk = proj_fm(ws, wk, j, 8, lambda kt: HT[:, kt, :], HTall)
                    if t < 10:
                        cp("act", XA[:, t, 3:515], bank, [bk], ["XA/%d" % t])
                    elif t < 20:
                        act(GBf[:, t - 10, :], bank, AF.Gelu_apprx_tanh, [bk], ["GB/%d" % (t - 10)])
                    elif t < 28:
                        cp("act", UB[:, t - 20, :], bank, [bk], ["UB/%d" % (t - 20)])
                    elif t < 36:
                        act(SMA[:, t - 28, :], bank, AF.Sigmoid, [bk], ["SMA/%d" % (t - 28)])
                    else:
                        act(SMB[:, t - 36, :], bank, AF.Sigmoid, [bk], ["SMB/%d" % (t - 36)])

            for c in range(5):
                win_chunk(c)

            def lru_tile(t, wgx, wgxk, wga, wgak):
                q = t % 2
                base = q * 2816
                xc = LT[:, base:base + 512]
                gx = LT[:, base + 512:base + 1024]
                aa = LT[:, base + 1024:base + 1536]
                m2 = LT[:, base + 1536:base + 2048]
                hh = LT[:, base + 2048:base + 2560]
                xcb = LT[:, base + 2560:base + 2816].bitcast(BF16)
                lk = "LT/%d" % q
                xk = "XA/%d" % t
                cp("dve", XA[:, t, 0:3], XAH[:, i, t, :], ["XAH/%d_%d" % (i, t)], [xk])
                cw = lambda j: SPm[:, i, 32 + t * 4 + j:33 + t * 4 + j]
                act(xc, XA[:, t, 0:512], AF.Identity, [xk, spk], [lk], bias=SPm[:, i, 72 + t:73 + t], scale=cw(0))
                for j in range(1, 4):
                    stt("dve", xc, XA[:, t, j:j + 512], cw(j), xc, ALU.mult, ALU.add, [xk, spk, lk], [lk])
                cp("dve", XAH[:, i, t, :], XA[:, t, 512:515], [xk], ["XAH/%d_%d" % (i, t)])
                cp("act", xcb, xc, [lk], [lk])
                bx, bxk = px()
                mm(bx, wgx[:, t, :], xcb, True, True, [wgxk, lk], [bxk])
                ba, bak = px()
                mm(ba, wga[:, t, :], xcb, True, True, [wgak, lk], [bak])
                act(gx, bx, AF.Sigmoid, [bxk, spk], [lk], bias=SPm[:, i, 82 + t:83 + t])
                act(hh, ba, AF.Sigmoid, [bak, spk], [lk], bias=SPm[:, i, 92 + t:93 + t])
                act(aa, hh, AF.Exp, [lk, "C8/%d" % i], [lk], scale=C8[:, i, t:t + 1])
                act(m2, hh, AF.Exp, [lk, "C8/%d" % i], [lk], scale=C8[:, i, 10 + t:11 + t])
                act(m2, m2, AF.Sqrt, [lk], [lk], bias=1.0, scale=-1.0)
                tt("dve", gx, gx, xc, ALU.mult, [lk], [lk])
                stt("dve", m2, m2, 1e-6, gx, ALU.max, ALU.mult, [lk], [lk])
                hk = "LRUH/%d_%d" % (i, t)
                P.op("dve", lambda e, hh=hh, aa=aa, m2=m2, t=t: e.tensor_tensor_scan(
                    out=hh, data0=aa, data1=m2, initial=LRUH[:, i, t:t + 1], op0=ALU.mult, op1=ALU.add),
                    r=[lk, hk], w=[lk], n=512)
                P.op("dve", lambda e, hh=hh, t=t: e.tensor_copy(out=LRUH[:, i, t:t + 1], in_=hh[:, 511:512]),
                     r=[lk], w=[hk], n=1, fs=True)
                tt("dve", OA[:, t, :], hh, GBf[:, t, :], ALU.mult, [lk, "GB/%d" % t], ["OA/%d" % t])

            L_lru = P.defer()
            dma("pool", WG[:, 0], gxw_v[i], "wg0", [], ["WG/0"])
            dma("pool", WG[:, 1], gaw_v[i], "wg1", [], ["WG/1"])
            wgx, wgxk, wga, wgak = WG[:, 0], "WG/0", WG[:, 1], "WG/1"
            for t in range(10):
                lru_tile(t, wgx, wgxk, wga, wgak)
            P.end_defer()

            L_b = P.defer()
            for c in range(5, 7):
                win_chunk(c)
            for j in range(8):
                bank, bk = pm()
                bv = bank.rearrange("p (g c) -> p g c", c=64)
                for gl in range(8):
                    for l in range(8):
                        mm(bv[:, gl, :], ZS[:, gl, 112 - 16 * l:240 - 16 * l], UB[:, j, l::8], l == 0, l == 7,
                           ["ZS", "UB/%d" % j], [bk])
                cp("act", U[:, j * 8:(j + 1) * 8, :], bv, [bk], ["U/%d" % j])
            for h in range(2):
                for half in range(2):
                    ws, wk = wload(s5m_d[i, h, :, half * 4096:(half + 1) * 4096].rearrange("p (g n) -> p g n", n=128),
                                   32, 128, cast=False, rk=["s5m%d_%d" % (i, h)])
                    for j4 in range(4):
                        j = half * 4 + j4
                        bank, bk = pm()
                        bv = bank.rearrange("p (g c) -> p g c", c=64)
                        for gl in range(8):
                            mm(bv[:, gl, :], ws[:, j4 * 8 + gl, :], U[:, j * 8 + gl, :], True, True,
                               [wk, "U/%d" % j], [bk])
                        cp("act", S2[:, h, j * 8:(j + 1) * 8, 1:65], bv, [bk], ["S2/%d" % h])
            cp("dve", S2[:, :, :, 0], S5CAR[:, i, :, :], ["S5CAR/%d" % i], ["S2"])
            ArB = S5C[:, i, 0, :].unsqueeze(1).to_broadcast([128, 2, 64])
            sck = "S5C/%d" % i
            for c in range(64):
                if c % 16 == 0:
                    win_chunk(7 + c // 16)
                tt("dve", T1[:], S2[:, :, :, c], ArB, ALU.mult, ["S2", sck], ["T1"])
                tt("dve", T2[:, 0, :], S2[:, 1, :, c], S5C[:, i, 1, :], ALU.mult, ["S2", sck], ["T2"])
                tt("dve", T2[:, 1, :], S2[:, 0, :, c], S5C[:, i, 2, :], ALU.mult, ["S2", sck], ["T2"])
                tt("dve", T1[:], T1[:], T2[:], ALU.add, ["T1", "T2"], ["T1"])
                tt("dve", S2[:, :, :, c + 1], S2[:, :, :, c + 1], T1[:], ALU.add, ["S2", "T1"], ["S2"])
            cp("dve", S5CAR[:, i, :, :], S2[:, :, :, 64], ["S2"], ["S5CAR/%d" % i])
            cp("act", XB[:], S2[:, 0, :, 0:64], ["S2"], ["XB"])
            P.end_defer()
            P.interleave([L_lru, L_b])
            tap("oa_%d_%d" % (i, blk), OA[:], [128, 10, TB], ["OA"])
            tap("xb_%d_%d" % (i, blk), XB[:], [128, 64, 64], ["XB"])
            dma("sp", PTOK[:], p_v[i][blk], "ptok", [], ["PTOK"])
            for half in range(2):
                wt_, wtk = wload(s5m_d[i, 3, :, half * 4096:(half + 1) * 4096].rearrange("p (g n) -> p g n", n=128),
                                 32, 128, cast=False, rk=["s5m%d_3" % i])
                wc_, wck = wload(s5m_d[i, 2, :, half * 4096:(half + 1) * 4096].rearrange("p (g n) -> p g n", n=128),
                                 32, 128, cast=False, rk=["s5m%d_2" % i])
                for j4 in range(4):
                    j = half * 4 + j4
                    bank, bk = pm()
                    bv = bank.rearrange("p (g c) -> p g c", c=64)
                    for gl in range(8):
                        g = j * 8 + gl
                        mm(bv[:, gl, :], wt_[:, j4 * 8 + gl, :], U[:, g, :], True, False, [wtk, "U/%d" % j], [bk])
                        mm(bv[:, gl, :], wc_[:, j4 * 8 + gl, :], XB[:, g, :], False, True, [wck, "XB"], [bk])
                    cp("act", YSB[:, j * 8:(j + 1) * 8, :], bv, [bk], ["YSB/%d" % j])
            for j in range(8):
                bank, bk = pm()
                for l in range(8):
                    for gl in range(8):
                        mm(bank[:, l::8], ZS[:, l, 112 - 16 * gl:240 - 16 * gl], YSB[:, j * 8 + gl, :],
                           gl == 0, gl == 7, ["ZS", "YSB/%d" % j], [bk])
                act(YS[:, j, :], bank, AF.Gelu_apprx_tanh, [bk], ["YS/%d" % j])
            tap("ys_%d_%d" % (i, blk), YS[:], [128, 8, TB], ["YS"])
            YSall = ["YS/%d" % j for j in range(8)]
            for c in (0, 2, 1, 3):
                ws, wk = wload(w_glu_v[i][:, :, c * 512:(c + 1) * 512], 8, 512)
                for j in range(4):
                    t = 4 * c + j
                    bank, bk = proj_fm(ws, wk, j, 8, lambda kt: YS[:, kt, :], YSall)
                    bcol = SPm[:, i, 112 + t:113 + t]
                    if t < 8:
                        act(YB[:, t, :], bank, AF.Identity, [bk, spk], ["YB/%d" % t], bias=bcol)
                    else:
                        s = t % 2
                        n = t - 8
                        act(TMP[:, s, :], bank, AF.Sigmoid, [bk, spk], ["TMP/%d" % s], bias=bcol)
                        tt("dve", TMP[:, s, :], TMP[:, s, :], SMB[:, n, :], ALU.mult,
                           ["TMP/%d" % s, "SMB/%d" % n], ["TMP/%d" % s])
                        tt("dve", YB[:, n, :], YB[:, n, :], TMP[:, s, :], ALU.mult,
                           ["YB/%d" % n, "TMP/%d" % s], ["YB/%d" % n])
            OAall = ["OA/%d" % t for t in range(10)]
            for c in range(4):
                ws, wk = wload(w_ao_v[i][:, :, c * 256:(c + 1) * 256], 10, 256)
                for j in range(2):
                    n = 2 * c + j
                    bank, bk = proj_fm(ws, wk, j, 10, lambda kt: OA[:, kt, :], OAall)
                    s = n % 2
                    tt("dve", TMP[:, s, :], bank, SMA[:, n, :], ALU.mult, [bk, "SMA/%d" % n], ["TMP/%d" % s])
                    tt("dve", MGB[:, n, :], TMP[:, s, :], YB[:, n, :], ALU.add, ["TMP/%d" % s, "YB/%d" % n],
                       ["MGB/%d" % n])
            tap("mg_%d_%d" % (i, blk), MGB[:], [128, 8, TB], ["MGB"])
            MGall = ["MGB/%d" % n for n in range(8)]
            for h in range(2):
                ws, wk = wload(w_o_v[i][:, :, h * 512:(h + 1) * 512], 8, 512)
                for t in range(NT):
                    bank, bk = pm()
                    for kt in range(8):
                        mm(bank, MGB[:, kt, t * 128:(t + 1) * 128], ws[:, kt, :], kt == 0, kt == 7, [wk] + MGall, [bk])
                    tt("dve", X[:, t, h * 512:(h + 1) * 512], bank, X[:, t, h * 512:(h + 1) * 512], ALU.add,
                       [bk, "X/%d" % t], ["X/%d" % t])
            tap("x1_%d_%d" % (i, blk), X[:], [128, NT, 1024], ["X"])

            norm(SPm[:, i, 8:16], spk)

            def ffn_tile(ws, wk, c, j, half):
                t = 4 * c + j
                o = t % 24
                q = (2 * half + (j % 2))
                bank, bk = proj_fm(ws, wk, j, 8, lambda kt: HT[:, kt, :], HTall)
                ck = "CV/%d" % q
                ftk = "FT/%d_%d" % (i, t)
                fw_ = lambda jj: SPm[:, i, 128 + t * 3 + jj:129 + t * 3 + jj]
                bcol = SPm[:, i, 272 + t:273 + t]
                act(CV[:, q, 2:512], bank[:, 0:510], AF.Identity, [bk, spk], [ck], bias=bcol, scale=fw_(0))
                ts("dve", CV[:, q, 0:2], FT[:, i, t, :], fw_(0), bcol, ALU.mult, ALU.add, [ftk, spk], [ck])
                stt("dve", CV[:, q, 1:512], bank[:, 0:511], fw_(1), CV[:, q, 1:512], ALU.mult, ALU.add,
                    [bk, spk, ck], [ck])
                stt("dve", CV[:, q, 0:1], FT[:, i, t, 1:2], fw_(1), CV[:, q, 0:1], ALU.mult, ALU.add,
                    [ftk, spk, ck], [ck])
                stt("dve", CV[:, q, :], bank, fw_(2), CV[:, q, :], ALU.mult, ALU.add, [bk, spk, ck], [ck])
                cp("dve", FT[:, i, t, :], bank[:, 510:512], [bk], [ftk])
                if half == 0:
                    act(ACTB[:, o, :], CV[:, q, :], AF.Gelu_apprx_tanh, [ck], ["ACTB/%d" % o])
                else:
                    tt("dve", ACTB[:, o, :], ACTB[:, o, :], CV[:, q, :], ALU.mult, ["ACTB/%d" % o, ck],
                       ["ACTB/%d" % o])

            for c6 in range(6):
                for half in range(2):
                    c = c6 + 6 * half
                    ws, wk = wload(w_up_v[i][:, :, c * 512:(c + 1) * 512], 8, 512)
                    for jp in range(2):
                        lists = []
                        for j in (2 * jp, 2 * jp + 1):
                            lists.append(P.defer())
                            ffn_tile(ws, wk, c, j, half)
                            P.end_defer()
                        P.interleave(lists)
            tap("actb_%d_%d" % (i, blk), ACTB[:], [128, 24, TB], ["ACTB"])
            ACall = ["ACTB/%d" % o for o in range(24)]
            for h in range(2):
                banks = [pm() for _ in range(NT)]
                for kc in range(3):
                    ws, wk = wload(w_dn_v[i][:, kc * 8:(kc + 1) * 8, h * 512:(h + 1) * 512], 8, 512)
                    for t in range(NT):
                        bank, bk = banks[t]
                        for k8 in range(8):
                            mm(bank, ACTB[:, kc * 8 + k8, t * 128:(t + 1) * 128], ws[:, k8, :],
                               kc == 0 and k8 == 0, kc == 2 and k8 == 7, [wk] + ACall, [bk])
                for t in range(NT):
                    bank, bk = banks[t]
                    tt("dve", X[:, t, h * 512:(h + 1) * 512], bank, X[:, t, h * 512:(h + 1) * 512], ALU.add,
                       [bk, "X/%d" % t], ["X/%d" % t])
            tap("x2_%d_%d" % (i, blk), X[:], [128, NT, 1024], ["X"])

            norm(SPm[:, i, 16:24], spk)
            cp("dve", PB[:], PTOK[:], ["PTOK"], ["PB"])
            bank, bk = ptr()
            bv = bank.rearrange("p (k n) -> p k n", n=TB)
            for t in range(NT):
                for kk in range(2):
                    tr(bv[:, kk, t * 128:(t + 1) * 128], PB[:, t, kk * 128:(kk + 1) * 128], IDB[:], ["PB", "IDB"], [bk])
            cp("act", PTT[:], bv, [bk], ["PTT"])
            for h in range(2):
                wg, wgk = wload(w_pg_v[i][:, :, h * 512:(h + 1) * 512], 8, 512)
                wp, wpk = wload(w_pp_v[i][:, :, h * 512:(h + 1) * 512], 2, 512)
                for t in range(NT):
                    bg, bgk = pm()
                    for kt in range(8):
                        mm(bg, HT[:, kt, t * 128:(t + 1) * 128], wg[:, kt, :], kt == 0, kt == 7, [wgk, "HT/%d" % t], [bgk])
                    bp, bpk = px()
                    for kk in range(2):
                        mm(bp, PTT[:, kk, t * 128:(t + 1) * 128], wp[:, kk, :], kk == 0, kk == 1, [wpk, "PTT"], [bpk])
                    s = t % 2
                    act(TMP[:, s, :], bg, AF.Sigmoid, [bgk], ["TMP/%d" % s])
                    tt("dve", TMP[:, s, :], TMP[:, s, :], bp, ALU.mult, ["TMP/%d" % s, bpk], ["TMP/%d" % s])
                    tt("dve", X[:, t, h * 512:(h + 1) * 512], TMP[:, s, :], X[:, t, h * 512:(h + 1) * 512], ALU.add,
                       ["TMP/%d" % s, "X/%d" % t], ["X/%d" % t])
            tap("x3_%d_%d" % (i, blk), X[:], [128, NT, 1024], ["X"])

        for blk in range(NBLK):
            dma("sp", X[:], x_v[blk], "xin", [], ["X"])
            for i in range(NLAY):
                layer(i, blk)
            if final:
                for t in range(NT):
                    act(SQ[:], X[:, t, :], AF.Square, ["X/%d" % t], ["SQ", "ST/ss"], accum=ST[:, t:t + 1])
                ts("dve", ST[:, 4:8], ST[:, 0:4], 1.0 / 1024, EPS, ALU.mult, ALU.add, ["ST/ss"], ["ST/rs"])
                tt("pool", ST[:, 4:8], ST[:, 4:8], ST[:, 8:12], ALU.pow, ["ST/rs", "ST/nh"], ["ST/rs"])
                for t in range(NT):
                    stt("dve", X[:, t, :], X[:, t, :], ST[:, 4 + t:5 + t], GF[:], ALU.mult, ALU.mult,
                        ["X/%d" % t, "ST/rs", "GF"], ["X/%d" % t])
            dma("sp", y_v[blk], X[:], "yout", ["X"], ["ydram/%d" % blk])

        with nc.allow_non_contiguous_dma(reason="small strided parameter loads"):
            info = P.emit()
    return nc, dbg_out, info


def _cols(v):
    return np.ascontiguousarray(v.reshape(-1, 128).T)


def host_layout(inp):
    f = np.float32
    spk = np.zeros((2, 128, SPC), f)
    s5f = np.zeros((2, 4, 128, 1024), f)
    for i in range(2):
        s = spk[i]
        s[:, 0:8] = _cols(inp["g_mix"][i])
        s[:, 8:16] = _cols(inp["g_ffn"][i])
        s[:, 16:24] = _cols(inp["g_ple"][i])
        s[:, 24:32] = _cols(inp["g_final"])
        caw = inp["conv_a_w"][i]
        for j in range(4):
            s[:, 32 + j:72:4] = _cols(caw[j])
        s[:, 72:82] = _cols(inp["conv_a_b"][i])
        s[:, 82:92] = _cols(inp["gate_x_b"][i])
        s[:, 92:102] = _cols(inp["gate_a_b"][i])
        s[:, 102:112] = _cols(inp["lru_lambda"][i])
        s[:, 112:128] = _cols(inp["b_glu"][i])
        cfw = inp["conv_f_w"][i]
        for j in range(3):
            s[:, 128 + j:272:3] = _cols(cfw[j])
        s[:, 272:320] = _cols(inp["conv_f_b"][i])
        are_t = inp["s5_a_re"][i].T
        aim_t = inp["s5_a_im"][i].T
        s[:, 320:384] = np.concatenate([are_t, are_t], 0)
        s[:, 384:448] = np.concatenate([aim_t, aim_t], 0)
        s[:, 448:512] = np.broadcast_to(inp["s5_log_dt"][i][None, :], (128, 64))
        dm = inp["s5_d"][i].reshape(64, 16).T
        s[:, 512:576] = np.tile(dm, (8, 1))
        bre = inp["s5_b_re"][i].transpose(1, 0, 2).reshape(64, 1024)
        bim = inp["s5_b_im"][i].transpose(1, 0, 2).reshape(64, 1024)
        cre = inp["s5_c_re"][i].transpose(2, 0, 1).reshape(64, 1024)
        cim = inp["s5_c_im"][i].transpose(2, 0, 1).reshape(64, 1024)
        s5f[i, 0] = np.concatenate([bre, bim], 0)
        s5f[i, 1] = np.concatenate([bim, bre], 0)
        s5f[i, 2] = np.concatenate([cre, cim], 0)
        s5f[i, 3] = np.concatenate([cim, cre], 0)
    cst = np.zeros((128, 128 + 1920 + 1 + 1024), f)
    cst[:, 0:128] = np.eye(128, dtype=f)
    z = np.zeros((128, 8, 240), f)
    for gl in range(8):
        for ch in range(16):
            z[gl * 16 + ch, gl, 112 + ch] = 1.0
    cst[:, 128:2048] = z.reshape(128, 1920)
    cst[0:64, 2048] = -1.0
    cst[64:128, 2048] = 1.0
    cst[:, 2049:] = np.broadcast_to(inp["g_final"][None, :], (128, 1024))
    shared = {"spk": spk, "s5f": s5f, "cst": cst}
    for k in ("w_in", "gate_x_w", "gate_a_w", "w_a_out", "w_glu", "w_o", "w_up", "w_down",
              "w_ple_gate", "w_ple_proj"):
        shared[k] = np.ascontiguousarray(inp[k], dtype=f)
    return shared


_CACHE = {}


def kernel(**inputs):
    inp = {k: np.asarray(v) for k, v in inputs.items()}
    shared = host_layout(inp)
    if "nc" not in _CACHE:
        _CACHE["nc"] = build()[0]
    nc = _CACHE["nc"]
    in_maps = []
    for b in range(8):
        m = dict(shared)
        m["x"] = np.ascontiguousarray(inp["x"][b], dtype=np.float32)
        m["p"] = np.ascontiguousarray(inp["p"][:, b], dtype=np.float32)
        in_maps.append(m)
    res = run_bass_kernel_spmd(nc, in_maps, core_ids=list(range(8)))
    return np.stack([np.asarray(r["y"], dtype=np.float32) for r in res.results], 0)
```

```python
import math
from contextlib import ExitStack

import numpy as np
import concourse.bass as bass
import concourse.mybir as mybir
from concourse.bass_utils import run_bass_kernel_spmd

F32 = mybir.dt.float32
BF16 = mybir.dt.bfloat16
I32 = mybir.dt.int32
AF = mybir.ActivationFunctionType
ALU = mybir.AluOpType

SAME_ENGINE_SYNC = ("act", "pool", "dve", "sp")
NSLOT = 4
BIG_N = 128
TB = 512
NT = 4
SEQ = 4096
EPS = 1e-6
PI = math.pi


class Prog:
    def __init__(self, nc, es):
        self.nc = nc
        self.es = es
        self.eng = {"pe": nc.tensor, "act": nc.scalar, "dve": nc.vector,
                    "pool": nc.gpsimd, "sp": nc.sync}
        self.ops = []
        self.state = {}
        self.alias = {}
        self._defer = None

    @staticmethod
    def _split(key):
        if "/" in key:
            r, s = key.split("/", 1)
            return r, s
        return key, None

    def _st(self, root):
        return self.state.setdefault(root, {"w": None, "r": [], "subs": {}})

    def _deps_read(self, key, me):
        root, sub = self._split(key)
        st = self._st(root)
        deps = []
        if st["w"] is not None:
            deps.append(st["w"])
        if sub is None:
            for s in st["subs"].values():
                if s["w"] is not None:
                    deps.append(s["w"])
            st["r"].append(me)
        else:
            s = st["subs"].setdefault(sub, {"w": None, "r": []})
            if s["w"] is not None:
                deps.append(s["w"])
            s["r"].append(me)
        return deps

    def _deps_write_whole(self, root, me):
        st = self._st(root)
        deps = []
        if st["w"] is not None:
            deps.append(st["w"])
        deps.extend(st["r"])
        for s in st["subs"].values():
            if s["w"] is not None:
                deps.append(s["w"])
            deps.extend(s["r"])
        st["subs"] = {}
        st["r"] = []
        st["w"] = me
        return deps

    def _deps_write(self, key, me):
        root, sub = self._split(key)
        deps = []
        for other in self.alias.get(root, ()):
            deps.extend(self._deps_write_whole(other, me))
        if sub is None:
            deps.extend(self._deps_write_whole(root, me))
            return deps
        st = self._st(root)
        if st["w"] is not None:
            deps.append(st["w"])
        deps.extend(st["r"])
        s = st["subs"].setdefault(sub, {"w": None, "r": []})
        if s["w"] is not None:
            deps.append(s["w"])
        deps.extend(s["r"])
        s["w"] = me
        s["r"] = []
        return deps

    def defer(self):
        self._defer = []
        return self._defer

    def end_defer(self):
        self._defer = None

    def interleave(self, lists):
        assert self._defer is None
        pos = [0] * len(lists)
        tot = sum(len(l) for l in lists)
        for _ in range(tot):
            best, bf = None, None
            for i, l in enumerate(lists):
                if pos[i] < len(l):
                    f = (pos[i] + 0.5) / len(l)
                    if bf is None or f < bf:
                        best, bf = i, f
            a = lists[best][pos[best]]
            pos[best] += 1
            self.op(*a[0], **a[1])

    def op(self, eng, fn, r=(), w=(), dma=None, n=0, fs=False):
        if self._defer is not None:
            self._defer.append(((eng, fn), {"r": list(r), "w": list(w), "dma": dma, "n": n, "fs": fs}))
            return -1
        idx = len(self.ops)
        deps = set()
        for k in r:
            deps.update(self._deps_read(k, idx))
        for k in w:
            deps.update(self._deps_write(k, idx))
        deps.discard(idx)
        self.ops.append({"eng": eng, "fn": fn, "deps": deps, "dma": dma, "sig": False, "n": n, "fs": fs})
        return idx

    def _skip(self, t, o):
        if t["dma"] is not None or o["dma"] is not None or t["eng"] != o["eng"]:
            return False
        if o["eng"] not in SAME_ENGINE_SYNC:
            return True
        return t["n"] >= BIG_N and not o["fs"]

    def emit(self):
        nc = self.nc
        ops = self.ops
        for o in ops:
            for d in o["deps"]:
                t = ops[d]
                if t["dma"] is None and not self._skip(t, o):
                    t["sig"] = True
        cnt = {}
        sems = {}

        def getsem(name):
            if name not in sems:
                sems[name] = self.es.enter_context(nc.semaphore("s_" + name))
            return sems[name]

        for o in ops:
            if o["dma"] is not None:
                k = "d_" + o["dma"]
                cnt[k] = cnt.get(k, 0) + 16
                o["tok"] = (k, cnt[k])
            elif o["sig"]:
                k = "e_" + o["eng"]
                cnt[k] = cnt.get(k, 0) + 1
                o["tok"] = (k, cnt[k])
            else:
                o["tok"] = None
        seen = {e: {} for e in self.eng}
        nwait = 0
        for o in ops:
            e = o["eng"]
            engine = self.eng[e]
            need = {}
            for d in o["deps"]:
                t = ops[d]
                if t["tok"] is None or self._skip(t, o):
                    continue
                k, v = t["tok"]
                if v > need.get(k, 0):
                    need[k] = v
            for k, v in need.items():
                if seen[e].get(k, 0) >= v:
                    continue
                engine.wait_ge(getsem(k), v)
                seen[e][k] = v
                nwait += 1
            ins = o["fn"](engine)
            if o["tok"] is not None:
                k, v = o["tok"]
                ins.then_inc(getsem(k), 16 if o["dma"] is not None else 1)
        sp = self.eng["sp"]
        for k, v in cnt.items():
            if k.startswith("d_"):
                sp.wait_ge(getsem(k), v)
        return {"n_ops": len(ops), "n_wait": nwait, "n_sems": len(sems)}


SPC = 576


def build(NBLK=8, NLAY=2, dbg=(), final=True):
    nc = bass.Bass("TRN2", target_bir_lowering=False)

    def din(name, shape, dt=F32):
        return nc.dram_tensor(name, shape, dt, kind="ExternalInput").ap()

    x_d = din("x", [SEQ, 1024])
    p_d = din("p", [2, SEQ, 256])
    w_in_d = din("w_in", [2, 1024, 5632])
    gxw_d = din("gate_x_w", [2, 10, 128, 128])
    gaw_d = din("gate_a_w", [2, 10, 128, 128])
    w_ao_d = din("w_a_out", [2, 1280, 1024])
    w_glu_d = din("w_glu", [2, 1024, 2048])
    w_o_d = din("w_o", [2, 1024, 1024])
    w_up_d = din("w_up", [2, 1024, 6144])
    w_dn_d = din("w_down", [2, 3072, 1024])
    w_pg_d = din("w_ple_gate", [2, 1024, 1024])
    w_pp_d = din("w_ple_proj", [2, 256, 1024])
    sp_d = din("spk", [2, 128, SPC])
    s5f_d = din("s5f", [2, 4, 128, 1024])
    cst_d = din("cst", [128, 128 + 8 * 240 + 1 + 1024])
    y_d = nc.dram_tensor("y", [SEQ, 1024], F32, kind="ExternalOutput").ap()
    s5m_d = nc.dram_tensor("s5m", [2, 4, 128, 8192], BF16, kind="Internal").ap()
    dbg_out = {}

    es = ExitStack()
    with es:
        P = Prog(nc, es)

        def sb(name, shape, dt=F32):
            return es.enter_context(nc.sbuf_tensor(name, shape, dt))

        X = sb("X", [128, NT, 1024])
        HT = sb("HT", [128, 8, TB], BF16)
        ST = sb("ST", [128, 16])
        WS = sb("WS", [128, NSLOT, 4096], BF16)
        IDB = sb("IDB", [128, 128], BF16)
        IDF = sb("IDF", [128, 128])
        ZS = sb("ZS", [128, 8, 240], BF16)
        SGN = sb("SGN", [128, 1])
        GF = sb("GF", [128, 1024])
        SPm = sb("SPm", [128, 2, SPC])
        C8 = sb("C8", [128, 2, 20])
        S5C = sb("S5C", [128, 2, 3, 64])
        LRUH = sb("LRUH", [128, 2, 10])
        XAH = sb("XAH", [128, 2, 10, 3])
        S5CAR = sb("S5CAR", [128, 2, 2, 64])
        FT = sb("FT", [128, 2, 48, 2])
        T1 = sb("T1", [128, 2, 64])
        T2 = sb("T2", [128, 2, 64])
        ARENA = 134776
        AR = sb("AR", [128, ARENA // 4])

        PM = es.enter_context(nc.psum_tensor("PM", [128, 4, 512], F32))
        PX = es.enter_context(nc.psum_tensor("PX", [128, 2, 512], F32))
        PTR = es.enter_context(nc.psum_tensor("PTR", [128, 2, 1024], BF16))

        arena_ranges = {}

        def av(name, off, nbytes, dt=F32, **re):
            assert off % 4 == 0 and nbytes % 4 == 0 and off + nbytes <= ARENA, (name, off, nbytes)
            arena_ranges[name] = (off, off + nbytes)
            v = AR[:, off // 4:(off + nbytes) // 4]
            if dt is not F32:
                v = v.bitcast(dt)
            if re:
                pat = re.pop("pat")
                v = v.rearrange(pat, **re)
            return v

        OA = av("OA", 0, 10240, BF16, pat="p (k n) -> p k n", n=TB)
        XA = av("XA", 10240, 20600, F32, pat="p (k n) -> p k n", n=515)
        GBf = av("GB", 30840, 10240, BF16, pat="p (k n) -> p k n", n=TB)
        LT = av("LT", 41080, 22528)
        YSB = av("YSB", 10240, 8192, BF16, pat="p (g c) -> p g c", c=64)
        YS = av("YS", 18432, 8192, BF16, pat="p (k n) -> p k n", n=TB)
        UT = av("UT", 63608, 8192, BF16, pat="p (g l c) -> p g l c", g=32, l=8)
        XB = av("XB", 63608, 8192, BF16, pat="p (g c) -> p g c", c=64)
        U = av("U", 71800, 8192, BF16, pat="p (g c) -> p g c", c=64)
        S2 = av("S2", 79992, 33280, F32, pat="p (h g c) -> p h g c", h=2, g=64)
        SMA = av("SMA", 113272, 8192, BF16, pat="p (k n) -> p k n", n=TB)
        SMB = av("SMB", 121464, 8192, BF16, pat="p (k n) -> p k n", n=TB)
        TMP = av("TMP", 41080, 4096, F32, pat="p (k n) -> p k n", n=TB)
        YT = av("YT", 41080, 4096, BF16, pat="p (s n) -> p s n", s=2)
        HS = av("HS", 45176, 4096, BF16, pat="p (k n) -> p k n", n=1024)
        SQ = av("SQ", 49272, 2048, BF16)
        WG = av("WG", 129656, 5120, BF16, pat="p (w k n) -> p w k n", w=2, k=10)
        YB = av("YB", 79992, 16384, F32, pat="p (k n) -> p k n", n=TB)
        MGB = av("MGB", 96376, 8192, BF16, pat="p (k n) -> p k n", n=TB)
        PTOK = av("PTOK", 104568, 4096, F32, pat="p (t n) -> p t n", n=256)
        PB = av("PB", 108664, 2048, BF16, pat="p (t n) -> p t n", n=256)
        PTT = av("PTT", 110712, 2048, BF16, pat="p (k n) -> p k n", n=TB)
        ACTB = av("ACTB", 0, 24576, BF16, pat="p (k n) -> p k n", n=TB)
        UPF = av("UPF", 24576, 8224, F32, pat="p (k n) -> p k n", n=514)
        CV = av("CV", 32800, 8192, F32, pat="p (k n) -> p k n", n=TB)
        PTMP = av("PTMP", 40992, 4096, F32, pat="p (k n) -> p k n", n=TB)
        EX = av("EX", 0, 15360, F32, pat="p (g n) -> p g n", n=240)
        F1 = av("F1", 15360, 1024, F32, pat="p (g n) -> p g n", n=16)
        F2 = av("F2", 16384, 1024, F32, pat="p (g n) -> p g n", n=16)
        G1 = av("G1", 17408, 1024, F32, pat="p (g n) -> p g n", n=16)
        G2 = av("G2", 18432, 1024, F32, pat="p (g n) -> p g n", n=16)
        BBv = av("BB", 19456, 1024, F32, pat="p (g n) -> p g n", n=16)
        BBS = av("BBS", 20480, 1024, F32, pat="p (g n) -> p g n", n=16)
        Q1 = av("Q1", 21504, 1024, F32, pat="p (g n) -> p g n", n=16)
        Q2 = av("Q2", 22528, 1024, F32, pat="p (g n) -> p g n", n=16)
        CST = av("CST", 23552, 1024, F32, pat="p (g n) -> p g n", n=16)
        WAB = av("WAB", 24576, 4096, BF16, pat="p (g n) -> p g n", n=128)
        WASB = av("WASB", 28672, 4096, BF16, pat="p (g n) -> p g n", n=128)
        WCB = av("WCB", 32768, 4096, BF16, pat="p (g n) -> p g n", n=128)
        TBB = av("TBB", 36864, 4096, BF16, pat="p (g n) -> p g n", n=128)
        SC = av("SC", 40960, 64 * 4 * 40, F32, pat="p (k n) -> p k n", n=64)
        names = list(arena_ranges)
        for a in names:
            for b in names:
                if a != b:
                    (a0, a1), (b0, b1) = arena_ranges[a], arena_ranges[b]
                    if a0 < b1 and b0 < a1:
                        P.alias.setdefault(a, []).append(b)

        def _fs(ap):
            k = 1
            for d in ap.shape[1:]:
                k *= d
            return k

        def mm(out, lhsT, rhs, start, stop, r, w):
            P.op("pe", lambda e: e.matmul(out, lhsT=lhsT, rhs=rhs, start=start, stop=stop), r=r, w=w)

        def tr(out, in_, ident, r, w):
            P.op("pe", lambda e: e.transpose(out, in_, ident), r=r, w=w)

        def act(out, in_, func, r, w, bias=None, scale=None, accum=None):
            kw = {}
            if bias is not None:
                kw["bias"] = bias
            if scale is not None:
                kw["scale"] = scale
            if accum is not None:
                kw["accum_out"] = accum
            P.op("act", lambda e: e.activation(out=out, in_=in_, func=func, **kw), r=r, w=w,
                 n=0 if accum is not None else _fs(out))

        def ts(eng, out, in0, s1, s2, op0, op1, r, w):
            if s2 is None:
                P.op(eng, lambda e: e.tensor_scalar(out=out, in0=in0, scalar1=s1, scalar2=None, op0=op0), r=r, w=w, n=_fs(out))
            else:
                P.op(eng, lambda e: e.tensor_scalar(out=out, in0=in0, scalar1=s1, scalar2=s2, op0=op0, op1=op1), r=r, w=w, n=_fs(out))

        def tt(eng, out, in0, in1, op, r, w):
            P.op(eng, lambda e: e.tensor_tensor(out=out, in0=in0, in1=in1, op=op), r=r, w=w, n=_fs(out))

        def stt(eng, out, in0, scalar, in1, op0, op1, r, w):
            P.op(eng, lambda e: e.scalar_tensor_tensor(out=out, in0=in0, scalar=scalar, in1=in1, op0=op0, op1=op1), r=r, w=w, n=_fs(out))

        def cp(eng, out, in_, r, w):
            if eng == "act":
                P.op("act", lambda e: e.copy(out=out, in_=in_), r=r, w=w, n=_fs(out))
            else:
                P.op(eng, lambda e: e.tensor_copy(out=out, in_=in_), r=r, w=w, n=_fs(out))

        def mset(eng, out, val, w):
            P.op(eng, lambda e: e.memset(out, val), w=w)

        def dma(eng, out, in_, key, r, w):
            P.op(eng, lambda e: e.dma_start(out=out, in_=in_), r=r, w=w, dma=key)

        def tap(name, ap, shape, r):
            if name in dbg:
                d = nc.dram_tensor("dbg_" + name, shape, ap.dtype, kind="ExternalOutput").ap()
                dbg_out[name] = d
                dma("sp", d, ap, "dbg_" + name, r, ["dbgdram_" + name])

        wctr = [0]

        def wload(src, nk, ncols, cast=True, rk=()):
            s = wctr[0] % NSLOT
            wctr[0] += 1
            v = WS[:, s, 0:nk * ncols].rearrange("p (k n) -> p k n", n=ncols)
            dma("pool" if cast else "sp", v, src, "w%d" % s, list(rk), ["ws/%d" % s])
            return v, "ws/%d" % s

        pmc = [0]

        def pm():
            b = pmc[0] % 4
            pmc[0] += 1
            return PM[:, b, :], "pm/%d" % b

        pxc = [0]

        def px():
            b = pxc[0] % 2
            pxc[0] += 1
            return PX[:, b, :], "px/%d" % b

        ptc = [0]

        def ptr():
            b = ptc[0] % 2
            ptc[0] += 1
            return PTR[:, b, :], "ptr/%d" % b

        dma("sp", IDF[:], cst_d[:, 0:128], "c0", [], ["IDF"])
        dma("pool", IDB[:], cst_d[:, 0:128], "c1", [], ["IDB"])
        dma("pool", ZS[:].rearrange("p a b -> p (a b)"), cst_d[:, 128:128 + 1920], "c2", [], ["ZS"])
        dma("sp", SGN[:], cst_d[:, 2048:2049], "c3", [], ["SGN"])
        dma("sp", GF[:], cst_d[:, 2049:2049 + 1024], "c4", [], ["GF"])
        mset("dve", ST[:, 8:12], -0.5, ["ST/nh"])
        mset("dve", LRUH[:], 0.0, ["LRUH"])
        mset("dve", XAH[:], 0.0, ["XAH"])
        mset("dve", S5CAR[:], 0.0, ["S5CAR"])
        mset("dve", FT[:], 0.0, ["FT"])

        for i in range(NLAY):
            dma("sp", SPm[:, i, :], sp_d[i], "spm%d" % i, [], ["SPm/%d" % i])
            spk = "SPm/%d" % i
            act(C8[:, i, 0:10], SPm[:, i, 102:112], AF.Exp, [spk], ["C8/%d" % i], scale=-1.0)
            act(C8[:, i, 0:10], C8[:, i, 0:10], AF.Ln, ["C8/%d" % i], ["C8/%d" % i], bias=1.0)
            ts("dve", C8[:, i, 10:20], C8[:, i, 0:10], -16.0, None, ALU.mult, None, ["C8/%d" % i], ["C8/%d" % i])
            ts("dve", C8[:, i, 0:10], C8[:, i, 0:10], -8.0, None, ALU.mult, None, ["C8/%d" % i], ["C8/%d" % i])

            a_re = SPm[:, i, 320:384]
            a_im = SPm[:, i, 384:448]
            ldt = SPm[:, i, 448:512]
            DM = SPm[:, i, 512:576]
            k = "SC"
            DT, ADT, TH, MAG, QQ, RR, MSK, SN, CS = (SC[:, j, :] for j in range(9))
            LBR, LBI, NR, DEN, CR, CI, W2, TA, TBt = (SC[:, 9 + j, :] for j in range(9))
            PRn = SC[:, 18:27, :]
            PIn = SC[:, 27:36, :]
            QI = SC[:, 36, :].bitcast(I32)
            act(DT, ldt, AF.Exp, [spk], [k])
            tt("dve", ADT, a_re, DT, ALU.mult, [spk, k], [k])
            tt("dve", TH, a_im, DT, ALU.mult, [spk, k], [k])
            act(MAG, ADT, AF.Exp, [k], [k])

            def sincos(dst, shift):
                ts("dve", RR, TH, shift, None, ALU.add, None, [k], [k])
                ts("dve", QQ, RR, 1.0 / (2 * PI), None, ALU.mult, None, [k], [k])
                cp("dve", QI, QQ, [k], [k])
                cp("dve", QQ, QI, [k], [k])
                stt("dve", RR, QQ, -2 * PI, RR, ALU.mult, ALU.add, [k], [k])
                ts("dve", MSK, RR, PI, -2 * PI, ALU.is_gt, ALU.mult, [k], [k])
                tt("dve", RR, RR, MSK, ALU.add, [k], [k])
                ts("dve", MSK, RR, -PI, 2 * PI, ALU.is_lt, ALU.mult, [k], [k])
                tt("dve", RR, RR, MSK, ALU.add, [k], [k])
                ts("dve", RR, RR, PI, -PI, ALU.min, ALU.max, [k], [k])
                act(dst, RR, AF.Sin, [k], [k])

            sincos(SN, 0.0)
            sincos(CS, PI / 2)
            tt("dve", LBR, MAG, CS, ALU.mult, [k], [k])
            tt("dve", LBI, MAG, SN, ALU.mult, [k], [k])
            mset("dve", PRn[:, 0, :], 1.0, [k])
            mset("dve", PIn[:, 0, :], 0.0, [k])
            for q in range(8):
                tt("dve", TA, PRn[:, q, :], LBR, ALU.mult, [k], [k])
                tt("dve", TBt, PIn[:, q, :], LBI, ALU.mult, [k], [k])
                tt("dve", PRn[:, q + 1, :], TA, TBt, ALU.subtract, [k], [k])
                tt("dve", TA, PRn[:, q, :], LBI, ALU.mult, [k], [k])
                tt("dve", TBt, PIn[:, q, :], LBR, ALU.mult, [k], [k])
                tt("dve", PIn[:, q + 1, :], TA, TBt, ALU.add, [k], [k])
            ts("dve", NR, LBR, -1.0, None, ALU.add, None, [k], [k])
            tt("dve", TA, a_re, a_re, ALU.mult, [spk, k], [k])
            tt("dve", TBt, a_im, a_im, ALU.mult, [spk, k], [k])
            tt("dve", DEN, TA, TBt, ALU.add, [k], [k])
            P.op("dve", lambda e, DEN=DEN: e.reciprocal(out=DEN, in_=DEN), r=[k], w=[k])
            tt("dve", TA, NR, a_re, ALU.mult, [spk, k], [k])
            tt("dve", TBt, LBI, a_im, ALU.mult, [spk, k], [k])
            tt("dve", TA, TA, TBt, ALU.add, [k], [k])
            tt("dve", CR, TA, DEN, ALU.mult, [k], [k])
            tt("dve", TA, LBI, a_re, ALU.mult, [spk, k], [k])
            tt("dve", TBt, NR, a_im, ALU.mult, [spk, k], [k])
            tt("dve", TA, TA, TBt, ALU.subtract, [k], [k])
            tt("dve", CI, TA, DEN, ALU.mult, [k], [k])
            ts("dve", W2, CI, SGN[:, 0:1], None, ALU.mult, None, [k, "SGN"], [k])
            cp("dve", S5C[:, i, 0, :], PRn[:, 8, :], [k], ["S5C/%d" % i])
            ts("dve", S5C[:, i, 1, :], PIn[:, 8, :], SGN[:, 0:1], None, ALU.mult, None, [k, "SGN"], ["S5C/%d" % i])
            ts("dve", S5C[:, i, 2, :], PIn[:, 8, :], SGN[:, 0:1], -1.0, ALU.mult, ALU.mult, [k, "SGN"], ["S5C/%d" % i])

            def bc(v, g0):
                return v[:, g0:g0 + 16].unsqueeze(2).to_broadcast([128, 16, 16])

            for gb in range(4):
                g0 = gb * 16
                for q, (tile_, nm) in enumerate(((F1, "F1"), (F2, "F2"), (G1, "G1"), (G2, "G2"))):
                    dma("sp", tile_, s5f_d[i, q, :, g0 * 16:(g0 + 16) * 16].rearrange("p (g n) -> p g n", n=16),
                        "s5f%d" % q, [], [nm])
                tt("dve", Q1, F1, bc(CR, g0), ALU.mult, ["F1", k], ["Q1"])
                tt("dve", Q2, F2, bc(W2, g0), ALU.mult, ["F2", k], ["Q2"])
                tt("dve", BBv, Q1, Q2, ALU.add, ["Q1", "Q2"], ["BB"])
                tt("dve", Q1, F2, bc(CR, g0), ALU.mult, ["F2", k], ["Q1"])
                tt("dve", Q2, F1, bc(W2, g0), ALU.mult, ["F1", k], ["Q2"])
                tt("dve", BBS, Q1, Q2, ALU.subtract, ["Q1", "Q2"], ["BBS"])
                mset("dve", EX[:, :, 128:240], 0.0, ["EX"])
                for j in range(8):
                    kk = 7 - j
                    tt("dve", Q1, BBv, bc(PRn[:, kk, :], g0), ALU.mult, ["BB", k], ["Q1"])
                    tt("dve", Q2, BBS, bc(PIn[:, kk, :], g0), ALU.mult, ["BBS", k], ["Q2"])
                    stt("dve", EX[:, :, j * 16:(j + 1) * 16], Q2, SGN[:, 0:1], Q1, ALU.mult, ALU.add,
                        ["Q1", "Q2", "SGN"], ["EX"])
                ts("dve", CST, G1, SGN[:, 0:1], -1.0, ALU.mult, ALU.mult, ["G1", "SGN"], ["CST"])
                for gq in range(4):
                    bank, bk = pm()
                    for gi in range(4):
                        gl = gq * 4 + gi
                        tr(bank[:, gi * 128:(gi + 1) * 128], EX[:, gl, 0:128], IDF[:], ["EX", "IDF"], [bk])
                    bv = bank.rearrange("p (g n) -> p g n", n=128)
                    cp("act", WAB[:, gq * 4:(gq + 1) * 4, :], bv, [bk], ["WAB"])
                    cp("dve", WASB[:, gq * 4:(gq + 1) * 4, 0:64], bv[:, :, 64:128], [bk], ["WASB"])
                    cp("dve", WASB[:, gq * 4:(gq + 1) * 4, 64:128], bv[:, :, 0:64], [bk], ["WASB"])
                    bank, bk = pm()
                    for gi in range(4):
                        gl = gq * 4 + gi
                        for l in range(8):
                            mm(bank[:, gi * 128 + l * 16: gi * 128 + (l + 1) * 16],
                               EX[:, gl, (7 - l) * 16:(7 - l) * 16 + 128], CST[:, gl, :], True, True,
                               ["EX", "CST"], [bk])
                        stt("dve", TBB[:, gl, :], IDF[:], DM[:, g0 + gl:g0 + gl + 1], bank[:, gi * 128:(gi + 1) * 128],
                            ALU.mult, ALU.add, [bk, "IDF", spk], ["TBB"])
                for l in range(8):
                    tt("dve", Q1, G1, bc(PRn[:, l + 1, :], g0), ALU.mult, ["G1", k], ["Q1"])
                    tt("dve", Q2, G2, bc(PIn[:, l + 1, :], g0), ALU.mult, ["G2", k], ["Q2"])
                    ts("dve", Q1, Q1, SGN[:, 0:1], -1.0, ALU.mult, ALU.mult, ["Q1", "SGN"], ["Q1"])
                    tt("dve", WCB[:, :, l * 16:(l + 1) * 16], Q1, Q2, ALU.subtract, ["Q1", "Q2"], ["WCB"])
                for q, (tile_, nm) in enumerate(((WAB, "WAB"), (WASB, "WASB"), (WCB, "WCB"), (TBB, "TBB"))):
                    dma("sp", s5m_d[i, q, :, g0 * 128:(g0 + 16) * 128].rearrange("p (g n) -> p g n", n=128),
                        tile_, "s5mo%d" % q, [nm], ["s5m%d_%d" % (i, q)])

        def kview(ap):
            return ap.rearrange("(k p) n -> p k n", p=128)

        w_in_v = [kview(w_in_d[i]) for i in range(2)]
        w_ao_v = [kview(w_ao_d[i]) for i in range(2)]
        w_glu_v = [kview(w_glu_d[i]) for i in range(2)]
        w_o_v = [kview(w_o_d[i]) for i in range(2)]
        w_up_v = [kview(w_up_d[i]) for i in range(2)]
        w_dn_v = [kview(w_dn_d[i]) for i in range(2)]
        w_pg_v = [kview(w_pg_d[i]) for i in range(2)]
        w_pp_v = [kview(w_pp_d[i]) for i in range(2)]
        gxw_v = [gxw_d[i].rearrange("h i j -> i h j") for i in range(2)]
        gaw_v = [gaw_d[i].rearrange("h i j -> i h j") for i in range(2)]
        x_v = x_d.rearrange("(b t p) d -> b p t d", t=NT, p=128)
        y_v = y_d.rearrange("(b t p) d -> b p t d", t=NT, p=128)
        p_v = [p_d[i].rearrange("(b t p) d -> b p t d", t=NT, p=128) for i in range(2)]

        def norm(gcols, gkey):
            for t in range(NT):
                act(SQ[:], X[:, t, :], AF.Square, ["X/%d" % t], ["SQ", "ST/ss"], accum=ST[:, t:t + 1])
            ts("dve", ST[:, 4:8], ST[:, 0:4], 1.0 / 1024, EPS, ALU.mult, ALU.add, ["ST/ss"], ["ST/rs"])
            tt("pool", ST[:, 4:8], ST[:, 4:8], ST[:, 8:12], ALU.pow, ["ST/rs", "ST/nh"], ["ST/rs"])
            for t in range(NT):
                s = t % 2
                act(HS[:, s, :], X[:, t, :], AF.Copy, ["X/%d" % t, "ST/rs"], ["HS/%d" % s], scale=ST[:, 4 + t:5 + t])
                bank, bk = ptr()
                for kt in range(8):
                    tr(bank[:, kt * 128:(kt + 1) * 128], HS[:, s, kt * 128:(kt + 1) * 128], IDB[:],
                       ["HS/%d" % s, "IDB"], [bk])
                tt("dve", HT[:, :, t * 128:(t + 1) * 128], bank.rearrange("p (k n) -> p k n", n=128),
                   gcols.unsqueeze(2).to_broadcast([128, 8, 128]), ALU.mult, [bk, gkey], ["HT/%d" % t])

        HTall = ["HT/%d" % t for t in range(NT)]

        def proj_fm(ws, wk, j, nk, rhs_of, rkeys):
            bank, bk = pm()
            for kt in range(nk):
                mm(bank, ws[:, kt, j * 128:(j + 1) * 128], rhs_of(kt), kt == 0, kt == nk - 1, [wk] + rkeys, [bk])
            return bank, bk

        def layer(i, blk):
            spk = "SPm/%d" % i
            norm(SPm[:, i, 0:8], spk)
            tap("ht1_%d_%d" % (i, blk), HT[:], [128, 8, TB], HTall)

            def win_chunk(c):
                ws, wk = wload(w_in_v[i][:, :, c * 512:(c + 1) * 512], 8, 512)
                for j in range(4):
                    t = 4 * c + j
                    bank, bk = proj_fm(ws, wk, j, 8, lambda kt: HT[:, kt, :], HTall)
                    if t < 10:
                        cp("act", XA[:, t, 3:515], bank, [bk], ["XA/%d" % t])
                    elif t < 20:
                        act(GBf[:, t - 10, :], bank, AF.Gelu_apprx_tanh, [bk], ["GB/%d" % (t - 10)])
                    elif t < 28:
                        raise AssertionError("ub goes through the token-chunk-major path")
                    elif t < 36:
                        act(SMA[:, t - 28, :], bank, AF.Sigmoid, [bk], ["SMA/%d" % (t - 28)])
                    else:
                        act(SMB[:, t - 36, :], bank, AF.Sigmoid, [bk], ["SMB/%d" % (t - 36)])

            for c in range(5):
                win_chunk(c)

            def lru_tile(t, wgx, wgxk, wga, wgak):
                q = t % 2
                base = q * 2816
                xc = LT[:, base:base + 512]
                gx = LT[:, base + 512:base + 1024]
                aa = LT[:, base + 1024:base + 1536]
                m2 = LT[:, base + 1536:base + 2048]
                hh = LT[:, base + 2048:base + 2560]
                xcb = LT[:, base + 2560:base + 2816].bitcast(BF16)
                lk = "LT/%d" % q
                xk = "XA/%d" % t
                cp("dve", XA[:, t, 0:3], XAH[:, i, t, :], ["XAH/%d_%d" % (i, t)], [xk])
                cw = lambda j: SPm[:, i, 32 + t * 4 + j:33 + t * 4 + j]
                act(xc, XA[:, t, 0:512], AF.Identity, [xk, spk], [lk], bias=SPm[:, i, 72 + t:73 + t], scale=cw(0))
                for j in range(1, 4):
                    stt("dve", xc, XA[:, t, j:j + 512], cw(j), xc, ALU.mult, ALU.add, [xk, spk, lk], [lk])
                cp("dve", XAH[:, i, t, :], XA[:, t, 512:515], [xk], ["XAH/%d_%d" % (i, t)])
                cp("act", xcb, xc, [lk], [lk])
                bx, bxk = PX[:, q, :], "px/%d" % q
                ba, bak = bx, bxk
                mm(bx, wgx[:, t, :], xcb, True, True, [wgxk, lk], [bxk])
                act(gx, bx, AF.Sigmoid, [bxk, spk], [lk], bias=SPm[:, i, 82 + t:83 + t])
                mm(ba, wga[:, t, :], xcb, True, True, [wgak, lk], [bak])
                act(hh, ba, AF.Sigmoid, [bak, spk], [lk], bias=SPm[:, i, 92 + t:93 + t])
                act(aa, hh, AF.Exp, [lk, "C8/%d" % i], [lk], scale=C8[:, i, t:t + 1])
                act(m2, hh, AF.Exp, [lk, "C8/%d" % i], [lk], scale=C8[:, i, 10 + t:11 + t])
                act(m2, m2, AF.Sqrt, [lk], [lk], bias=1.0, scale=-1.0)
                tt("dve", gx, gx, xc, ALU.mult, [lk], [lk])
                stt("dve", m2, m2, 1e-6, gx, ALU.max, ALU.mult, [lk], [lk])
                hk = "LRUH/%d_%d" % (i, t)
                P.op("dve", lambda e, hh=hh, aa=aa, m2=m2, t=t: e.tensor_tensor_scan(
                    out=hh, data0=aa, data1=m2, initial=LRUH[:, i, t:t + 1], op0=ALU.mult, op1=ALU.add),
                    r=[lk, hk], w=[lk], n=512)
                P.op("dve", lambda e, hh=hh, t=t: e.tensor_copy(out=LRUH[:, i, t:t + 1], in_=hh[:, 511:512]),
                     r=[lk], w=[hk], n=1, fs=True)
                tt("dve", OA[:, t, :], hh, GBf[:, t, :], ALU.mult, [lk, "GB/%d" % t], ["OA/%d" % t])

            L_lru = P.defer()
            dma("pool", WG[:, 0], gxw_v[i], "wg0", [], ["WG/0"])
            dma("pool", WG[:, 1], gaw_v[i], "wg1", [], ["WG/1"])
            wgx, wgxk, wga, wgak = WG[:, 0], "WG/0", WG[:, 1], "WG/1"
            for t in range(0, 10, 2):
                lru_tile(t, wgx, wgxk, wga, wgak)
            P.end_defer()
            L_lru_odd = P.defer()
            for t in range(1, 10, 2):
                lru_tile(t, wgx, wgxk, wga, wgak)
            P.end_defer()

            L_b = P.defer()
            for hh in range(2):
                ws, wk = wload(w_in_v[i][:, :, (5 + hh) * 512:(6 + hh) * 512], 8, 512)
                for l in range(8):
                    bank, bk = pm()
                    for kt in range(8):
                        mm(bank[0:64, :], HT[:, kt, l::8], ws[:, kt, :], kt == 0, kt == 7, [wk] + HTall, [bk])
                    cp("act" if l % 2 else "dve", UT[0:64, :, l, :], bank[0:64, :].rearrange("p (g c) -> p g c", c=16),
                       [bk], ["UT"])
                for gq in range(2):
                    tb, tbk = ptr()
                    tv = tb.rearrange("p (g c) -> p g c", c=64)
                    for gi in range(16):
                        tr(tv[:, gi, :], UT[0:64, gq * 16 + gi, :, :].rearrange("p l c -> p (l c)"), IDB[0:64, 0:64],
                           ["UT", "IDB"], [tbk])
                    g0 = hh * 32 + gq * 16
                    cp("act" if gq else "dve", U[:, g0:g0 + 16, :], tv, [tbk],
                       ["U/%d" % (g0 // 8), "U/%d" % (g0 // 8 + 1)])
            for h in range(2):
                for half in range(2):
                    ws, wk = wload(s5m_d[i, h, :, half * 4096:(half + 1) * 4096].rearrange("p (g n) -> p g n", n=128),
                                   32, 128, cast=False, rk=["s5m%d_%d" % (i, h)])
                    for j4 in range(4):
                        j = half * 4 + j4
                        bank, bk = pm()
                        bv = bank.rearrange("p (g c) -> p g c", c=64)
                        for gl in range(8):
                            mm(bv[:, gl, :], ws[:, j4 * 8 + gl, :], U[:, j * 8 + gl, :], True, True,
                               [wk, "U/%d" % j], [bk])
                        cp("act", S2[:, h, j * 8:(j + 1) * 8, 1:65], bv, [bk], ["S2/%d" % h])
            cp("dve", S2[:, :, :, 0], S5CAR[:, i, :, :], ["S5CAR/%d" % i], ["S2"])
            ArB = S5C[:, i, 0, :].unsqueeze(1).to_broadcast([128, 2, 64])
            sck = "S5C/%d" % i
            for c in range(64):
                if c % 16 == 0:
                    win_chunk(7 + c // 16)
                tt("dve", T1[:], S2[:, :, :, c], ArB, ALU.mult, ["S2", sck], ["T1"])
                tt("dve", T2[:, 0, :], S2[:, 1, :, c], S5C[:, i, 1, :], ALU.mult, ["S2", sck], ["T2"])
                tt("dve", T2[:, 1, :], S2[:, 0, :, c], S5C[:, i, 2, :], ALU.mult, ["S2", sck], ["T2"])
                tt("dve", T1[:], T1[:], T2[:], ALU.add, ["T1", "T2"], ["T1"])
                tt("dve", S2[:, :, :, c + 1], S2[:, :, :, c + 1], T1[:], ALU.add, ["S2", "T1"], ["S2"])
            cp("dve", S5CAR[:, i, :, :], S2[:, :, :, 64], ["S2"], ["S5CAR/%d" % i])
            cp("act", XB[:], S2[:, 0, :, 0:64], ["S2"], ["XB"])
            P.end_defer()
            P.interleave([L_lru, L_lru_odd, L_b])
            tap("oa_%d_%d" % (i, blk), OA[:], [128, 10, TB], ["OA"])
            tap("xb_%d_%d" % (i, blk), XB[:], [128, 64, 64], ["XB"])
            dma("sp", PTOK[:], p_v[i][blk], "ptok", [], ["PTOK"])
            for half in range(2):
                wt_, wtk = wload(s5m_d[i, 3, :, half * 4096:(half + 1) * 4096].rearrange("p (g n) -> p g n", n=128),
                                 32, 128, cast=False, rk=["s5m%d_3" % i])
                wc_, wck = wload(s5m_d[i, 2, :, half * 4096:(half + 1) * 4096].rearrange("p (g n) -> p g n", n=128),
                                 32, 128, cast=False, rk=["s5m%d_2" % i])
                for j4 in range(4):
                    j = half * 4 + j4
                    bank, bk = pm()
                    bv = bank.rearrange("p (g c) -> p g c", c=64)
                    for gl in range(8):
                        g = j * 8 + gl
                        mm(bv[:, gl, :], wt_[:, j4 * 8 + gl, :], U[:, g, :], True, False, [wtk, "U/%d" % j], [bk])
                        mm(bv[:, gl, :], wc_[:, j4 * 8 + gl, :], XB[:, g, :], False, True, [wck, "XB"], [bk])
                    cp("act", YSB[:, j * 8:(j + 1) * 8, :], bv, [bk], ["YSB/%d" % j])
            for j in range(8):
                tb, tbk = ptr()
                tv = tb[0:64, :].rearrange("p (g n) -> p g n", n=128)
                for gl in range(8):
                    tr(tv[:, gl, :], YSB[:, j * 8 + gl, :], IDB[:], ["YSB/%d" % j, "IDB"], [tbk])
                sl = j % 2
                cp("dve" if j % 2 else "act", YT[0:64, sl, :].rearrange("p (l g c) -> p g l c", l=8, g=8),
                   tb[0:64, :].rearrange("p (g l c) -> p g l c", g=8, l=8), [tbk], ["YT/%d" % sl])
                tb2, tb2k = ptr()
                for l in range(8):
                    tr(tb2[:, l * 64:(l + 1) * 64], YT[0:64, sl, l * 128:(l + 1) * 128], IDB[0:64, 0:64],
                       ["YT/%d" % sl, "IDB"], [tb2k])
                act(YS[:, j, :].rearrange("p (c l) -> p c l", l=8), tb2[:, 0:512].rearrange("p (l c) -> p c l", l=8),
                    AF.Gelu_apprx_tanh, [tb2k], ["YS/%d" % j])
            tap("ys_%d_%d" % (i, blk), YS[:], [128, 8, TB], ["YS"])
            YSall = ["YS/%d" % j for j in range(8)]
            for c in (0, 2, 1, 3):
                ws, wk = wload(w_glu_v[i][:, :, c * 512:(c + 1) * 512], 8, 512)
                for j in range(4):
                    t = 4 * c + j
                    bank, bk = proj_fm(ws, wk, j, 8, lambda kt: YS[:, kt, :], YSall)
                    bcol = SPm[:, i, 112 + t:113 + t]
                    if t < 8:
                        act(YB[:, t, :], bank, AF.Identity, [bk, spk], ["YB/%d" % t], bias=bcol)
                    else:
                        s = t % 2
                        n = t - 8
                        act(TMP[:, s, :], bank, AF.Sigmoid, [bk, spk], ["TMP/%d" % s], bias=bcol)
                        tt("dve", TMP[:, s, :], TMP[:, s, :], SMB[:, n, :], ALU.mult,
                           ["TMP/%d" % s, "SMB/%d" % n], ["TMP/%d" % s])
                        tt("dve", YB[:, n, :], YB[:, n, :], TMP[:, s, :], ALU.mult,
                           ["YB/%d" % n, "TMP/%d" % s], ["YB/%d" % n])
            OAall = ["OA/%d" % t for t in range(10)]
            for c in range(4):
                ws, wk = wload(w_ao_v[i][:, :, c * 256:(c + 1) * 256], 10, 256)
                for j in range(2):
                    n = 2 * c + j
                    bank, bk = proj_fm(ws, wk, j, 10, lambda kt: OA[:, kt, :], OAall)
                    s = n % 2
                    tt("dve", TMP[:, s, :], bank, SMA[:, n, :], ALU.mult, [bk, "SMA/%d" % n], ["TMP/%d" % s])
                    tt("dve", MGB[:, n, :], TMP[:, s, :], YB[:, n, :], ALU.add, ["TMP/%d" % s, "YB/%d" % n],
                       ["MGB/%d" % n])
            tap("mg_%d_%d" % (i, blk), MGB[:], [128, 8, TB], ["MGB"])
            MGall = ["MGB/%d" % n for n in range(8)]
            for h in range(2):
                ws, wk = wload(w_o_v[i][:, :, h * 512:(h + 1) * 512], 8, 512)
                for t in range(NT):
                    bank, bk = pm()
                    for kt in range(8):
                        mm(bank, MGB[:, kt, t * 128:(t + 1) * 128], ws[:, kt, :], kt == 0, kt == 7, [wk] + MGall, [bk])
                    tt("dve", X[:, t, h * 512:(h + 1) * 512], bank, X[:, t, h * 512:(h + 1) * 512], ALU.add,
                       [bk, "X/%d" % t], ["X/%d" % t])
            tap("x1_%d_%d" % (i, blk), X[:], [128, NT, 1024], ["X"])

            norm(SPm[:, i, 8:16], spk)

            def ffn_tile(ws, wk, c, j, half):
                t = 4 * c + j
                o = t % 24
                q = (2 * half + (j % 2))
                bank, bk = proj_fm(ws, wk, j, 8, lambda kt: HT[:, kt, :], HTall)
                ck = "CV/%d" % q
                ftk = "FT/%d_%d" % (i, t)
                fw_ = lambda jj: SPm[:, i, 128 + t * 3 + jj:129 + t * 3 + jj]
                bcol = SPm[:, i, 272 + t:273 + t]
                act(CV[:, q, 2:512], bank[:, 0:510], AF.Identity, [bk, spk], [ck], bias=bcol, scale=fw_(0))
                ts("dve", CV[:, q, 0:2], FT[:, i, t, :], fw_(0), bcol, ALU.mult, ALU.add, [ftk, spk], [ck])
                stt("dve", CV[:, q, 1:512], bank[:, 0:511], fw_(1), CV[:, q, 1:512], ALU.mult, ALU.add,
                    [bk, spk, ck], [ck])
                stt("dve", CV[:, q, 0:1], FT[:, i, t, 1:2], fw_(1), CV[:, q, 0:1], ALU.mult, ALU.add,
                    [ftk, spk, ck], [ck])
                stt("dve", CV[:, q, :], bank, fw_(2), CV[:, q, :], ALU.mult, ALU.add, [bk, spk, ck], [ck])
                cp("dve", FT[:, i, t, :], bank[:, 510:512], [bk], [ftk])
                if half == 0:
                    act(ACTB[:, o, :], CV[:, q, :], AF.Gelu_apprx_tanh, [ck], ["ACTB/%d" % o])
                else:
                    tt("dve", ACTB[:, o, :], ACTB[:, o, :], CV[:, q, :], ALU.mult, ["ACTB/%d" % o, ck],
                       ["ACTB/%d" % o])

            for c6 in range(6):
                for half in range(2):
                    c = c6 + 6 * half
                    ws, wk = wload(w_up_v[i][:, :, c * 512:(c + 1) * 512], 8, 512)
                    for jp in range(2):
                        lists = []
                        for j in (2 * jp, 2 * jp + 1):
                            lists.append(P.defer())
                            ffn_tile(ws, wk, c, j, half)
                            P.end_defer()
                        P.interleave(lists)
            tap("actb_%d_%d" % (i, blk), ACTB[:], [128, 24, TB], ["ACTB"])
            ACall = ["ACTB/%d" % o for o in range(24)]
            for h in range(2):
                banks = [pm() for _ in range(NT)]
                for kc in range(3):
                    ws, wk = wload(w_dn_v[i][:, kc * 8:(kc + 1) * 8, h * 512:(h + 1) * 512], 8, 512)
                    for t in range(NT):
                        bank, bk = banks[t]
                        for k8 in range(8):
                            mm(bank, ACTB[:, kc * 8 + k8, t * 128:(t + 1) * 128], ws[:, k8, :],
                               kc == 0 and k8 == 0, kc == 2 and k8 == 7, [wk] + ACall, [bk])
                for t in range(NT):
                    bank, bk = banks[t]
                    tt("dve", X[:, t, h * 512:(h + 1) * 512], bank, X[:, t, h * 512:(h + 1) * 512], ALU.add,
                       [bk, "X/%d" % t], ["X/%d" % t])
            tap("x2_%d_%d" % (i, blk), X[:], [128, NT, 1024], ["X"])

            norm(SPm[:, i, 16:24], spk)
            cp("dve", PB[:], PTOK[:], ["PTOK"], ["PB"])
            bank, bk = ptr()
            bv = bank.rearrange("p (k n) -> p k n", n=TB)
            for t in range(NT):
                for kk in range(2):
                    tr(bv[:, kk, t * 128:(t + 1) * 128], PB[:, t, kk * 128:(kk + 1) * 128], IDB[:], ["PB", "IDB"], [bk])
            cp("act", PTT[:], bv, [bk], ["PTT"])
            for h in range(2):
                wg, wgk = wload(w_pg_v[i][:, :, h * 512:(h + 1) * 512], 8, 512)
                wp, wpk = wload(w_pp_v[i][:, :, h * 512:(h + 1) * 512], 2, 512)
                for t in range(NT):
                    bg, bgk = pm()
                    for kt in range(8):
                        mm(bg, HT[:, kt, t * 128:(t + 1) * 128], wg[:, kt, :], kt == 0, kt == 7, [wgk, "HT/%d" % t], [bgk])
                    bp, bpk = px()
                    for kk in range(2):
                        mm(bp, PTT[:, kk, t * 128:(t + 1) * 128], wp[:, kk, :], kk == 0, kk == 1, [wpk, "PTT"], [bpk])
                    s = t % 2
                    act(TMP[:, s, :], bg, AF.Sigmoid, [bgk], ["TMP/%d" % s])
                    tt("dve", TMP[:, s, :], TMP[:, s, :], bp, ALU.mult, ["TMP/%d" % s, bpk], ["TMP/%d" % s])
                    tt("dve", X[:, t, h * 512:(h + 1) * 512], TMP[:, s, :], X[:, t, h * 512:(h + 1) * 512], ALU.add,
                       ["TMP/%d" % s, "X/%d" % t], ["X/%d" % t])
            tap("x3_%d_%d" % (i, blk), X[:], [128, NT, 1024], ["X"])

        for blk in range(NBLK):
            dma("sp", X[:], x_v[blk], "xin", [], ["X"])
            for i in range(NLAY):
                layer(i, blk)
            if final:
                for t in range(NT):
                    act(SQ[:], X[:, t, :], AF.Square, ["X/%d" % t], ["SQ", "ST/ss"], accum=ST[:, t:t + 1])
                ts("dve", ST[:, 4:8], ST[:, 0:4], 1.0 / 1024, EPS, ALU.mult, ALU.add, ["ST/ss"], ["ST/rs"])
                tt("pool", ST[:, 4:8], ST[:, 4:8], ST[:, 8:12], ALU.pow, ["ST/rs", "ST/nh"], ["ST/rs"])
                for t in range(NT):
                    stt("dve", X[:, t, :], X[:, t, :], ST[:, 4 + t:5 + t], GF[:], ALU.mult, ALU.mult,
                        ["X/%d" % t, "ST/rs", "GF"], ["X/%d" % t])
            dma("sp", y_v[blk], X[:], "yout", ["X"], ["ydram/%d" % blk])

        with nc.allow_non_contiguous_dma(reason="small strided parameter loads"):
            info = P.emit()
    return nc, dbg_out, info


def _cols(v):
    return np.ascontiguousarray(v.reshape(-1, 128).T)


def host_layout(inp):
    f = np.float32
    spk = np.zeros((2, 128, SPC), f)
    s5f = np.zeros((2, 4, 128, 1024), f)
    for i in range(2):
        s = spk[i]
        s[:, 0:8] = _cols(inp["g_mix"][i])
        s[:, 8:16] = _cols(inp["g_ffn"][i])
        s[:, 16:24] = _cols(inp["g_ple"][i])
        s[:, 24:32] = _cols(inp["g_final"])
        caw = inp["conv_a_w"][i]
        for j in range(4):
            s[:, 32 + j:72:4] = _cols(caw[j])
        s[:, 72:82] = _cols(inp["conv_a_b"][i])
        s[:, 82:92] = _cols(inp["gate_x_b"][i])
        s[:, 92:102] = _cols(inp["gate_a_b"][i])
        s[:, 102:112] = _cols(inp["lru_lambda"][i])
        s[:, 112:128] = _cols(inp["b_glu"][i])
        cfw = inp["conv_f_w"][i]
        for j in range(3):
            s[:, 128 + j:272:3] = _cols(cfw[j])
        s[:, 272:320] = _cols(inp["conv_f_b"][i])
        are_t = inp["s5_a_re"][i].T
        aim_t = inp["s5_a_im"][i].T
        s[:, 320:384] = np.concatenate([are_t, are_t], 0)
        s[:, 384:448] = np.concatenate([aim_t, aim_t], 0)
        s[:, 448:512] = np.broadcast_to(inp["s5_log_dt"][i][None, :], (128, 64))
        dm = inp["s5_d"][i].reshape(64, 16).T
        s[:, 512:576] = np.tile(dm, (8, 1))
        bre = inp["s5_b_re"][i].transpose(1, 0, 2).reshape(64, 1024)
        bim = inp["s5_b_im"][i].transpose(1, 0, 2).reshape(64, 1024)
        cre = inp["s5_c_re"][i].transpose(2, 0, 1).reshape(64, 1024)
        cim = inp["s5_c_im"][i].transpose(2, 0, 1).reshape(64, 1024)
        s5f[i, 0] = np.concatenate([bre, bim], 0)
        s5f[i, 1] = np.concatenate([bim, bre], 0)
        s5f[i, 2] = np.concatenate([cre, cim], 0)
        s5f[i, 3] = np.concatenate([cim, cre], 0)
    cst = np.zeros((128, 128 + 1920 + 1 + 1024), f)
    cst[:, 0:128] = np.eye(128, dtype=f)
    z = np.zeros((128, 8, 240), f)
    for gl in range(8):
        for ch in range(16):
            z[gl * 16 + ch, gl, 112 + ch] = 1.0
    cst[:, 128:2048] = z.reshape(128, 1920)
    cst[0:64, 2048] = -1.0
    cst[64:128, 2048] = 1.0
    cst[:, 2049:] = np.broadcast_to(inp["g_final"][None, :], (128, 1024))
    shared = {"spk": spk, "s5f": s5f, "cst": cst}
    for k in ("w_in", "gate_x_w", "gate_a_w", "w_a_out", "w_glu", "w_o", "w_up", "w_down",
              "w_ple_gate", "w_ple_proj"):
        shared[k] = np.ascontiguousarray(inp[k], dtype=f)
    return shared


_CACHE = {}


def kernel(**inputs):
    inp = {k: np.asarray(v) for k, v in inputs.items()}
    shared = host_layout(inp)
    if "nc" not in _CACHE:
        _CACHE["nc"] = build()[0]
    nc = _CACHE["nc"]
    in_maps = []
    for b in range(8):
        m = dict(shared)
        m["x"] = np.ascontiguousarray(inp["x"][b], dtype=np.float32)
        m["p"] = np.ascontiguousarray(inp["p"][:, b], dtype=np.float32)
        in_maps.append(m)
    res = run_bass_kernel_spmd(nc, in_maps, core_ids=list(range(8)))
    return np.stack([np.asarray(r["y"], dtype=np.float32) for r in res.results], 0)
```

```python
import math
from contextlib import ExitStack

import numpy as np
import concourse.bass as bass
import concourse.mybir as mybir
from concourse.bass_utils import run_bass_kernel_spmd

F32 = mybir.dt.float32
BF16 = mybir.dt.bfloat16
I32 = mybir.dt.int32
AF = mybir.ActivationFunctionType
ALU = mybir.AluOpType

SAME_ENGINE_SYNC = ("act", "pool", "dve", "sp")
NSLOT = 4
BIG_N = 128
TB = 512
NT = 4
SEQ = 4096
EPS = 1e-6
PI = math.pi


class Prog:
    def __init__(self, nc, es):
        self.nc = nc
        self.es = es
        self.eng = {"pe": nc.tensor, "act": nc.scalar, "dve": nc.vector,
                    "pool": nc.gpsimd, "sp": nc.sync}
        self.ops = []
        self.state = {}
        self.alias = {}
        self._defer = None

    @staticmethod
    def _split(key):
        if "/" in key:
            r, s = key.split("/", 1)
            return r, s
        return key, None

    def _st(self, root):
        return self.state.setdefault(root, {"w": None, "r": [], "subs": {}})

    def _deps_read(self, key, me):
        root, sub = self._split(key)
        st = self._st(root)
        deps = []
        if st["w"] is not None:
            deps.append(st["w"])
        if sub is None:
            for s in st["subs"].values():
                if s["w"] is not None:
                    deps.append(s["w"])
            st["r"].append(me)
        else:
            s = st["subs"].setdefault(sub, {"w": None, "r": []})
            if s["w"] is not None:
                deps.append(s["w"])
            s["r"].append(me)
        return deps

    def _deps_write_whole(self, root, me):
        st = self._st(root)
        deps = []
        if st["w"] is not None:
            deps.append(st["w"])
        deps.extend(st["r"])
        for s in st["subs"].values():
            if s["w"] is not None:
                deps.append(s["w"])
            deps.extend(s["r"])
        st["subs"] = {}
        st["r"] = []
        st["w"] = me
        return deps

    def _deps_write(self, key, me):
        root, sub = self._split(key)
        deps = []
        for other in self.alias.get(root, ()):
            deps.extend(self._deps_write_whole(other, me))
        if sub is None:
            deps.extend(self._deps_write_whole(root, me))
            return deps
        st = self._st(root)
        if st["w"] is not None:
            deps.append(st["w"])
        deps.extend(st["r"])
        s = st["subs"].setdefault(sub, {"w": None, "r": []})
        if s["w"] is not None:
            deps.append(s["w"])
        deps.extend(s["r"])
        s["w"] = me
        s["r"] = []
        return deps

    def defer(self):
        self._defer = []
        return self._defer

    def end_defer(self):
        self._defer = None

    def interleave(self, lists):
        assert self._defer is None
        pos = [0] * len(lists)
        tot = sum(len(l) for l in lists)
        for _ in range(tot):
            best, bf = None, None
            for i, l in enumerate(lists):
                if pos[i] < len(l):
                    f = (pos[i] + 0.5) / len(l)
                    if bf is None or f < bf:
                        best, bf = i, f
            a = lists[best][pos[best]]
            pos[best] += 1
            self.op(*a[0], **a[1])

    def op(self, eng, fn, r=(), w=(), dma=None, n=0, fs=False):
        if self._defer is not None:
            self._defer.append(((eng, fn), {"r": list(r), "w": list(w), "dma": dma, "n": n, "fs": fs}))
            return -1
        idx = len(self.ops)
        deps = set()
        for k in r:
            deps.update(self._deps_read(k, idx))
        for k in w:
            deps.update(self._deps_write(k, idx))
        deps.discard(idx)
        self.ops.append({"eng": eng, "fn": fn, "deps": deps, "dma": dma, "sig": False, "n": n, "fs": fs})
        return idx

    def _skip(self, t, o):
        if t["dma"] is not None or o["dma"] is not None or t["eng"] != o["eng"]:
            return False
        if o["eng"] not in SAME_ENGINE_SYNC:
            return True
        return t["n"] >= BIG_N and not o["fs"]

    def emit(self):
        nc = self.nc
        ops = self.ops
        for o in ops:
            for d in o["deps"]:
                t = ops[d]
                if t["dma"] is None and not self._skip(t, o):
                    t["sig"] = True
        cnt = {}
        sems = {}

        def getsem(name):
            if name not in sems:
                sems[name] = self.es.enter_context(nc.semaphore("s_" + name))
            return sems[name]

        for o in ops:
            if o["dma"] is not None:
                k = "d_" + o["dma"]
                cnt[k] = cnt.get(k, 0) + 16
                o["tok"] = (k, cnt[k])
            elif o["sig"]:
                k = "e_" + o["eng"]
                cnt[k] = cnt.get(k, 0) + 1
                o["tok"] = (k, cnt[k])
            else:
                o["tok"] = None
        seen = {e: {} for e in self.eng}
        nwait = 0
        for o in ops:
            e = o["eng"]
            engine = self.eng[e]
            need = {}
            for d in o["deps"]:
                t = ops[d]
                if t["tok"] is None or self._skip(t, o):
                    continue
                k, v = t["tok"]
                if v > need.get(k, 0):
                    need[k] = v
            for k, v in need.items():
                if seen[e].get(k, 0) >= v:
                    continue
                engine.wait_ge(getsem(k), v)
                seen[e][k] = v
                nwait += 1
            ins = o["fn"](engine)
            if o["tok"] is not None:
                k, v = o["tok"]
                ins.then_inc(getsem(k), 16 if o["dma"] is not None else 1)
        sp = self.eng["sp"]
        for k, v in cnt.items():
            if k.startswith("d_"):
                sp.wait_ge(getsem(k), v)
        return {"n_ops": len(ops), "n_wait": nwait, "n_sems": len(sems)}


SPC = 576


def build(NBLK=8, NLAY=2, dbg=(), final=True):
    nc = bass.Bass("TRN2", target_bir_lowering=False)

    def din(name, shape, dt=F32):
        return nc.dram_tensor(name, shape, dt, kind="ExternalInput").ap()

    x_d = din("x", [SEQ, 1024])
    p_d = din("p", [2, SEQ, 256])
    w_in_d = din("w_in", [2, 1024, 5632])
    gxw_d = din("gate_x_w", [2, 10, 128, 128])
    gaw_d = din("gate_a_w", [2, 10, 128, 128])
    w_ao_d = din("w_a_out", [2, 1280, 1024])
    w_glu_d = din("w_glu", [2, 1024, 2048])
    w_o_d = din("w_o", [2, 1024, 1024])
    w_up_d = din("w_up", [2, 1024, 6144])
    w_dn_d = din("w_down", [2, 3072, 1024])
    w_pg_d = din("w_ple_gate", [2, 1024, 1024])
    w_pp_d = din("w_ple_proj", [2, 256, 1024])
    sp_d = din("spk", [2, 128, SPC])
    s5f_d = din("s5f", [2, 4, 128, 1024])
    cst_d = din("cst", [128, 128 + 8 * 240 + 1 + 1024])
    y_d = nc.dram_tensor("y", [SEQ, 1024], F32, kind="ExternalOutput").ap()
    s5m_d = nc.dram_tensor("s5m", [2, 4, 128, 8192], BF16, kind="Internal").ap()
    dbg_out = {}

    es = ExitStack()
    with es:
        P = Prog(nc, es)

        def sb(name, shape, dt=F32):
            return es.enter_context(nc.sbuf_tensor(name, shape, dt))

        X = sb("X", [128, NT, 1024])
        HT = sb("HT", [128, 8, TB], BF16)
        ST = sb("ST", [128, 16])
        WS = sb("WS", [128, NSLOT, 4096], BF16)
        IDB = sb("IDB", [128, 128], BF16)
        IDF = sb("IDF", [128, 128])
        ZS = sb("ZS", [128, 8, 240], BF16)
        SGN = sb("SGN", [128, 1])
        GF = sb("GF", [128, 1024])
        SPm = sb("SPm", [128, 2, SPC])
        C8 = sb("C8", [128, 2, 20])
        S5C = sb("S5C", [128, 2, 3, 64])
        LRUH = sb("LRUH", [128, 2, 10])
        XAH = sb("XAH", [128, 2, 10, 3])
        S5CAR = sb("S5CAR", [128, 2, 2, 64])
        FT = sb("FT", [128, 2, 48, 2])
        T1 = sb("T1", [128, 2, 64])
        T2 = sb("T2", [128, 2, 64])
        ARENA = 134776
        AR = sb("AR", [128, ARENA // 4])

        PM = es.enter_context(nc.psum_tensor("PM", [128, 4, 512], F32))
        PX = es.enter_context(nc.psum_tensor("PX", [128, 2, 512], F32))
        PTR = es.enter_context(nc.psum_tensor("PTR", [128, 2, 1024], BF16))

        arena_ranges = {}

        def av(name, off, nbytes, dt=F32, **re):
            assert off % 4 == 0 and nbytes % 4 == 0 and off + nbytes <= ARENA, (name, off, nbytes)
            arena_ranges[name] = (off, off + nbytes)
            v = AR[:, off // 4:(off + nbytes) // 4]
            if dt is not F32:
                v = v.bitcast(dt)
            if re:
                pat = re.pop("pat")
                v = v.rearrange(pat, **re)
            return v

        OA = av("OA", 0, 10240, BF16, pat="p (k n) -> p k n", n=TB)
        XA = av("XA", 10240, 20600, F32, pat="p (k n) -> p k n", n=515)
        GBf = av("GB", 30840, 10240, BF16, pat="p (k n) -> p k n", n=TB)
        LT = av("LT", 41080, 22528)
        YSB = av("YSB", 10240, 8192, BF16, pat="p (g c) -> p g c", c=64)
        YS = av("YS", 18432, 8192, BF16, pat="p (k n) -> p k n", n=TB)
        UT = av("UT", 63608, 8192, BF16, pat="p (g l c) -> p g l c", g=32, l=8)
        XB = av("XB", 63608, 8192, BF16, pat="p (g c) -> p g c", c=64)
        U = av("U", 71800, 8192, BF16, pat="p (g c) -> p g c", c=64)
        S2 = av("S2", 79992, 33280, F32, pat="p (h g c) -> p h g c", h=2, g=64)
        SMA = av("SMA", 113272, 8192, BF16, pat="p (k n) -> p k n", n=TB)
        SMB = av("SMB", 121464, 8192, BF16, pat="p (k n) -> p k n", n=TB)
        TMP = av("TMP", 41080, 4096, F32, pat="p (k n) -> p k n", n=TB)
        YT = av("YT", 41080, 4096, BF16, pat="p (s n) -> p s n", s=2)
        HS = av("HS", 45176, 4096, BF16, pat="p (k n) -> p k n", n=1024)
        SQ = av("SQ", 49272, 2048, BF16)
        WG = av("WG", 129656, 5120, BF16, pat="p (w k n) -> p w k n", w=2, k=10)
        YB = av("YB", 79992, 16384, F32, pat="p (k n) -> p k n", n=TB)
        MGB = av("MGB", 96376, 8192, BF16, pat="p (k n) -> p k n", n=TB)
        PTOK = av("PTOK", 104568, 4096, F32, pat="p (t n) -> p t n", n=256)
        PB = av("PB", 108664, 2048, BF16, pat="p (t n) -> p t n", n=256)
        PTT = av("PTT", 110712, 2048, BF16, pat="p (k n) -> p k n", n=TB)
        ACTB = av("ACTB", 0, 24576, BF16, pat="p (k n) -> p k n", n=TB)
        UPF = av("UPF", 24576, 8224, F32, pat="p (k n) -> p k n", n=514)
        CV = av("CV", 32800, 8192, F32, pat="p (k n) -> p k n", n=TB)
        PTMP = av("PTMP", 40992, 4096, F32, pat="p (k n) -> p k n", n=TB)
        EX = av("EX", 0, 15360, F32, pat="p (g n) -> p g n", n=240)
        F1 = av("F1", 15360, 1024, F32, pat="p (g n) -> p g n", n=16)
        F2 = av("F2", 16384, 1024, F32, pat="p (g n) -> p g n", n=16)
        G1 = av("G1", 17408, 1024, F32, pat="p (g n) -> p g n", n=16)
        G2 = av("G2", 18432, 1024, F32, pat="p (g n) -> p g n", n=16)
        BBv = av("BB", 19456, 1024, F32, pat="p (g n) -> p g n", n=16)
        BBS = av("BBS", 20480, 1024, F32, pat="p (g n) -> p g n", n=16)
        Q1 = av("Q1", 21504, 1024, F32, pat="p (g n) -> p g n", n=16)
        Q2 = av("Q2", 22528, 1024, F32, pat="p (g n) -> p g n", n=16)
        CST = av("CST", 23552, 1024, F32, pat="p (g n) -> p g n", n=16)
        WAB = av("WAB", 24576, 4096, BF16, pat="p (g n) -> p g n", n=128)
        WASB = av("WASB", 28672, 4096, BF16, pat="p (g n) -> p g n", n=128)
        WCB = av("WCB", 32768, 4096, BF16, pat="p (g n) -> p g n", n=128)
        TBB = av("TBB", 36864, 4096, BF16, pat="p (g n) -> p g n", n=128)
        SC = av("SC", 40960, 64 * 4 * 40, F32, pat="p (k n) -> p k n", n=64)
        EXB = av("EXB", 51200, 7680, BF16, pat="p (g n) -> p g n", n=240)
        CSTB = av("CSTB", 58880, 512, BF16, pat="p (g n) -> p g n", n=16)
        names = list(arena_ranges)
        for a in names:
            for b in names:
                if a != b:
                    (a0, a1), (b0, b1) = arena_ranges[a], arena_ranges[b]
                    if a0 < b1 and b0 < a1:
                        P.alias.setdefault(a, []).append(b)

        def _fs(ap):
            k = 1
            for d in ap.shape[1:]:
                k *= d
            return k

        def mm(out, lhsT, rhs, start, stop, r, w):
            P.op("pe", lambda e: e.matmul(out, lhsT=lhsT, rhs=rhs, start=start, stop=stop), r=r, w=w)

        def tr(out, in_, ident, r, w):
            P.op("pe", lambda e: e.transpose(out, in_, ident), r=r, w=w)

        def act(out, in_, func, r, w, bias=None, scale=None, accum=None):
            kw = {}
            if bias is not None:
                kw["bias"] = bias
            if scale is not None:
                kw["scale"] = scale
            if accum is not None:
                kw["accum_out"] = accum
            P.op("act", lambda e: e.activation(out=out, in_=in_, func=func, **kw), r=r, w=w,
                 n=0 if accum is not None else _fs(out))

        def ts(eng, out, in0, s1, s2, op0, op1, r, w):
            if s2 is None:
                P.op(eng, lambda e: e.tensor_scalar(out=out, in0=in0, scalar1=s1, scalar2=None, op0=op0), r=r, w=w, n=_fs(out))
            else:
                P.op(eng, lambda e: e.tensor_scalar(out=out, in0=in0, scalar1=s1, scalar2=s2, op0=op0, op1=op1), r=r, w=w, n=_fs(out))

        def tt(eng, out, in0, in1, op, r, w):
            P.op(eng, lambda e: e.tensor_tensor(out=out, in0=in0, in1=in1, op=op), r=r, w=w, n=_fs(out))

        def stt(eng, out, in0, scalar, in1, op0, op1, r, w):
            P.op(eng, lambda e: e.scalar_tensor_tensor(out=out, in0=in0, scalar=scalar, in1=in1, op0=op0, op1=op1), r=r, w=w, n=_fs(out))

        def cp(eng, out, in_, r, w):
            if eng == "act":
                P.op("act", lambda e: e.copy(out=out, in_=in_), r=r, w=w, n=_fs(out))
            else:
                P.op(eng, lambda e: e.tensor_copy(out=out, in_=in_), r=r, w=w, n=_fs(out))

        def mset(eng, out, val, w):
            P.op(eng, lambda e: e.memset(out, val), w=w)

        def dma(eng, out, in_, key, r, w):
            P.op(eng, lambda e: e.dma_start(out=out, in_=in_), r=r, w=w, dma=key)

        def tap(name, ap, shape, r):
            if name in dbg:
                d = nc.dram_tensor("dbg_" + name, shape, ap.dtype, kind="ExternalOutput").ap()
                dbg_out[name] = d
                dma("sp", d, ap, "dbg_" + name, r, ["dbgdram_" + name])

        wctr = [0]

        def wload(src, nk, ncols, cast=True, rk=()):
            s = wctr[0] % NSLOT
            wctr[0] += 1
            v = WS[:, s, 0:nk * ncols].rearrange("p (k n) -> p k n", n=ncols)
            dma("pool" if cast else "sp", v, src, "w%d" % s, list(rk), ["ws/%d" % s])
            return v, "ws/%d" % s

        pmc = [0]

        def pm():
            b = pmc[0] % 4
            pmc[0] += 1
            return PM[:, b, :], "pm/%d" % b

        pxc = [0]

        def px():
            b = pxc[0] % 2
            pxc[0] += 1
            return PX[:, b, :], "px/%d" % b

        ptc = [0]

        def ptr():
            b = ptc[0] % 2
            ptc[0] += 1
            return PTR[:, b, :], "ptr/%d" % b

        dma("sp", IDF[:], cst_d[:, 0:128], "c0", [], ["IDF"])
        dma("pool", IDB[:], cst_d[:, 0:128], "c1", [], ["IDB"])
        dma("pool", ZS[:].rearrange("p a b -> p (a b)"), cst_d[:, 128:128 + 1920], "c2", [], ["ZS"])
        dma("sp", SGN[:], cst_d[:, 2048:2049], "c3", [], ["SGN"])
        dma("sp", GF[:], cst_d[:, 2049:2049 + 1024], "c4", [], ["GF"])
        mset("dve", ST[:, 8:12], -0.5, ["ST/nh"])
        mset("dve", LRUH[:], 0.0, ["LRUH"])
        mset("dve", XAH[:], 0.0, ["XAH"])
        mset("dve", S5CAR[:], 0.0, ["S5CAR"])
        mset("dve", FT[:], 0.0, ["FT"])

        for i in range(NLAY):
            dma("sp", SPm[:, i, :], sp_d[i], "spm%d" % i, [], ["SPm/%d" % i])
            spk = "SPm/%d" % i
            act(C8[:, i, 0:10], SPm[:, i, 102:112], AF.Exp, [spk], ["C8/%d" % i], scale=-1.0)
            act(C8[:, i, 0:10], C8[:, i, 0:10], AF.Ln, ["C8/%d" % i], ["C8/%d" % i], bias=1.0)
            ts("dve", C8[:, i, 10:20], C8[:, i, 0:10], -16.0, None, ALU.mult, None, ["C8/%d" % i], ["C8/%d" % i])
            ts("dve", C8[:, i, 0:10], C8[:, i, 0:10], -8.0, None, ALU.mult, None, ["C8/%d" % i], ["C8/%d" % i])

            a_re = SPm[:, i, 320:384]
            a_im = SPm[:, i, 384:448]
            ldt = SPm[:, i, 448:512]
            DM = SPm[:, i, 512:576]
            k = "SC"
            DT, ADT, TH, MAG, QQ, RR, MSK, SN, CS = (SC[:, j, :] for j in range(9))
            LBR, LBI, NR, DEN, CR, CI, W2, TA, TBt = (SC[:, 9 + j, :] for j in range(9))
            PRn = SC[:, 18:27, :]
            PIn = SC[:, 27:36, :]
            QI = SC[:, 36, :].bitcast(I32)
            act(DT, ldt, AF.Exp, [spk], [k])
            tt("dve", ADT, a_re, DT, ALU.mult, [spk, k], [k])
            tt("dve", TH, a_im, DT, ALU.mult, [spk, k], [k])
            act(MAG, ADT, AF.Exp, [k], [k])

            def sincos(dst, shift):
                ts("dve", RR, TH, shift, None, ALU.add, None, [k], [k])
                ts("dve", QQ, RR, 1.0 / (2 * PI), None, ALU.mult, None, [k], [k])
                cp("dve", QI, QQ, [k], [k])
                cp("dve", QQ, QI, [k], [k])
                stt("dve", RR, QQ, -2 * PI, RR, ALU.mult, ALU.add, [k], [k])
                ts("dve", MSK, RR, PI, -2 * PI, ALU.is_gt, ALU.mult, [k], [k])
                tt("dve", RR, RR, MSK, ALU.add, [k], [k])
                ts("dve", MSK, RR, -PI, 2 * PI, ALU.is_lt, ALU.mult, [k], [k])
                tt("dve", RR, RR, MSK, ALU.add, [k], [k])
                ts("dve", RR, RR, PI, -PI, ALU.min, ALU.max, [k], [k])
                act(dst, RR, AF.Sin, [k], [k])

            sincos(SN, 0.0)
            sincos(CS, PI / 2)
            tt("dve", LBR, MAG, CS, ALU.mult, [k], [k])
            tt("dve", LBI, MAG, SN, ALU.mult, [k], [k])
            mset("dve", PRn[:, 0, :], 1.0, [k])
            mset("dve", PIn[:, 0, :], 0.0, [k])
            for q in range(8):
                tt("dve", TA, PRn[:, q, :], LBR, ALU.mult, [k], [k])
                tt("dve", TBt, PIn[:, q, :], LBI, ALU.mult, [k], [k])
                tt("dve", PRn[:, q + 1, :], TA, TBt, ALU.subtract, [k], [k])
                tt("dve", TA, PRn[:, q, :], LBI, ALU.mult, [k], [k])
                tt("dve", TBt, PIn[:, q, :], LBR, ALU.mult, [k], [k])
                tt("dve", PIn[:, q + 1, :], TA, TBt, ALU.add, [k], [k])
            ts("dve", NR, LBR, -1.0, None, ALU.add, None, [k], [k])
            tt("dve", TA, a_re, a_re, ALU.mult, [spk, k], [k])
            tt("dve", TBt, a_im, a_im, ALU.mult, [spk, k], [k])
            tt("dve", DEN, TA, TBt, ALU.add, [k], [k])
            P.op("dve", lambda e, DEN=DEN: e.reciprocal(out=DEN, in_=DEN), r=[k], w=[k])
            tt("dve", TA, NR, a_re, ALU.mult, [spk, k], [k])
            tt("dve", TBt, LBI, a_im, ALU.mult, [spk, k], [k])
            tt("dve", TA, TA, TBt, ALU.add, [k], [k])
            tt("dve", CR, TA, DEN, ALU.mult, [k], [k])
            tt("dve", TA, LBI, a_re, ALU.mult, [spk, k], [k])
            tt("dve", TBt, NR, a_im, ALU.mult, [spk, k], [k])
            tt("dve", TA, TA, TBt, ALU.subtract, [k], [k])
            tt("dve", CI, TA, DEN, ALU.mult, [k], [k])
            ts("dve", W2, CI, SGN[:, 0:1], None, ALU.mult, None, [k, "SGN"], [k])
            cp("dve", S5C[:, i, 0, :], PRn[:, 8, :], [k], ["S5C/%d" % i])
            ts("dve", S5C[:, i, 1, :], PIn[:, 8, :], SGN[:, 0:1], None, ALU.mult, None, [k, "SGN"], ["S5C/%d" % i])
            ts("dve", S5C[:, i, 2, :], PIn[:, 8, :], SGN[:, 0:1], -1.0, ALU.mult, ALU.mult, [k, "SGN"], ["S5C/%d" % i])

            def bc(v, g0):
                return v[:, g0:g0 + 16].unsqueeze(2).to_broadcast([128, 16, 16])

            for gb in range(4):
                g0 = gb * 16
                for q, (tile_, nm) in enumerate(((F1, "F1"), (F2, "F2"), (G1, "G1"), (G2, "G2"))):
                    dma("sp", tile_, s5f_d[i, q, :, g0 * 16:(g0 + 16) * 16].rearrange("p (g n) -> p g n", n=16),
                        "s5f%d" % q, [], [nm])
                tt("dve", Q1, F1, bc(CR, g0), ALU.mult, ["F1", k], ["Q1"])
                tt("dve", Q2, F2, bc(W2, g0), ALU.mult, ["F2", k], ["Q2"])
                tt("dve", BBv, Q1, Q2, ALU.add, ["Q1", "Q2"], ["BB"])
                tt("dve", Q1, F2, bc(CR, g0), ALU.mult, ["F2", k], ["Q1"])
                tt("dve", Q2, F1, bc(W2, g0), ALU.mult, ["F1", k], ["Q2"])
                tt("dve", BBS, Q1, Q2, ALU.subtract, ["Q1", "Q2"], ["BBS"])
                mset("dve", EX[:, :, 128:240], 0.0, ["EX"])
                for j in range(8):
                    kk = 7 - j
                    tt("dve", Q1, BBv, bc(PRn[:, kk, :], g0), ALU.mult, ["BB", k], ["Q1"])
                    tt("dve", Q2, BBS, bc(PIn[:, kk, :], g0), ALU.mult, ["BBS", k], ["Q2"])
                    stt("dve", EX[:, :, j * 16:(j + 1) * 16], Q2, SGN[:, 0:1], Q1, ALU.mult, ALU.add,
                        ["Q1", "Q2", "SGN"], ["EX"])
                ts("dve", CST, G1, SGN[:, 0:1], -1.0, ALU.mult, ALU.mult, ["G1", "SGN"], ["CST"])
                cp("act", EXB[:], EX[:], ["EX"], ["EXB"])
                cp("dve", CSTB[:], CST[:], ["CST"], ["CSTB"])
                for gq in range(4):
                    bank, bk = pm()
                    for gi in range(4):
                        gl = gq * 4 + gi
                        tr(bank[:, gi * 128:(gi + 1) * 128], EX[:, gl, 0:128], IDF[:], ["EX", "IDF"], [bk])
                    bv = bank.rearrange("p (g n) -> p g n", n=128)
                    cp("act", WAB[:, gq * 4:(gq + 1) * 4, :], bv, [bk], ["WAB"])
                    cp("dve", WASB[:, gq * 4:(gq + 1) * 4, 0:64], bv[:, :, 64:128], [bk], ["WASB"])
                    cp("dve", WASB[:, gq * 4:(gq + 1) * 4, 64:128], bv[:, :, 0:64], [bk], ["WASB"])
                    bank, bk = pm()
                    for gi in range(4):
                        gl = gq * 4 + gi
                        for l in range(8):
                            mm(bank[:, gi * 128 + l * 16: gi * 128 + (l + 1) * 16],
                               EXB[:, gl, (7 - l) * 16:(7 - l) * 16 + 128], CSTB[:, gl, :], True, True,
                               ["EXB", "CSTB"], [bk])
                        stt("dve", TBB[:, gl, :], IDF[:], DM[:, g0 + gl:g0 + gl + 1], bank[:, gi * 128:(gi + 1) * 128],
                            ALU.mult, ALU.add, [bk, "IDF", spk], ["TBB"])
                for l in range(8):
                    tt("dve", Q1, G1, bc(PRn[:, l + 1, :], g0), ALU.mult, ["G1", k], ["Q1"])
                    tt("dve", Q2, G2, bc(PIn[:, l + 1, :], g0), ALU.mult, ["G2", k], ["Q2"])
                    ts("dve", Q1, Q1, SGN[:, 0:1], -1.0, ALU.mult, ALU.mult, ["Q1", "SGN"], ["Q1"])
                    tt("dve", WCB[:, :, l * 16:(l + 1) * 16], Q1, Q2, ALU.subtract, ["Q1", "Q2"], ["WCB"])
                for q, (tile_, nm) in enumerate(((WAB, "WAB"), (WASB, "WASB"), (WCB, "WCB"), (TBB, "TBB"))):
                    dma("sp", s5m_d[i, q, :, g0 * 128:(g0 + 16) * 128].rearrange("p (g n) -> p g n", n=128),
                        tile_, "s5mo%d" % q, [nm], ["s5m%d_%d" % (i, q)])

        def kview(ap):
            return ap.rearrange("(k p) n -> p k n", p=128)

        w_in_v = [kview(w_in_d[i]) for i in range(2)]
        w_ao_v = [kview(w_ao_d[i]) for i in range(2)]
        w_glu_v = [kview(w_glu_d[i]) for i in range(2)]
        w_o_v = [kview(w_o_d[i]) for i in range(2)]
        w_up_v = [kview(w_up_d[i]) for i in range(2)]
        w_dn_v = [kview(w_dn_d[i]) for i in range(2)]
        w_pg_v = [kview(w_pg_d[i]) for i in range(2)]
        w_pp_v = [kview(w_pp_d[i]) for i in range(2)]
        gxw_v = [gxw_d[i].rearrange("h i j -> i h j") for i in range(2)]
        gaw_v = [gaw_d[i].rearrange("h i j -> i h j") for i in range(2)]
        x_v = x_d.rearrange("(b t p) d -> b p t d", t=NT, p=128)
        y_v = y_d.rearrange("(b t p) d -> b p t d", t=NT, p=128)
        p_v = [p_d[i].rearrange("(b t p) d -> b p t d", t=NT, p=128) for i in range(2)]

        def norm(gcols, gkey):
            for t in range(NT):
                act(SQ[:], X[:, t, :], AF.Square, ["X/%d" % t], ["SQ", "ST/ss"], accum=ST[:, t:t + 1])
            ts("dve", ST[:, 4:8], ST[:, 0:4], 1.0 / 1024, EPS, ALU.mult, ALU.add, ["ST/ss"], ["ST/rs"])
            tt("pool", ST[:, 4:8], ST[:, 4:8], ST[:, 8:12], ALU.pow, ["ST/rs", "ST/nh"], ["ST/rs"])
            for t in range(NT):
                s = t % 2
                act(HS[:, s, :], X[:, t, :], AF.Copy, ["X/%d" % t, "ST/rs"], ["HS/%d" % s], scale=ST[:, 4 + t:5 + t])
                bank, bk = ptr()
                for kt in range(8):
                    tr(bank[:, kt * 128:(kt + 1) * 128], HS[:, s, kt * 128:(kt + 1) * 128], IDB[:],
                       ["HS/%d" % s, "IDB"], [bk])
                tt("dve", HT[:, :, t * 128:(t + 1) * 128], bank.rearrange("p (k n) -> p k n", n=128),
                   gcols.unsqueeze(2).to_broadcast([128, 8, 128]), ALU.mult, [bk, gkey], ["HT/%d" % t])

        HTall = ["HT/%d" % t for t in range(NT)]

        def proj_fm(ws, wk, j, nk, rhs_of, rkeys):
            bank, bk = pm()
            for kt in range(nk):
                mm(bank, ws[:, kt, j * 128:(j + 1) * 128], rhs_of(kt), kt == 0, kt == nk - 1, [wk] + rkeys, [bk])
            return bank, bk

        def layer(i, blk):
            spk = "SPm/%d" % i
            norm(SPm[:, i, 0:8], spk)
            tap("ht1_%d_%d" % (i, blk), HT[:], [128, 8, TB], HTall)

            def win_chunk(c):
                ws, wk = wload(w_in_v[i][:, :, c * 512:(c + 1) * 512], 8, 512)
                for j in range(4):
                    t = 4 * c + j
                    bank, bk = proj_fm(ws, wk, j, 8, lambda kt: HT[:, kt, :], HTall)
                    if t < 10:
                        cp("act", XA[:, t, 3:515], bank, [bk], ["XA/%d" % t])
                    elif t < 20:
                        act(GBf[:, t - 10, :], bank, AF.Gelu_apprx_tanh, [bk], ["GB/%d" % (t - 10)])
                    elif t < 28:
                        raise AssertionError("ub goes through the token-chunk-major path")
                    elif t < 36:
                        act(SMA[:, t - 28, :], bank, AF.Sigmoid, [bk], ["SMA/%d" % (t - 28)])
                    else:
                        act(SMB[:, t - 36, :], bank, AF.Sigmoid, [bk], ["SMB/%d" % (t - 36)])

            for c in range(5):
                win_chunk(c)

            def lru_tile(t, wgx, wgxk, wga, wgak):
                q = t % 2
                base = q * 2816
                xc = LT[:, base:base + 512]
                gx = LT[:, base + 512:base + 1024]
                aa = LT[:, base + 1024:base + 1536]
                m2 = LT[:, base + 1536:base + 2048]
                hh = LT[:, base + 2048:base + 2560]
                xcb = LT[:, base + 2560:base + 2816].bitcast(BF16)
                lk = "LT/%d" % q
                xk = "XA/%d" % t
                cp("dve", XA[:, t, 0:3], XAH[:, i, t, :], ["XAH/%d_%d" % (i, t)], [xk])
                cw = lambda j: SPm[:, i, 32 + t * 4 + j:33 + t * 4 + j]
                act(xc, XA[:, t, 0:512], AF.Identity, [xk, spk], [lk], bias=SPm[:, i, 72 + t:73 + t], scale=cw(0))
                for j in range(1, 4):
                    stt("dve", xc, XA[:, t, j:j + 512], cw(j), xc, ALU.mult, ALU.add, [xk, spk, lk], [lk])
                cp("dve", XAH[:, i, t, :], XA[:, t, 512:515], [xk], ["XAH/%d_%d" % (i, t)])
                cp("act", xcb, xc, [lk], [lk])
                bx, bxk = PX[:, q, :], "px/%d" % q
                ba, bak = bx, bxk
                mm(bx, wgx[:, t, :], xcb, True, True, [wgxk, lk], [bxk])
                act(gx, bx, AF.Sigmoid, [bxk, spk], [lk], bias=SPm[:, i, 82 + t:83 + t])
                mm(ba, wga[:, t, :], xcb, True, True, [wgak, lk], [bak])
                act(hh, ba, AF.Sigmoid, [bak, spk], [lk], bias=SPm[:, i, 92 + t:93 + t])
                act(aa, hh, AF.Exp, [lk, "C8/%d" % i], [lk], scale=C8[:, i, t:t + 1])
                act(m2, hh, AF.Exp, [lk, "C8/%d" % i], [lk], scale=C8[:, i, 10 + t:11 + t])
                act(m2, m2, AF.Sqrt, [lk], [lk], bias=1.0, scale=-1.0)
                tt("dve", gx, gx, xc, ALU.mult, [lk], [lk])
                stt("dve", m2, m2, 1e-6, gx, ALU.max, ALU.mult, [lk], [lk])
                hk = "LRUH/%d_%d" % (i, t)
                P.op("dve", lambda e, hh=hh, aa=aa, m2=m2, t=t: e.tensor_tensor_scan(
                    out=hh, data0=aa, data1=m2, initial=LRUH[:, i, t:t + 1], op0=ALU.mult, op1=ALU.add),
                    r=[lk, hk], w=[lk], n=512)
                P.op("dve", lambda e, hh=hh, t=t: e.tensor_copy(out=LRUH[:, i, t:t + 1], in_=hh[:, 511:512]),
                     r=[lk], w=[hk], n=1, fs=True)
                tt("dve", OA[:, t, :], hh, GBf[:, t, :], ALU.mult, [lk, "GB/%d" % t], ["OA/%d" % t])

            L_lru = P.defer()
            dma("pool", WG[:, 0], gxw_v[i], "wg0", [], ["WG/0"])
            dma("pool", WG[:, 1], gaw_v[i], "wg1", [], ["WG/1"])
            wgx, wgxk, wga, wgak = WG[:, 0], "WG/0", WG[:, 1], "WG/1"
            for t in range(0, 10, 2):
                lru_tile(t, wgx, wgxk, wga, wgak)
            P.end_defer()
            L_lru_odd = P.defer()
            for t in range(1, 10, 2):
                lru_tile(t, wgx, wgxk, wga, wgak)
            P.end_defer()

            L_b = P.defer()
            for hh in range(2):
                ws, wk = wload(w_in_v[i][:, :, (5 + hh) * 512:(6 + hh) * 512], 8, 512)
                for l in range(8):
                    bank, bk = pm()
                    for kt in range(8):
                        mm(bank[0:64, :], HT[:, kt, l::8], ws[:, kt, :], kt == 0, kt == 7, [wk] + HTall, [bk])
                    cp("act" if l % 2 else "dve", UT[0:64, :, l, :], bank[0:64, :].rearrange("p (g c) -> p g c", c=16),
                       [bk], ["UT"])
                for gq in range(2):
                    tb, tbk = ptr()
                    tv = tb.rearrange("p (g c) -> p g c", c=64)
                    for gi in range(16):
                        tr(tv[:, gi, :], UT[0:64, gq * 16 + gi, :, :].rearrange("p l c -> p (l c)"), IDB[0:64, 0:64],
                           ["UT", "IDB"], [tbk])
                    g0 = hh * 32 + gq * 16
                    cp("act" if gq else "dve", U[:, g0:g0 + 16, :], tv, [tbk],
                       ["U/%d" % (g0 // 8), "U/%d" % (g0 // 8 + 1)])
            for h in range(2):
                for half in range(2):
                    ws, wk = wload(s5m_d[i, h, :, half * 4096:(half + 1) * 4096].rearrange("p (g n) -> p g n", n=128),
                                   32, 128, cast=False, rk=["s5m%d_%d" % (i, h)])
                    for j4 in range(4):
                        j = half * 4 + j4
                        bank, bk = pm()
                        bv = bank.rearrange("p (g c) -> p g c", c=64)
                        for gl in range(8):
                            mm(bv[:, gl, :], ws[:, j4 * 8 + gl, :], U[:, j * 8 + gl, :], True, True,
                               [wk, "U/%d" % j], [bk])
                        cp("act", S2[:, h, j * 8:(j + 1) * 8, 1:65], bv, [bk], ["S2/%d" % h])
            cp("dve", S2[:, :, :, 0], S5CAR[:, i, :, :], ["S5CAR/%d" % i], ["S2"])
            ArB = S5C[:, i, 0, :].unsqueeze(1).to_broadcast([128, 2, 64])
            sck = "S5C/%d" % i
            for c in range(64):
                if c % 16 == 0:
                    win_chunk(7 + c // 16)
                tt("dve", T1[:], S2[:, :, :, c], ArB, ALU.mult, ["S2", sck], ["T1"])
                tt("dve", T2[:, 0, :], S2[:, 1, :, c], S5C[:, i, 1, :], ALU.mult, ["S2", sck], ["T2"])
                tt("dve", T2[:, 1, :], S2[:, 0, :, c], S5C[:, i, 2, :], ALU.mult, ["S2", sck], ["T2"])
                tt("dve", T1[:], T1[:], T2[:], ALU.add, ["T1", "T2"], ["T1"])
                tt("dve", S2[:, :, :, c + 1], S2[:, :, :, c + 1], T1[:], ALU.add, ["S2", "T1"], ["S2"])
            cp("dve", S5CAR[:, i, :, :], S2[:, :, :, 64], ["S2"], ["S5CAR/%d" % i])
            cp("act", XB[:], S2[:, 0, :, 0:64], ["S2"], ["XB"])
            P.end_defer()
            P.interleave([L_lru, L_lru_odd, L_b])
            tap("oa_%d_%d" % (i, blk), OA[:], [128, 10, TB], ["OA"])
            tap("xb_%d_%d" % (i, blk), XB[:], [128, 64, 64], ["XB"])
            dma("sp", PTOK[:], p_v[i][blk], "ptok", [], ["PTOK"])
            for half in range(2):
                wt_, wtk = wload(s5m_d[i, 3, :, half * 4096:(half + 1) * 4096].rearrange("p (g n) -> p g n", n=128),
                                 32, 128, cast=False, rk=["s5m%d_3" % i])
                wc_, wck = wload(s5m_d[i, 2, :, half * 4096:(half + 1) * 4096].rearrange("p (g n) -> p g n", n=128),
                                 32, 128, cast=False, rk=["s5m%d_2" % i])
                for j4 in range(4):
                    j = half * 4 + j4
                    bank, bk = pm()
                    bv = bank.rearrange("p (g c) -> p g c", c=64)
                    for gl in range(8):
                        g = j * 8 + gl
                        mm(bv[:, gl, :], wt_[:, j4 * 8 + gl, :], U[:, g, :], True, False, [wtk, "U/%d" % j], [bk])
                        mm(bv[:, gl, :], wc_[:, j4 * 8 + gl, :], XB[:, g, :], False, True, [wck, "XB"], [bk])
                    cp("act", YSB[:, j * 8:(j + 1) * 8, :], bv, [bk], ["YSB/%d" % j])
            for j in range(8):
                tb, tbk = ptr()
                tv = tb[0:64, :].rearrange("p (g n) -> p g n", n=128)
                for gl in range(8):
                    tr(tv[:, gl, :], YSB[:, j * 8 + gl, :], IDB[:], ["YSB/%d" % j, "IDB"], [tbk])
                sl = j % 2
                cp("dve" if j % 2 else "act", YT[0:64, sl, :].rearrange("p (l g c) -> p g l c", l=8, g=8),
                   tb[0:64, :].rearrange("p (g l c) -> p g l c", g=8, l=8), [tbk], ["YT/%d" % sl])
                tb2, tb2k = ptr()
                for l in range(8):
                    tr(tb2[:, l * 64:(l + 1) * 64], YT[0:64, sl, l * 128:(l + 1) * 128], IDB[0:64, 0:64],
                       ["YT/%d" % sl, "IDB"], [tb2k])
                act(YS[:, j, :].rearrange("p (c l) -> p c l", l=8), tb2[:, 0:512].rearrange("p (l c) -> p c l", l=8),
                    AF.Gelu_apprx_tanh, [tb2k], ["YS/%d" % j])
            tap("ys_%d_%d" % (i, blk), YS[:], [128, 8, TB], ["YS"])
            YSall = ["YS/%d" % j for j in range(8)]
            for c in (0, 2, 1, 3):
                ws, wk = wload(w_glu_v[i][:, :, c * 512:(c + 1) * 512], 8, 512)
                for j in range(4):
                    t = 4 * c + j
                    bank, bk = proj_fm(ws, wk, j, 8, lambda kt: YS[:, kt, :], YSall)
                    bcol = SPm[:, i, 112 + t:113 + t]
                    if t < 8:
                        act(YB[:, t, :], bank, AF.Identity, [bk, spk], ["YB/%d" % t], bias=bcol)
                    else:
                        s = t % 2
                        n = t - 8
                        act(TMP[:, s, :], bank, AF.Sigmoid, [bk, spk], ["TMP/%d" % s], bias=bcol)
                        tt("dve", TMP[:, s, :], TMP[:, s, :], SMB[:, n, :], ALU.mult,
                           ["TMP/%d" % s, "SMB/%d" % n], ["TMP/%d" % s])
                        tt("dve", YB[:, n, :], YB[:, n, :], TMP[:, s, :], ALU.mult,
                           ["YB/%d" % n, "TMP/%d" % s], ["YB/%d" % n])
            OAall = ["OA/%d" % t for t in range(10)]
            for c in range(4):
                ws, wk = wload(w_ao_v[i][:, :, c * 256:(c + 1) * 256], 10, 256)
                for j in range(2):
                    n = 2 * c + j
                    bank, bk = proj_fm(ws, wk, j, 10, lambda kt: OA[:, kt, :], OAall)
                    s = n % 2
                    tt("dve", TMP[:, s, :], bank, SMA[:, n, :], ALU.mult, [bk, "SMA/%d" % n], ["TMP/%d" % s])
                    tt("dve", MGB[:, n, :], TMP[:, s, :], YB[:, n, :], ALU.add, ["TMP/%d" % s, "YB/%d" % n],
                       ["MGB/%d" % n])
            tap("mg_%d_%d" % (i, blk), MGB[:], [128, 8, TB], ["MGB"])
            MGall = ["MGB/%d" % n for n in range(8)]
            for h in range(2):
                ws, wk = wload(w_o_v[i][:, :, h * 512:(h + 1) * 512], 8, 512)
                for t in range(NT):
                    bank, bk = pm()
                    for kt in range(8):
                        mm(bank, MGB[:, kt, t * 128:(t + 1) * 128], ws[:, kt, :], kt == 0, kt == 7, [wk] + MGall, [bk])
                    tt("dve", X[:, t, h * 512:(h + 1) * 512], bank, X[:, t, h * 512:(h + 1) * 512], ALU.add,
                       [bk, "X/%d" % t], ["X/%d" % t])
            tap("x1_%d_%d" % (i, blk), X[:], [128, NT, 1024], ["X"])

            norm(SPm[:, i, 8:16], spk)

            def ffn_tile(ws, wk, c, j, half):
                t = 4 * c + j
                o = t % 24
                q = (2 * half + (j % 2))
                bank, bk = proj_fm(ws, wk, j, 8, lambda kt: HT[:, kt, :], HTall)
                ck = "CV/%d" % q
                ftk = "FT/%d_%d" % (i, t)
                fw_ = lambda jj: SPm[:, i, 128 + t * 3 + jj:129 + t * 3 + jj]
                bcol = SPm[:, i, 272 + t:273 + t]
                act(CV[:, q, 2:512], bank[:, 0:510], AF.Identity, [bk, spk], [ck], bias=bcol, scale=fw_(0))
                ts("dve", CV[:, q, 0:2], FT[:, i, t, :], fw_(0), bcol, ALU.mult, ALU.add, [ftk, spk], [ck])
                stt("dve", CV[:, q, 1:512], bank[:, 0:511], fw_(1), CV[:, q, 1:512], ALU.mult, ALU.add,
                    [bk, spk, ck], [ck])
                stt("dve", CV[:, q, 0:1], FT[:, i, t, 1:2], fw_(1), CV[:, q, 0:1], ALU.mult, ALU.add,
                    [ftk, spk, ck], [ck])
                stt("dve", CV[:, q, :], bank, fw_(2), CV[:, q, :], ALU.mult, ALU.add, [bk, spk, ck], [ck])
                cp("dve", FT[:, i, t, :], bank[:, 510:512], [bk], [ftk])
                if half == 0:
                    act(ACTB[:, o, :], CV[:, q, :], AF.Gelu_apprx_tanh, [ck], ["ACTB/%d" % o])
                else:
                    tt("dve", ACTB[:, o, :], ACTB[:, o, :], CV[:, q, :], ALU.mult, ["ACTB/%d" % o, ck],
                       ["ACTB/%d" % o])

            for c6 in range(6):
                for half in range(2):
                    c = c6 + 6 * half
                    ws, wk = wload(w_up_v[i][:, :, c * 512:(c + 1) * 512], 8, 512)
                    for jp in range(2):
                        lists = []
                        for j in (2 * jp, 2 * jp + 1):
                            lists.append(P.defer())
                            ffn_tile(ws, wk, c, j, half)
                            P.end_defer()
                        P.interleave(lists)
            tap("actb_%d_%d" % (i, blk), ACTB[:], [128, 24, TB], ["ACTB"])
            ACall = ["ACTB/%d" % o for o in range(24)]
            for h in range(2):
                banks = [pm() for _ in range(NT)]
                for kc in range(3):
                    ws, wk = wload(w_dn_v[i][:, kc * 8:(kc + 1) * 8, h * 512:(h + 1) * 512], 8, 512)
                    for t in range(NT):
                        bank, bk = banks[t]
                        for k8 in range(8):
                            mm(bank, ACTB[:, kc * 8 + k8, t * 128:(t + 1) * 128], ws[:, k8, :],
                               kc == 0 and k8 == 0, kc == 2 and k8 == 7, [wk] + ACall, [bk])
                for t in range(NT):
                    bank, bk = banks[t]
                    tt("dve", X[:, t, h * 512:(h + 1) * 512], bank, X[:, t, h * 512:(h + 1) * 512], ALU.add,
                       [bk, "X/%d" % t], ["X/%d" % t])
            tap("x2_%d_%d" % (i, blk), X[:], [128, NT, 1024], ["X"])

            norm(SPm[:, i, 16:24], spk)
            cp("dve", PB[:], PTOK[:], ["PTOK"], ["PB"])
            bank, bk = ptr()
            bv = bank.rearrange("p (k n) -> p k n", n=TB)
            for t in range(NT):
                for kk in range(2):
                    tr(bv[:, kk, t * 128:(t + 1) * 128], PB[:, t, kk * 128:(kk + 1) * 128], IDB[:], ["PB", "IDB"], [bk])
            cp("act", PTT[:], bv, [bk], ["PTT"])
            for h in range(2):
                wg, wgk = wload(w_pg_v[i][:, :, h * 512:(h + 1) * 512], 8, 512)
                wp, wpk = wload(w_pp_v[i][:, :, h * 512:(h + 1) * 512], 2, 512)
                for t in range(NT):
                    bg, bgk = pm()
                    for kt in range(8):
                        mm(bg, HT[:, kt, t * 128:(t + 1) * 128], wg[:, kt, :], kt == 0, kt == 7, [wgk, "HT/%d" % t], [bgk])
                    bp, bpk = px()
                    for kk in range(2):
                        mm(bp, PTT[:, kk, t * 128:(t + 1) * 128], wp[:, kk, :], kk == 0, kk == 1, [wpk, "PTT"], [bpk])
                    s = t % 2
                    act(TMP[:, s, :], bg, AF.Sigmoid, [bgk], ["TMP/%d" % s])
                    tt("dve", TMP[:, s, :], TMP[:, s, :], bp, ALU.mult, ["TMP/%d" % s, bpk], ["TMP/%d" % s])
                    tt("dve", X[:, t, h * 512:(h + 1) * 512], TMP[:, s, :], X[:, t, h * 512:(h + 1) * 512], ALU.add,
                       ["TMP/%d" % s, "X/%d" % t], ["X/%d" % t])
            tap("x3_%d_%d" % (i, blk), X[:], [128, NT, 1024], ["X"])

        for t in range(NT):
            dma("sp", X[:, t, :], x_v[0][:, t, :], "xin%d" % t, [], ["X/%d" % t])
        for blk in range(NBLK):
            for i in range(NLAY):
                layer(i, blk)
            if final:
                for t in range(NT):
                    act(SQ[:], X[:, t, :], AF.Square, ["X/%d" % t], ["SQ", "ST/ss"], accum=ST[:, t:t + 1])
                ts("dve", ST[:, 4:8], ST[:, 0:4], 1.0 / 1024, EPS, ALU.mult, ALU.add, ["ST/ss"], ["ST/rs"])
                tt("pool", ST[:, 4:8], ST[:, 4:8], ST[:, 8:12], ALU.pow, ["ST/rs", "ST/nh"], ["ST/rs"])
            for t in range(NT):
                if final:
                    stt("dve", X[:, t, :], X[:, t, :], ST[:, 4 + t:5 + t], GF[:], ALU.mult, ALU.mult,
                        ["X/%d" % t, "ST/rs", "GF"], ["X/%d" % t])
                dma("sp", y_v[blk][:, t, :], X[:, t, :], "yout%d" % t, ["X/%d" % t], ["ydram/%d_%d" % (blk, t)])
                if blk + 1 < NBLK:
                    dma("sp", X[:, t, :], x_v[blk + 1][:, t, :], "xin%d" % t, [], ["X/%d" % t])

        with nc.allow_non_contiguous_dma(reason="small strided parameter loads"):
            info = P.emit()
    return nc, dbg_out, info


def _cols(v):
    return np.ascontiguousarray(v.reshape(-1, 128).T)


def host_layout(inp):
    f = np.float32
    spk = np.zeros((2, 128, SPC), f)
    s5f = np.zeros((2, 4, 128, 1024), f)
    for i in range(2):
        s = spk[i]
        s[:, 0:8] = _cols(inp["g_mix"][i])
        s[:, 8:16] = _cols(inp["g_ffn"][i])
        s[:, 16:24] = _cols(inp["g_ple"][i])
        s[:, 24:32] = _cols(inp["g_final"])
        caw = inp["conv_a_w"][i]
        for j in range(4):
            s[:, 32 + j:72:4] = _cols(caw[j])
        s[:, 72:82] = _cols(inp["conv_a_b"][i])
        s[:, 82:92] = _cols(inp["gate_x_b"][i])
        s[:, 92:102] = _cols(inp["gate_a_b"][i])
        s[:, 102:112] = _cols(inp["lru_lambda"][i])
        s[:, 112:128] = _cols(inp["b_glu"][i])
        cfw = inp["conv_f_w"][i]
        for j in range(3):
            s[:, 128 + j:272:3] = _cols(cfw[j])
        s[:, 272:320] = _cols(inp["conv_f_b"][i])
        are_t = inp["s5_a_re"][i].T
        aim_t = inp["s5_a_im"][i].T
        s[:, 320:384] = np.concatenate([are_t, are_t], 0)
        s[:, 384:448] = np.concatenate([aim_t, aim_t], 0)
        s[:, 448:512] = np.broadcast_to(inp["s5_log_dt"][i][None, :], (128, 64))
        dm = inp["s5_d"][i].reshape(64, 16).T
        s[:, 512:576] = np.tile(dm, (8, 1))
        bre = inp["s5_b_re"][i].transpose(1, 0, 2).reshape(64, 1024)
        bim = inp["s5_b_im"][i].transpose(1, 0, 2).reshape(64, 1024)
        cre = inp["s5_c_re"][i].transpose(2, 0, 1).reshape(64, 1024)
        cim = inp["s5_c_im"][i].transpose(2, 0, 1).reshape(64, 1024)
        s5f[i, 0] = np.concatenate([bre, bim], 0)
        s5f[i, 1] = np.concatenate([bim, bre], 0)
        s5f[i, 2] = np.concatenate([cre, cim], 0)
        s5f[i, 3] = np.concatenate([cim, cre], 0)
    cst = np.zeros((128, 128 + 1920 + 1 + 1024), f)
    cst[:, 0:128] = np.eye(128, dtype=f)
    z = np.zeros((128, 8, 240), f)
    for gl in range(8):
        for ch in range(16):
            z[gl * 16 + ch, gl, 112 + ch] = 1.0
    cst[:, 128:2048] = z.reshape(128, 1920)
    cst[0:64, 2048] = -1.0
    cst[64:128, 2048] = 1.0
    cst[:, 2049:] = np.broadcast_to(inp["g_final"][None, :], (128, 1024))
    shared = {"spk": spk, "s5f": s5f, "cst": cst}
    for k in ("w_in", "gate_x_w", "gate_a_w", "w_a_out", "w_glu", "w_o", "w_up", "w_down",
              "w_ple_gate", "w_ple_proj"):
        shared[k] = np.ascontiguousarray(inp[k], dtype=f)
    return shared


_CACHE = {}


def kernel(**inputs):
    inp = {k: np.asarray(v) for k, v in inputs.items()}
    shared = host_layout(inp)
    if "nc" not in _CACHE:
        _CACHE["nc"] = build()[0]
    nc = _CACHE["nc"]
    in_maps = []
    for b in range(8):
        m = dict(shared)
        m["x"] = np.ascontiguousarray(inp["x"][b], dtype=np.float32)
        m["p"] = np.ascontiguousarray(inp["p"][:, b], dtype=np.float32)
        in_maps.append(m)
    res = run_bass_kernel_spmd(nc, in_maps, core_ids=list(range(8)))
    return np.stack([np.asarray(r["y"], dtype=np.float32) for r in res.results], 0)
```

```python
import math
from contextlib import ExitStack

import numpy as np
import concourse.bass as bass
import concourse.mybir as mybir
from concourse.bass_utils import run_bass_kernel_spmd

F32 = mybir.dt.float32
BF16 = mybir.dt.bfloat16
I32 = mybir.dt.int32
AF = mybir.ActivationFunctionType
ALU = mybir.AluOpType

SAME_ENGINE_SYNC = ("act", "pool", "dve", "sp")
NSLOT = 4
BIG_N = 128
TB = 512
NT = 4
SEQ = 4096
EPS = 1e-6
PI = math.pi


class Prog:
    def __init__(self, nc, es):
        self.nc = nc
        self.es = es
        self.eng = {"pe": nc.tensor, "act": nc.scalar, "dve": nc.vector,
                    "pool": nc.gpsimd, "sp": nc.sync}
        self.ops = []
        self.state = {}
        self.alias = {}
        self._defer = None

    @staticmethod
    def _split(key):
        if "/" in key:
            r, s = key.split("/", 1)
            return r, s
        return key, None

    def _st(self, root):
        return self.state.setdefault(root, {"w": None, "r": [], "subs": {}})

    def _deps_read(self, key, me):
        root, sub = self._split(key)
        st = self._st(root)
        deps = []
        if st["w"] is not None:
            deps.append(st["w"])
        if sub is None:
            for s in st["subs"].values():
                if s["w"] is not None:
                    deps.append(s["w"])
            st["r"].append(me)
        else:
            s = st["subs"].setdefault(sub, {"w": None, "r": []})
            if s["w"] is not None:
                deps.append(s["w"])
            s["r"].append(me)
        return deps

    def _deps_write_whole(self, root, me):
        st = self._st(root)
        deps = []
        if st["w"] is not None:
            deps.append(st["w"])
        deps.extend(st["r"])
        for s in st["subs"].values():
            if s["w"] is not None:
                deps.append(s["w"])
            deps.extend(s["r"])
        st["subs"] = {}
        st["r"] = []
        st["w"] = me
        return deps

    def _deps_write(self, key, me):
        root, sub = self._split(key)
        deps = []
        for other in self.alias.get(root, ()):
            deps.extend(self._deps_write_whole(other, me))
        if sub is None:
            deps.extend(self._deps_write_whole(root, me))
            return deps
        st = self._st(root)
        if st["w"] is not None:
            deps.append(st["w"])
        deps.extend(st["r"])
        s = st["subs"].setdefault(sub, {"w": None, "r": []})
        if s["w"] is not None:
            deps.append(s["w"])
        deps.extend(s["r"])
        s["w"] = me
        s["r"] = []
        return deps

    def defer(self):
        self._defer = []
        return self._defer

    def end_defer(self):
        self._defer = None

    def interleave(self, lists):
        assert self._defer is None
        pos = [0] * len(lists)
        tot = sum(len(l) for l in lists)
        for _ in range(tot):
            best, bf = None, None
            for i, l in enumerate(lists):
                if pos[i] < len(l):
                    f = (pos[i] + 0.5) / len(l)
                    if bf is None or f < bf:
                        best, bf = i, f
            a = lists[best][pos[best]]
            pos[best] += 1
            self.op(*a[0], **a[1])

    def op(self, eng, fn, r=(), w=(), dma=None, n=0, fs=False):
        if self._defer is not None:
            self._defer.append(((eng, fn), {"r": list(r), "w": list(w), "dma": dma, "n": n, "fs": fs}))
            return -1
        idx = len(self.ops)
        deps = set()
        for k in r:
            deps.update(self._deps_read(k, idx))
        for k in w:
            deps.update(self._deps_write(k, idx))
        deps.discard(idx)
        self.ops.append({"eng": eng, "fn": fn, "deps": deps, "dma": dma, "sig": False, "n": n, "fs": fs})
        return idx

    def _skip(self, t, o):
        if t["dma"] is not None or o["dma"] is not None or t["eng"] != o["eng"]:
            return False
        if o["eng"] not in SAME_ENGINE_SYNC:
            return True
        return t["n"] >= BIG_N and not o["fs"]

    def emit(self):
        nc = self.nc
        ops = self.ops
        for o in ops:
            for d in o["deps"]:
                t = ops[d]
                if t["dma"] is None and not self._skip(t, o):
                    t["sig"] = True
        cnt = {}
        sems = {}

        def getsem(name):
            if name not in sems:
                sems[name] = self.es.enter_context(nc.semaphore("s_" + name))
            return sems[name]

        for o in ops:
            if o["dma"] is not None:
                k = "d_" + o["dma"]
                cnt[k] = cnt.get(k, 0) + 16
                o["tok"] = (k, cnt[k])
            elif o["sig"]:
                k = "e_" + o["eng"]
                cnt[k] = cnt.get(k, 0) + 1
                o["tok"] = (k, cnt[k])
            else:
                o["tok"] = None
        seen = {e: {} for e in self.eng}
        nwait = 0
        for o in ops:
            e = o["eng"]
            engine = self.eng[e]
            need = {}
            for d in o["deps"]:
                t = ops[d]
                if t["tok"] is None or self._skip(t, o):
                    continue
                k, v = t["tok"]
                if v > need.get(k, 0):
                    need[k] = v
            for k, v in need.items():
                if seen[e].get(k, 0) >= v:
                    continue
                engine.wait_ge(getsem(k), v)
                seen[e][k] = v
                nwait += 1
            ins = o["fn"](engine)
            if o["tok"] is not None:
                k, v = o["tok"]
                ins.then_inc(getsem(k), 16 if o["dma"] is not None else 1)
        sp = self.eng["sp"]
        for k, v in cnt.items():
            if k.startswith("d_"):
                sp.wait_ge(getsem(k), v)
        return {"n_ops": len(ops), "n_wait": nwait, "n_sems": len(sems)}


SPC = 576


def build(NBLK=8, NLAY=2, dbg=(), final=True):
    nc = bass.Bass("TRN2", target_bir_lowering=False)

    def din(name, shape, dt=F32):
        return nc.dram_tensor(name, shape, dt, kind="ExternalInput").ap()

    x_d = din("x", [SEQ, 1024])
    p_d = din("p", [2, SEQ, 256])
    w_in_d = din("w_in", [2, 1024, 5632])
    gxw_d = din("gate_x_w", [2, 10, 128, 128])
    gaw_d = din("gate_a_w", [2, 10, 128, 128])
    w_ao_d = din("w_a_out", [2, 1280, 1024])
    w_glu_d = din("w_glu", [2, 1024, 2048])
    w_o_d = din("w_o", [2, 1024, 1024])
    w_up_d = din("w_up", [2, 1024, 6144])
    w_dn_d = din("w_down", [2, 3072, 1024])
    w_pg_d = din("w_ple_gate", [2, 1024, 1024])
    w_pp_d = din("w_ple_proj", [2, 256, 1024])
    sp_d = din("spk", [2, 128, SPC])
    s5f_d = din("s5f", [2, 4, 128, 1024])
    cst_d = din("cst", [128, 128 + 8 * 240 + 1 + 1024])
    y_d = nc.dram_tensor("y", [SEQ, 1024], F32, kind="ExternalOutput").ap()
    s5m_d = nc.dram_tensor("s5m", [2, 4, 128, 8192], BF16, kind="Internal").ap()
    dbg_out = {}

    es = ExitStack()
    with es:
        P = Prog(nc, es)

        def sb(name, shape, dt=F32):
            return es.enter_context(nc.sbuf_tensor(name, shape, dt))

        X = sb("X", [128, NT, 1024])
        HT = sb("HT", [128, 8, TB], BF16)
        ST = sb("ST", [128, 16])
        WS = sb("WS", [128, NSLOT, 4096], BF16)
        IDB = sb("IDB", [128, 128], BF16)
        IDF = sb("IDF", [128, 128])
        ZS = sb("ZS", [128, 8, 240], BF16)
        SGN = sb("SGN", [128, 1])
        GF = sb("GF", [128, 1024])
        SPm = sb("SPm", [128, 2, SPC])
        C8 = sb("C8", [128, 2, 20])
        S5C = sb("S5C", [128, 2, 3, 64])
        LRUH = sb("LRUH", [128, 2, 10])
        XAH = sb("XAH", [128, 2, 10, 3])
        S5CAR = sb("S5CAR", [128, 2, 2, 64])
        FT = sb("FT", [128, 2, 48, 2])
        HCt = sb("HCt", [128, 144])
        T1 = sb("T1", [128, 2, 64])
        T2 = sb("T2", [128, 2, 64])
        ARENA = 134776
        AR = sb("AR", [128, ARENA // 4])

        PM = es.enter_context(nc.psum_tensor("PM", [128, 4, 512], F32))
        PX = es.enter_context(nc.psum_tensor("PX", [128, 2, 512], F32))
        PTR = es.enter_context(nc.psum_tensor("PTR", [128, 2, 1024], BF16))

        arena_ranges = {}

        def av(name, off, nbytes, dt=F32, **re):
            assert off % 4 == 0 and nbytes % 4 == 0 and off + nbytes <= ARENA, (name, off, nbytes)
            arena_ranges[name] = (off, off + nbytes)
            v = AR[:, off // 4:(off + nbytes) // 4]
            if dt is not F32:
                v = v.bitcast(dt)
            if re:
                pat = re.pop("pat")
                v = v.rearrange(pat, **re)
            return v

        OA = av("OA", 0, 10240, BF16, pat="p (k n) -> p k n", n=TB)
        XA = av("XA", 10240, 20600, F32, pat="p (k n) -> p k n", n=515)
        GBf = av("GB", 30840, 10240, BF16, pat="p (k n) -> p k n", n=TB)
        LT = av("LT", 41080, 22528)
        YSB = av("YSB", 10240, 8192, BF16, pat="p (g c) -> p g c", c=64)
        YS = av("YS", 18432, 8192, BF16, pat="p (k n) -> p k n", n=TB)
        UT = av("UT", 63608, 8192, BF16, pat="p (g l c) -> p g l c", g=32, l=8)
        XB = av("XB", 63608, 8192, BF16, pat="p (g c) -> p g c", c=64)
        U = av("U", 71800, 8192, BF16, pat="p (g c) -> p g c", c=64)
        S2 = av("S2", 79992, 33280, F32, pat="p (c h g) -> p c h g", c=65, h=2)
        SMA = av("SMA", 113272, 8192, BF16, pat="p (k n) -> p k n", n=TB)
        SMB = av("SMB", 121464, 8192, BF16, pat="p (k n) -> p k n", n=TB)
        TMP = av("TMP", 41080, 4096, F32, pat="p (k n) -> p k n", n=TB)
        YT = av("YT", 41080, 4096, BF16, pat="p (s n) -> p s n", s=2)
        HS = av("HS", 45176, 4096, BF16, pat="p (k n) -> p k n", n=1024)
        SQ = av("SQ", 49272, 2048, BF16)
        WG = av("WG", 129656, 5120, BF16, pat="p (w k n) -> p w k n", w=2, k=10)
        YB = av("YB", 79992, 16384, F32, pat="p (k n) -> p k n", n=TB)
        MGB = av("MGB", 96376, 8192, BF16, pat="p (k n) -> p k n", n=TB)
        PTOK = av("PTOK", 104568, 4096, F32, pat="p (t n) -> p t n", n=256)
        PB = av("PB", 108664, 2048, BF16, pat="p (t n) -> p t n", n=256)
        PTT = av("PTT", 110712, 2048, BF16, pat="p (k n) -> p k n", n=TB)
        ACTB = av("ACTB", 0, 24576, BF16, pat="p (k n) -> p k n", n=TB)
        UPF = av("UPF", 24576, 8224, F32, pat="p (k n) -> p k n", n=514)
        CV = av("CV", 32800, 8192, F32, pat="p (k n) -> p k n", n=TB)
        PTMP = av("PTMP", 40992, 4096, F32, pat="p (k n) -> p k n", n=TB)
        EX = av("EX", 0, 15360, F32, pat="p (g n) -> p g n", n=240)
        F1 = av("F1", 15360, 1024, F32, pat="p (g n) -> p g n", n=16)
        F2 = av("F2", 16384, 1024, F32, pat="p (g n) -> p g n", n=16)
        G1 = av("G1", 17408, 1024, F32, pat="p (g n) -> p g n", n=16)
        G2 = av("G2", 18432, 1024, F32, pat="p (g n) -> p g n", n=16)
        BBv = av("BB", 19456, 1024, F32, pat="p (g n) -> p g n", n=16)
        BBS = av("BBS", 20480, 1024, F32, pat="p (g n) -> p g n", n=16)
        Q1 = av("Q1", 21504, 1024, F32, pat="p (g n) -> p g n", n=16)
        Q2 = av("Q2", 22528, 1024, F32, pat="p (g n) -> p g n", n=16)
        CST = av("CST", 23552, 1024, F32, pat="p (g n) -> p g n", n=16)
        WAB = av("WAB", 24576, 4096, BF16, pat="p (g n) -> p g n", n=128)
        WASB = av("WASB", 28672, 4096, BF16, pat="p (g n) -> p g n", n=128)
        WCB = av("WCB", 32768, 4096, BF16, pat="p (g n) -> p g n", n=128)
        TBB = av("TBB", 36864, 4096, BF16, pat="p (g n) -> p g n", n=128)
        SC = av("SC", 40960, 64 * 4 * 40, F32, pat="p (k n) -> p k n", n=64)
        EXB = av("EXB", 51200, 7680, BF16, pat="p (g n) -> p g n", n=240)
        CSTB = av("CSTB", 58880, 512, BF16, pat="p (g n) -> p g n", n=16)
        names = list(arena_ranges)
        for a in names:
            for b in names:
                if a != b:
                    (a0, a1), (b0, b1) = arena_ranges[a], arena_ranges[b]
                    if a0 < b1 and b0 < a1:
                        P.alias.setdefault(a, []).append(b)

        def _fs(ap):
            k = 1
            for d in ap.shape[1:]:
                k *= d
            return k

        def mm(out, lhsT, rhs, start, stop, r, w):
            P.op("pe", lambda e: e.matmul(out, lhsT=lhsT, rhs=rhs, start=start, stop=stop), r=r, w=w)

        def tr(out, in_, ident, r, w):
            P.op("pe", lambda e: e.transpose(out, in_, ident), r=r, w=w)

        def act(out, in_, func, r, w, bias=None, scale=None, accum=None):
            kw = {}
            if bias is not None:
                kw["bias"] = bias
            if scale is not None:
                kw["scale"] = scale
            if accum is not None:
                kw["accum_out"] = accum
            P.op("act", lambda e: e.activation(out=out, in_=in_, func=func, **kw), r=r, w=w,
                 n=0 if accum is not None else _fs(out))

        def ts(eng, out, in0, s1, s2, op0, op1, r, w):
            if s2 is None:
                P.op(eng, lambda e: e.tensor_scalar(out=out, in0=in0, scalar1=s1, scalar2=None, op0=op0), r=r, w=w, n=_fs(out))
            else:
                P.op(eng, lambda e: e.tensor_scalar(out=out, in0=in0, scalar1=s1, scalar2=s2, op0=op0, op1=op1), r=r, w=w, n=_fs(out))

        def tt(eng, out, in0, in1, op, r, w):
            P.op(eng, lambda e: e.tensor_tensor(out=out, in0=in0, in1=in1, op=op), r=r, w=w, n=_fs(out))

        def stt(eng, out, in0, scalar, in1, op0, op1, r, w):
            P.op(eng, lambda e: e.scalar_tensor_tensor(out=out, in0=in0, scalar=scalar, in1=in1, op0=op0, op1=op1), r=r, w=w, n=_fs(out))

        def cp(eng, out, in_, r, w):
            if eng == "act":
                P.op("act", lambda e: e.copy(out=out, in_=in_), r=r, w=w, n=_fs(out))
            else:
                P.op(eng, lambda e: e.tensor_copy(out=out, in_=in_), r=r, w=w, n=_fs(out))

        def mset(eng, out, val, w):
            P.op(eng, lambda e: e.memset(out, val), w=w)

        def dma(eng, out, in_, key, r, w):
            P.op(eng, lambda e: e.dma_start(out=out, in_=in_), r=r, w=w, dma=key)

        def tap(name, ap, shape, r):
            if name in dbg:
                d = nc.dram_tensor("dbg_" + name, shape, ap.dtype, kind="ExternalOutput").ap()
                dbg_out[name] = d
                dma("sp", d, ap, "dbg_" + name, r, ["dbgdram_" + name])

        wctr = [0]

        def wload(src, nk, ncols, cast=True, rk=()):
            s = wctr[0] % NSLOT
            wctr[0] += 1
            v = WS[:, s, 0:nk * ncols].rearrange("p (k n) -> p k n", n=ncols)
            dma("pool" if cast else "sp", v, src, "w%d" % s, list(rk), ["ws/%d" % s])
            return v, "ws/%d" % s

        pmc = [0]

        def pm():
            b = pmc[0] % 4
            pmc[0] += 1
            return PM[:, b, :], "pm/%d" % b

        pxc = [0]

        def px():
            b = pxc[0] % 2
            pxc[0] += 1
            return PX[:, b, :], "px/%d" % b

        ptc = [0]

        def ptr():
            b = ptc[0] % 2
            ptc[0] += 1
            return PTR[:, b, :], "ptr/%d" % b

        dma("sp", IDF[:], cst_d[:, 0:128], "c0", [], ["IDF"])
        dma("pool", IDB[:], cst_d[:, 0:128], "c1", [], ["IDB"])
        dma("pool", ZS[:].rearrange("p a b -> p (a b)"), cst_d[:, 128:128 + 1920], "c2", [], ["ZS"])
        dma("sp", SGN[:], cst_d[:, 2048:2049], "c3", [], ["SGN"])
        dma("sp", GF[:], cst_d[:, 2049:2049 + 1024], "c4", [], ["GF"])
        mset("dve", ST[:, 8:12], -0.5, ["ST/nh"])
        mset("dve", LRUH[:], 0.0, ["LRUH"])
        mset("dve", XAH[:], 0.0, ["XAH"])
        mset("dve", S5CAR[:], 0.0, ["S5CAR"])
        mset("dve", FT[:], 0.0, ["FT"])

        for i in range(NLAY):
            dma("sp", SPm[:, i, :], sp_d[i], "spm%d" % i, [], ["SPm/%d" % i])
            spk = "SPm/%d" % i
            act(C8[:, i, 0:10], SPm[:, i, 102:112], AF.Exp, [spk], ["C8/%d" % i], scale=-1.0)
            act(C8[:, i, 0:10], C8[:, i, 0:10], AF.Ln, ["C8/%d" % i], ["C8/%d" % i], bias=1.0)
            ts("dve", C8[:, i, 10:20], C8[:, i, 0:10], -16.0, None, ALU.mult, None, ["C8/%d" % i], ["C8/%d" % i])
            ts("dve", C8[:, i, 0:10], C8[:, i, 0:10], -8.0, None, ALU.mult, None, ["C8/%d" % i], ["C8/%d" % i])

            a_re = SPm[:, i, 320:384]
            a_im = SPm[:, i, 384:448]
            ldt = SPm[:, i, 448:512]
            DM = SPm[:, i, 512:576]
            k = "SC"
            DT, ADT, TH, MAG, QQ, RR, MSK, SN, CS = (SC[:, j, :] for j in range(9))
            LBR, LBI, NR, DEN, CR, CI, W2, TA, TBt = (SC[:, 9 + j, :] for j in range(9))
            PRn = SC[:, 18:27, :]
            PIn = SC[:, 27:36, :]
            QI = SC[:, 36, :].bitcast(I32)
            act(DT, ldt, AF.Exp, [spk], [k])
            tt("dve", ADT, a_re, DT, ALU.mult, [spk, k], [k])
            tt("dve", TH, a_im, DT, ALU.mult, [spk, k], [k])
            act(MAG, ADT, AF.Exp, [k], [k])

            def sincos(dst, shift):
                ts("dve", RR, TH, shift, None, ALU.add, None, [k], [k])
                ts("dve", QQ, RR, 1.0 / (2 * PI), None, ALU.mult, None, [k], [k])
                cp("dve", QI, QQ, [k], [k])
                cp("dve", QQ, QI, [k], [k])
                stt("dve", RR, QQ, -2 * PI, RR, ALU.mult, ALU.add, [k], [k])
                ts("dve", MSK, RR, PI, -2 * PI, ALU.is_gt, ALU.mult, [k], [k])
                tt("dve", RR, RR, MSK, ALU.add, [k], [k])
                ts("dve", MSK, RR, -PI, 2 * PI, ALU.is_lt, ALU.mult, [k], [k])
                tt("dve", RR, RR, MSK, ALU.add, [k], [k])
                ts("dve", RR, RR, PI, -PI, ALU.min, ALU.max, [k], [k])
                act(dst, RR, AF.Sin, [k], [k])

            sincos(SN, 0.0)
            sincos(CS, PI / 2)
            tt("dve", LBR, MAG, CS, ALU.mult, [k], [k])
            tt("dve", LBI, MAG, SN, ALU.mult, [k], [k])
            mset("dve", PRn[:, 0, :], 1.0, [k])
            mset("dve", PIn[:, 0, :], 0.0, [k])
            for q in range(8):
                tt("dve", TA, PRn[:, q, :], LBR, ALU.mult, [k], [k])
                tt("dve", TBt, PIn[:, q, :], LBI, ALU.mult, [k], [k])
                tt("dve", PRn[:, q + 1, :], TA, TBt, ALU.subtract, [k], [k])
                tt("dve", TA, PRn[:, q, :], LBI, ALU.mult, [k], [k])
                tt("dve", TBt, PIn[:, q, :], LBR, ALU.mult, [k], [k])
                tt("dve", PIn[:, q + 1, :], TA, TBt, ALU.add, [k], [k])
            ts("dve", NR, LBR, -1.0, None, ALU.add, None, [k], [k])
            tt("dve", TA, a_re, a_re, ALU.mult, [spk, k], [k])
            tt("dve", TBt, a_im, a_im, ALU.mult, [spk, k], [k])
            tt("dve", DEN, TA, TBt, ALU.add, [k], [k])
            P.op("dve", lambda e, DEN=DEN: e.reciprocal(out=DEN, in_=DEN), r=[k], w=[k])
            tt("dve", TA, NR, a_re, ALU.mult, [spk, k], [k])
            tt("dve", TBt, LBI, a_im, ALU.mult, [spk, k], [k])
            tt("dve", TA, TA, TBt, ALU.add, [k], [k])
            tt("dve", CR, TA, DEN, ALU.mult, [k], [k])
            tt("dve", TA, LBI, a_re, ALU.mult, [spk, k], [k])
            tt("dve", TBt, NR, a_im, ALU.mult, [spk, k], [k])
            tt("dve", TA, TA, TBt, ALU.subtract, [k], [k])
            tt("dve", CI, TA, DEN, ALU.mult, [k], [k])
            ts("dve", W2, CI, SGN[:, 0:1], None, ALU.mult, None, [k, "SGN"], [k])
            cp("dve", S5C[:, i, 0, :], PRn[:, 8, :], [k], ["S5C/%d" % i])
            ts("dve", S5C[:, i, 1, :], PIn[:, 8, :], SGN[:, 0:1], None, ALU.mult, None, [k, "SGN"], ["S5C/%d" % i])
            ts("dve", S5C[:, i, 2, :], PIn[:, 8, :], SGN[:, 0:1], -1.0, ALU.mult, ALU.mult, [k, "SGN"], ["S5C/%d" % i])

            def bc(v, g0):
                return v[:, g0:g0 + 16].unsqueeze(2).to_broadcast([128, 16, 16])

            for gb in range(4):
                g0 = gb * 16
                for q, (tile_, nm) in enumerate(((F1, "F1"), (F2, "F2"), (G1, "G1"), (G2, "G2"))):
                    dma("sp", tile_, s5f_d[i, q, :, g0 * 16:(g0 + 16) * 16].rearrange("p (g n) -> p g n", n=16),
                        "s5f%d" % q, [], [nm])
                tt("dve", Q1, F1, bc(CR, g0), ALU.mult, ["F1", k], ["Q1"])
                tt("dve", Q2, F2, bc(W2, g0), ALU.mult, ["F2", k], ["Q2"])
                tt("dve", BBv, Q1, Q2, ALU.add, ["Q1", "Q2"], ["BB"])
                tt("dve", Q1, F2, bc(CR, g0), ALU.mult, ["F2", k], ["Q1"])
                tt("dve", Q2, F1, bc(W2, g0), ALU.mult, ["F1", k], ["Q2"])
                tt("dve", BBS, Q1, Q2, ALU.subtract, ["Q1", "Q2"], ["BBS"])
                mset("dve", EX[:, :, 128:240], 0.0, ["EX"])
                for j in range(8):
                    kk = 7 - j
                    tt("dve", Q1, BBv, bc(PRn[:, kk, :], g0), ALU.mult, ["BB", k], ["Q1"])
                    tt("dve", Q2, BBS, bc(PIn[:, kk, :], g0), ALU.mult, ["BBS", k], ["Q2"])
                    stt("dve", EX[:, :, j * 16:(j + 1) * 16], Q2, SGN[:, 0:1], Q1, ALU.mult, ALU.add,
                        ["Q1", "Q2", "SGN"], ["EX"])
                ts("dve", CST, G1, SGN[:, 0:1], -1.0, ALU.mult, ALU.mult, ["G1", "SGN"], ["CST"])
                cp("act", EXB[:], EX[:], ["EX"], ["EXB"])
                cp("dve", CSTB[:], CST[:], ["CST"], ["CSTB"])
                for gq in range(4):
                    bank, bk = pm()
                    for gi in range(4):
                        gl = gq * 4 + gi
                        tr(bank[:, gi * 128:(gi + 1) * 128], EX[:, gl, 0:128], IDF[:], ["EX", "IDF"], [bk])
                    bv = bank.rearrange("p (g n) -> p g n", n=128)
                    cp("act", WAB[:, gq * 4:(gq + 1) * 4, :], bv, [bk], ["WAB"])
                    cp("dve", WASB[:, gq * 4:(gq + 1) * 4, 0:64], bv[:, :, 64:128], [bk], ["WASB"])
                    cp("dve", WASB[:, gq * 4:(gq + 1) * 4, 64:128], bv[:, :, 0:64], [bk], ["WASB"])
                    bank, bk = pm()
                    for gi in range(4):
                        gl = gq * 4 + gi
                        for l in range(8):
                            mm(bank[:, gi * 128 + l * 16: gi * 128 + (l + 1) * 16],
                               EXB[:, gl, (7 - l) * 16:(7 - l) * 16 + 128], CSTB[:, gl, :], True, True,
                               ["EXB", "CSTB"], [bk])
                        stt("dve", TBB[:, gl, :], IDF[:], DM[:, g0 + gl:g0 + gl + 1], bank[:, gi * 128:(gi + 1) * 128],
                            ALU.mult, ALU.add, [bk, "IDF", spk], ["TBB"])
                for l in range(8):
                    tt("dve", Q1, G1, bc(PRn[:, l + 1, :], g0), ALU.mult, ["G1", k], ["Q1"])
                    tt("dve", Q2, G2, bc(PIn[:, l + 1, :], g0), ALU.mult, ["G2", k], ["Q2"])
                    ts("dve", Q1, Q1, SGN[:, 0:1], -1.0, ALU.mult, ALU.mult, ["Q1", "SGN"], ["Q1"])
                    tt("dve", WCB[:, :, l * 16:(l + 1) * 16], Q1, Q2, ALU.subtract, ["Q1", "Q2"], ["WCB"])
                for q, (tile_, nm) in enumerate(((WAB, "WAB"), (WASB, "WASB"), (WCB, "WCB"), (TBB, "TBB"))):
                    dma("sp", s5m_d[i, q, :, g0 * 128:(g0 + 16) * 128].rearrange("p (g n) -> p g n", n=128),
                        tile_, "s5mo%d" % q, [nm], ["s5m%d_%d" % (i, q)])

        def kview(ap):
            return ap.rearrange("(k p) n -> p k n", p=128)

        w_in_v = [kview(w_in_d[i]) for i in range(2)]
        w_ao_v = [kview(w_ao_d[i]) for i in range(2)]
        w_glu_v = [kview(w_glu_d[i]) for i in range(2)]
        w_o_v = [kview(w_o_d[i]) for i in range(2)]
        w_up_v = [kview(w_up_d[i]) for i in range(2)]
        w_dn_v = [kview(w_dn_d[i]) for i in range(2)]
        w_pg_v = [kview(w_pg_d[i]) for i in range(2)]
        w_pp_v = [kview(w_pp_d[i]) for i in range(2)]
        gxw_v = [gxw_d[i].rearrange("h i j -> i h j") for i in range(2)]
        gaw_v = [gaw_d[i].rearrange("h i j -> i h j") for i in range(2)]
        x_v = x_d.rearrange("(b t p) d -> b p t d", t=NT, p=128)
        y_v = y_d.rearrange("(b t p) d -> b p t d", t=NT, p=128)
        p_v = [p_d[i].rearrange("(b t p) d -> b p t d", t=NT, p=128) for i in range(2)]

        def norm(gcols, gkey):
            for t in range(NT):
                act(SQ[:], X[:, t, :], AF.Square, ["X/%d" % t], ["SQ", "ST/ss"], accum=ST[:, t:t + 1])
            ts("dve", ST[:, 4:8], ST[:, 0:4], 1.0 / 1024, EPS, ALU.mult, ALU.add, ["ST/ss"], ["ST/rs"])
            tt("pool", ST[:, 4:8], ST[:, 4:8], ST[:, 8:12], ALU.pow, ["ST/rs", "ST/nh"], ["ST/rs"])
            for t in range(NT):
                s = t % 2
                act(HS[:, s, :], X[:, t, :], AF.Copy, ["X/%d" % t, "ST/rs"], ["HS/%d" % s], scale=ST[:, 4 + t:5 + t])
                bank, bk = ptr()
                for kt in range(8):
                    tr(bank[:, kt * 128:(kt + 1) * 128], HS[:, s, kt * 128:(kt + 1) * 128], IDB[:],
                       ["HS/%d" % s, "IDB"], [bk])
                tt("dve", HT[:, :, t * 128:(t + 1) * 128], bank.rearrange("p (k n) -> p k n", n=128),
                   gcols.unsqueeze(2).to_broadcast([128, 8, 128]), ALU.mult, [bk, gkey], ["HT/%d" % t])

        HTall = ["HT/%d" % t for t in range(NT)]

        f6c = [0]

        def pm6():
            b = f6c[0] % 6
            f6c[0] += 1
            if b < 4:
                return PM[:, b, :], "pm/%d" % b
            return PX[:, b - 4, :], "px/%d" % (b - 4)

        def proj_fm(ws, wk, j, nk, rhs_of, rkeys, bankfn=None):
            bank, bk = (bankfn or pm)()
            for kt in range(nk):
                mm(bank, ws[:, kt, j * 128:(j + 1) * 128], rhs_of(kt), kt == 0, kt == nk - 1, [wk] + rkeys, [bk])
            return bank, bk

        def layer(i, blk):
            spk = "SPm/%d" % i
            norm(SPm[:, i, 0:8], spk)
            tap("ht1_%d_%d" % (i, blk), HT[:], [128, 8, TB], HTall)

            def win_chunk(c):
                ws, wk = wload(w_in_v[i][:, :, c * 512:(c + 1) * 512], 8, 512)
                for j in range(4):
                    t = 4 * c + j
                    bank, bk = proj_fm(ws, wk, j, 8, lambda kt: HT[:, kt, :], HTall)
                    if t < 10:
                        cp("act", XA[:, t, 3:515], bank, [bk], ["XA/%d" % t])
                    elif t < 20:
                        act(GBf[:, t - 10, :], bank, AF.Gelu_apprx_tanh, [bk], ["GB/%d" % (t - 10)])
                    elif t < 28:
                        raise AssertionError("ub goes through the token-chunk-major path")
                    elif t < 36:
                        act(SMA[:, t - 28, :], bank, AF.Sigmoid, [bk], ["SMA/%d" % (t - 28)])
                    else:
                        act(SMB[:, t - 36, :], bank, AF.Sigmoid, [bk], ["SMB/%d" % (t - 36)])

            for c in range(5):
                win_chunk(c)

            def lru_tile(t, wgx, wgxk, wga, wgak):
                q = t % 2
                base = q * 2816
                xc = LT[:, base:base + 512]
                gx = LT[:, base + 512:base + 1024]
                aa = LT[:, base + 1024:base + 1536]
                m2 = LT[:, base + 1536:base + 2048]
                hh = LT[:, base + 2048:base + 2560]
                xcb = LT[:, base + 2560:base + 2816].bitcast(BF16)
                lk = "LT/%d" % q
                xk = "XA/%d" % t
                cp("dve", XA[:, t, 0:3], XAH[:, i, t, :], ["XAH/%d_%d" % (i, t)], [xk])
                cw = lambda j: SPm[:, i, 32 + t * 4 + j:33 + t * 4 + j]
                act(xc, XA[:, t, 0:512], AF.Identity, [xk, spk], [lk], bias=SPm[:, i, 72 + t:73 + t], scale=cw(0))
                for j in range(1, 4):
                    stt("dve", xc, XA[:, t, j:j + 512], cw(j), xc, ALU.mult, ALU.add, [xk, spk, lk], [lk])
                cp("dve", XAH[:, i, t, :], XA[:, t, 512:515], [xk], ["XAH/%d_%d" % (i, t)])
                cp("act", xcb, xc, [lk], [lk])
                bx, bxk = PX[:, q, :], "px/%d" % q
                ba, bak = bx, bxk
                mm(bx, wgx[:, t, :], xcb, True, True, [wgxk, lk], [bxk])
                act(gx, bx, AF.Sigmoid, [bxk, spk], [lk], bias=SPm[:, i, 82 + t:83 + t])
                mm(ba, wga[:, t, :], xcb, True, True, [wgak, lk], [bak])
                act(hh, ba, AF.Sigmoid, [bak, spk], [lk], bias=SPm[:, i, 92 + t:93 + t])
                act(aa, hh, AF.Exp, [lk, "C8/%d" % i], [lk], scale=C8[:, i, t:t + 1])
                act(m2, hh, AF.Exp, [lk, "C8/%d" % i], [lk], scale=C8[:, i, 10 + t:11 + t])
                act(m2, m2, AF.Sqrt, [lk], [lk], bias=1.0, scale=-1.0)
                tt("dve", gx, gx, xc, ALU.mult, [lk], [lk])
                stt("dve", m2, m2, 1e-6, gx, ALU.max, ALU.mult, [lk], [lk])
                hk = "LRUH/%d_%d" % (i, t)
                P.op("dve", lambda e, hh=hh, aa=aa, m2=m2, t=t: e.tensor_tensor_scan(
                    out=hh, data0=aa, data1=m2, initial=LRUH[:, i, t:t + 1], op0=ALU.mult, op1=ALU.add),
                    r=[lk, hk], w=[lk], n=512)
                P.op("dve", lambda e, hh=hh, t=t: e.tensor_copy(out=LRUH[:, i, t:t + 1], in_=hh[:, 511:512]),
                     r=[lk], w=[hk], n=1, fs=True)
                tt("dve", OA[:, t, :], hh, GBf[:, t, :], ALU.mult, [lk, "GB/%d" % t], ["OA/%d" % t])

            L_lru = P.defer()
            dma("pool", WG[:, 0], gxw_v[i], "wg0", [], ["WG/0"])
            dma("pool", WG[:, 1], gaw_v[i], "wg1", [], ["WG/1"])
            wgx, wgxk, wga, wgak = WG[:, 0], "WG/0", WG[:, 1], "WG/1"
            for t in range(0, 10, 2):
                lru_tile(t, wgx, wgxk, wga, wgak)
            P.end_defer()
            L_lru_odd = P.defer()
            for t in range(1, 10, 2):
                lru_tile(t, wgx, wgxk, wga, wgak)
            P.end_defer()

            L_b = P.defer()
            for hh in range(2):
                ws, wk = wload(w_in_v[i][:, :, (5 + hh) * 512:(6 + hh) * 512], 8, 512)
                for l in range(8):
                    bank, bk = pm()
                    for kt in range(8):
                        mm(bank[0:64, :], HT[:, kt, l::8], ws[:, kt, :], kt == 0, kt == 7, [wk] + HTall, [bk])
                    cp("act" if l % 2 else "dve", UT[0:64, :, l, :], bank[0:64, :].rearrange("p (g c) -> p g c", c=16),
                       [bk], ["UT"])
                for gq in range(2):
                    tb, tbk = ptr()
                    tv = tb.rearrange("p (g c) -> p g c", c=64)
                    for gi in range(16):
                        tr(tv[:, gi, :], UT[0:64, gq * 16 + gi, :, :].rearrange("p l c -> p (l c)"), IDB[0:64, 0:64],
                           ["UT", "IDB"], [tbk])
                    g0 = hh * 32 + gq * 16
                    cp("act" if gq else "dve", U[:, g0:g0 + 16, :], tv, [tbk],
                       ["U/%d" % (g0 // 8), "U/%d" % (g0 // 8 + 1)])
            for h in range(2):
                for half in range(2):
                    ws, wk = wload(s5m_d[i, h, :, half * 4096:(half + 1) * 4096].rearrange("p (g n) -> p g n", n=128),
                                   32, 128, cast=False, rk=["s5m%d_%d" % (i, h)])
                    for j4 in range(4):
                        j = half * 4 + j4
                        bank, bk = pm()
                        bv = bank.rearrange("p (g c) -> p g c", c=64)
                        for gl in range(8):
                            mm(bv[:, gl, :], ws[:, j4 * 8 + gl, :], U[:, j * 8 + gl, :], True, True,
                               [wk, "U/%d" % j], [bk])
                        cp("act", S2[:, 1:65, h, j * 8:(j + 1) * 8].rearrange("p c g -> p g c"), bv, [bk],
                           ["S2/%d" % h])
            cp("dve", S2[:, 0, :, :], S5CAR[:, i, :, :], ["S5CAR/%d" % i], ["S2"])
            ArB = S5C[:, i, 0, :].unsqueeze(1).to_broadcast([128, 2, 64])
            sck = "S5C/%d" % i
            for c in range(64):
                if c % 16 == 0:
                    win_chunk(7 + c // 16)
                tt("dve", T1[:], S2[:, c, :, :], ArB, ALU.mult, ["S2", sck], ["T1"])
                tt("dve", T2[:, 0, :], S2[:, c, 1, :], S5C[:, i, 1, :], ALU.mult, ["S2", sck], ["T2"])
                tt("dve", T2[:, 1, :], S2[:, c, 0, :], S5C[:, i, 2, :], ALU.mult, ["S2", sck], ["T2"])
                tt("dve", T1[:], T1[:], T2[:], ALU.add, ["T1", "T2"], ["T1"])
                tt("dve", S2[:, c + 1, :, :], S2[:, c + 1, :, :], T1[:], ALU.add, ["S2", "T1"], ["S2"])
            cp("dve", S5CAR[:, i, :, :], S2[:, 64, :, :], ["S2"], ["S5CAR/%d" % i])
            cp("act", XB[:], S2[:, 0:64, 0, :].rearrange("p c g -> p g c"), ["S2"], ["XB"])
            P.end_defer()
            P.interleave([L_lru, L_lru_odd, L_b])
            tap("oa_%d_%d" % (i, blk), OA[:], [128, 10, TB], ["OA"])
            tap("xb_%d_%d" % (i, blk), XB[:], [128, 64, 64], ["XB"])
            dma("sp", PTOK[:], p_v[i][blk], "ptok", [], ["PTOK"])
            for half in range(2):
                wt_, wtk = wload(s5m_d[i, 3, :, half * 4096:(half + 1) * 4096].rearrange("p (g n) -> p g n", n=128),
                                 32, 128, cast=False, rk=["s5m%d_3" % i])
                wc_, wck = wload(s5m_d[i, 2, :, half * 4096:(half + 1) * 4096].rearrange("p (g n) -> p g n", n=128),
                                 32, 128, cast=False, rk=["s5m%d_2" % i])
                for j4 in range(4):
                    j = half * 4 + j4
                    bank, bk = pm()
                    bv = bank.rearrange("p (g c) -> p g c", c=64)
                    for gl in range(8):
                        g = j * 8 + gl
                        mm(bv[:, gl, :], wt_[:, j4 * 8 + gl, :], U[:, g, :], True, False, [wtk, "U/%d" % j], [bk])
                        mm(bv[:, gl, :], wc_[:, j4 * 8 + gl, :], XB[:, g, :], False, True, [wck, "XB"], [bk])
                    cp("act", YSB[:, j * 8:(j + 1) * 8, :], bv, [bk], ["YSB/%d" % j])
            for j in range(8):
                tb, tbk = ptr()
                tv = tb[0:64, :].rearrange("p (g n) -> p g n", n=128)
                for gl in range(8):
                    tr(tv[:, gl, :], YSB[:, j * 8 + gl, :], IDB[:], ["YSB/%d" % j, "IDB"], [tbk])
                sl = j % 2
                cp("dve" if j % 2 else "act", YT[0:64, sl, :].rearrange("p (l g c) -> p g l c", l=8, g=8),
                   tb[0:64, :].rearrange("p (g l c) -> p g l c", g=8, l=8), [tbk], ["YT/%d" % sl])
                tb2, tb2k = ptr()
                for l in range(8):
                    tr(tb2[:, l * 64:(l + 1) * 64], YT[0:64, sl, l * 128:(l + 1) * 128], IDB[0:64, 0:64],
                       ["YT/%d" % sl, "IDB"], [tb2k])
                act(YS[:, j, :].rearrange("p (c l) -> p c l", l=8), tb2[:, 0:512].rearrange("p (l c) -> p c l", l=8),
                    AF.Gelu_apprx_tanh, [tb2k], ["YS/%d" % j])
            tap("ys_%d_%d" % (i, blk), YS[:], [128, 8, TB], ["YS"])
            YSall = ["YS/%d" % j for j in range(8)]
            for c in (0, 2, 1, 3):
                ws, wk = wload(w_glu_v[i][:, :, c * 512:(c + 1) * 512], 8, 512)
                for j in range(4):
                    t = 4 * c + j
                    bank, bk = proj_fm(ws, wk, j, 8, lambda kt: YS[:, kt, :], YSall)
                    bcol = SPm[:, i, 112 + t:113 + t]
                    if t < 8:
                        act(YB[:, t, :], bank, AF.Identity, [bk, spk], ["YB/%d" % t], bias=bcol)
                    else:
                        s = t % 2
                        n = t - 8
                        act(TMP[:, s, :], bank, AF.Sigmoid, [bk, spk], ["TMP/%d" % s], bias=bcol)
                        tt("dve", TMP[:, s, :], TMP[:, s, :], SMB[:, n, :], ALU.mult,
                           ["TMP/%d" % s, "SMB/%d" % n], ["TMP/%d" % s])
                        tt("dve", YB[:, n, :], YB[:, n, :], TMP[:, s, :], ALU.mult,
                           ["YB/%d" % n, "TMP/%d" % s], ["YB/%d" % n])
            OAall = ["OA/%d" % t for t in range(10)]
            for c in range(4):
                ws, wk = wload(w_ao_v[i][:, :, c * 256:(c + 1) * 256], 10, 256)
                for j in range(2):
                    n = 2 * c + j
                    bank, bk = proj_fm(ws, wk, j, 10, lambda kt: OA[:, kt, :], OAall)
                    s = n % 2
                    tt("dve", TMP[:, s, :], bank, SMA[:, n, :], ALU.mult, [bk, "SMA/%d" % n], ["TMP/%d" % s])
                    tt("dve", MGB[:, n, :], TMP[:, s, :], YB[:, n, :], ALU.add, ["TMP/%d" % s, "YB/%d" % n],
                       ["MGB/%d" % n])
            tap("mg_%d_%d" % (i, blk), MGB[:], [128, 8, TB], ["MGB"])
            MGall = ["MGB/%d" % n for n in range(8)]
            for h in range(2):
                ws, wk = wload(w_o_v[i][:, :, h * 512:(h + 1) * 512], 8, 512)
                for t in range(NT):
                    bank, bk = pm()
                    for kt in range(8):
                        mm(bank, MGB[:, kt, t * 128:(t + 1) * 128], ws[:, kt, :], kt == 0, kt == 7, [wk] + MGall, [bk])
                    tt("dve", X[:, t, h * 512:(h + 1) * 512], bank, X[:, t, h * 512:(h + 1) * 512], ALU.add,
                       [bk, "X/%d" % t], ["X/%d" % t])
            tap("x1_%d_%d" % (i, blk), X[:], [128, NT, 1024], ["X"])

            norm(SPm[:, i, 8:16], spk)

            def ffn_tile(ws, wk, c, j, half):
                t = 4 * c + j
                o = t % 24
                q = (2 * half + (j % 2))
                bank, bk = proj_fm(ws, wk, j, 8, lambda kt: HT[:, kt, :], HTall)
                ck = "CV/%d" % q
                ftk = "FT/%d_%d" % (i, t)
                fw_ = lambda jj: SPm[:, i, 128 + t * 3 + jj:129 + t * 3 + jj]
                bcol = SPm[:, i, 272 + t:273 + t]
                act(CV[:, q, 2:512], bank[:, 0:510], AF.Identity, [bk, spk], [ck], bias=bcol, scale=fw_(0))
                cp("act", CV[:, q, 0:2], HC[:, t, :], ["HC"], [ck])
                stt("dve", CV[:, q, 1:512], bank[:, 0:511], fw_(1), CV[:, q, 1:512], ALU.mult, ALU.add,
                    [bk, spk, ck], [ck])
                stt("dve", CV[:, q, :], bank, fw_(2), CV[:, q, :], ALU.mult, ALU.add, [bk, spk, ck], [ck])
                cp("act", FT[:, i, t, :], bank[:, 510:512], [bk], [ftk])
                if half == 0:
                    act(ACTB[:, o, :], CV[:, q, :], AF.Gelu_apprx_tanh, [ck], ["ACTB/%d" % o])
                else:
                    tt("dve", ACTB[:, o, :], ACTB[:, o, :], CV[:, q, :], ALU.mult, ["ACTB/%d" % o, ck],
                       ["ACTB/%d" % o])

            HC = HCt[:, 0:96].rearrange("p (t k) -> p t k", k=2)
            HTM = HCt[:, 96:144]
            FTall = ["FT/%d_%d" % (i, t) for t in range(48)]
            W0 = SPm[:, i, 128:272:3]
            W1 = SPm[:, i, 129:272:3]
            BC = SPm[:, i, 272:320]
            tt("dve", HC[:, :, 0], FT[:, i, :, 0], W0, ALU.mult, FTall + [spk], ["HC"])
            tt("dve", HTM, FT[:, i, :, 1], W1, ALU.mult, FTall + [spk], ["HTM"])
            tt("dve", HC[:, :, 0], HC[:, :, 0], HTM, ALU.add, ["HC", "HTM"], ["HC"])
            tt("dve", HC[:, :, 0], HC[:, :, 0], BC, ALU.add, ["HC", spk], ["HC"])
            tt("dve", HC[:, :, 1], FT[:, i, :, 1], W0, ALU.mult, FTall + [spk], ["HC"])
            tt("dve", HC[:, :, 1], HC[:, :, 1], BC, ALU.add, ["HC", spk], ["HC"])
            for c6 in range(6):
                for half in range(2):
                    c = c6 + 6 * half
                    ws, wk = wload(w_up_v[i][:, :, c * 512:(c + 1) * 512], 8, 512)
                    for jp in range(2):
                        lists = []
                        for j in (2 * jp, 2 * jp + 1):
                            lists.append(P.defer())
                            ffn_tile(ws, wk, c, j, half)
                            P.end_defer()
                        P.interleave(lists)
            tap("actb_%d_%d" % (i, blk), ACTB[:], [128, 24, TB], ["ACTB"])
            ACall = ["ACTB/%d" % o for o in range(24)]
            for h in range(2):
                banks = [pm() for _ in range(NT)]
                for kc in range(3):
                    ws, wk = wload(w_dn_v[i][:, kc * 8:(kc + 1) * 8, h * 512:(h + 1) * 512], 8, 512)
                    for t in range(NT):
                        bank, bk = banks[t]
                        for k8 in range(8):
                            mm(bank, ACTB[:, kc * 8 + k8, t * 128:(t + 1) * 128], ws[:, k8, :],
                               kc == 0 and k8 == 0, kc == 2 and k8 == 7, [wk] + ACall, [bk])
                for t in range(NT):
                    bank, bk = banks[t]
                    tt("dve", X[:, t, h * 512:(h + 1) * 512], bank, X[:, t, h * 512:(h + 1) * 512], ALU.add,
                       [bk, "X/%d" % t], ["X/%d" % t])
            tap("x2_%d_%d" % (i, blk), X[:], [128, NT, 1024], ["X"])

            norm(SPm[:, i, 16:24], spk)
            cp("dve", PB[:], PTOK[:], ["PTOK"], ["PB"])
            bank, bk = ptr()
            bv = bank.rearrange("p (k n) -> p k n", n=TB)
            for t in range(NT):
                for kk in range(2):
                    tr(bv[:, kk, t * 128:(t + 1) * 128], PB[:, t, kk * 128:(kk + 1) * 128], IDB[:], ["PB", "IDB"], [bk])
            cp("act", PTT[:], bv, [bk], ["PTT"])
            for h in range(2):
                wg, wgk = wload(w_pg_v[i][:, :, h * 512:(h + 1) * 512], 8, 512)
                wp, wpk = wload(w_pp_v[i][:, :, h * 512:(h + 1) * 512], 2, 512)
                for t in range(NT):
                    bg, bgk = pm()
                    for kt in range(8):
                        mm(bg, HT[:, kt, t * 128:(t + 1) * 128], wg[:, kt, :], kt == 0, kt == 7, [wgk, "HT/%d" % t], [bgk])
                    bp, bpk = px()
                    for kk in range(2):
                        mm(bp, PTT[:, kk, t * 128:(t + 1) * 128], wp[:, kk, :], kk == 0, kk == 1, [wpk, "PTT"], [bpk])
                    s = t % 2
                    act(TMP[:, s, :], bg, AF.Sigmoid, [bgk], ["TMP/%d" % s])
                    tt("dve", TMP[:, s, :], TMP[:, s, :], bp, ALU.mult, ["TMP/%d" % s, bpk], ["TMP/%d" % s])
                    tt("dve", X[:, t, h * 512:(h + 1) * 512], TMP[:, s, :], X[:, t, h * 512:(h + 1) * 512], ALU.add,
                       ["TMP/%d" % s, "X/%d" % t], ["X/%d" % t])
            tap("x3_%d_%d" % (i, blk), X[:], [128, NT, 1024], ["X"])

        for t in range(NT):
            dma("sp", X[:, t, :], x_v[0][:, t, :], "xin%d" % t, [], ["X/%d" % t])
        for blk in range(NBLK):
            for i in range(NLAY):
                layer(i, blk)
            if final:
                for t in range(NT):
                    act(SQ[:], X[:, t, :], AF.Square, ["X/%d" % t], ["SQ", "ST/ss"], accum=ST[:, t:t + 1])
                ts("dve", ST[:, 4:8], ST[:, 0:4], 1.0 / 1024, EPS, ALU.mult, ALU.add, ["ST/ss"], ["ST/rs"])
                tt("pool", ST[:, 4:8], ST[:, 4:8], ST[:, 8:12], ALU.pow, ["ST/rs", "ST/nh"], ["ST/rs"])
            for t in range(NT):
                if final:
                    stt("dve", X[:, t, :], X[:, t, :], ST[:, 4 + t:5 + t], GF[:], ALU.mult, ALU.mult,
                        ["X/%d" % t, "ST/rs", "GF"], ["X/%d" % t])
                dma("sp", y_v[blk][:, t, :], X[:, t, :], "yout%d" % t, ["X/%d" % t], ["ydram/%d_%d" % (blk, t)])
                if blk + 1 < NBLK:
                    dma("sp", X[:, t, :], x_v[blk + 1][:, t, :], "xin%d" % t, [], ["X/%d" % t])

        with nc.allow_non_contiguous_dma(reason="small strided parameter loads"):
            info = P.emit()
    return nc, dbg_out, info


def _cols(v):
    return np.ascontiguousarray(v.reshape(-1, 128).T)


def host_layout(inp):
    f = np.float32
    spk = np.zeros((2, 128, SPC), f)
    s5f = np.zeros((2, 4, 128, 1024), f)
    for i in range(2):
        s = spk[i]
        s[:, 0:8] = _cols(inp["g_mix"][i])
        s[:, 8:16] = _cols(inp["g_ffn"][i])
        s[:, 16:24] = _cols(inp["g_ple"][i])
        s[:, 24:32] = _cols(inp["g_final"])
        caw = inp["conv_a_w"][i]
        for j in range(4):
            s[:, 32 + j:72:4] = _cols(caw[j])
        s[:, 72:82] = _cols(inp["conv_a_b"][i])
        s[:, 82:92] = _cols(inp["gate_x_b"][i])
        s[:, 92:102] = _cols(inp["gate_a_b"][i])
        s[:, 102:112] = _cols(inp["lru_lambda"][i])
        s[:, 112:128] = _cols(inp["b_glu"][i])
        cfw = inp["conv_f_w"][i]
        for j in range(3):
            s[:, 128 + j:272:3] = _cols(cfw[j])
        s[:, 272:320] = _cols(inp["conv_f_b"][i])
        are_t = inp["s5_a_re"][i].T
        aim_t = inp["s5_a_im"][i].T
        s[:, 320:384] = np.concatenate([are_t, are_t], 0)
        s[:, 384:448] = np.concatenate([aim_t, aim_t], 0)
        s[:, 448:512] = np.broadcast_to(inp["s5_log_dt"][i][None, :], (128, 64))
        dm = inp["s5_d"][i].reshape(64, 16).T
        s[:, 512:576] = np.tile(dm, (8, 1))
        bre = inp["s5_b_re"][i].transpose(1, 0, 2).reshape(64, 1024)
        bim = inp["s5_b_im"][i].transpose(1, 0, 2).reshape(64, 1024)
        cre = inp["s5_c_re"][i].transpose(2, 0, 1).reshape(64, 1024)
        cim = inp["s5_c_im"][i].transpose(2, 0, 1).reshape(64, 1024)
        s5f[i, 0] = np.concatenate([bre, bim], 0)
        s5f[i, 1] = np.concatenate([bim, bre], 0)
        s5f[i, 2] = np.concatenate([cre, cim], 0)
        s5f[i, 3] = np.concatenate([cim, cre], 0)
    cst = np.zeros((128, 128 + 1920 + 1 + 1024), f)
    cst[:, 0:128] = np.eye(128, dtype=f)
    z = np.zeros((128, 8, 240), f)
    for gl in range(8):
        for ch in range(16):
            z[gl * 16 + ch, gl, 112 + ch] = 1.0
    cst[:, 128:2048] = z.reshape(128, 1920)
    cst[0:64, 2048] = -1.0
    cst[64:128, 2048] = 1.0
    cst[:, 2049:] = np.broadcast_to(inp["g_final"][None, :], (128, 1024))
    shared = {"spk": spk, "s5f": s5f, "cst": cst}
    for k in ("w_in", "gate_x_w", "gate_a_w", "w_a_out", "w_glu", "w_o", "w_up", "w_down",
              "w_ple_gate", "w_ple_proj"):
        shared[k] = np.ascontiguousarray(inp[k], dtype=f)
    return shared


_CACHE = {}


def kernel(**inputs):
    inp = {k: np.asarray(v) for k, v in inputs.items()}
    shared = host_layout(inp)
    if "nc" not in _CACHE:
        _CACHE["nc"] = build()[0]
    nc = _CACHE["nc"]
    in_maps = []
    for b in range(8):
        m = dict(shared)
        m["x"] = np.ascontiguousarray(inp["x"][b], dtype=np.float32)
        m["p"] = np.ascontiguousarray(inp["p"][:, b], dtype=np.float32)
        in_maps.append(m)
    res = run_bass_kernel_spmd(nc, in_maps, core_ids=list(range(8)))
    return np.stack([np.asarray(r["y"], dtype=np.float32) for r in res.results], 0)
```

```python
import math
from contextlib import ExitStack

import numpy as np
import concourse.bass as bass
import concourse.mybir as mybir
from concourse.bass_utils import run_bass_kernel_spmd

F32 = mybir.dt.float32
BF16 = mybir.dt.bfloat16
I32 = mybir.dt.int32
AF = mybir.ActivationFunctionType
ALU = mybir.AluOpType

SAME_ENGINE_SYNC = ("act", "pool", "dve", "sp")
NSLOT = 4
BIG_N = 128
TB = 512
NT = 4
SEQ = 4096
EPS = 1e-6
PI = math.pi


class Prog:
    def __init__(self, nc, es):
        self.nc = nc
        self.es = es
        self.eng = {"pe": nc.tensor, "act": nc.scalar, "dve": nc.vector,
                    "pool": nc.gpsimd, "sp": nc.sync}
        self.ops = []
        self.state = {}
        self.alias = {}
        self._defer = None

    @staticmethod
    def _split(key):
        if "/" in key:
            r, s = key.split("/", 1)
            return r, s
        return key, None

    def _st(self, root):
        return self.state.setdefault(root, {"w": None, "r": [], "subs": {}})

    def _deps_read(self, key, me):
        root, sub = self._split(key)
        st = self._st(root)
        deps = []
        if st["w"] is not None:
            deps.append(st["w"])
        if sub is None:
            for s in st["subs"].values():
                if s["w"] is not None:
                    deps.append(s["w"])
            st["r"].append(me)
        else:
            s = st["subs"].setdefault(sub, {"w": None, "r": []})
            if s["w"] is not None:
                deps.append(s["w"])
            s["r"].append(me)
        return deps

    def _deps_write_whole(self, root, me):
        st = self._st(root)
        deps = []
        if st["w"] is not None:
            deps.append(st["w"])
        deps.extend(st["r"])
        for s in st["subs"].values():
            if s["w"] is not None:
                deps.append(s["w"])
            deps.extend(s["r"])
        st["subs"] = {}
        st["r"] = []
        st["w"] = me
        return deps

    def _deps_write(self, key, me):
        root, sub = self._split(key)
        deps = []
        for other in self.alias.get(root, ()):
            deps.extend(self._deps_write_whole(other, me))
        if sub is None:
            deps.extend(self._deps_write_whole(root, me))
            return deps
        st = self._st(root)
        if st["w"] is not None:
            deps.append(st["w"])
        deps.extend(st["r"])
        s = st["subs"].setdefault(sub, {"w": None, "r": []})
        if s["w"] is not None:
            deps.append(s["w"])
        deps.extend(s["r"])
        s["w"] = me
        s["r"] = []
        return deps

    def defer(self):
        self._defer = []
        return self._defer

    def end_defer(self):
        self._defer = None

    def interleave(self, lists):
        assert self._defer is None
        pos = [0] * len(lists)
        tot = sum(len(l) for l in lists)
        for _ in range(tot):
            best, bf = None, None
            for i, l in enumerate(lists):
                if pos[i] < len(l):
                    f = (pos[i] + 0.5) / len(l)
                    if bf is None or f < bf:
                        best, bf = i, f
            a = lists[best][pos[best]]
            pos[best] += 1
            self.op(*a[0], **a[1])

    def op(self, eng, fn, r=(), w=(), dma=None, n=0, fs=False):
        if self._defer is not None:
            self._defer.append(((eng, fn), {"r": list(r), "w": list(w), "dma": dma, "n": n, "fs": fs}))
            return -1
        idx = len(self.ops)
        deps = set()
        for k in r:
            deps.update(self._deps_read(k, idx))
        for k in w:
            deps.update(self._deps_write(k, idx))
        deps.discard(idx)
        self.ops.append({"eng": eng, "fn": fn, "deps": deps, "dma": dma, "sig": False, "n": n, "fs": fs})
        return idx

    def _skip(self, t, o):
        if t["dma"] is not None or o["dma"] is not None or t["eng"] != o["eng"]:
            return False
        if o["eng"] not in SAME_ENGINE_SYNC:
            return True
        return t["n"] >= BIG_N and not o["fs"]

    def emit(self):
        nc = self.nc
        ops = self.ops
        for o in ops:
            for d in o["deps"]:
                t = ops[d]
                if t["dma"] is None and not self._skip(t, o):
                    t["sig"] = True
        cnt = {}
        sems = {}

        def getsem(name):
            if name not in sems:
                sems[name] = self.es.enter_context(nc.semaphore("s_" + name))
            return sems[name]

        for o in ops:
            if o["dma"] is not None:
                k = "d_" + o["dma"]
                cnt[k] = cnt.get(k, 0) + 16
                o["tok"] = (k, cnt[k])
            elif o["sig"]:
                k = "e_" + o["eng"]
                cnt[k] = cnt.get(k, 0) + 1
                o["tok"] = (k, cnt[k])
            else:
                o["tok"] = None
        seen = {e: {} for e in self.eng}
        nwait = 0
        for o in ops:
            e = o["eng"]
            engine = self.eng[e]
            need = {}
            for d in o["deps"]:
                t = ops[d]
                if t["tok"] is None or self._skip(t, o):
                    continue
                k, v = t["tok"]
                if v > need.get(k, 0):
                    need[k] = v
            for k, v in need.items():
                if seen[e].get(k, 0) >= v:
                    continue
                engine.wait_ge(getsem(k), v)
                seen[e][k] = v
                nwait += 1
            ins = o["fn"](engine)
            if o["tok"] is not None:
                k, v = o["tok"]
                ins.then_inc(getsem(k), 16 if o["dma"] is not None else 1)
        sp = self.eng["sp"]
        for k, v in cnt.items():
            if k.startswith("d_"):
                sp.wait_ge(getsem(k), v)
        return {"n_ops": len(ops), "n_wait": nwait, "n_sems": len(sems)}


SPC = 576


def build(NBLK=8, NLAY=2, dbg=(), final=True):
    nc = bass.Bass("TRN2", target_bir_lowering=False)

    def din(name, shape, dt=F32):
        return nc.dram_tensor(name, shape, dt, kind="ExternalInput").ap()

    x_d = din("x", [SEQ, 1024])
    p_d = din("p", [2, SEQ, 256])
    w_in_d = din("w_in", [2, 1024, 5632])
    gxw_d = din("gate_x_w", [2, 10, 128, 128])
    gaw_d = din("gate_a_w", [2, 10, 128, 128])
    w_ao_d = din("w_a_out", [2, 1280, 1024])
    w_glu_d = din("w_glu", [2, 1024, 2048])
    w_o_d = din("w_o", [2, 1024, 1024])
    w_up_d = din("w_up", [2, 1024, 6144])
    w_dn_d = din("w_down", [2, 3072, 1024])
    w_pg_d = din("w_ple_gate", [2, 1024, 1024])
    w_pp_d = din("w_ple_proj", [2, 256, 1024])
    sp_d = din("spk", [2, 128, SPC])
    s5f_d = din("s5f", [2, 4, 128, 1024])
    cst_d = din("cst", [128, 128 + 8 * 240 + 1 + 1024])
    y_d = nc.dram_tensor("y", [SEQ, 1024], F32, kind="ExternalOutput").ap()
    s5m_d = nc.dram_tensor("s5m", [2, 4, 128, 8192], BF16, kind="Internal").ap()
    dbg_out = {}

    es = ExitStack()
    with es:
        P = Prog(nc, es)

        def sb(name, shape, dt=F32):
            return es.enter_context(nc.sbuf_tensor(name, shape, dt))

        X = sb("X", [128, NT, 1024])
        HT = sb("HT", [128, 8, TB], BF16)
        ST = sb("ST", [128, 16])
        WS = sb("WS", [128, NSLOT, 4096], BF16)
        IDB = sb("IDB", [128, 128], BF16)
        IDF = sb("IDF", [128, 128])
        ZS = sb("ZS", [128, 8, 240], BF16)
        SGN = sb("SGN", [128, 1])
        GF = sb("GF", [128, 1024])
        SPm = sb("SPm", [128, 2, SPC])
        C8 = sb("C8", [128, 2, 20])
        S5C = sb("S5C", [128, 2, 3, 64])
        LRUH = sb("LRUH", [128, 2, 10])
        XAH = sb("XAH", [128, 2, 10, 3])
        S5CAR = sb("S5CAR", [128, 2, 2, 64])
        FT = sb("FT", [128, 2, 48, 2])
        HCt = sb("HCt", [128, 144])
        T1 = sb("T1", [128, 2, 64])
        T2 = sb("T2", [128, 2, 64])
        ARENA = 134776
        AR = sb("AR", [128, ARENA // 4])

        PM = es.enter_context(nc.psum_tensor("PM", [128, 4, 512], F32))
        PX = es.enter_context(nc.psum_tensor("PX", [128, 2, 512], F32))
        PTR = es.enter_context(nc.psum_tensor("PTR", [128, 2, 1024], BF16))

        arena_ranges = {}

        def av(name, off, nbytes, dt=F32, **re):
            assert off % 4 == 0 and nbytes % 4 == 0 and off + nbytes <= ARENA, (name, off, nbytes)
            arena_ranges[name] = (off, off + nbytes)
            v = AR[:, off // 4:(off + nbytes) // 4]
            if dt is not F32:
                v = v.bitcast(dt)
            if re:
                pat = re.pop("pat")
                v = v.rearrange(pat, **re)
            return v

        OA = av("OA", 0, 10240, BF16, pat="p (k n) -> p k n", n=TB)
        XA = av("XA", 10240, 20600, F32, pat="p (k n) -> p k n", n=515)
        GBf = av("GB", 30840, 10240, BF16, pat="p (k n) -> p k n", n=TB)
        LT = av("LT", 41080, 22528)
        YSB = av("YSB", 10240, 8192, BF16, pat="p (g c) -> p g c", c=64)
        YS = av("YS", 18432, 8192, BF16, pat="p (k n) -> p k n", n=TB)
        UT = av("UT", 63608, 8192, BF16, pat="p (g l c) -> p g l c", g=32, l=8)
        XB = av("XB", 63608, 8192, BF16, pat="p (g c) -> p g c", c=64)
        U = av("U", 71800, 8192, BF16, pat="p (g c) -> p g c", c=64)
        S2 = av("S2", 79992, 33280, F32, pat="p (c h g) -> p c h g", c=65, h=2)
        SMA = av("SMA", 113272, 8192, BF16, pat="p (k n) -> p k n", n=TB)
        SMB = av("SMB", 121464, 8192, BF16, pat="p (k n) -> p k n", n=TB)
        TMP = av("TMP", 41080, 4096, F32, pat="p (k n) -> p k n", n=TB)
        YT = av("YT", 41080, 4096, BF16, pat="p (s n) -> p s n", s=2)
        HS = av("HS", 45176, 4096, BF16, pat="p (k n) -> p k n", n=1024)
        SQ = av("SQ", 49272, 2048, BF16)
        WG = av("WG", 129656, 5120, BF16, pat="p (w k n) -> p w k n", w=2, k=10)
        YB = av("YB", 79992, 16384, F32, pat="p (k n) -> p k n", n=TB)
        MGB = av("MGB", 96376, 8192, BF16, pat="p (k n) -> p k n", n=TB)
        PTOK = av("PTOK", 104568, 4096, F32, pat="p (t n) -> p t n", n=256)
        PB = av("PB", 108664, 2048, BF16, pat="p (t n) -> p t n", n=256)
        PTT = av("PTT", 110712, 2048, BF16, pat="p (k n) -> p k n", n=TB)
        ACTB = av("ACTB", 0, 24576, BF16, pat="p (k n) -> p k n", n=TB)
        UPF = av("UPF", 24576, 8224, F32, pat="p (k n) -> p k n", n=514)
        CV = av("CV", 32800, 8192, F32, pat="p (k n) -> p k n", n=TB)
        PTMP = av("PTMP", 40992, 4096, F32, pat="p (k n) -> p k n", n=TB)
        GBN = 32
        _o = [0]

        def pav(name, nbytes, dt, n):
            v = av(name, _o[0], nbytes, dt, pat="p (g n) -> p g n", n=n)
            _o[0] += nbytes
            return v

        EX = pav("EX", GBN * 960, F32, 240)
        F1 = pav("F1", GBN * 64, F32, 16)
        F2 = pav("F2", GBN * 64, F32, 16)
        G1 = pav("G1", GBN * 64, F32, 16)
        G2 = pav("G2", GBN * 64, F32, 16)
        BBv = pav("BB", GBN * 64, F32, 16)
        BBS = pav("BBS", GBN * 64, F32, 16)
        Q1 = pav("Q1", GBN * 64, F32, 16)
        Q2 = pav("Q2", GBN * 64, F32, 16)
        CST = pav("CST", GBN * 64, F32, 16)
        WAB = pav("WAB", GBN * 256, BF16, 128)
        WASB = pav("WASB", GBN * 256, BF16, 128)
        WCB = pav("WCB", GBN * 256, BF16, 128)
        TBB = pav("TBB", GBN * 256, BF16, 128)
        EXB = pav("EXB", GBN * 480, BF16, 240)
        CSTB = pav("CSTB", GBN * 32, BF16, 16)
        SC = av("SC", _o[0], 64 * 4 * 40, F32, pat="p (k n) -> p k n", n=64)
        names = list(arena_ranges)
        for a in names:
            for b in names:
                if a != b:
                    (a0, a1), (b0, b1) = arena_ranges[a], arena_ranges[b]
                    if a0 < b1 and b0 < a1:
                        P.alias.setdefault(a, []).append(b)

        def _fs(ap):
            k = 1
            for d in ap.shape[1:]:
                k *= d
            return k

        def mm(out, lhsT, rhs, start, stop, r, w):
            P.op("pe", lambda e: e.matmul(out, lhsT=lhsT, rhs=rhs, start=start, stop=stop), r=r, w=w)

        def tr(out, in_, ident, r, w):
            P.op("pe", lambda e: e.transpose(out, in_, ident), r=r, w=w)

        def act(out, in_, func, r, w, bias=None, scale=None, accum=None):
            kw = {}
            if bias is not None:
                kw["bias"] = bias
            if scale is not None:
                kw["scale"] = scale
            if accum is not None:
                kw["accum_out"] = accum
            P.op("act", lambda e: e.activation(out=out, in_=in_, func=func, **kw), r=r, w=w,
                 n=0 if accum is not None else _fs(out))

        def ts(eng, out, in0, s1, s2, op0, op1, r, w):
            if s2 is None:
                P.op(eng, lambda e: e.tensor_scalar(out=out, in0=in0, scalar1=s1, scalar2=None, op0=op0), r=r, w=w, n=_fs(out))
            else:
                P.op(eng, lambda e: e.tensor_scalar(out=out, in0=in0, scalar1=s1, scalar2=s2, op0=op0, op1=op1), r=r, w=w, n=_fs(out))

        def tt(eng, out, in0, in1, op, r, w):
            P.op(eng, lambda e: e.tensor_tensor(out=out, in0=in0, in1=in1, op=op), r=r, w=w, n=_fs(out))

        def stt(eng, out, in0, scalar, in1, op0, op1, r, w):
            P.op(eng, lambda e: e.scalar_tensor_tensor(out=out, in0=in0, scalar=scalar, in1=in1, op0=op0, op1=op1), r=r, w=w, n=_fs(out))

        def cp(eng, out, in_, r, w):
            if eng == "act":
                P.op("act", lambda e: e.copy(out=out, in_=in_), r=r, w=w, n=_fs(out))
            else:
                P.op(eng, lambda e: e.tensor_copy(out=out, in_=in_), r=r, w=w, n=_fs(out))

        def mset(eng, out, val, w):
            P.op(eng, lambda e: e.memset(out, val), w=w)

        def dma(eng, out, in_, key, r, w):
            P.op(eng, lambda e: e.dma_start(out=out, in_=in_), r=r, w=w, dma=key)

        def tap(name, ap, shape, r):
            if name in dbg:
                d = nc.dram_tensor("dbg_" + name, shape, ap.dtype, kind="ExternalOutput").ap()
                dbg_out[name] = d
                dma("sp", d, ap, "dbg_" + name, r, ["dbgdram_" + name])

        wctr = [0]

        def wload(src, nk, ncols, cast=True, rk=()):
            s = wctr[0] % NSLOT
            wctr[0] += 1
            v = WS[:, s, 0:nk * ncols].rearrange("p (k n) -> p k n", n=ncols)
            dma("pool" if cast else "sp", v, src, "w%d" % s, list(rk), ["ws/%d" % s])
            return v, "ws/%d" % s

        pmc = [0]

        def pm():
            b = pmc[0] % 4
            pmc[0] += 1
            return PM[:, b, :], "pm/%d" % b

        pxc = [0]

        def px():
            b = pxc[0] % 2
            pxc[0] += 1
            return PX[:, b, :], "px/%d" % b

        ptc = [0]

        def ptr():
            b = ptc[0] % 2
            ptc[0] += 1
            return PTR[:, b, :], "ptr/%d" % b

        dma("sp", IDF[:], cst_d[:, 0:128], "c0", [], ["IDF"])
        dma("pool", IDB[:], cst_d[:, 0:128], "c1", [], ["IDB"])
        dma("pool", ZS[:].rearrange("p a b -> p (a b)"), cst_d[:, 128:128 + 1920], "c2", [], ["ZS"])
        dma("sp", SGN[:], cst_d[:, 2048:2049], "c3", [], ["SGN"])
        dma("sp", GF[:], cst_d[:, 2049:2049 + 1024], "c4", [], ["GF"])
        mset("dve", ST[:, 8:12], -0.5, ["ST/nh"])
        mset("dve", LRUH[:], 0.0, ["LRUH"])
        mset("dve", XAH[:], 0.0, ["XAH"])
        mset("dve", S5CAR[:], 0.0, ["S5CAR"])
        mset("dve", FT[:], 0.0, ["FT"])

        for i in range(NLAY):
            dma("sp", SPm[:, i, :], sp_d[i], "spm%d" % i, [], ["SPm/%d" % i])
            spk = "SPm/%d" % i
            act(C8[:, i, 0:10], SPm[:, i, 102:112], AF.Exp, [spk], ["C8/%d" % i], scale=-1.0)
            act(C8[:, i, 0:10], C8[:, i, 0:10], AF.Ln, ["C8/%d" % i], ["C8/%d" % i], bias=1.0)
            ts("dve", C8[:, i, 10:20], C8[:, i, 0:10], -16.0, None, ALU.mult, None, ["C8/%d" % i], ["C8/%d" % i])
            ts("dve", C8[:, i, 0:10], C8[:, i, 0:10], -8.0, None, ALU.mult, None, ["C8/%d" % i], ["C8/%d" % i])

            a_re = SPm[:, i, 320:384]
            a_im = SPm[:, i, 384:448]
            ldt = SPm[:, i, 448:512]
            DM = SPm[:, i, 512:576]
            k = "SC"
            DT, ADT, TH, MAG, QQ, RR, MSK, SN, CS = (SC[:, j, :] for j in range(9))
            LBR, LBI, NR, DEN, CR, CI, W2, TA, TBt = (SC[:, 9 + j, :] for j in range(9))
            PRn = SC[:, 18:27, :]
            PIn = SC[:, 27:36, :]
            QI = SC[:, 36, :].bitcast(I32)
            act(DT, ldt, AF.Exp, [spk], [k])
            tt("dve", ADT, a_re, DT, ALU.mult, [spk, k], [k])
            tt("dve", TH, a_im, DT, ALU.mult, [spk, k], [k])
            act(MAG, ADT, AF.Exp, [k], [k])

            def sincos(dst, shift):
                ts("dve", RR, TH, shift, None, ALU.add, None, [k], [k])
                ts("dve", QQ, RR, 1.0 / (2 * PI), None, ALU.mult, None, [k], [k])
                cp("dve", QI, QQ, [k], [k])
                cp("dve", QQ, QI, [k], [k])
                stt("dve", RR, QQ, -2 * PI, RR, ALU.mult, ALU.add, [k], [k])
                ts("dve", MSK, RR, PI, -2 * PI, ALU.is_gt, ALU.mult, [k], [k])
                tt("dve", RR, RR, MSK, ALU.add, [k], [k])
                ts("dve", MSK, RR, -PI, 2 * PI, ALU.is_lt, ALU.mult, [k], [k])
                tt("dve", RR, RR, MSK, ALU.add, [k], [k])
                ts("dve", RR, RR, PI, -PI, ALU.min, ALU.max, [k], [k])
                act(dst, RR, AF.Sin, [k], [k])

            sincos(SN, 0.0)
            sincos(CS, PI / 2)
            tt("dve", LBR, MAG, CS, ALU.mult, [k], [k])
            tt("dve", LBI, MAG, SN, ALU.mult, [k], [k])
            mset("dve", PRn[:, 0, :], 1.0, [k])
            mset("dve", PIn[:, 0, :], 0.0, [k])
            for q in range(8):
                tt("dve", TA, PRn[:, q, :], LBR, ALU.mult, [k], [k])
                tt("dve", TBt, PIn[:, q, :], LBI, ALU.mult, [k], [k])
                tt("dve", PRn[:, q + 1, :], TA, TBt, ALU.subtract, [k], [k])
                tt("dve", TA, PRn[:, q, :], LBI, ALU.mult, [k], [k])
                tt("dve", TBt, PIn[:, q, :], LBR, ALU.mult, [k], [k])
                tt("dve", PIn[:, q + 1, :], TA, TBt, ALU.add, [k], [k])
            ts("dve", NR, LBR, -1.0, None, ALU.add, None, [k], [k])
            tt("dve", TA, a_re, a_re, ALU.mult, [spk, k], [k])
            tt("dve", TBt, a_im, a_im, ALU.mult, [spk, k], [k])
            tt("dve", DEN, TA, TBt, ALU.add, [k], [k])
            P.op("dve", lambda e, DEN=DEN: e.reciprocal(out=DEN, in_=DEN), r=[k], w=[k])
            tt("dve", TA, NR, a_re, ALU.mult, [spk, k], [k])
            tt("dve", TBt, LBI, a_im, ALU.mult, [spk, k], [k])
            tt("dve", TA, TA, TBt, ALU.add, [k], [k])
            tt("dve", CR, TA, DEN, ALU.mult, [k], [k])
            tt("dve", TA, LBI, a_re, ALU.mult, [spk, k], [k])
            tt("dve", TBt, NR, a_im, ALU.mult, [spk, k], [k])
            tt("dve", TA, TA, TBt, ALU.subtract, [k], [k])
            tt("dve", CI, TA, DEN, ALU.mult, [k], [k])
            ts("dve", W2, CI, SGN[:, 0:1], None, ALU.mult, None, [k, "SGN"], [k])
            cp("dve", S5C[:, i, 0, :], PRn[:, 8, :], [k], ["S5C/%d" % i])
            ts("dve", S5C[:, i, 1, :], PIn[:, 8, :], SGN[:, 0:1], None, ALU.mult, None, [k, "SGN"], ["S5C/%d" % i])
            ts("dve", S5C[:, i, 2, :], PIn[:, 8, :], SGN[:, 0:1], -1.0, ALU.mult, ALU.mult, [k, "SGN"], ["S5C/%d" % i])

            def bc(v, g0):
                return v[:, g0:g0 + GBN].unsqueeze(2).to_broadcast([128, GBN, 16])

            for gb in range(64 // GBN):
                g0 = gb * GBN
                for q, (tile_, nm) in enumerate(((F1, "F1"), (F2, "F2"), (G1, "G1"), (G2, "G2"))):
                    dma("sp", tile_, s5f_d[i, q, :, g0 * 16:(g0 + GBN) * 16].rearrange("p (g n) -> p g n", n=16),
                        "s5f%d" % q, [], [nm])
                tt("dve", Q1, F1, bc(CR, g0), ALU.mult, ["F1", k], ["Q1"])
                tt("dve", Q2, F2, bc(W2, g0), ALU.mult, ["F2", k], ["Q2"])
                tt("dve", BBv, Q1, Q2, ALU.add, ["Q1", "Q2"], ["BB"])
                tt("dve", Q1, F2, bc(CR, g0), ALU.mult, ["F2", k], ["Q1"])
                tt("dve", Q2, F1, bc(W2, g0), ALU.mult, ["F1", k], ["Q2"])
                tt("dve", BBS, Q1, Q2, ALU.subtract, ["Q1", "Q2"], ["BBS"])
                mset("dve", EX[:, :, 128:240], 0.0, ["EX"])
                for j in range(8):
                    kk = 7 - j
                    tt("dve", Q1, BBv, bc(PRn[:, kk, :], g0), ALU.mult, ["BB", k], ["Q1"])
                    tt("dve", Q2, BBS, bc(PIn[:, kk, :], g0), ALU.mult, ["BBS", k], ["Q2"])
                    stt("dve", EX[:, :, j * 16:(j + 1) * 16], Q2, SGN[:, 0:1], Q1, ALU.mult, ALU.add,
                        ["Q1", "Q2", "SGN"], ["EX"])
                ts("dve", CST, G1, SGN[:, 0:1], -1.0, ALU.mult, ALU.mult, ["G1", "SGN"], ["CST"])
                cp("act", EXB[:], EX[:], ["EX"], ["EXB"])
                cp("dve", CSTB[:], CST[:], ["CST"], ["CSTB"])
                for gq in range(GBN // 4):
                    bank, bk = pm()
                    for gi in range(4):
                        gl = gq * 4 + gi
                        tr(bank[:, gi * 128:(gi + 1) * 128], EX[:, gl, 0:128], IDF[:], ["EX", "IDF"], [bk])
                    bv = bank.rearrange("p (g n) -> p g n", n=128)
                    cp("act", WAB[:, gq * 4:(gq + 1) * 4, :], bv, [bk], ["WAB"])
                    cp("dve", WASB[:, gq * 4:(gq + 1) * 4, 0:64], bv[:, :, 64:128], [bk], ["WASB"])
                    cp("dve", WASB[:, gq * 4:(gq + 1) * 4, 64:128], bv[:, :, 0:64], [bk], ["WASB"])
                    bank, bk = pm()
                    for gi in range(4):
                        gl = gq * 4 + gi
                        for l in range(8):
                            mm(bank[:, gi * 128 + l * 16: gi * 128 + (l + 1) * 16],
                               EXB[:, gl, (7 - l) * 16:(7 - l) * 16 + 128], CSTB[:, gl, :], True, True,
                               ["EXB", "CSTB"], [bk])
                        stt("dve", TBB[:, gl, :], IDF[:], DM[:, g0 + gl:g0 + gl + 1], bank[:, gi * 128:(gi + 1) * 128],
                            ALU.mult, ALU.add, [bk, "IDF", spk], ["TBB"])
                for l in range(8):
                    tt("dve", Q1, G1, bc(PRn[:, l + 1, :], g0), ALU.mult, ["G1", k], ["Q1"])
                    tt("dve", Q2, G2, bc(PIn[:, l + 1, :], g0), ALU.mult, ["G2", k], ["Q2"])
                    ts("dve", Q1, Q1, SGN[:, 0:1], -1.0, ALU.mult, ALU.mult, ["Q1", "SGN"], ["Q1"])
                    tt("dve", WCB[:, :, l * 16:(l + 1) * 16], Q1, Q2, ALU.subtract, ["Q1", "Q2"], ["WCB"])
                for q, (tile_, nm) in enumerate(((WAB, "WAB"), (WASB, "WASB"), (WCB, "WCB"), (TBB, "TBB"))):
                    dma("sp", s5m_d[i, q, :, g0 * 128:(g0 + GBN) * 128].rearrange("p (g n) -> p g n", n=128),
                        tile_, "s5mo%d" % q, [nm], ["s5m%d_%d" % (i, q)])

        def kview(ap):
            return ap.rearrange("(k p) n -> p k n", p=128)

        w_in_v = [kview(w_in_d[i]) for i in range(2)]
        w_ao_v = [kview(w_ao_d[i]) for i in range(2)]
        w_glu_v = [kview(w_glu_d[i]) for i in range(2)]
        w_o_v = [kview(w_o_d[i]) for i in range(2)]
        w_up_v = [kview(w_up_d[i]) for i in range(2)]
        w_dn_v = [kview(w_dn_d[i]) for i in range(2)]
        w_pg_v = [kview(w_pg_d[i]) for i in range(2)]
        w_pp_v = [kview(w_pp_d[i]) for i in range(2)]
        gxw_v = [gxw_d[i].rearrange("h i j -> i h j") for i in range(2)]
        gaw_v = [gaw_d[i].rearrange("h i j -> i h j") for i in range(2)]
        x_v = x_d.rearrange("(b t p) d -> b p t d", t=NT, p=128)
        y_v = y_d.rearrange("(b t p) d -> b p t d", t=NT, p=128)
        p_v = [p_d[i].rearrange("(b t p) d -> b p t d", t=NT, p=128) for i in range(2)]

        def norm(gcols, gkey):
            for t in range(NT):
                act(SQ[:], X[:, t, :], AF.Square, ["X/%d" % t], ["SQ", "ST/ss"], accum=ST[:, t:t + 1])
            ts("dve", ST[:, 4:8], ST[:, 0:4], 1.0 / 1024, EPS, ALU.mult, ALU.add, ["ST/ss"], ["ST/rs"])
            tt("pool", ST[:, 4:8], ST[:, 4:8], ST[:, 8:12], ALU.pow, ["ST/rs", "ST/nh"], ["ST/rs"])
            for t in range(NT):
                s = t % 2
                act(HS[:, s, :], X[:, t, :], AF.Copy, ["X/%d" % t, "ST/rs"], ["HS/%d" % s], scale=ST[:, 4 + t:5 + t])
                bank, bk = ptr()
                for kt in range(8):
                    tr(bank[:, kt * 128:(kt + 1) * 128], HS[:, s, kt * 128:(kt + 1) * 128], IDB[:],
                       ["HS/%d" % s, "IDB"], [bk])
                tt("dve", HT[:, :, t * 128:(t + 1) * 128], bank.rearrange("p (k n) -> p k n", n=128),
                   gcols.unsqueeze(2).to_broadcast([128, 8, 128]), ALU.mult, [bk, gkey], ["HT/%d" % t])

        HTall = ["HT/%d" % t for t in range(NT)]

        f6c = [0]

        def pm6():
            b = f6c[0] % 6
            f6c[0] += 1
            if b < 4:
                return PM[:, b, :], "pm/%d" % b
            return PX[:, b - 4, :], "px/%d" % (b - 4)

        def proj_fm(ws, wk, j, nk, rhs_of, rkeys, bankfn=None):
            bank, bk = (bankfn or pm)()
            for kt in range(nk):
                mm(bank, ws[:, kt, j * 128:(j + 1) * 128], rhs_of(kt), kt == 0, kt == nk - 1, [wk] + rkeys, [bk])
            return bank, bk

        def layer(i, blk):
            spk = "SPm/%d" % i
            norm(SPm[:, i, 0:8], spk)
            tap("ht1_%d_%d" % (i, blk), HT[:], [128, 8, TB], HTall)

            def win_chunk(c):
                ws, wk = wload(w_in_v[i][:, :, c * 512:(c + 1) * 512], 8, 512)
                for j in range(4):
                    t = 4 * c + j
                    bank, bk = proj_fm(ws, wk, j, 8, lambda kt: HT[:, kt, :], HTall)
                    if t < 10:
                        cp("act", XA[:, t, 3:515], bank, [bk], ["XA/%d" % t])
                    elif t < 20:
                        act(GBf[:, t - 10, :], bank, AF.Gelu_apprx_tanh, [bk], ["GB/%d" % (t - 10)])
                    elif t < 28:
                        raise AssertionError("ub goes through the token-chunk-major path")
                    elif t < 36:
                        act(SMA[:, t - 28, :], bank, AF.Sigmoid, [bk], ["SMA/%d" % (t - 28)])
                    else:
                        act(SMB[:, t - 36, :], bank, AF.Sigmoid, [bk], ["SMB/%d" % (t - 36)])

            for c in range(5):
                win_chunk(c)

            def lru_tile(t, wgx, wgxk, wga, wgak):
                q = t % 2
                base = q * 2816
                xc = LT[:, base:base + 512]
                gx = LT[:, base + 512:base + 1024]
                aa = LT[:, base + 1024:base + 1536]
                m2 = LT[:, base + 1536:base + 2048]
                hh = LT[:, base + 2048:base + 2560]
                xcb = LT[:, base + 2560:base + 2816].bitcast(BF16)
                lk = "LT/%d" % q
                xk = "XA/%d" % t
                cp("act", XA[:, t, 0:3], XAH[:, i, t, :], ["XAH/%d_%d" % (i, t)], [xk])
                cw = lambda j: SPm[:, i, 32 + t * 4 + j:33 + t * 4 + j]
                act(xc, XA[:, t, 0:512], AF.Identity, [xk, spk], [lk], bias=SPm[:, i, 72 + t:73 + t], scale=cw(0))
                for j in range(1, 4):
                    stt("dve", xc, XA[:, t, j:j + 512], cw(j), xc, ALU.mult, ALU.add, [xk, spk, lk], [lk])
                cp("act", XAH[:, i, t, :], XA[:, t, 512:515], [xk], ["XAH/%d_%d" % (i, t)])
                cp("act", xcb, xc, [lk], [lk])
                bx, bxk = PX[:, q, :], "px/%d" % q
                ba, bak = bx, bxk
                mm(bx, wgx[:, t, :], xcb, True, True, [wgxk, lk], [bxk])
                act(gx, bx, AF.Sigmoid, [bxk, spk], [lk], bias=SPm[:, i, 82 + t:83 + t])
                mm(ba, wga[:, t, :], xcb, True, True, [wgak, lk], [bak])
                act(hh, ba, AF.Sigmoid, [bak, spk], [lk], bias=SPm[:, i, 92 + t:93 + t])
                act(aa, hh, AF.Exp, [lk, "C8/%d" % i], [lk], scale=C8[:, i, t:t + 1])
                act(m2, hh, AF.Exp, [lk, "C8/%d" % i], [lk], scale=C8[:, i, 10 + t:11 + t])
                act(m2, m2, AF.Sqrt, [lk], [lk], bias=1.0, scale=-1.0)
                tt("dve", gx, gx, xc, ALU.mult, [lk], [lk])
                stt("dve", m2, m2, 1e-6, gx, ALU.max, ALU.mult, [lk], [lk])
                hk = "LRUH/%d_%d" % (i, t)
                P.op("dve", lambda e, hh=hh, aa=aa, m2=m2, t=t: e.tensor_tensor_scan(
                    out=hh, data0=aa, data1=m2, initial=LRUH[:, i, t:t + 1], op0=ALU.mult, op1=ALU.add),
                    r=[lk, hk], w=[lk], n=512)
                cp("act", LRUH[:, i, t:t + 1], hh[:, 511:512], [lk], [hk])
                tt("dve", OA[:, t, :], hh, GBf[:, t, :], ALU.mult, [lk, "GB/%d" % t], ["OA/%d" % t])

            L_lru = P.defer()
            dma("pool", WG[:, 0], gxw_v[i], "wg0", [], ["WG/0"])
            dma("pool", WG[:, 1], gaw_v[i], "wg1", [], ["WG/1"])
            wgx, wgxk, wga, wgak = WG[:, 0], "WG/0", WG[:, 1], "WG/1"
            for t in range(0, 10, 2):
                lru_tile(t, wgx, wgxk, wga, wgak)
            P.end_defer()
            L_lru_odd = P.defer()
            for t in range(1, 10, 2):
                lru_tile(t, wgx, wgxk, wga, wgak)
            P.end_defer()

            L_b = P.defer()
            for hh in range(2):
                ws, wk = wload(w_in_v[i][:, :, (5 + hh) * 512:(6 + hh) * 512], 8, 512)
                for l in range(8):
                    bank, bk = pm()
                    for kt in range(8):
                        mm(bank[0:64, :], HT[:, kt, l::8], ws[:, kt, :], kt == 0, kt == 7, [wk] + HTall, [bk])
                    cp("act" if l % 2 else "dve", UT[0:64, :, l, :], bank[0:64, :].rearrange("p (g c) -> p g c", c=16),
                       [bk], ["UT"])
                for gq in range(2):
                    tb, tbk = ptr()
                    tv = tb.rearrange("p (g c) -> p g c", c=64)
                    for gi in range(16):
                        tr(tv[:, gi, :], UT[0:64, gq * 16 + gi, :, :].rearrange("p l c -> p (l c)"), IDB[0:64, 0:64],
                           ["UT", "IDB"], [tbk])
                    g0 = hh * 32 + gq * 16
                    cp("act" if gq else "dve", U[:, g0:g0 + 16, :], tv, [tbk],
                       ["U/%d" % (g0 // 8), "U/%d" % (g0 // 8 + 1)])
            for h in range(2):
                for half in range(2):
                    ws, wk = wload(s5m_d[i, h, :, half * 4096:(half + 1) * 4096].rearrange("p (g n) -> p g n", n=128),
                                   32, 128, cast=False, rk=["s5m%d_%d" % (i, h)])
                    for j4 in range(4):
                        j = half * 4 + j4
                        bank, bk = pm()
                        bv = bank.rearrange("p (g c) -> p g c", c=64)
                        for gl in range(8):
                            mm(bv[:, gl, :], ws[:, j4 * 8 + gl, :], U[:, j * 8 + gl, :], True, True,
                               [wk, "U/%d" % j], [bk])
                        cp("act", S2[:, 1:65, h, j * 8:(j + 1) * 8].rearrange("p c g -> p g c"), bv, [bk],
                           ["S2/%d" % h])
            cp("dve", S2[:, 0, :, :], S5CAR[:, i, :, :], ["S5CAR/%d" % i], ["S2"])
            ArB = S5C[:, i, 0, :].unsqueeze(1).to_broadcast([128, 2, 64])
            sck = "S5C/%d" % i
            for c in range(64):
                if c % 16 == 0:
                    win_chunk(7 + c // 16)
                tt("dve", T1[:], S2[:, c, :, :], ArB, ALU.mult, ["S2", sck], ["T1"])
                tt("dve", T2[:, 0, :], S2[:, c, 1, :], S5C[:, i, 1, :], ALU.mult, ["S2", sck], ["T2"])
                tt("dve", T2[:, 1, :], S2[:, c, 0, :], S5C[:, i, 2, :], ALU.mult, ["S2", sck], ["T2"])
                tt("dve", T1[:], T1[:], T2[:], ALU.add, ["T1", "T2"], ["T1"])
                tt("dve", S2[:, c + 1, :, :], S2[:, c + 1, :, :], T1[:], ALU.add, ["S2", "T1"], ["S2"])
            cp("dve", S5CAR[:, i, :, :], S2[:, 64, :, :], ["S2"], ["S5CAR/%d" % i])
            cp("act", XB[:], S2[:, 0:64, 0, :].rearrange("p c g -> p g c"), ["S2"], ["XB"])
            P.end_defer()
            P.interleave([L_lru, L_lru_odd, L_b])
            tap("oa_%d_%d" % (i, blk), OA[:], [128, 10, TB], ["OA"])
            tap("xb_%d_%d" % (i, blk), XB[:], [128, 64, 64], ["XB"])
            dma("sp", PTOK[:], p_v[i][blk], "ptok", [], ["PTOK"])
            for half in range(2):
                wt_, wtk = wload(s5m_d[i, 3, :, half * 4096:(half + 1) * 4096].rearrange("p (g n) -> p g n", n=128),
                                 32, 128, cast=False, rk=["s5m%d_3" % i])
                wc_, wck = wload(s5m_d[i, 2, :, half * 4096:(half + 1) * 4096].rearrange("p (g n) -> p g n", n=128),
                                 32, 128, cast=False, rk=["s5m%d_2" % i])
                for j4 in range(4):
                    j = half * 4 + j4
                    bank, bk = pm()
                    bv = bank.rearrange("p (g c) -> p g c", c=64)
                    for gl in range(8):
                        g = j * 8 + gl
                        mm(bv[:, gl, :], wt_[:, j4 * 8 + gl, :], U[:, g, :], True, False, [wtk, "U/%d" % j], [bk])
                        mm(bv[:, gl, :], wc_[:, j4 * 8 + gl, :], XB[:, g, :], False, True, [wck, "XB"], [bk])
                    cp("act", YSB[:, j * 8:(j + 1) * 8, :], bv, [bk], ["YSB/%d" % j])
            for j in range(8):
                tb, tbk = ptr()
                tv = tb[0:64, :].rearrange("p (g n) -> p g n", n=128)
                for gl in range(8):
                    tr(tv[:, gl, :], YSB[:, j * 8 + gl, :], IDB[:], ["YSB/%d" % j, "IDB"], [tbk])
                sl = j % 2
                cp("dve" if j % 2 else "act", YT[0:64, sl, :].rearrange("p (l g c) -> p g l c", l=8, g=8),
                   tb[0:64, :].rearrange("p (g l c) -> p g l c", g=8, l=8), [tbk], ["YT/%d" % sl])
                tb2, tb2k = ptr()
                for l in range(8):
                    tr(tb2[:, l * 64:(l + 1) * 64], YT[0:64, sl, l * 128:(l + 1) * 128], IDB[0:64, 0:64],
                       ["YT/%d" % sl, "IDB"], [tb2k])
                act(YS[:, j, :].rearrange("p (c l) -> p c l", l=8), tb2[:, 0:512].rearrange("p (l c) -> p c l", l=8),
                    AF.Gelu_apprx_tanh, [tb2k], ["YS/%d" % j])
            tap("ys_%d_%d" % (i, blk), YS[:], [128, 8, TB], ["YS"])
            YSall = ["YS/%d" % j for j in range(8)]
            for c in (0, 2, 1, 3):
                ws, wk = wload(w_glu_v[i][:, :, c * 512:(c + 1) * 512], 8, 512)
                for j in range(4):
                    t = 4 * c + j
                    bank, bk = proj_fm(ws, wk, j, 8, lambda kt: YS[:, kt, :], YSall)
                    bcol = SPm[:, i, 112 + t:113 + t]
                    if t < 8:
                        act(YB[:, t, :], bank, AF.Identity, [bk, spk], ["YB/%d" % t], bias=bcol)
                    else:
                        s = t % 2
                        n = t - 8
                        act(TMP[:, s, :], bank, AF.Sigmoid, [bk, spk], ["TMP/%d" % s], bias=bcol)
                        tt("dve", TMP[:, s, :], TMP[:, s, :], SMB[:, n, :], ALU.mult,
                           ["TMP/%d" % s, "SMB/%d" % n], ["TMP/%d" % s])
                        tt("dve", YB[:, n, :], YB[:, n, :], TMP[:, s, :], ALU.mult,
                           ["YB/%d" % n, "TMP/%d" % s], ["YB/%d" % n])
            OAall = ["OA/%d" % t for t in range(10)]
            for c in range(4):
                ws, wk = wload(w_ao_v[i][:, :, c * 256:(c + 1) * 256], 10, 256)
                for j in range(2):
                    n = 2 * c + j
                    bank, bk = proj_fm(ws, wk, j, 10, lambda kt: OA[:, kt, :], OAall)
                    s = n % 2
                    tt("dve", TMP[:, s, :], bank, SMA[:, n, :], ALU.mult, [bk, "SMA/%d" % n], ["TMP/%d" % s])
                    tt("dve", MGB[:, n, :], TMP[:, s, :], YB[:, n, :], ALU.add, ["TMP/%d" % s, "YB/%d" % n],
                       ["MGB/%d" % n])
            tap("mg_%d_%d" % (i, blk), MGB[:], [128, 8, TB], ["MGB"])
            MGall = ["MGB/%d" % n for n in range(8)]
            for h in range(2):
                ws, wk = wload(w_o_v[i][:, :, h * 512:(h + 1) * 512], 8, 512)
                for t in range(NT):
                    bank, bk = pm()
                    for kt in range(8):
                        mm(bank, MGB[:, kt, t * 128:(t + 1) * 128], ws[:, kt, :], kt == 0, kt == 7, [wk] + MGall, [bk])
                    tt("dve", X[:, t, h * 512:(h + 1) * 512], bank, X[:, t, h * 512:(h + 1) * 512], ALU.add,
                       [bk, "X/%d" % t], ["X/%d" % t])
            tap("x1_%d_%d" % (i, blk), X[:], [128, NT, 1024], ["X"])

            norm(SPm[:, i, 8:16], spk)

            def ffn_tile(ws, wk, c, j, half):
                t = 4 * c + j
                o = t % 24
                q = (2 * half + (j % 2))
                bank, bk = proj_fm(ws, wk, j, 8, lambda kt: HT[:, kt, :], HTall)
                ck = "CV/%d" % q
                ftk = "FT/%d_%d" % (i, t)
                fw_ = lambda jj: SPm[:, i, 128 + t * 3 + jj:129 + t * 3 + jj]
                bcol = SPm[:, i, 272 + t:273 + t]
                act(CV[:, q, 2:512], bank[:, 0:510], AF.Identity, [bk, spk], [ck], bias=bcol, scale=fw_(0))
                cp("act", CV[:, q, 0:2], HC[:, t, :], ["HC"], [ck])
                stt("dve", CV[:, q, 1:512], bank[:, 0:511], fw_(1), CV[:, q, 1:512], ALU.mult, ALU.add,
                    [bk, spk, ck], [ck])
                stt("dve", CV[:, q, :], bank, fw_(2), CV[:, q, :], ALU.mult, ALU.add, [bk, spk, ck], [ck])
                cp("act", FT[:, i, t, :], bank[:, 510:512], [bk], [ftk])
                if half == 0:
                    act(ACTB[:, o, :], CV[:, q, :], AF.Gelu_apprx_tanh, [ck], ["ACTB/%d" % o])
                else:
                    tt("dve", ACTB[:, o, :], ACTB[:, o, :], CV[:, q, :], ALU.mult, ["ACTB/%d" % o, ck],
                       ["ACTB/%d" % o])

            HC = HCt[:, 0:96].rearrange("p (t k) -> p t k", k=2)
            HTM = HCt[:, 96:144]
            FTall = ["FT/%d_%d" % (i, t) for t in range(48)]
            W0 = SPm[:, i, 128:272:3]
            W1 = SPm[:, i, 129:272:3]
            BC = SPm[:, i, 272:320]
            tt("dve", HC[:, :, 0], FT[:, i, :, 0], W0, ALU.mult, FTall + [spk], ["HC"])
            tt("dve", HTM, FT[:, i, :, 1], W1, ALU.mult, FTall + [spk], ["HTM"])
            tt("dve", HC[:, :, 0], HC[:, :, 0], HTM, ALU.add, ["HC", "HTM"], ["HC"])
            tt("dve", HC[:, :, 0], HC[:, :, 0], BC, ALU.add, ["HC", spk], ["HC"])
            tt("dve", HC[:, :, 1], FT[:, i, :, 1], W0, ALU.mult, FTall + [spk], ["HC"])
            tt("dve", HC[:, :, 1], HC[:, :, 1], BC, ALU.add, ["HC", spk], ["HC"])
            for c6 in range(6):
                for half in range(2):
                    c = c6 + 6 * half
                    ws, wk = wload(w_up_v[i][:, :, c * 512:(c + 1) * 512], 8, 512)
                    for jp in range(2):
                        lists = []
                        for j in (2 * jp, 2 * jp + 1):
                            lists.append(P.defer())
                            ffn_tile(ws, wk, c, j, half)
                            P.end_defer()
                        P.interleave(lists)
            tap("actb_%d_%d" % (i, blk), ACTB[:], [128, 24, TB], ["ACTB"])
            ACall = ["ACTB/%d" % o for o in range(24)]
            for h in range(2):
                banks = [pm() for _ in range(NT)]
                for kc in range(3):
                    ws, wk = wload(w_dn_v[i][:, kc * 8:(kc + 1) * 8, h * 512:(h + 1) * 512], 8, 512)
                    for t in range(NT):
                        bank, bk = banks[t]
                        for k8 in range(8):
                            mm(bank, ACTB[:, kc * 8 + k8, t * 128:(t + 1) * 128], ws[:, k8, :],
                               kc == 0 and k8 == 0, kc == 2 and k8 == 7, [wk] + ACall, [bk])
                for t in range(NT):
                    bank, bk = banks[t]
                    tt("dve", X[:, t, h * 512:(h + 1) * 512], bank, X[:, t, h * 512:(h + 1) * 512], ALU.add,
                       [bk, "X/%d" % t], ["X/%d" % t])
            tap("x2_%d_%d" % (i, blk), X[:], [128, NT, 1024], ["X"])

            norm(SPm[:, i, 16:24], spk)
            cp("dve", PB[:], PTOK[:], ["PTOK"], ["PB"])
            bank, bk = ptr()
            bv = bank.rearrange("p (k n) -> p k n", n=TB)
            for t in range(NT):
                for kk in range(2):
                    tr(bv[:, kk, t * 128:(t + 1) * 128], PB[:, t, kk * 128:(kk + 1) * 128], IDB[:], ["PB", "IDB"], [bk])
            cp("act", PTT[:], bv, [bk], ["PTT"])
            for h in range(2):
                wg, wgk = wload(w_pg_v[i][:, :, h * 512:(h + 1) * 512], 8, 512)
                wp, wpk = wload(w_pp_v[i][:, :, h * 512:(h + 1) * 512], 2, 512)
                for t in range(NT):
                    bg, bgk = pm()
                    for kt in range(8):
                        mm(bg, HT[:, kt, t * 128:(t + 1) * 128], wg[:, kt, :], kt == 0, kt == 7, [wgk, "HT/%d" % t], [bgk])
                    bp, bpk = px()
                    for kk in range(2):
                        mm(bp, PTT[:, kk, t * 128:(t + 1) * 128], wp[:, kk, :], kk == 0, kk == 1, [wpk, "PTT"], [bpk])
                    s = t % 2
                    act(TMP[:, s, :], bg, AF.Sigmoid, [bgk], ["TMP/%d" % s])
                    tt("dve", TMP[:, s, :], TMP[:, s, :], bp, ALU.mult, ["TMP/%d" % s, bpk], ["TMP/%d" % s])
                    tt("dve", X[:, t, h * 512:(h + 1) * 512], TMP[:, s, :], X[:, t, h * 512:(h + 1) * 512], ALU.add,
                       ["TMP/%d" % s, "X/%d" % t], ["X/%d" % t])
            tap("x3_%d_%d" % (i, blk), X[:], [128, NT, 1024], ["X"])

        for t in range(NT):
            dma("sp", X[:, t, :], x_v[0][:, t, :], "xin%d" % t, [], ["X/%d" % t])
        for blk in range(NBLK):
            for i in range(NLAY):
                layer(i, blk)
            if final:
                for t in range(NT):
                    act(SQ[:], X[:, t, :], AF.Square, ["X/%d" % t], ["SQ", "ST/ss"], accum=ST[:, t:t + 1])
                ts("dve", ST[:, 4:8], ST[:, 0:4], 1.0 / 1024, EPS, ALU.mult, ALU.add, ["ST/ss"], ["ST/rs"])
                tt("pool", ST[:, 4:8], ST[:, 4:8], ST[:, 8:12], ALU.pow, ["ST/rs", "ST/nh"], ["ST/rs"])
            for t in range(NT):
                if final:
                    stt("dve", X[:, t, :], X[:, t, :], ST[:, 4 + t:5 + t], GF[:], ALU.mult, ALU.mult,
                        ["X/%d" % t, "ST/rs", "GF"], ["X/%d" % t])
                dma("sp", y_v[blk][:, t, :], X[:, t, :], "yout%d" % t, ["X/%d" % t], ["ydram/%d_%d" % (blk, t)])
                if blk + 1 < NBLK:
                    dma("sp", X[:, t, :], x_v[blk + 1][:, t, :], "xin%d" % t, [], ["X/%d" % t])

        with nc.allow_non_contiguous_dma(reason="small strided parameter loads"):
            info = P.emit()
    return nc, dbg_out, info


def _cols(v):
    return np.ascontiguousarray(v.reshape(-1, 128).T)


def host_layout(inp):
    f = np.float32
    spk = np.zeros((2, 128, SPC), f)
    s5f = np.zeros((2, 4, 128, 1024), f)
    for i in range(2):
        s = spk[i]
        s[:, 0:8] = _cols(inp["g_mix"][i])
        s[:, 8:16] = _cols(inp["g_ffn"][i])
        s[:, 16:24] = _cols(inp["g_ple"][i])
        s[:, 24:32] = _cols(inp["g_final"])
        caw = inp["conv_a_w"][i]
        for j in range(4):
            s[:, 32 + j:72:4] = _cols(caw[j])
        s[:, 72:82] = _cols(inp["conv_a_b"][i])
        s[:, 82:92] = _cols(inp["gate_x_b"][i])
        s[:, 92:102] = _cols(inp["gate_a_b"][i])
        s[:, 102:112] = _cols(inp["lru_lambda"][i])
        s[:, 112:128] = _cols(inp["b_glu"][i])
        cfw = inp["conv_f_w"][i]
        for j in range(3):
            s[:, 128 + j:272:3] = _cols(cfw[j])
        s[:, 272:320] = _cols(inp["conv_f_b"][i])
        are_t = inp["s5_a_re"][i].T
        aim_t = inp["s5_a_im"][i].T
        s[:, 320:384] = np.concatenate([are_t, are_t], 0)
        s[:, 384:448] = np.concatenate([aim_t, aim_t], 0)
        s[:, 448:512] = np.broadcast_to(inp["s5_log_dt"][i][None, :], (128, 64))
        dm = inp["s5_d"][i].reshape(64, 16).T
        s[:, 512:576] = np.tile(dm, (8, 1))
        bre = inp["s5_b_re"][i].transpose(1, 0, 2).reshape(64, 1024)
        bim = inp["s5_b_im"][i].transpose(1, 0, 2).reshape(64, 1024)
        cre = inp["s5_c_re"][i].transpose(2, 0, 1).reshape(64, 1024)
        cim = inp["s5_c_im"][i].transpose(2, 0, 1).reshape(64, 1024)
        s5f[i, 0] = np.concatenate([bre, bim], 0)
        s5f[i, 1] = np.concatenate([bim, bre], 0)
        s5f[i, 2] = np.concatenate([cre, cim], 0)
        s5f[i, 3] = np.concatenate([cim, cre], 0)
    cst = np.zeros((128, 128 + 1920 + 1 + 1024), f)
    cst[:, 0:128] = np.eye(128, dtype=f)
    z = np.zeros((128, 8, 240), f)
    for gl in range(8):
        for ch in range(16):
            z[gl * 16 + ch, gl, 112 + ch] = 1.0
    cst[:, 128:2048] = z.reshape(128, 1920)
    cst[0:64, 2048] = -1.0
    cst[64:128, 2048] = 1.0
    cst[:, 2049:] = np.broadcast_to(inp["g_final"][None, :], (128, 1024))
    shared = {"spk": spk, "s5f": s5f, "cst": cst}
    for k in ("w_in", "gate_x_w", "gate_a_w", "w_a_out", "w_glu", "w_o", "w_up", "w_down",
              "w_ple_gate", "w_ple_proj"):
        shared[k] = np.ascontiguousarray(inp[k], dtype=f)
    return shared


_CACHE = {}


def kernel(**inputs):
    inp = {k: np.asarray(v) for k, v in inputs.items()}
    shared = host_layout(inp)
    if "nc" not in _CACHE:
        _CACHE["nc"] = build()[0]
    nc = _CACHE["nc"]
    in_maps = []
    for b in range(8):
        m = dict(shared)
        m["x"] = np.ascontiguousarray(inp["x"][b], dtype=np.float32)
        m["p"] = np.ascontiguousarray(inp["p"][:, b], dtype=np.float32)
        in_maps.append(m)
    res = run_bass_kernel_spmd(nc, in_maps, core_ids=list(range(8)))
    return np.stack([np.asarray(r["y"], dtype=np.float32) for r in res.results], 0)
```

```python
import math
from contextlib import ExitStack

import numpy as np
import concourse.bass as bass
import concourse.mybir as mybir
from concourse.bass_utils import run_bass_kernel_spmd

F32 = mybir.dt.float32
BF16 = mybir.dt.bfloat16
I32 = mybir.dt.int32
AF = mybir.ActivationFunctionType
ALU = mybir.AluOpType

SAME_ENGINE_SYNC = ("act", "pool", "dve", "sp")
NSLOT = 4
BIG_N = 128
TB = 512
NT = 4
SEQ = 4096
EPS = 1e-6
PI = math.pi


class Prog:
    def __init__(self, nc, es):
        self.nc = nc
        self.es = es
        self.eng = {"pe": nc.tensor, "act": nc.scalar, "dve": nc.vector,
                    "pool": nc.gpsimd, "sp": nc.sync}
        self.ops = []
        self.state = {}
        self.alias = {}
        self._defer = None

    @staticmethod
    def _split(key):
        if "/" in key:
            r, s = key.split("/", 1)
            return r, s
        return key, None

    def _st(self, root):
        return self.state.setdefault(root, {"w": None, "r": [], "subs": {}})

    def _deps_read(self, key, me):
        root, sub = self._split(key)
        st = self._st(root)
        deps = []
        if st["w"] is not None:
            deps.append(st["w"])
        if sub is None:
            for s in st["subs"].values():
                if s["w"] is not None:
                    deps.append(s["w"])
            st["r"].append(me)
        else:
            s = st["subs"].setdefault(sub, {"w": None, "r": []})
            if s["w"] is not None:
                deps.append(s["w"])
            s["r"].append(me)
        return deps

    def _deps_write_whole(self, root, me):
        st = self._st(root)
        deps = []
        if st["w"] is not None:
            deps.append(st["w"])
        deps.extend(st["r"])
        for s in st["subs"].values():
            if s["w"] is not None:
                deps.append(s["w"])
            deps.extend(s["r"])
        st["subs"] = {}
        st["r"] = []
        st["w"] = me
        return deps

    def _deps_write(self, key, me):
        root, sub = self._split(key)
        deps = []
        for other in self.alias.get(root, ()):
            deps.extend(self._deps_write_whole(other, me))
        if sub is None:
            deps.extend(self._deps_write_whole(root, me))
            return deps
        st = self._st(root)
        if st["w"] is not None:
            deps.append(st["w"])
        deps.extend(st["r"])
        s = st["subs"].setdefault(sub, {"w": None, "r": []})
        if s["w"] is not None:
            deps.append(s["w"])
        deps.extend(s["r"])
        s["w"] = me
        s["r"] = []
        return deps

    def defer(self):
        self._defer = []
        return self._defer

    def end_defer(self):
        self._defer = None

    def interleave(self, lists):
        assert self._defer is None
        pos = [0] * len(lists)
        tot = sum(len(l) for l in lists)
        for _ in range(tot):
            best, bf = None, None
            for i, l in enumerate(lists):
                if pos[i] < len(l):
                    f = (pos[i] + 0.5) / len(l)
                    if bf is None or f < bf:
                        best, bf = i, f
            a = lists[best][pos[best]]
            pos[best] += 1
            self.op(*a[0], **a[1])

    def op(self, eng, fn, r=(), w=(), dma=None, n=0, fs=False):
        if self._defer is not None:
            self._defer.append(((eng, fn), {"r": list(r), "w": list(w), "dma": dma, "n": n, "fs": fs}))
            return -1
        idx = len(self.ops)
        deps = set()
        for k in r:
            deps.update(self._deps_read(k, idx))
        for k in w:
            deps.update(self._deps_write(k, idx))
        deps.discard(idx)
        self.ops.append({"eng": eng, "fn": fn, "deps": deps, "dma": dma, "sig": False, "n": n, "fs": fs})
        return idx

    def _skip(self, t, o):
        if t["dma"] is not None or o["dma"] is not None or t["eng"] != o["eng"]:
            return False
        if o["eng"] not in SAME_ENGINE_SYNC:
            return True
        return t["n"] >= BIG_N and not o["fs"]

    def emit(self):
        nc = self.nc
        ops = self.ops
        for o in ops:
            for d in o["deps"]:
                t = ops[d]
                if t["dma"] is None and not self._skip(t, o):
                    t["sig"] = True
        cnt = {}
        sems = {}

        def getsem(name):
            if name not in sems:
                sems[name] = self.es.enter_context(nc.semaphore("s_" + name))
            return sems[name]

        for o in ops:
            if o["dma"] is not None:
                k = "d_" + o["dma"]
                cnt[k] = cnt.get(k, 0) + 16
                o["tok"] = (k, cnt[k])
            elif o["sig"]:
                k = "e_" + o["eng"]
                cnt[k] = cnt.get(k, 0) + 1
                o["tok"] = (k, cnt[k])
            else:
                o["tok"] = None
        seen = {e: {} for e in self.eng}
        nwait = 0
        for o in ops:
            e = o["eng"]
            engine = self.eng[e]
            need = {}
            for d in o["deps"]:
                t = ops[d]
                if t["tok"] is None or self._skip(t, o):
                    continue
                k, v = t["tok"]
                if v > need.get(k, 0):
                    need[k] = v
            for k, v in need.items():
                if seen[e].get(k, 0) >= v:
                    continue
                engine.wait_ge(getsem(k), v)
                seen[e][k] = v
                nwait += 1
            ins = o["fn"](engine)
            if o["tok"] is not None:
                k, v = o["tok"]
                ins.then_inc(getsem(k), 16 if o["dma"] is not None else 1)
        sp = self.eng["sp"]
        for k, v in cnt.items():
            if k.startswith("d_"):
                sp.wait_ge(getsem(k), v)
        return {"n_ops": len(ops), "n_wait": nwait, "n_sems": len(sems)}


SPC = 576


def build(NBLK=8, NLAY=2, dbg=(), final=True):
    nc = bass.Bass("TRN2", target_bir_lowering=False)

    def din(name, shape, dt=F32):
        return nc.dram_tensor(name, shape, dt, kind="ExternalInput").ap()

    x_d = din("x", [SEQ, 1024])
    p_d = din("p", [2, SEQ, 256])
    w_in_d = din("w_in", [2, 1024, 5632])
    gxw_d = din("gate_x_w", [2, 10, 128, 128])
    gaw_d = din("gate_a_w", [2, 10, 128, 128])
    w_ao_d = din("w_a_out", [2, 1280, 1024])
    w_glu_d = din("w_glu", [2, 1024, 2048])
    w_o_d = din("w_o", [2, 1024, 1024])
    w_up_d = din("w_up", [2, 1024, 6144])
    w_dn_d = din("w_down", [2, 3072, 1024])
    w_pg_d = din("w_ple_gate", [2, 1024, 1024])
    w_pp_d = din("w_ple_proj", [2, 256, 1024])
    sp_d = din("spk", [2, 128, SPC])
    s5f_d = din("s5f", [2, 4, 128, 1024])
    cst_d = din("cst", [128, 128 + 8 * 240 + 1 + 1024])
    y_d = nc.dram_tensor("y", [SEQ, 1024], F32, kind="ExternalOutput").ap()
    s5m_d = nc.dram_tensor("s5m", [2, 4, 128, 8192], BF16, kind="Internal").ap()
    dbg_out = {}

    es = ExitStack()
    with es:
        P = Prog(nc, es)

        def sb(name, shape, dt=F32):
            return es.enter_context(nc.sbuf_tensor(name, shape, dt))

        X = sb("X", [128, NT, 1024])
        HT = sb("HT", [128, 8, TB], BF16)
        ST = sb("ST", [128, 16])
        WS = sb("WS", [128, NSLOT, 4096], BF16)
        IDB = sb("IDB", [128, 128], BF16)
        IDF = sb("IDF", [128, 128])
        ZS = sb("ZS", [128, 8, 240], BF16)
        SGN = sb("SGN", [128, 1])
        GF = sb("GF", [128, 1024])
        SPm = sb("SPm", [128, 2, SPC])
        C8 = sb("C8", [128, 2, 20])
        S5C = sb("S5C", [128, 2, 3, 64])
        LRUH = sb("LRUH", [128, 2, 10])
        XAH = sb("XAH", [128, 2, 10, 3])
        S5CAR = sb("S5CAR", [128, 2, 2, 64])
        FT = sb("FT", [128, 2, 48, 2])
        HCt = sb("HCt", [128, 144])
        T1 = sb("T1", [128, 2, 64])
        T2 = sb("T2", [128, 2, 64])
        ARENA = 134776
        AR = sb("AR", [128, ARENA // 4])

        PM = es.enter_context(nc.psum_tensor("PM", [128, 4, 512], F32))
        PX = es.enter_context(nc.psum_tensor("PX", [128, 2, 512], F32))
        PTR = es.enter_context(nc.psum_tensor("PTR", [128, 2, 1024], BF16))

        arena_ranges = {}

        def av(name, off, nbytes, dt=F32, **re):
            assert off % 4 == 0 and nbytes % 4 == 0 and off + nbytes <= ARENA, (name, off, nbytes)
            arena_ranges[name] = (off, off + nbytes)
            v = AR[:, off // 4:(off + nbytes) // 4]
            if dt is not F32:
                v = v.bitcast(dt)
            if re:
                pat = re.pop("pat")
                v = v.rearrange(pat, **re)
            return v

        OA = av("OA", 0, 10240, BF16, pat="p (k n) -> p k n", n=TB)
        XA = av("XA", 10240, 20600, F32, pat="p (k n) -> p k n", n=515)
        GBf = av("GB", 30840, 10240, BF16, pat="p (k n) -> p k n", n=TB)
        LT = av("LT", 41080, 22528)
        YSB = av("YSB", 10240, 8192, BF16, pat="p (g c) -> p g c", c=64)
        YS = av("YS", 18432, 8192, BF16, pat="p (k n) -> p k n", n=TB)
        UT = av("UT", 63608, 8192, BF16, pat="p (g l c) -> p g l c", g=32, l=8)
        XB = av("XB", 63608, 8192, BF16, pat="p (g c) -> p g c", c=64)
        U = av("U", 71800, 8192, BF16, pat="p (g c) -> p g c", c=64)
        S2 = av("S2", 79992, 33280, F32, pat="p (c h g) -> p c h g", c=65, h=2)
        SMA = av("SMA", 113272, 8192, BF16, pat="p (k n) -> p k n", n=TB)
        SMB = av("SMB", 121464, 8192, BF16, pat="p (k n) -> p k n", n=TB)
        TMP = av("TMP", 41080, 4096, F32, pat="p (k n) -> p k n", n=TB)
        YT = av("YT", 41080, 4096, BF16, pat="p (s n) -> p s n", s=2)
        HS = av("HS", 45176, 4096, BF16, pat="p (k n) -> p k n", n=1024)
        SQ = av("SQ", 49272, 2048, BF16)
        WG = av("WG", 129656, 5120, BF16, pat="p (w k n) -> p w k n", w=2, k=10)
        YB = av("YB", 79992, 16384, F32, pat="p (k n) -> p k n", n=TB)
        MGB = av("MGB", 96376, 8192, BF16, pat="p (k n) -> p k n", n=TB)
        PTOK = av("PTOK", 104568, 4096, F32, pat="p (t n) -> p t n", n=256)
        PB = av("PB", 108664, 2048, BF16, pat="p (t n) -> p t n", n=256)
        PTT = av("PTT", 110712, 2048, BF16, pat="p (k n) -> p k n", n=TB)
        ACTB = av("ACTB", 0, 24576, BF16, pat="p (k n) -> p k n", n=TB)
        UPF = av("UPF", 24576, 8224, F32, pat="p (k n) -> p k n", n=514)
        CV = av("CV", 32800, 8192, F32, pat="p (k n) -> p k n", n=TB)
        PTMP = av("PTMP", 40992, 4096, F32, pat="p (k n) -> p k n", n=TB)
        GBN = 32
        _o = [0]

        def pav(name, nbytes, dt, n):
            v = av(name, _o[0], nbytes, dt, pat="p (g n) -> p g n", n=n)
            _o[0] += nbytes
            return v

        EX = pav("EX", GBN * 960, F32, 240)
        F1 = pav("F1", GBN * 64, F32, 16)
        F2 = pav("F2", GBN * 64, F32, 16)
        G1 = pav("G1", GBN * 64, F32, 16)
        G2 = pav("G2", GBN * 64, F32, 16)
        BBv = pav("BB", GBN * 64, F32, 16)
        BBS = pav("BBS", GBN * 64, F32, 16)
        Q1 = pav("Q1", GBN * 64, F32, 16)
        Q2 = pav("Q2", GBN * 64, F32, 16)
        CST = pav("CST", GBN * 64, F32, 16)
        WAB = pav("WAB", GBN * 256, BF16, 128)
        WASB = pav("WASB", GBN * 256, BF16, 128)
        WCB = pav("WCB", GBN * 256, BF16, 128)
        TBB = pav("TBB", GBN * 256, BF16, 128)
        EXB = pav("EXB", GBN * 480, BF16, 240)
        CSTB = pav("CSTB", GBN * 32, BF16, 16)
        SC = av("SC", _o[0], 64 * 4 * 40, F32, pat="p (k n) -> p k n", n=64)
        names = list(arena_ranges)
        for a in names:
            for b in names:
                if a != b:
                    (a0, a1), (b0, b1) = arena_ranges[a], arena_ranges[b]
                    if a0 < b1 and b0 < a1:
                        P.alias.setdefault(a, []).append(b)

        def _fs(ap):
            k = 1
            for d in ap.shape[1:]:
                k *= d
            return k

        def mm(out, lhsT, rhs, start, stop, r, w):
            P.op("pe", lambda e: e.matmul(out, lhsT=lhsT, rhs=rhs, start=start, stop=stop), r=r, w=w)

        def tr(out, in_, ident, r, w):
            P.op("pe", lambda e: e.transpose(out, in_, ident), r=r, w=w)

        def act(out, in_, func, r, w, bias=None, scale=None, accum=None):
            kw = {}
            if bias is not None:
                kw["bias"] = bias
            if scale is not None:
                kw["scale"] = scale
            if accum is not None:
                kw["accum_out"] = accum
            P.op("act", lambda e: e.activation(out=out, in_=in_, func=func, **kw), r=r, w=w,
                 n=0 if accum is not None else _fs(out))

        def ts(eng, out, in0, s1, s2, op0, op1, r, w):
            if s2 is None:
                P.op(eng, lambda e: e.tensor_scalar(out=out, in0=in0, scalar1=s1, scalar2=None, op0=op0), r=r, w=w, n=_fs(out))
            else:
                P.op(eng, lambda e: e.tensor_scalar(out=out, in0=in0, scalar1=s1, scalar2=s2, op0=op0, op1=op1), r=r, w=w, n=_fs(out))

        def tt(eng, out, in0, in1, op, r, w):
            P.op(eng, lambda e: e.tensor_tensor(out=out, in0=in0, in1=in1, op=op), r=r, w=w, n=_fs(out))

        def stt(eng, out, in0, scalar, in1, op0, op1, r, w):
            P.op(eng, lambda e: e.scalar_tensor_tensor(out=out, in0=in0, scalar=scalar, in1=in1, op0=op0, op1=op1), r=r, w=w, n=_fs(out))

        def cp(eng, out, in_, r, w):
            if eng == "act":
                P.op("act", lambda e: e.copy(out=out, in_=in_), r=r, w=w, n=_fs(out))
            else:
                P.op(eng, lambda e: e.tensor_copy(out=out, in_=in_), r=r, w=w, n=_fs(out))

        def mset(eng, out, val, w):
            P.op(eng, lambda e: e.memset(out, val), w=w)

        def dma(eng, out, in_, key, r, w):
            P.op(eng, lambda e: e.dma_start(out=out, in_=in_), r=r, w=w, dma=key)

        def tap(name, ap, shape, r):
            if name in dbg:
                d = nc.dram_tensor("dbg_" + name, shape, ap.dtype, kind="ExternalOutput").ap()
                dbg_out[name] = d
                dma("sp", d, ap, "dbg_" + name, r, ["dbgdram_" + name])

        wctr = [0]

        def wload(src, nk, ncols, cast=True, rk=()):
            s = wctr[0] % NSLOT
            wctr[0] += 1
            v = WS[:, s, 0:nk * ncols].rearrange("p (k n) -> p k n", n=ncols)
            dma("pool" if cast else "sp", v, src, "w%d" % s, list(rk), ["ws/%d" % s])
            return v, "ws/%d" % s

        pmc = [0]

        def pm():
            b = pmc[0] % 4
            pmc[0] += 1
            return PM[:, b, :], "pm/%d" % b

        pxc = [0]

        def px():
            b = pxc[0] % 2
            pxc[0] += 1
            return PX[:, b, :], "px/%d" % b

        ptc = [0]

        def ptr():
            b = ptc[0] % 2
            ptc[0] += 1
            return PTR[:, b, :], "ptr/%d" % b

        dma("sp", IDF[:], cst_d[:, 0:128], "c0", [], ["IDF"])
        dma("pool", IDB[:], cst_d[:, 0:128], "c1", [], ["IDB"])
        dma("pool", ZS[:].rearrange("p a b -> p (a b)"), cst_d[:, 128:128 + 1920], "c2", [], ["ZS"])
        dma("sp", SGN[:], cst_d[:, 2048:2049], "c3", [], ["SGN"])
        dma("sp", GF[:], cst_d[:, 2049:2049 + 1024], "c4", [], ["GF"])
        mset("dve", ST[:, 8:12], -0.5, ["ST/nh"])
        mset("dve", LRUH[:], 0.0, ["LRUH"])
        mset("dve", XAH[:], 0.0, ["XAH"])
        mset("dve", S5CAR[:], 0.0, ["S5CAR"])
        mset("dve", FT[:], 0.0, ["FT"])

        for i in range(NLAY):
            dma("sp", SPm[:, i, :], sp_d[i], "spm%d" % i, [], ["SPm/%d" % i])
            spk = "SPm/%d" % i
            act(C8[:, i, 0:10], SPm[:, i, 102:112], AF.Exp, [spk], ["C8/%d" % i], scale=-1.0)
            act(C8[:, i, 0:10], C8[:, i, 0:10], AF.Ln, ["C8/%d" % i], ["C8/%d" % i], bias=1.0)
            ts("dve", C8[:, i, 10:20], C8[:, i, 0:10], -16.0, None, ALU.mult, None, ["C8/%d" % i], ["C8/%d" % i])
            ts("dve", C8[:, i, 0:10], C8[:, i, 0:10], -8.0, None, ALU.mult, None, ["C8/%d" % i], ["C8/%d" % i])

            a_re = SPm[:, i, 320:384]
            a_im = SPm[:, i, 384:448]
            ldt = SPm[:, i, 448:512]
            DM = SPm[:, i, 512:576]
            k = "SC"
            DT, ADT, TH, MAG, QQ, RR, MSK, SN, CS = (SC[:, j, :] for j in range(9))
            LBR, LBI, NR, DEN, CR, CI, W2, TA, TBt = (SC[:, 9 + j, :] for j in range(9))
            PRn = SC[:, 18:27, :]
            PIn = SC[:, 27:36, :]
            QI = SC[:, 36, :].bitcast(I32)
            act(DT, ldt, AF.Exp, [spk], [k])
            tt("dve", ADT, a_re, DT, ALU.mult, [spk, k], [k])
            tt("dve", TH, a_im, DT, ALU.mult, [spk, k], [k])
            act(MAG, ADT, AF.Exp, [k], [k])

            def sincos(dst, shift):
                ts("dve", RR, TH, shift, None, ALU.add, None, [k], [k])
                ts("dve", QQ, RR, 1.0 / (2 * PI), None, ALU.mult, None, [k], [k])
                cp("dve", QI, QQ, [k], [k])
                cp("dve", QQ, QI, [k], [k])
                stt("dve", RR, QQ, -2 * PI, RR, ALU.mult, ALU.add, [k], [k])
                ts("dve", MSK, RR, PI, -2 * PI, ALU.is_gt, ALU.mult, [k], [k])
                tt("dve", RR, RR, MSK, ALU.add, [k], [k])
                ts("dve", MSK, RR, -PI, 2 * PI, ALU.is_lt, ALU.mult, [k], [k])
                tt("dve", RR, RR, MSK, ALU.add, [k], [k])
                ts("dve", RR, RR, PI, -PI, ALU.min, ALU.max, [k], [k])
                act(dst, RR, AF.Sin, [k], [k])

            sincos(SN, 0.0)
            sincos(CS, PI / 2)
            tt("dve", LBR, MAG, CS, ALU.mult, [k], [k])
            tt("dve", LBI, MAG, SN, ALU.mult, [k], [k])
            mset("dve", PRn[:, 0, :], 1.0, [k])
            mset("dve", PIn[:, 0, :], 0.0, [k])
            for q in range(8):
                tt("dve", TA, PRn[:, q, :], LBR, ALU.mult, [k], [k])
                tt("dve", TBt, PIn[:, q, :], LBI, ALU.mult, [k], [k])
                tt("dve", PRn[:, q + 1, :], TA, TBt, ALU.subtract, [k], [k])
                tt("dve", TA, PRn[:, q, :], LBI, ALU.mult, [k], [k])
                tt("dve", TBt, PIn[:, q, :], LBR, ALU.mult, [k], [k])
                tt("dve", PIn[:, q + 1, :], TA, TBt, ALU.add, [k], [k])
            ts("dve", NR, LBR, -1.0, None, ALU.add, None, [k], [k])
            tt("dve", TA, a_re, a_re, ALU.mult, [spk, k], [k])
            tt("dve", TBt, a_im, a_im, ALU.mult, [spk, k], [k])
            tt("dve", DEN, TA, TBt, ALU.add, [k], [k])
            P.op("dve", lambda e, DEN=DEN: e.reciprocal(out=DEN, in_=DEN), r=[k], w=[k])
            tt("dve", TA, NR, a_re, ALU.mult, [spk, k], [k])
            tt("dve", TBt, LBI, a_im, ALU.mult, [spk, k], [k])
            tt("dve", TA, TA, TBt, ALU.add, [k], [k])
            tt("dve", CR, TA, DEN, ALU.mult, [k], [k])
            tt("dve", TA, LBI, a_re, ALU.mult, [spk, k], [k])
            tt("dve", TBt, NR, a_im, ALU.mult, [spk, k], [k])
            tt("dve", TA, TA, TBt, ALU.subtract, [k], [k])
            tt("dve", CI, TA, DEN, ALU.mult, [k], [k])
            ts("dve", W2, CI, SGN[:, 0:1], None, ALU.mult, None, [k, "SGN"], [k])
            cp("dve", S5C[:, i, 0, :], PRn[:, 8, :], [k], ["S5C/%d" % i])
            ts("dve", S5C[:, i, 1, :], PIn[:, 8, :], SGN[:, 0:1], None, ALU.mult, None, [k, "SGN"], ["S5C/%d" % i])
            ts("dve", S5C[:, i, 2, :], PIn[:, 8, :], SGN[:, 0:1], -1.0, ALU.mult, ALU.mult, [k, "SGN"], ["S5C/%d" % i])

            def bc(v, g0):
                return v[:, g0:g0 + GBN].unsqueeze(2).to_broadcast([128, GBN, 16])

            for gb in range(64 // GBN):
                g0 = gb * GBN
                for q, (tile_, nm) in enumerate(((F1, "F1"), (F2, "F2"), (G1, "G1"), (G2, "G2"))):
                    dma("sp", tile_, s5f_d[i, q, :, g0 * 16:(g0 + GBN) * 16].rearrange("p (g n) -> p g n", n=16),
                        "s5f%d" % q, [], [nm])
                tt("dve", Q1, F1, bc(CR, g0), ALU.mult, ["F1", k], ["Q1"])
                tt("dve", Q2, F2, bc(W2, g0), ALU.mult, ["F2", k], ["Q2"])
                tt("dve", BBv, Q1, Q2, ALU.add, ["Q1", "Q2"], ["BB"])
                tt("dve", Q1, F2, bc(CR, g0), ALU.mult, ["F2", k], ["Q1"])
                tt("dve", Q2, F1, bc(W2, g0), ALU.mult, ["F1", k], ["Q2"])
                tt("dve", BBS, Q1, Q2, ALU.subtract, ["Q1", "Q2"], ["BBS"])
                mset("dve", EX[:, :, 128:240], 0.0, ["EX"])
                for j in range(8):
                    kk = 7 - j
                    tt("dve", Q1, BBv, bc(PRn[:, kk, :], g0), ALU.mult, ["BB", k], ["Q1"])
                    tt("dve", Q2, BBS, bc(PIn[:, kk, :], g0), ALU.mult, ["BBS", k], ["Q2"])
                    stt("dve", EX[:, :, j * 16:(j + 1) * 16], Q2, SGN[:, 0:1], Q1, ALU.mult, ALU.add,
                        ["Q1", "Q2", "SGN"], ["EX"])
                ts("dve", CST, G1, SGN[:, 0:1], -1.0, ALU.mult, ALU.mult, ["G1", "SGN"], ["CST"])
                cp("act", EXB[:], EX[:], ["EX"], ["EXB"])
                cp("dve", CSTB[:], CST[:], ["CST"], ["CSTB"])
                for gq in range(GBN // 4):
                    bank, bk = pm()
                    for gi in range(4):
                        gl = gq * 4 + gi
                        tr(bank[:, gi * 128:(gi + 1) * 128], EX[:, gl, 0:128], IDF[:], ["EX", "IDF"], [bk])
                    bv = bank.rearrange("p (g n) -> p g n", n=128)
                    cp("act", WAB[:, gq * 4:(gq + 1) * 4, :], bv, [bk], ["WAB"])
                    cp("dve", WASB[:, gq * 4:(gq + 1) * 4, 0:64], bv[:, :, 64:128], [bk], ["WASB"])
                    cp("dve", WASB[:, gq * 4:(gq + 1) * 4, 64:128], bv[:, :, 0:64], [bk], ["WASB"])
                    bank, bk = pm()
                    for gi in range(4):
                        gl = gq * 4 + gi
                        for l in range(8):
                            mm(bank[:, gi * 128 + l * 16: gi * 128 + (l + 1) * 16],
                               EXB[:, gl, (7 - l) * 16:(7 - l) * 16 + 128], CSTB[:, gl, :], True, True,
                               ["EXB", "CSTB"], [bk])
                        stt("dve", TBB[:, gl, :], IDF[:], DM[:, g0 + gl:g0 + gl + 1], bank[:, gi * 128:(gi + 1) * 128],
                            ALU.mult, ALU.add, [bk, "IDF", spk], ["TBB"])
                for l in range(8):
                    tt("dve", Q1, G1, bc(PRn[:, l + 1, :], g0), ALU.mult, ["G1", k], ["Q1"])
                    tt("dve", Q2, G2, bc(PIn[:, l + 1, :], g0), ALU.mult, ["G2", k], ["Q2"])
                    ts("dve", Q1, Q1, SGN[:, 0:1], -1.0, ALU.mult, ALU.mult, ["Q1", "SGN"], ["Q1"])
                    tt("dve", WCB[:, :, l * 16:(l + 1) * 16], Q1, Q2, ALU.subtract, ["Q1", "Q2"], ["WCB"])
                for q, (tile_, nm) in enumerate(((WAB, "WAB"), (WASB, "WASB"), (WCB, "WCB"), (TBB, "TBB"))):
                    dma("sp", s5m_d[i, q, :, g0 * 128:(g0 + GBN) * 128].rearrange("p (g n) -> p g n", n=128),
                        tile_, "s5mo%d" % q, [nm], ["s5m%d_%d" % (i, q)])

        def kview(ap):
            return ap.rearrange("(k p) n -> p k n", p=128)

        w_in_v = [kview(w_in_d[i]) for i in range(2)]
        w_ao_v = [kview(w_ao_d[i]) for i in range(2)]
        w_glu_v = [kview(w_glu_d[i]) for i in range(2)]
        w_o_v = [kview(w_o_d[i]) for i in range(2)]
        w_up_v = [kview(w_up_d[i]) for i in range(2)]
        w_dn_v = [kview(w_dn_d[i]) for i in range(2)]
        w_pg_v = [kview(w_pg_d[i]) for i in range(2)]
        w_pp_v = [kview(w_pp_d[i]) for i in range(2)]
        gxw_v = [gxw_d[i].rearrange("h i j -> i h j") for i in range(2)]
        gaw_v = [gaw_d[i].rearrange("h i j -> i h j") for i in range(2)]
        x_v = x_d.rearrange("(b t p) d -> b p t d", t=NT, p=128)
        y_v = y_d.rearrange("(b t p) d -> b p t d", t=NT, p=128)
        p_v = [p_d[i].rearrange("(b t p) d -> b p t d", t=NT, p=128) for i in range(2)]

        def norm(gcols, gkey):
            for t in range(NT):
                act(SQ[:], X[:, t, :], AF.Square, ["X/%d" % t], ["SQ", "ST/ss"], accum=ST[:, t:t + 1])
            ts("dve", ST[:, 4:8], ST[:, 0:4], 1.0 / 1024, EPS, ALU.mult, ALU.add, ["ST/ss"], ["ST/rs"])
            tt("pool", ST[:, 4:8], ST[:, 4:8], ST[:, 8:12], ALU.pow, ["ST/rs", "ST/nh"], ["ST/rs"])
            for t in range(NT):
                s = t % 2
                act(HS[:, s, :], X[:, t, :], AF.Copy, ["X/%d" % t, "ST/rs"], ["HS/%d" % s], scale=ST[:, 4 + t:5 + t])
                bank, bk = ptr()
                for kt in range(8):
                    tr(bank[:, kt * 128:(kt + 1) * 128], HS[:, s, kt * 128:(kt + 1) * 128], IDB[:],
                       ["HS/%d" % s, "IDB"], [bk])
                tt("dve", HT[:, :, t * 128:(t + 1) * 128], bank.rearrange("p (k n) -> p k n", n=128),
                   gcols.unsqueeze(2).to_broadcast([128, 8, 128]), ALU.mult, [bk, gkey], ["HT/%d" % t])

        HTall = ["HT/%d" % t for t in range(NT)]

        f6c = [0]

        def pm6():
            b = f6c[0] % 6
            f6c[0] += 1
            if b < 4:
                return PM[:, b, :], "pm/%d" % b
            return PX[:, b - 4, :], "px/%d" % (b - 4)

        def proj_fm(ws, wk, j, nk, rhs_of, rkeys, bankfn=None):
            bank, bk = (bankfn or pm)()
            for kt in range(nk):
                mm(bank, ws[:, kt, j * 128:(j + 1) * 128], rhs_of(kt), kt == 0, kt == nk - 1, [wk] + rkeys, [bk])
            return bank, bk

        def layer(i, blk):
            spk = "SPm/%d" % i
            norm(SPm[:, i, 0:8], spk)
            tap("ht1_%d_%d" % (i, blk), HT[:], [128, 8, TB], HTall)

            def win_chunk(c):
                ws, wk = wload(w_in_v[i][:, :, c * 512:(c + 1) * 512], 8, 512)
                for j in range(4):
                    t = 4 * c + j
                    bank, bk = proj_fm(ws, wk, j, 8, lambda kt: HT[:, kt, :], HTall)
                    if t < 10:
                        cp("act", XA[:, t, 3:515], bank, [bk], ["XA/%d" % t])
                    elif t < 20:
                        act(GBf[:, t - 10, :], bank, AF.Gelu_apprx_tanh, [bk], ["GB/%d" % (t - 10)])
                    elif t < 28:
                        raise AssertionError("ub goes through the token-chunk-major path")
                    elif t < 36:
                        act(SMA[:, t - 28, :], bank, AF.Sigmoid, [bk], ["SMA/%d" % (t - 28)])
                    else:
                        act(SMB[:, t - 36, :], bank, AF.Sigmoid, [bk], ["SMB/%d" % (t - 36)])

            for c in range(5):
                win_chunk(c)

            def lru_tile(t, wgx, wgxk, wga, wgak):
                q = t % 2
                base = q * 2816
                xc = LT[:, base:base + 512]
                gx = LT[:, base + 512:base + 1024]
                aa = LT[:, base + 1024:base + 1536]
                m2 = LT[:, base + 1536:base + 2048]
                hh = LT[:, base + 2048:base + 2560]
                xcb = LT[:, base + 2560:base + 2816].bitcast(BF16)
                lk = "LT/%d" % q
                xk = "XA/%d" % t
                cp("act", XA[:, t, 0:3], XAH[:, i, t, :], ["XAH/%d_%d" % (i, t)], [xk])
                cw = lambda j: SPm[:, i, 32 + t * 4 + j:33 + t * 4 + j]
                act(xc, XA[:, t, 0:512], AF.Identity, [xk, spk], [lk], bias=SPm[:, i, 72 + t:73 + t], scale=cw(0))
                for j in range(1, 4):
                    stt("dve", xc, XA[:, t, j:j + 512], cw(j), xc, ALU.mult, ALU.add, [xk, spk, lk], [lk])
                cp("act", XAH[:, i, t, :], XA[:, t, 512:515], [xk], ["XAH/%d_%d" % (i, t)])
                cp("act", xcb, xc, [lk], [lk])
                bx, bxk = PX[:, q, :], "px/%d" % q
                ba, bak = bx, bxk
                mm(bx, wgx[:, t, :], xcb, True, True, [wgxk, lk], [bxk])
                act(gx, bx, AF.Sigmoid, [bxk, spk], [lk], bias=SPm[:, i, 82 + t:83 + t])
                mm(ba, wga[:, t, :], xcb, True, True, [wgak, lk], [bak])
                act(hh, ba, AF.Sigmoid, [bak, spk], [lk], bias=SPm[:, i, 92 + t:93 + t])
                act(aa, hh, AF.Exp, [lk, "C8/%d" % i], [lk], scale=C8[:, i, t:t + 1])
                act(m2, hh, AF.Exp, [lk, "C8/%d" % i], [lk], scale=C8[:, i, 10 + t:11 + t])
                act(m2, m2, AF.Sqrt, [lk], [lk], bias=1.0, scale=-1.0)
                tt("dve", gx, gx, xc, ALU.mult, [lk], [lk])
                stt("dve", m2, m2, 1e-6, gx, ALU.max, ALU.mult, [lk], [lk])
                hk = "LRUH/%d_%d" % (i, t)
                P.op("dve", lambda e, hh=hh, aa=aa, m2=m2, t=t: e.tensor_tensor_scan(
                    out=hh, data0=aa, data1=m2, initial=LRUH[:, i, t:t + 1], op0=ALU.mult, op1=ALU.add),
                    r=[lk, hk], w=[lk], n=512)
                cp("act", LRUH[:, i, t:t + 1], hh[:, 511:512], [lk], [hk])
                tt("dve", OA[:, t, :], hh, GBf[:, t, :], ALU.mult, [lk, "GB/%d" % t], ["OA/%d" % t])

            L_lru = P.defer()
            dma("pool", WG[:, 0], gxw_v[i], "wg0", [], ["WG/0"])
            dma("pool", WG[:, 1], gaw_v[i], "wg1", [], ["WG/1"])
            wgx, wgxk, wga, wgak = WG[:, 0], "WG/0", WG[:, 1], "WG/1"
            for t in range(0, 10, 2):
                lru_tile(t, wgx, wgxk, wga, wgak)
            P.end_defer()
            L_lru_odd = P.defer()
            for t in range(1, 10, 2):
                lru_tile(t, wgx, wgxk, wga, wgak)
            P.end_defer()

            L_b = P.defer()
            for hh in range(2):
                ws, wk = wload(w_in_v[i][:, :, (5 + hh) * 512:(6 + hh) * 512], 8, 512)
                for l in range(8):
                    bank, bk = pm()
                    for kt in range(8):
                        mm(bank[0:64, :], HT[:, kt, l::8], ws[:, kt, :], kt == 0, kt == 7, [wk] + HTall, [bk])
                    cp("act" if l % 2 else "dve", UT[0:64, :, l, :], bank[0:64, :].rearrange("p (g c) -> p g c", c=16),
                       [bk], ["UT"])
                for gq in range(2):
                    tb, tbk = ptr()
                    tv = tb.rearrange("p (g c) -> p g c", c=64)
                    for gi in range(16):
                        tr(tv[:, gi, :], UT[0:64, gq * 16 + gi, :, :].rearrange("p l c -> p (l c)"), IDB[0:64, 0:64],
                           ["UT", "IDB"], [tbk])
                    g0 = hh * 32 + gq * 16
                    cp("act" if gq else "dve", U[:, g0:g0 + 16, :], tv, [tbk],
                       ["U/%d" % (g0 // 8), "U/%d" % (g0 // 8 + 1)])
            for h in range(2):
                for half in range(2):
                    ws, wk = wload(s5m_d[i, h, :, half * 4096:(half + 1) * 4096].rearrange("p (g n) -> p g n", n=128),
                                   32, 128, cast=False, rk=["s5m%d_%d" % (i, h)])
                    for j4 in range(4):
                        j = half * 4 + j4
                        bank, bk = pm()
                        bv = bank.rearrange("p (g c) -> p g c", c=64)
                        for gl in range(8):
                            mm(bv[:, gl, :], ws[:, j4 * 8 + gl, :], U[:, j * 8 + gl, :], True, True,
                               [wk, "U/%d" % j], [bk])
                        cp("act", S2[:, 1:65, h, j * 8:(j + 1) * 8].rearrange("p c g -> p g c"), bv, [bk],
                           ["S2/%d" % h])
            cp("dve", S2[:, 0, :, :], S5CAR[:, i, :, :], ["S5CAR/%d" % i], ["S2"])
            ArB = S5C[:, i, 0, :].unsqueeze(1).to_broadcast([128, 2, 64])
            sck = "S5C/%d" % i
            for c in range(64):
                if c % 16 == 0:
                    win_chunk(7 + c // 16)
                tt("dve", T1[:], S2[:, c, :, :], ArB, ALU.mult, ["S2", sck], ["T1"])
                tt("dve", T2[:, 0, :], S2[:, c, 1, :], S5C[:, i, 1, :], ALU.mult, ["S2", sck], ["T2"])
                tt("dve", T2[:, 1, :], S2[:, c, 0, :], S5C[:, i, 2, :], ALU.mult, ["S2", sck], ["T2"])
                tt("dve", T1[:], T1[:], T2[:], ALU.add, ["T1", "T2"], ["T1"])
                tt("dve", S2[:, c + 1, :, :], S2[:, c + 1, :, :], T1[:], ALU.add, ["S2", "T1"], ["S2"])
            cp("dve", S5CAR[:, i, :, :], S2[:, 64, :, :], ["S2"], ["S5CAR/%d" % i])
            cp("act", XB[:], S2[:, 0:64, 0, :].rearrange("p c g -> p g c"), ["S2"], ["XB"])
            P.end_defer()
            P.interleave([L_lru, L_lru_odd, L_b])
            tap("oa_%d_%d" % (i, blk), OA[:], [128, 10, TB], ["OA"])
            tap("xb_%d_%d" % (i, blk), XB[:], [128, 64, 64], ["XB"])
            dma("sp", PTOK[:], p_v[i][blk], "ptok", [], ["PTOK"])
            for half in range(2):
                wt_, wtk = wload(s5m_d[i, 3, :, half * 4096:(half + 1) * 4096].rearrange("p (g n) -> p g n", n=128),
                                 32, 128, cast=False, rk=["s5m%d_3" % i])
                wc_, wck = wload(s5m_d[i, 2, :, half * 4096:(half + 1) * 4096].rearrange("p (g n) -> p g n", n=128),
                                 32, 128, cast=False, rk=["s5m%d_2" % i])
                for j4 in range(4):
                    j = half * 4 + j4
                    bank, bk = pm()
                    bv = bank.rearrange("p (g c) -> p g c", c=64)
                    for gl in range(8):
                        g = j * 8 + gl
                        mm(bv[:, gl, :], wt_[:, j4 * 8 + gl, :], U[:, g, :], True, False, [wtk, "U/%d" % j], [bk])
                        mm(bv[:, gl, :], wc_[:, j4 * 8 + gl, :], XB[:, g, :], False, True, [wck, "XB"], [bk])
                    cp("act", YSB[:, j * 8:(j + 1) * 8, :], bv, [bk], ["YSB/%d" % j])
            for j in range(8):
                tb, tbk = ptr()
                tv = tb[0:64, :].rearrange("p (g n) -> p g n", n=128)
                for gl in range(8):
                    tr(tv[:, gl, :], YSB[:, j * 8 + gl, :], IDB[:], ["YSB/%d" % j, "IDB"], [tbk])
                sl = j % 2
                cp("dve" if j % 2 else "act", YT[0:64, sl, :].rearrange("p (l g c) -> p g l c", l=8, g=8),
                   tb[0:64, :].rearrange("p (g l c) -> p g l c", g=8, l=8), [tbk], ["YT/%d" % sl])
                tb2, tb2k = ptr()
                for l in range(8):
                    tr(tb2[:, l * 64:(l + 1) * 64], YT[0:64, sl, l * 128:(l + 1) * 128], IDB[0:64, 0:64],
                       ["YT/%d" % sl, "IDB"], [tb2k])
                act(YS[:, j, :].rearrange("p (c l) -> p c l", l=8), tb2[:, 0:512].rearrange("p (l c) -> p c l", l=8),
                    AF.Gelu_apprx_tanh, [tb2k], ["YS/%d" % j])
            tap("ys_%d_%d" % (i, blk), YS[:], [128, 8, TB], ["YS"])
            YSall = ["YS/%d" % j for j in range(8)]
            for c in (0, 2, 1, 3):
                ws, wk = wload(w_glu_v[i][:, :, c * 512:(c + 1) * 512], 8, 512)
                for j in range(4):
                    t = 4 * c + j
                    bank, bk = proj_fm(ws, wk, j, 8, lambda kt: YS[:, kt, :], YSall)
                    bcol = SPm[:, i, 112 + t:113 + t]
                    if t < 8:
                        act(YB[:, t, :], bank, AF.Identity, [bk, spk], ["YB/%d" % t], bias=bcol)
                    else:
                        s = t % 2
                        n = t - 8
                        act(TMP[:, s, :], bank, AF.Sigmoid, [bk, spk], ["TMP/%d" % s], bias=bcol)
                        tt("dve", TMP[:, s, :], TMP[:, s, :], SMB[:, n, :], ALU.mult,
                           ["TMP/%d" % s, "SMB/%d" % n], ["TMP/%d" % s])
                        tt("dve", YB[:, n, :], YB[:, n, :], TMP[:, s, :], ALU.mult,
                           ["YB/%d" % n, "TMP/%d" % s], ["YB/%d" % n])
            OAall = ["OA/%d" % t for t in range(10)]
            for c in range(4):
                ws, wk = wload(w_ao_v[i][:, :, c * 256:(c + 1) * 256], 10, 256)
                for j in range(2):
                    n = 2 * c + j
                    bank, bk = proj_fm(ws, wk, j, 10, lambda kt: OA[:, kt, :], OAall)
                    s = n % 2
                    tt("dve", TMP[:, s, :], bank, SMA[:, n, :], ALU.mult, [bk, "SMA/%d" % n], ["TMP/%d" % s])
                    tt("dve", MGB[:, n, :], TMP[:, s, :], YB[:, n, :], ALU.add, ["TMP/%d" % s, "YB/%d" % n],
                       ["MGB/%d" % n])
            tap("mg_%d_%d" % (i, blk), MGB[:], [128, 8, TB], ["MGB"])
            MGall = ["MGB/%d" % n for n in range(8)]
            for h in range(2):
                ws, wk = wload(w_o_v[i][:, :, h * 512:(h + 1) * 512], 8, 512)
                for t in range(NT):
                    bank, bk = pm()
                    for kt in range(8):
                        mm(bank, MGB[:, kt, t * 128:(t + 1) * 128], ws[:, kt, :], kt == 0, kt == 7, [wk] + MGall, [bk])
                    tt("dve", X[:, t, h * 512:(h + 1) * 512], bank, X[:, t, h * 512:(h + 1) * 512], ALU.add,
                       [bk, "X/%d" % t], ["X/%d" % t])
            tap("x1_%d_%d" % (i, blk), X[:], [128, NT, 1024], ["X"])

            norm(SPm[:, i, 8:16], spk)

            def ffn_tile(ws, wk, c, j, half):
                t = 4 * c + j
                o = t % 24
                q = (2 * half + (j % 2))
                bank, bk = proj_fm(ws, wk, j, 8, lambda kt: HT[:, kt, :], HTall)
                ck = "CV/%d" % q
                ftk = "FT/%d_%d" % (i, t)
                fw_ = lambda jj: SPm[:, i, 128 + t * 3 + jj:129 + t * 3 + jj]
                bcol = SPm[:, i, 272 + t:273 + t]
                act(CV[:, q, 2:512], bank[:, 0:510], AF.Identity, [bk, spk], [ck], bias=bcol, scale=fw_(0))
                cp("act", CV[:, q, 0:2], HC[:, t, :], ["HC"], [ck])
                stt("dve", CV[:, q, 1:512], bank[:, 0:511], fw_(1), CV[:, q, 1:512], ALU.mult, ALU.add,
                    [bk, spk, ck], [ck])
                stt("dve", CV[:, q, :], bank, fw_(2), CV[:, q, :], ALU.mult, ALU.add, [bk, spk, ck], [ck])
                cp("act", FT[:, i, t, :], bank[:, 510:512], [bk], [ftk])
                if half == 0:
                    act(ACTB[:, o, :], CV[:, q, :], AF.Gelu_apprx_tanh, [ck], ["ACTB/%d" % o])
                else:
                    tt("dve", ACTB[:, o, :], ACTB[:, o, :], CV[:, q, :], ALU.mult, ["ACTB/%d" % o, ck],
                       ["ACTB/%d" % o])

            HC = HCt[:, 0:96].rearrange("p (t k) -> p t k", k=2)
            HTM = HCt[:, 96:144]
            FTall = ["FT/%d_%d" % (i, t) for t in range(48)]
            W0 = SPm[:, i, 128:272:3]
            W1 = SPm[:, i, 129:272:3]
            BC = SPm[:, i, 272:320]
            tt("dve", HC[:, :, 0], FT[:, i, :, 0], W0, ALU.mult, FTall + [spk], ["HC"])
            tt("dve", HTM, FT[:, i, :, 1], W1, ALU.mult, FTall + [spk], ["HTM"])
            tt("dve", HC[:, :, 0], HC[:, :, 0], HTM, ALU.add, ["HC", "HTM"], ["HC"])
            tt("dve", HC[:, :, 0], HC[:, :, 0], BC, ALU.add, ["HC", spk], ["HC"])
            tt("dve", HC[:, :, 1], FT[:, i, :, 1], W0, ALU.mult, FTall + [spk], ["HC"])
            tt("dve", HC[:, :, 1], HC[:, :, 1], BC, ALU.add, ["HC", spk], ["HC"])
            for c6 in range(6):
                ws0, wk0 = wload(w_up_v[i][:, :, c6 * 512:(c6 + 1) * 512], 8, 512)
                ws1, wk1 = wload(w_up_v[i][:, :, (c6 + 6) * 512:(c6 + 7) * 512], 8, 512)
                for j in range(4):
                    lists = [P.defer()]
                    ffn_tile(ws0, wk0, c6, j, 0)
                    P.end_defer()
                    lists.append(P.defer())
                    ffn_tile(ws1, wk1, c6 + 6, j, 1)
                    P.end_defer()
                    P.interleave(lists)
            tap("actb_%d_%d" % (i, blk), ACTB[:], [128, 24, TB], ["ACTB"])
            ACall = ["ACTB/%d" % o for o in range(24)]
            for h in range(2):
                banks = [pm() for _ in range(NT)]
                for kc in range(3):
                    ws, wk = wload(w_dn_v[i][:, kc * 8:(kc + 1) * 8, h * 512:(h + 1) * 512], 8, 512)
                    for t in range(NT):
                        bank, bk = banks[t]
                        for k8 in range(8):
                            mm(bank, ACTB[:, kc * 8 + k8, t * 128:(t + 1) * 128], ws[:, k8, :],
                               kc == 0 and k8 == 0, kc == 2 and k8 == 7, [wk] + ACall, [bk])
                for t in range(NT):
                    bank, bk = banks[t]
                    tt("dve", X[:, t, h * 512:(h + 1) * 512], bank, X[:, t, h * 512:(h + 1) * 512], ALU.add,
                       [bk, "X/%d" % t], ["X/%d" % t])
            tap("x2_%d_%d" % (i, blk), X[:], [128, NT, 1024], ["X"])

            norm(SPm[:, i, 16:24], spk)
            cp("dve", PB[:], PTOK[:], ["PTOK"], ["PB"])
            bank, bk = ptr()
            bv = bank.rearrange("p (k n) -> p k n", n=TB)
            for t in range(NT):
                for kk in range(2):
                    tr(bv[:, kk, t * 128:(t + 1) * 128], PB[:, t, kk * 128:(kk + 1) * 128], IDB[:], ["PB", "IDB"], [bk])
            cp("act", PTT[:], bv, [bk], ["PTT"])
            for h in range(2):
                wg, wgk = wload(w_pg_v[i][:, :, h * 512:(h + 1) * 512], 8, 512)
                wp, wpk = wload(w_pp_v[i][:, :, h * 512:(h + 1) * 512], 2, 512)
                for t in range(NT):
                    bg, bgk = pm()
                    for kt in range(8):
                        mm(bg, HT[:, kt, t * 128:(t + 1) * 128], wg[:, kt, :], kt == 0, kt == 7, [wgk, "HT/%d" % t], [bgk])
                    bp, bpk = px()
                    for kk in range(2):
                        mm(bp, PTT[:, kk, t * 128:(t + 1) * 128], wp[:, kk, :], kk == 0, kk == 1, [wpk, "PTT"], [bpk])
                    s = t % 2
                    act(TMP[:, s, :], bg, AF.Sigmoid, [bgk], ["TMP/%d" % s])
                    tt("dve", TMP[:, s, :], TMP[:, s, :], bp, ALU.mult, ["TMP/%d" % s, bpk], ["TMP/%d" % s])
                    tt("dve", X[:, t, h * 512:(h + 1) * 512], TMP[:, s, :], X[:, t, h * 512:(h + 1) * 512], ALU.add,
                       ["TMP/%d" % s, "X/%d" % t], ["X/%d" % t])
            tap("x3_%d_%d" % (i, blk), X[:], [128, NT, 1024], ["X"])

        for t in range(NT):
            dma("sp", X[:, t, :], x_v[0][:, t, :], "xin%d" % t, [], ["X/%d" % t])
        for blk in range(NBLK):
            for i in range(NLAY):
                layer(i, blk)
            if final:
                for t in range(NT):
                    act(SQ[:], X[:, t, :], AF.Square, ["X/%d" % t], ["SQ", "ST/ss"], accum=ST[:, t:t + 1])
                ts("dve", ST[:, 4:8], ST[:, 0:4], 1.0 / 1024, EPS, ALU.mult, ALU.add, ["ST/ss"], ["ST/rs"])
                tt("pool", ST[:, 4:8], ST[:, 4:8], ST[:, 8:12], ALU.pow, ["ST/rs", "ST/nh"], ["ST/rs"])
            for t in range(NT):
                if final:
                    stt("dve", X[:, t, :], X[:, t, :], ST[:, 4 + t:5 + t], GF[:], ALU.mult, ALU.mult,
                        ["X/%d" % t, "ST/rs", "GF"], ["X/%d" % t])
                dma("sp", y_v[blk][:, t, :], X[:, t, :], "yout%d" % t, ["X/%d" % t], ["ydram/%d_%d" % (blk, t)])
                if blk + 1 < NBLK:
                    dma("sp", X[:, t, :], x_v[blk + 1][:, t, :], "xin%d" % t, [], ["X/%d" % t])

        with nc.allow_non_contiguous_dma(reason="small strided parameter loads"):
            info = P.emit()
    return nc, dbg_out, info


def _cols(v):
    return np.ascontiguousarray(v.reshape(-1, 128).T)


def host_layout(inp):
    f = np.float32
    spk = np.zeros((2, 128, SPC), f)
    s5f = np.zeros((2, 4, 128, 1024), f)
    for i in range(2):
        s = spk[i]
        s[:, 0:8] = _cols(inp["g_mix"][i])
        s[:, 8:16] = _cols(inp["g_ffn"][i])
        s[:, 16:24] = _cols(inp["g_ple"][i])
        s[:, 24:32] = _cols(inp["g_final"])
        caw = inp["conv_a_w"][i]
        for j in range(4):
            s[:, 32 + j:72:4] = _cols(caw[j])
        s[:, 72:82] = _cols(inp["conv_a_b"][i])
        s[:, 82:92] = _cols(inp["gate_x_b"][i])
        s[:, 92:102] = _cols(inp["gate_a_b"][i])
        s[:, 102:112] = _cols(inp["lru_lambda"][i])
        s[:, 112:128] = _cols(inp["b_glu"][i])
        cfw = inp["conv_f_w"][i]
        for j in range(3):
            s[:, 128 + j:272:3] = _cols(cfw[j])
        s[:, 272:320] = _cols(inp["conv_f_b"][i])
        are_t = inp["s5_a_re"][i].T
        aim_t = inp["s5_a_im"][i].T
        s[:, 320:384] = np.concatenate([are_t, are_t], 0)
        s[:, 384:448] = np.concatenate([aim_t, aim_t], 0)
        s[:, 448:512] = np.broadcast_to(inp["s5_log_dt"][i][None, :], (128, 64))
        dm = inp["s5_d"][i].reshape(64, 16).T
        s[:, 512:576] = np.tile(dm, (8, 1))
        bre = inp["s5_b_re"][i].transpose(1, 0, 2).reshape(64, 1024)
        bim = inp["s5_b_im"][i].transpose(1, 0, 2).reshape(64, 1024)
        cre = inp["s5_c_re"][i].transpose(2, 0, 1).reshape(64, 1024)
        cim = inp["s5_c_im"][i].transpose(2, 0, 1).reshape(64, 1024)
        s5f[i, 0] = np.concatenate([bre, bim], 0)
        s5f[i, 1] = np.concatenate([bim, bre], 0)
        s5f[i, 2] = np.concatenate([cre, cim], 0)
        s5f[i, 3] = np.concatenate([cim, cre], 0)
    cst = np.zeros((128, 128 + 1920 + 1 + 1024), f)
    cst[:, 0:128] = np.eye(128, dtype=f)
    z = np.zeros((128, 8, 240), f)
    for gl in range(8):
        for ch in range(16):
            z[gl * 16 + ch, gl, 112 + ch] = 1.0
    cst[:, 128:2048] = z.reshape(128, 1920)
    cst[0:64, 2048] = -1.0
    cst[64:128, 2048] = 1.0
    cst[:, 2049:] = np.broadcast_to(inp["g_final"][None, :], (128, 1024))
    shared = {"spk": spk, "s5f": s5f, "cst": cst}
    for k in ("w_in", "gate_x_w", "gate_a_w", "w_a_out", "w_glu", "w_o", "w_up", "w_down",
              "w_ple_gate", "w_ple_proj"):
        shared[k] = np.ascontiguousarray(inp[k], dtype=f)
    return shared


_CACHE = {}


def kernel(**inputs):
    inp = {k: np.asarray(v) for k, v in inputs.items()}
    shared = host_layout(inp)
    if "nc" not in _CACHE:
        _CACHE["nc"] = build()[0]
    nc = _CACHE["nc"]
    in_maps = []
    for b in range(8):
        m = dict(shared)
        m["x"] = np.ascontiguousarray(inp["x"][b], dtype=np.float32)
        m["p"] = np.ascontiguousarray(inp["p"][:, b], dtype=np.float32)
        in_maps.append(m)
    res = run_bass_kernel_spmd(nc, in_maps, core_ids=list(range(8)))
    return np.stack([np.asarray(r["y"], dtype=np.float32) for r in res.results], 0)
```
